# Optimizing a Trainium2 kernel written in Bass

```python
import math
import jax, jax.numpy as jnp
from jax import lax
import numpy as np

D_MODEL = 1024
BATCH = 4
SEQ = 4096
DEPTH = 4
DEC_BATCH = 128
DEC_SEQ = 8
PAST_LEN = 2048
PAGE_SIZE = 128

N_MIXERS = 3
LAYER_KINDS = tuple(i % N_MIXERS for i in range(DEPTH))
LAYER_SLOT = tuple(LAYER_KINDS[:i].count(LAYER_KINDS[i]) for i in range(DEPTH))
N_SSM = LAYER_KINDS.count(0)
N_ATTN = LAYER_KINDS.count(1)
N_MLSTM = LAYER_KINDS.count(2)

EXPAND = 2
D_INNER = EXPAND * D_MODEL
RMS_EPS = 1e-6

SSM_GROUP = 16
SSM_GROUPS = D_INNER // SSM_GROUP
SSM_STATE = 64
DT_MIN = 1e-3
DT_MAX = 1e-1

N_HEADS = 32
HEAD_DIM = D_INNER // N_HEADS
KV_HEADS = 4
GQA = N_HEADS // KV_HEADS
KV_WIDTH = KV_HEADS * HEAD_DIM
IDX_HEADS = 8
IDX_DIM = 64
TOPK = 256
Q_BLOCK = 128
ROPE_THETA = 10000.0
ATTN_IN = 2 * D_INNER + 2 * KV_WIDTH + IDX_HEADS * IDX_DIM + IDX_DIM + IDX_HEADS

M_HEADS = 8
M_HEAD_DIM = D_INNER // M_HEADS
CONV_W = 4
M_CHUNK = 64
LN_EPS = 1e-5

kernel_name = 'hybrid_s5_dsa_mlstm_step'


def rms_norm(x, g):
    xf = x.astype(jnp.float32)
    y = xf * lax.rsqrt(jnp.mean(xf * xf, axis=-1, keepdims=True) + RMS_EPS)
    return (y * g.astype(jnp.float32)).astype(x.dtype)


def rope(x, pos):
    half = x.shape[-1] // 2
    inv = ROPE_THETA ** (-jnp.arange(half, dtype=jnp.float32) / half)
    ang = pos.astype(jnp.float32)[:, None] * inv[None, :]
    cos = jnp.cos(ang)[:, None, :]
    sin = jnp.sin(ang)[:, None, :]
    xf = x.astype(jnp.float32)
    x1, x2 = xf[..., :half], xf[..., half:]
    return jnp.concatenate([x1 * cos - x2 * sin, x2 * cos + x1 * sin], axis=-1).astype(x.dtype)


def gather_pages(pool, page_table):
    rows = pool[page_table]
    return rows.reshape((rows.shape[0], rows.shape[1] * rows.shape[2]) + rows.shape[3:])


def complex_linear_combine(e1, e2):
    a1r, a1i, b1r, b1i = e1
    a2r, a2i, b2r, b2i = e2
    return (a2r * a1r - a2i * a1i, a2r * a1i + a2i * a1r,
            a2r * b1r - a2i * b1i + b2r, a2r * b1i + a2i * b1r + b2i)


def s5_ssm(u, x0_re, x0_im, a_re, a_im, log_dt, b_re, b_im, c_re, c_im, d_skip):
    f32 = jnp.float32
    n, t, e = u.shape
    ar, ai = a_re.astype(f32), a_im.astype(f32)
    dt = jnp.exp(log_dt.astype(f32))[:, None]
    mag = jnp.exp(dt * ar)
    abar_re, abar_im = mag * jnp.cos(dt * ai), mag * jnp.sin(dt * ai)
    den = ar * ar + ai * ai
    zr, zi = abar_re - 1.0, abar_im
    fr, fi = (zr * ar + zi * ai) / den, (zi * ar - zr * ai) / den
    br, bi = b_re.astype(f32), b_im.astype(f32)
    bbr = fr[..., None] * br - fi[..., None] * bi
    bbi = fr[..., None] * bi + fi[..., None] * br
    cr, ci = c_re.astype(f32), c_im.astype(f32)
    dd = d_skip.astype(f32).reshape(SSM_GROUPS, SSM_GROUP)

    def one_seq(args):
        us, sr0, si0 = args
        bur = jnp.einsum('tgj,gpj->tgp', us, bbr)
        bui = jnp.einsum('tgj,gpj->tgp', us, bbi)
        bur = bur.at[0].add(abar_re * sr0 - abar_im * si0)
        bui = bui.at[0].add(abar_re * si0 + abar_im * sr0)
        aar = jnp.broadcast_to(abar_re, bur.shape)
        aai = jnp.broadcast_to(abar_im, bur.shape)
        _, _, xr, xi = lax.associative_scan(complex_linear_combine, (aar, aai, bur, bui), axis=0)
        y = (jnp.einsum('tgp,gjp->tgj', xr, cr) - jnp.einsum('tgp,gjp->tgj', xi, ci) + dd * us)
        return y, xr[-1], xi[-1]

    us = u.astype(f32).reshape(n, t, SSM_GROUPS, SSM_GROUP)
    y, sr, si = lax.map(one_seq, (us, x0_re.astype(f32), x0_im.astype(f32)))
    return y.reshape(n, t, e), sr, si


def ssm_mixer(xn, x0_re, x0_im, w_in, a_re, a_im, log_dt, b_re, b_im, c_re, c_im, d_skip, w_glu, b_glu, w_out):
    f32 = jnp.float32
    uz = xn @ w_in
    u, z = uz[..., :D_INNER], uz[..., D_INNER:]
    y, s_re, s_im = s5_ssm(u, x0_re, x0_im, a_re, a_im, log_dt, b_re, b_im, c_re, c_im, d_skip)
    g = jax.nn.gelu(y)
    g = g * jax.nn.sigmoid(g @ w_glu.astype(f32) + b_glu.astype(f32))
    out = (g * jax.nn.silu(z.astype(f32))).astype(xn.dtype) @ w_out
    return out, s_re, s_im


def dsa_attend(q, k, v, qi, ki, wi, pos):
    n, t = q.shape[0], q.shape[1]
    l = k.shape[1]
    n_sel = min(TOPK, l // 4)
    blk = Q_BLOCK if t % Q_BLOCK == 0 else t
    nb = t // blk
    key_pos = jnp.arange(l, dtype=jnp.int32)
    ki32 = ki.astype(jnp.float32)

    def to_blocks(a):
        return jnp.moveaxis(a.reshape((n, nb, blk) + a.shape[2:]), 1, 0)

    def block(args):
        qb, qib, wib, pb = args
        si = jnp.einsum('nqhd,nld->nqhl', qib.astype(jnp.float32), ki32) * IDX_DIM ** -0.5
        si = jnp.einsum('nqh,nqhl->nql', wib.astype(jnp.float32), jax.nn.relu(si))
        si = jnp.where((key_pos[None, :] <= pb[:, None])[None], si, -jnp.inf)
        _, sel = lax.top_k(si, n_sel)
        valid = sel <= pb[None, :, None]
        kg = jax.vmap(lambda kk, ii: kk[ii])(k, sel)
        vg = jax.vmap(lambda vv, ii: vv[ii])(v, sel)
        s = jnp.einsum('nqhgd,nqkhd->nqhgk', qb, kg, preferred_element_type=jnp.float32) * HEAD_DIM ** -0.5
        s = jnp.where(valid[:, :, None, None, :], s, -jnp.inf)
        p = jax.nn.softmax(s, axis=-1)
        return jnp.einsum('nqhgk,nqkhd->nqhgd', p.astype(vg.dtype), vg)

    ob = lax.map(block, (to_blocks(q), to_blocks(qi), to_blocks(wi), pos.reshape(nb, blk)))
    return jnp.moveaxis(ob, 0, 1).reshape(n, t, N_HEADS * HEAD_DIM)


def dsa_mixer(xn, pos0, past_k, past_v, past_kidx, w_in, w_out):
    n, t, _ = xn.shape
    proj = xn @ w_in
    o1 = D_INNER
    o2 = o1 + KV_WIDTH
    o3 = o2 + KV_WIDTH
    o4 = o3 + D_INNER
    o5 = o4 + IDX_HEADS * IDX_DIM
    o6 = o5 + IDX_DIM
    pos = pos0 + jnp.arange(t, dtype=jnp.int32)
    q = rope(proj[..., :o1].reshape(n, t, N_HEADS, HEAD_DIM), pos)
    k = rope(proj[..., o1:o2].reshape(n, t, KV_HEADS, HEAD_DIM), pos)
    v = proj[..., o2:o3].reshape(n, t, KV_HEADS, HEAD_DIM)
    z = proj[..., o3:o4]
    qi = rope(proj[..., o4:o5].reshape(n, t, IDX_HEADS, IDX_DIM), pos)
    ki = rope(proj[..., o5:o6].reshape(n, t, 1, IDX_DIM), pos)[:, :, 0]
    wi = proj[..., o6:] * IDX_HEADS ** -0.5
    if past_k is None:
        k_all, v_all, ki_all = k, v, ki
    else:
        k_all = jnp.concatenate([past_k.astype(k.dtype), k], axis=1)
        v_all = jnp.concatenate([past_v.astype(v.dtype), v], axis=1)
        ki_all = jnp.concatenate([past_kidx.astype(ki.dtype), ki], axis=1)
    o = dsa_attend(q.reshape(n, t, KV_HEADS, GQA, HEAD_DIM), k_all, v_all, qi, ki_all, wi, pos)
    out = (o * jax.nn.silu(z)).astype(xn.dtype) @ w_out
    return out, k, v, ki


def mlstm_chunked(q, k, v, ig, lf, c0, n0, m0):
    f32 = jnp.float32
    n, h, t, d = q.shape
    L = M_CHUNK if t % M_CHUNK == 0 else t
    nc = t // L

    def to_chunks(a):
        return jnp.moveaxis(a.reshape((n, h, nc, L) + a.shape[3:]), 2, 0)

    causal = jnp.tril(jnp.ones((L, L), dtype=bool))

    def step(carry, xs):
        c, nv, m = carry
        qc, kc, vc, ic, fc = xs
        b = jnp.cumsum(fc, axis=-1)
        dmat = jnp.where(causal, b[..., :, None] - b[..., None, :] + ic[..., None, :], -jnp.inf)
        inter = b + m[..., None]
        mrow = jnp.maximum(jnp.max(dmat, axis=-1), inter)
        wmat = jnp.exp(dmat - mrow[..., None])
        winter = jnp.exp(inter - mrow)
        s = jnp.einsum('nhtd,nhsd->nhts', qc, kc) * wmat
        num = jnp.einsum('nhts,nhsd->nhtd', s, vc) + winter[..., None] * jnp.einsum('nhtd,nhde->nhte', qc, c)
        den = jnp.sum(s, axis=-1) + winter * jnp.einsum('nhtd,nhd->nht', qc, nv)
        hout = num / jnp.maximum(jnp.abs(den), jnp.exp(-mrow))[..., None]
        btot = b[..., -1]
        g = btot[..., None] - b + ic
        m_new = jnp.maximum(btot + m, jnp.max(g, axis=-1))
        wst = jnp.exp(g - m_new[..., None])
        wprev = jnp.exp(btot + m - m_new)
        c_new = wprev[..., None, None] * c + jnp.einsum('nhs,nhsd,nhse->nhde', wst, kc, vc)
        n_new = wprev[..., None] * nv + jnp.einsum('nhs,nhsd->nhd', wst, kc)
        return (c_new, n_new, m_new), hout

    (c1, n1, m1), hs = lax.scan(step, (c0.astype(f32), n0.astype(f32), m0.astype(f32)),
                                (to_chunks(q), to_chunks(k), to_chunks(v), to_chunks(ig), to_chunks(lf)))
    hs = jnp.moveaxis(hs, 0, 2).reshape(n, h, t, d)
    return hs, c1, n1, m1


def mlstm_mixer(xn, c0, n0, m0, conv0, w_in, conv_w, conv_b, w_q, w_k, w_v, w_o, b_o,
                w_gates, b_gates, ln_w, skip, w_out):
    f32 = jnp.float32
    n, t, _ = xn.shape
    uz = xn @ w_in
    u, z = uz[..., :D_INNER], uz[..., D_INNER:]
    u_ext = jnp.concatenate([conv0.astype(u.dtype), u], axis=1)
    cw = conv_w.astype(f32)
    acc = conv_b.astype(f32) + cw[0] * u_ext[:, 0:t].astype(f32)
    for j in range(1, CONV_W):
        acc = acc + cw[j] * u_ext[:, j:j + t].astype(f32)
    ca = jax.nn.silu(acc)
    ch = ca.reshape(n, t, M_HEADS, M_HEAD_DIM)
    uh = u.astype(f32).reshape(n, t, M_HEADS, M_HEAD_DIM)
    q = jnp.einsum('nthd,hde->nthe', ch, w_q.astype(f32))
    k = jnp.einsum('nthd,hde->nthe', ch, w_k.astype(f32))
    v = jnp.einsum('nthd,hde->nthe', uh, w_v.astype(f32))
    o = jax.nn.sigmoid(jnp.einsum('nthd,hde->nthe', uh, w_o.astype(f32))
                       + b_o.astype(f32).reshape(M_HEADS, M_HEAD_DIM))
    qkv = jnp.concatenate([q.reshape(n, t, D_INNER), k.reshape(n, t, D_INNER), v.reshape(n, t, D_INNER)], axis=-1)
    gates = qkv @ w_gates.astype(f32) + b_gates.astype(f32)
    ig = jnp.swapaxes(gates[..., :M_HEADS], 1, 2)
    lf = jnp.swapaxes(jax.nn.log_sigmoid(gates[..., M_HEADS:]), 1, 2)
    hh, c1, n1, m1 = mlstm_chunked(jnp.swapaxes(q, 1, 2), jnp.swapaxes(k, 1, 2) * M_HEAD_DIM ** -0.5,
                                   jnp.swapaxes(v, 1, 2), ig, lf, c0, n0, m0)
    hh = jnp.swapaxes(hh, 1, 2)
    mu = jnp.mean(hh, axis=-1, keepdims=True)
    var = jnp.mean(jnp.square(hh - mu), axis=-1, keepdims=True)
    hh = o * (hh - mu) * lax.rsqrt(var + LN_EPS) * ln_w.astype(f32).reshape(M_HEADS, M_HEAD_DIM)
    hh = hh.reshape(n, t, D_INNER) + skip.astype(f32) * ca
    out = (hh * jax.nn.silu(z.astype(f32))).astype(xn.dtype) @ w_out
    return out, c1, n1, m1, u_ext[:, -(CONV_W - 1):]


def setup_inputs(seed: int = 0) -> dict:
    key = jax.random.key(seed)
    keys = iter(jax.random.split(key, 64))
    f32 = jnp.float32

    def nrm(shape, scale):
        return jax.random.normal(next(keys), shape, f32) * scale

    n_pages = PAST_LEN // PAGE_SIZE
    n_used = DEC_BATCH * n_pages
    n_pool = n_used + n_used // 4
    E, G, P, J = D_INNER, SSM_GROUPS, SSM_STATE, SSM_GROUP
    MH, MD = M_HEADS, M_HEAD_DIM
    page_table = jax.random.permutation(next(keys), n_pool)[:n_used].reshape(DEC_BATCH, n_pages).astype(jnp.int32)
    a_im = jnp.broadcast_to(math.pi * jnp.arange(P, dtype=f32), (N_SSM, G, P))
    gate_base = jnp.concatenate([jnp.zeros((MH,), f32), jnp.linspace(3.0, 6.0, MH, dtype=f32)])
    return {
        'x_prompt': nrm((BATCH, SEQ, D_MODEL), 1.0),
        'x_sample': nrm((DEC_BATCH, DEC_SEQ, D_MODEL), 1.0),
        'state_ssm_re': nrm((N_SSM, DEC_BATCH, G, P), 0.5),
        'state_ssm_im': nrm((N_SSM, DEC_BATCH, G, P), 0.5),
        'cache_k': nrm((N_ATTN, n_pool, PAGE_SIZE, KV_HEADS, HEAD_DIM), 1.0),
        'cache_v': nrm((N_ATTN, n_pool, PAGE_SIZE, KV_HEADS, HEAD_DIM), 1.0),
        'cache_kidx': nrm((N_ATTN, n_pool, PAGE_SIZE, IDX_DIM), 1.0),
        'page_table': page_table,
        'state_mlstm_c': nrm((N_MLSTM, DEC_BATCH, MH, MD, MD), 0.1),
        'state_mlstm_n': nrm((N_MLSTM, DEC_BATCH, MH, MD), 1.0),
        'state_mlstm_m': nrm((N_MLSTM, DEC_BATCH, MH), 0.5),
        'state_mlstm_conv': nrm((N_MLSTM, DEC_BATCH, CONV_W - 1, E), 1.0),
        'final_norm': 1.0 + nrm((D_MODEL,), 0.02),
        'ssm_norm': 1.0 + nrm((N_SSM, D_MODEL), 0.02),
        'ssm_w_in': nrm((N_SSM, D_MODEL, 2 * E), D_MODEL ** -0.5),
        'ssm_a_re': -0.5 + nrm((N_SSM, G, P), 0.01),
        'ssm_a_im': a_im,
        'ssm_log_dt': jax.random.uniform(next(keys), (N_SSM, G), f32, math.log(DT_MIN), math.log(DT_MAX)),
        'ssm_b_re': nrm((N_SSM, G, P, J), (2 * J) ** -0.5),
        'ssm_b_im': nrm((N_SSM, G, P, J), (2 * J) ** -0.5),
        'ssm_c_re': nrm((N_SSM, G, J, P), (2 * P) ** -0.5),
        'ssm_c_im': nrm((N_SSM, G, J, P), (2 * P) ** -0.5),
        'ssm_d': nrm((N_SSM, E), 0.5),
        'ssm_w_glu': nrm((N_SSM, E, E), E ** -0.5),
        'ssm_b_glu': nrm((N_SSM, E), 0.01),
        'ssm_w_out': nrm((N_SSM, E, D_MODEL), E ** -0.5),
        'attn_norm': 1.0 + nrm((N_ATTN, D_MODEL), 0.02),
        'attn_w_in': nrm((N_ATTN, D_MODEL, ATTN_IN), D_MODEL ** -0.5),
        'attn_w_out': nrm((N_ATTN, E, D_MODEL), E ** -0.5),
        'mlstm_norm': 1.0 + nrm((N_MLSTM, D_MODEL), 0.02),
        'mlstm_w_in': nrm((N_MLSTM, D_MODEL, 2 * E), D_MODEL ** -0.5),
        'mlstm_conv_w': nrm((N_MLSTM, CONV_W, E), CONV_W ** -0.5),
        'mlstm_conv_b': nrm((N_MLSTM, E), 0.01),
        'mlstm_w_q': nrm((N_MLSTM, MH, MD, MD), MD ** -0.5),
        'mlstm_w_k': nrm((N_MLSTM, MH, MD, MD), MD ** -0.5),
        'mlstm_w_v': nrm((N_MLSTM, MH, MD, MD), MD ** -0.5),
        'mlstm_w_o': nrm((N_MLSTM, MH, MD, MD), MD ** -0.5),
        'mlstm_b_o': nrm((N_MLSTM, E), 0.01),
        'mlstm_w_gates': nrm((N_MLSTM, 3 * E, 2 * MH), (3 * E) ** -0.5),
        'mlstm_b_gates': gate_base + nrm((N_MLSTM, 2 * MH), 0.1),
        'mlstm_ln_w': 1.0 + nrm((N_MLSTM, E), 0.02),
        'mlstm_skip': 1.0 + nrm((N_MLSTM, E), 0.02),
        'mlstm_w_out': nrm((N_MLSTM, E, D_MODEL), E ** -0.5),
    }


def reference(x_prompt, x_sample, state_ssm_re, state_ssm_im, cache_k, cache_v, cache_kidx, page_table,
              state_mlstm_c, state_mlstm_n, state_mlstm_m, state_mlstm_conv, final_norm,
              ssm_norm, ssm_w_in, ssm_a_re, ssm_a_im, ssm_log_dt, ssm_b_re, ssm_b_im, ssm_c_re, ssm_c_im,
              ssm_d, ssm_w_glu, ssm_b_glu, ssm_w_out, attn_norm, attn_w_in, attn_w_out,
              mlstm_norm, mlstm_w_in, mlstm_conv_w, mlstm_conv_b, mlstm_w_q, mlstm_w_k, mlstm_w_v,
              mlstm_w_o, mlstm_b_o, mlstm_w_gates, mlstm_b_gates, mlstm_ln_w, mlstm_skip, mlstm_w_out):
    f32 = jnp.float32
    nbp = x_prompt.shape[0]
    past_len = page_table.shape[1] * cache_k.shape[2]
    hp, hs = x_prompt, x_sample
    sre_p, sim_p, sre_s, sim_s = [], [], [], []
    kp_l, vp_l, kip_l, ks_l, vs_l, kis_l = [], [], [], [], [], []
    mcp_l, mnp_l, mmp_l, mvp_l, mcs_l, mns_l, mms_l, mvs_l = [], [], [], [], [], [], [], []
    for layer in range(DEPTH):
        kind, j = LAYER_KINDS[layer], LAYER_SLOT[layer]
        if kind == 0:
            w = (ssm_w_in[j], ssm_a_re[j], ssm_a_im[j], ssm_log_dt[j], ssm_b_re[j], ssm_b_im[j],
                 ssm_c_re[j], ssm_c_im[j], ssm_d[j], ssm_w_glu[j], ssm_b_glu[j], ssm_w_out[j])
            zero = jnp.zeros((nbp, SSM_GROUPS, SSM_STATE), f32)
            yp, r_p, i_p = ssm_mixer(rms_norm(hp, ssm_norm[j]), zero, zero, *w)
            ys, r_s, i_s = ssm_mixer(rms_norm(hs, ssm_norm[j]), state_ssm_re[j], state_ssm_im[j], *w)
            sre_p.append(r_p)
            sim_p.append(i_p)
            sre_s.append(r_s)
            sim_s.append(i_s)
        elif kind == 1:
            yp, k_p, v_p, ki_p = dsa_mixer(rms_norm(hp, attn_norm[j]), 0, None, None, None,
                                           attn_w_in[j], attn_w_out[j])
            ys, k_s, v_s, ki_s = dsa_mixer(rms_norm(hs, attn_norm[j]), past_len,
                                           gather_pages(cache_k[j], page_table),
                                           gather_pages(cache_v[j], page_table),
                                           gather_pages(cache_kidx[j], page_table),
                                           attn_w_in[j], attn_w_out[j])
            kp_l.append(k_p)
            vp_l.append(v_p)
            kip_l.append(ki_p)
            ks_l.append(k_s)
            vs_l.append(v_s)
            kis_l.append(ki_s)
        else:
            w = (mlstm_w_in[j], mlstm_conv_w[j], mlstm_conv_b[j], mlstm_w_q[j], mlstm_w_k[j], mlstm_w_v[j],
                 mlstm_w_o[j], mlstm_b_o[j], mlstm_w_gates[j], mlstm_b_gates[j], mlstm_ln_w[j],
                 mlstm_skip[j], mlstm_w_out[j])
            c0 = jnp.zeros((nbp, M_HEADS, M_HEAD_DIM, M_HEAD_DIM), f32)
            n0 = jnp.zeros((nbp, M_HEADS, M_HEAD_DIM), f32)
            m0 = jnp.zeros((nbp, M_HEADS), f32)
            v0 = jnp.zeros((nbp, CONV_W - 1, D_INNER), hp.dtype)
            yp, c_p, n_p, m_p, cv_p = mlstm_mixer(rms_norm(hp, mlstm_norm[j]), c0, n0, m0, v0, *w)
            ys, c_s, n_s, m_s, cv_s = mlstm_mixer(rms_norm(hs, mlstm_norm[j]), state_mlstm_c[j], state_mlstm_n[j],
                                                  state_mlstm_m[j], state_mlstm_conv[j], *w)
            mcp_l.append(c_p)
            mnp_l.append(n_p)
            mmp_l.append(m_p)
            mvp_l.append(cv_p)
            mcs_l.append(c_s)
            mns_l.append(n_s)
            mms_l.append(m_s)
            mvs_l.append(cv_s)
        hp = hp + yp.astype(hp.dtype)
        hs = hs + ys.astype(hs.dtype)
    y_prompt = rms_norm(hp, final_norm)
    y_sample = rms_norm(hs, final_norm)
    ssm_re_p, ssm_im_p = jnp.stack(sre_p), jnp.stack(sim_p)
    ssm_re_s, ssm_im_s = jnp.stack(sre_s), jnp.stack(sim_s)
    k_p, v_p, kidx_p = jnp.stack(kp_l), jnp.stack(vp_l), jnp.stack(kip_l)
    k_s, v_s, kidx_s = jnp.stack(ks_l), jnp.stack(vs_l), jnp.stack(kis_l)
    mc_p, mn_p, mm_p, mconv_p = jnp.stack(mcp_l), jnp.stack(mnp_l), jnp.stack(mmp_l), jnp.stack(mvp_l)
    mc_s, mn_s, mm_s, mconv_s = jnp.stack(mcs_l), jnp.stack(mns_l), jnp.stack(mms_l), jnp.stack(mvs_l)
    return (y_prompt, y_sample, ssm_re_p, ssm_im_p, ssm_re_s, ssm_im_s,
            k_p, v_p, kidx_p, k_s, v_s, kidx_s,
            mc_p, mn_p, mm_p, mconv_p, mc_s, mn_s, mm_s, mconv_s)
```

```python
import math
import numpy as np
from contextlib import ExitStack
import concourse.bass as bass
import concourse.mybir as mybir
from concourse.bass_utils import run_bass_kernel_spmd

F32 = mybir.dt.float32
BF16 = mybir.dt.bfloat16
I32 = mybir.dt.int32
ALU = mybir.AluOpType
AF = mybir.ActivationFunctionType
AX = mybir.AxisListType

D_MODEL = 1024
E = 2048
T_P = 4096
N_S = 16
T_S = 128
TOK = T_P + T_S
NTILE = TOK // 128
NGRP = 9
TWO_PI = 2.0 * math.pi
ATTN_IN = 5192


def grp_tok(tg):
    return (tg * 512, 512) if tg < 8 else (T_P, T_S)


class Res:
    __slots__ = ("name", "w", "r")

    def __init__(self, name=""):
        self.name = name
        self.w = None
        self.r = {}


class Prog:
    ENG = ["pe", "dve", "act", "pool", "sp"]
    NS = 8

    def __init__(self, nc, stack):
        self.nc = nc
        self.q = {e: [] for e in self.ENG}
        self.cnt = {e: 0 for e in self.ENG}
        self.waited = {e: {} for e in self.ENG}
        self.ndma = {e: 0 for e in self.ENG}
        self.latest = {}
        self.sem = {}
        for e in ["pe", "dve", "act", "pool"]:
            self.sem[e] = stack.enter_context(nc.semaphore("s_" + e))
        for e in ["sp", "pool", "act"]:
            for i in range(self.NS):
                self.sem[("dma", e, i)] = stack.enter_context(nc.semaphore("d_%s_%d" % (e, i)))
        self.out_tokens = []
        self.uid = 0

    def sb(self, st, name, shape, dtype=F32):
        self.uid += 1
        t = st.enter_context(self.nc.sbuf_tensor("%s_%d" % (name, self.uid), list(shape), dtype))
        return t, Res(name)

    def ps(self, st, name, shape, dtype=F32):
        self.uid += 1
        t = st.enter_context(self.nc.psum_tensor("%s_%d" % (name, self.uid), list(shape), dtype))
        return t, Res(name)

    def _deps(self, reads, writes):
        deps = {}
        for r in reads:
            if r.w is not None:
                k, v = r.w
                if deps.get(k, 0) < v:
                    deps[k] = v
        for w in writes:
            if w.w is not None:
                k, v = w.w
                if deps.get(k, 0) < v:
                    deps[k] = v
            for k, v in w.r.items():
                if deps.get(k, 0) < v:
                    deps[k] = v
        return deps

    def _waits(self, eng, deps):
        ws = []
        wd = self.waited[eng]
        for k, v in deps.items():
            if eng == "pe" and k == "pe":
                continue
            if wd.get(k, 0) < v:
                wd[k] = v
                ws.append((self.sem[k], v))
        return ws

    def _update(self, tok, reads, writes):
        k, v = tok
        self.latest[k] = v
        for r in reads:
            if r.r.get(k, 0) < v:
                r.r[k] = v
        for w in writes:
            w.w = tok
            w.r = {}

    def op(self, eng, fn, reads=(), writes=()):
        deps = self._deps(reads, writes)
        ws = self._waits(eng, deps)
        self.cnt[eng] += 1
        tok = (eng, self.cnt[eng])
        sem = self.sem[eng]

        def emit(e, ws=ws, fn=fn, sem=sem):
            for s, v in ws:
                e.wait_ge(s, v)
            fn(e).then_inc(sem, 1)

        self.q[eng].append(emit)
        self._update(tok, reads, writes)
        return tok

    def dma(self, queue, fn, reads=(), writes=(), is_output=False):
        deps = self._deps(reads, writes)
        n = self.ndma[queue]
        self.ndma[queue] += 1
        idx = n % self.NS
        target = 16 * (n // self.NS + 1)
        key = ("dma", queue, idx)
        if target > 16 and deps.get(key, 0) < target - 16:
            deps[key] = target - 16
        ws = self._waits(queue, deps)
        sem = self.sem[key]

        def emit(e, ws=ws, fn=fn, sem=sem):
            for s, v in ws:
                e.wait_ge(s, v)
            fn(e).then_inc(sem, 16)

        self.q[queue].append(emit)
        tok = (key, target)
        self._update(tok, reads, writes)
        if is_output:
            self.out_tokens.append(tok)
        return tok

    def barrier(self):
        deps = dict(self.latest)
        for eng in self.ENG:
            ws = self._waits(eng, deps)
            if ws:
                def emit(e, ws=ws):
                    for s, v in ws:
                        e.wait_ge(s, v)
                self.q[eng].append(emit)

    def finish(self):
        self.barrier()

    def emit_all(self):
        nc = self.nc
        q = self.q
        with nc.Block() as block:
            @block.sync
            def _(e):
                for f in q["sp"]:
                    f(e)

            @block.tensor
            def _(e):
                for f in q["pe"]:
                    f(e)

            @block.vector
            def _(e):
                for f in q["dve"]:
                    f(e)

            @block.scalar
            def _(e):
                for f in q["act"]:
                    f(e)

            @block.gpsimd
            def _(e):
                for f in q["pool"]:
                    f(e)


def bc(ap, axis, n):
    a = ap.unsqueeze(axis)
    shp = list(a.shape)
    shp[axis] = n
    return a.broadcast_to(shp)


class KB:
    def __init__(self, nlayers=4, dbg=False):
        self.nlayers = nlayers
        self.dsa_stage = 3
        self.ml_stage = 3
        self.dbg = dbg
        nc = bass.Bass("TRN2", target_bir_lowering=False)
        self.nc = nc
        self.inputs = {}
        self.outputs = {}

    def din(self, name, shape, dtype=F32):
        t = self.nc.dram_tensor(name, list(shape), dtype, kind="ExternalInput").ap()
        self.inputs[name] = (tuple(shape), dtype)
        return t

    def dout(self, name, shape, dtype=F32):
        t = self.nc.dram_tensor(name, list(shape), dtype, kind="ExternalOutput").ap()
        self.outputs[name] = tuple(shape)
        return t

    def dscr(self, name, shape, dtype=F32):
        return self.nc.dram_tensor(name, list(shape), dtype, kind="Internal").ap()

    def dve(self, fn, r=(), w=()):
        return self.P.op("dve", fn, r, w)

    def act(self, fn, r=(), w=()):
        return self.P.op("act", fn, r, w)

    def pool(self, fn, r=(), w=()):
        return self.P.op("pool", fn, r, w)

    def pe(self, fn, r=(), w=()):
        return self.P.op("pe", fn, r, w)

    def ld(self, out, in_, r=(), w=(), q="sp"):
        return self.P.dma(q, lambda e: e.dma_start(out=out, in_=in_), r, w)

    def ldc(self, out, in_, r=(), w=()):
        return self.P.dma("pool", lambda e: e.dma_start(out=out, in_=in_), r, w)

    def store(self, out, in_, r=(), w=(), q="sp"):
        return self.P.dma(q, lambda e: e.dma_start(out=out, in_=in_), r, w, is_output=True)

    def next_ps(self):
        self.ps_i = (self.ps_i + 1) % self.ps_lim
        return self.psb[self.ps_i]

    def build(self):
        nc = self.nc
        self.xin = self.din("xin", [TOK, D_MODEL])
        self.ident_f_d = self.din("ident_f", [128, 128])
        self.final_norm = self.din("final_norm", [1, D_MODEL])
        self.ssm_norm = self.din("ssm_norm", [2, D_MODEL])
        self.ssm_w_in = self.din("ssm_w_in", [2, D_MODEL, 2 * E])
        self.ssm_a_re = self.din("ssm_a_re", [2, 128, 64])
        self.ssm_a_im = self.din("ssm_a_im", [2, 128, 64])
        self.ssm_log_dt = self.din("ssm_log_dt", [2, 128, 1])
        self.ssm_b_re = self.din("ssm_b_re", [2, 128, 64, 16])
        self.ssm_b_im = self.din("ssm_b_im", [2, 128, 64, 16])
        self.ssm_c_re = self.din("ssm_c_re", [2, 128, 16, 64])
        self.ssm_c_im = self.din("ssm_c_im", [2, 128, 16, 64])
        self.ssm_d = self.din("ssm_d", [2, E])
        self.ssm_w_glu = self.din("ssm_w_glu", [2, E, E])
        self.ssm_b_glu = self.din("ssm_b_glu", [2, E])
        self.ssm_w_out = self.din("ssm_w_out", [2, E, D_MODEL])
        self.st_re = self.din("st_re", [2, N_S, 8192])
        self.st_im = self.din("st_im", [2, N_S, 8192])
        self.attn_norm = self.din("attn_norm", [1, D_MODEL])
        self.attn_w_in = self.din("attn_w_in", [D_MODEL, ATTN_IN])
        self.attn_w_out = self.din("attn_w_out", [E, D_MODEL])
        self.rope_cs = self.din("rope_cs", [TOK, 64])
        self.cmask_d = self.din("cmask", [128, 128])
        self.cmask_s_d = self.din("cmask_s", [128, 8])
        self.selm_d = self.din("selm", [64, 8])
        self.blockm_d = self.din("blockm", [128, 128])
        self.iota_d = self.din("iota_p", [128, 1])
        self.page_table = self.din("page_table", [1, N_S * 16], I32)
        self.cache_k = self.din("cache_k", [2560 * 128, 256])
        self.cache_v = self.din("cache_v", [2560 * 128, 256])
        self.cache_kidx = self.din("cache_kidx", [2560 * 128, 64])
        self.os_d = self.dscr("os_d", [T_S, E])
        self.o_k_p = self.dout("o_k_p", [T_P, 256])
        self.o_v_p = self.dout("o_v_p", [T_P, 256])
        self.o_kidx_p = self.dout("o_kidx_p", [T_P, 64])
        self.o_k_s = self.dout("o_k_s", [T_S, 256])
        self.o_v_s = self.dout("o_v_s", [T_S, 256])
        self.o_kidx_s = self.dout("o_kidx_s", [T_S, 64])
        self.qT_d = self.dscr("qT_d", [16, 128, TOK], BF16)
        self.kT2_d = self.dscr("kT2_d", [4, 128, TOK], BF16)
        self.qiT_d = self.dscr("qiT_d", [4, 128, TOK], BF16)
        self.kiT2_d = self.dscr("kiT2_d", [1, 128, TOK], BF16)
        self.Vx_d = self.dscr("Vx_d", [TOK, 260], BF16)
        self.sz_d = self.dscr("sz_d", [TOK, E], BF16)
        self.wi_d = self.dscr("wi_d", [TOK, 8])
        self.qTs_d = self.dscr("qTs_d", [64, 32, T_S], BF16)
        self.qiTs_d = self.dscr("qiTs_d", [64, 8, T_S], BF16)
        self.dsa_r = Res("dsa_scratch")
        self.ml_norm = self.din("mlstm_norm", [1, D_MODEL])
        self.ml_w_in = self.din("mlstm_w_in", [D_MODEL, 2 * E])
        self.ml_conv_w = self.din("mlstm_conv_w", [4, E])
        self.ml_conv_b = self.din("mlstm_conv_b", [E])
        self.ml_w_q = self.din("mlstm_w_q", [8, 256, 256])
        self.ml_w_k = self.din("mlstm_w_k", [8, 256, 256])
        self.ml_w_v = self.din("mlstm_w_v", [8, 256, 256])
        self.ml_w_o = self.din("mlstm_w_o", [8, 256, 256])
        self.ml_b_o = self.din("mlstm_b_o", [1, E])
        self.ml_w_gates = self.din("mlstm_w_gates", [3 * E, 16])
        self.ml_b_gates = self.din("mlstm_b_gates", [16, 1])
        self.ml_ln_w = self.din("mlstm_ln_w", [1, E])
        self.ml_skip = self.din("mlstm_skip", [E])
        self.ml_w_out = self.din("mlstm_w_out", [E, D_MODEL])
        self.ml_c0 = self.din("ml_c0", [N_S, 8, 256, 256])
        self.ml_n0 = self.din("ml_n0", [N_S, 8, 256])
        self.ml_m0 = self.din("ml_m0", [N_S, 8])
        self.ml_conv0 = self.din("ml_conv0", [N_S * 3, E])
        self.hmask_d = self.din("hmask", [8, 8 * 128])
        self.ones8_d = self.din("ones8", [8, 128])
        self.cm64_d = self.din("cm64", [64, 64])
        self.seqsel_d = self.din("seqsel", [128, N_S])
        self.smask_d = self.din("smask", [128, 128])
        self.o_mc_p = self.dout("o_mc_p", [8, 256, 256])
        self.o_mn_p = self.dout("o_mn_p", [16, 128])
        self.o_mm_p = self.dout("o_mm_p", [8, 1])
        self.o_mconv_p = self.dout("o_mconv_p", [3, E])
        self.o_mc_s = self.dout("o_mc_s", [N_S, 8, 256, 256])
        self.o_mn_s = self.dout("o_mn_s", [N_S * 16, 128])
        self.o_mm_s = self.dout("o_mm_s", [8, N_S])
        self.o_mconv_s = self.dout("o_mconv_s", [N_S, 3, E])
        self.muT_d = self.dscr("muT_d", [16, 128, TOK], BF16)
        self.mca_d = self.dscr("mca_d", [16, 128, TOK], BF16)
        self.mq_d = self.dscr("mq_d", [16, 128, TOK], BF16)
        self.mk_d = self.dscr("mk_d", [16, 128, TOK], BF16)
        self.mvt_d = self.dscr("mvt_d", [TOK, 8 * 257], BF16)
        self.mkt_d = self.dscr("mkt_d", [TOK, E], BF16)
        self.mo_d = self.dscr("mo_d", [TOK, E], BF16)
        self.gq_d = self.dscr("gq_d", [4, 8, TOK])
        self.ml_r = Res("ml_scratch")
        self.y = self.dout("y", [TOK, D_MODEL])
        self.o_sre_p = self.dout("o_sre_p", [2, 64, 128])
        self.o_sim_p = self.dout("o_sim_p", [2, 64, 128])
        self.o_sre_s = self.dout("o_sre_s", [2, N_S, 64, 128])
        self.o_sim_s = self.dout("o_sim_s", [2, N_S, 64, 128])
        self.H = self.dscr("H", [TOK, D_MODEL])
        self.H_r = [Res("H%d" % i) for i in range(NTILE)]
        self.uT_d = self.dscr("uT_d", [16, 128, 8, 8, 64], BF16)
        self.uTs_d = self.dscr("uTs_d", [16, 128, 8, N_S], BF16)
        self.szT_d = self.dscr("szT_d", [16, 128, TOK], BF16)
        self.gT_d = self.dscr("gT_d", [16, 128, TOK], BF16)
        self.Wd = self.dscr("Wd", [16, 128, 16, 64])
        self.Vd = self.dscr("Vd", [16, 128, 64, 16])
        self.Kd = self.dscr("Kd", [8, 128, 16, 16])
        self.KCd = self.dscr("KCd", [128, 64, 30])
        self.uT_r = [Res() for _ in range(16)]
        self.szT_r = [Res() for _ in range(16)]
        self.gT_r = [Res() for _ in range(16)]
        self.Wd_r, self.Vd_r, self.Kd_r, self.KCd_r = Res(), Res(), Res(), Res()

        with ExitStack() as gst:
            self.P = P = Prog(nc, gst)
            self.psb = [P.ps(gst, "psb", [128, 512]) for _ in range(7)]
            self.pst = P.ps(gst, "pst", [128, 1024], BF16)
            self.ps_i = 0
            self.ps_lim = 7
            self.ident_f, self.ident_f_r = P.sb(gst, "identf", [128, 128])
            self.ident_b, self.ident_b_r = P.sb(gst, "identb", [128, 128], BF16)
            self.ld(self.ident_f[:], self.ident_f_d, w=[self.ident_f_r])
            self.ldc(self.ident_b[:], self.ident_f_d, w=[self.ident_b_r])

            kinds = [0, 1, 2, 0]
            slots = [0, 0, 0, 1]
            if getattr(self, "only", None) == "ml_sample":
                with ExitStack() as st:
                    self.ml_sample(st)
                P.barrier()
                self.nlayers = 0
            for layer in range(self.nlayers):
                if kinds[layer] == 0:
                    self.s5_layer(slots[layer], first=(layer == 0))
                elif kinds[layer] == 1:
                    self.dsa_layer()
                else:
                    self.ml_layer()
                P.barrier()
            self.final_phase(from_x=(self.nlayers == 0))
            P.finish()
            P.emit_all()
        return nc

    def norm_tile(self, src_ap, src_r, gt, gt_r, ht, ht_r, junk, junk_r, ss, ss_r, xn, xn_r, xT_out, xT_r):
        self.ld(ht[:], src_ap, r=[src_r], w=[ht_r])
        self.act(lambda e: e.activation(out=junk[:], in_=ht[:], func=AF.Square, accum_out=ss[:]),
                 r=[ht_r], w=[junk_r, ss_r])
        self.dve(lambda e: e.tensor_scalar(out=ss[:], in0=ss[:], scalar1=1.0 / D_MODEL, scalar2=1e-6,
                                           op0=ALU.mult, op1=ALU.add), r=[ss_r], w=[ss_r])
        self.act(lambda e: e.activation(out=ss[:], in_=ss[:], func=AF.Sqrt), r=[ss_r], w=[ss_r])
        self.dve(lambda e: e.reciprocal(out=ss[:], in_=ss[:]), r=[ss_r], w=[ss_r])
        self.dve(lambda e: e.scalar_tensor_tensor(out=xn[:], in0=ht[:], scalar=ss[:, 0:1], in1=gt[:],
                                                  op0=ALU.mult, op1=ALU.mult), r=[ht_r, ss_r, gt_r], w=[xn_r])
        pt, pt_r = self.pst
        for k in range(8):
            self.pe(lambda e, k=k: e.transpose(out=pt[:, k * 128:(k + 1) * 128], in_=xn[:, k * 128:(k + 1) * 128],
                                               identity=self.ident_b[:]), r=[xn_r, self.ident_b_r], w=[pt_r])
        self.act(lambda e: e.copy(out=xT_out, in_=pt[:].rearrange("p (k t) -> p k t", k=8)), r=[pt_r], w=[xT_r])

    def h_src(self, first, i):
        src = self.xin if first else self.H
        return src[i * 128:(i + 1) * 128, :]

    def s5_layer(self, slot, first):
        P = self.P
        nc = self.nc
        with ExitStack() as st:
            self.s5_setup(st, slot)
            win, win_r = P.sb(st, "win", [128, 8, 2 * E], BF16)
            wv = self.ssm_w_in[slot].rearrange("(k p) n -> p k n", p=128)
            for k in range(8):
                for hf in range(2):
                    self.ldc(win[:, k, hf * E:(hf + 1) * E], wv[:, k, hf * E:(hf + 1) * E], w=[win_r])
            gt, gt_r = P.sb(st, "gt", [128, D_MODEL])
            self.ld(gt[:], self.ssm_norm[slot:slot + 1, :].partition_broadcast(128), w=[gt_r])
            hts = [P.sb(st, "ht", [128, D_MODEL]) for _ in range(2)]
            junk, junk_r = P.sb(st, "junk", [128, D_MODEL], BF16)
            sss = [P.sb(st, "ss", [128, 1]) for _ in range(2)]
            xns = [P.sb(st, "xn", [128, D_MODEL], BF16) for _ in range(2)]
            xTs = [P.sb(st, "xT", [128, 8, 512], BF16) for _ in range(2)]
            obufs = [P.sb(st, "obuf", [128, 512], BF16) for _ in range(4)]
            ob_i = 0
            ti = 0
            for tg in range(NGRP):
                tok0, ntok = grp_tok(tg)
                xT, xT_r = xTs[tg % 2]
                for il in range(ntok // 128):
                    i = tok0 // 128 + il
                    ht, ht_r = hts[ti % 2]
                    ss, ss_r = sss[ti % 2]
                    xn, xn_r = xns[ti % 2]
                    ti += 1
                    self.norm_tile(self.h_src(first, i), self.H_r[i], gt, gt_r, ht, ht_r, junk, junk_r, ss, ss_r,
                                   xn, xn_r, xT[:, :, il * 128:(il + 1) * 128], xT_r)
                for fo in range(32):
                    ps, ps_r = self.next_ps()
                    for k in range(8):
                        self.pe(lambda e, k=k, fo=fo, ps=ps, xT=xT, ntok=ntok: e.matmul(
                            ps[:, 0:ntok], lhsT=win[:, k, fo * 128:(fo + 1) * 128], rhs=xT[:, k, 0:ntok],
                            start=(k == 0), stop=(k == 7)), r=[win_r, xT_r], w=[ps_r])
                    ob, ob_r = obufs[ob_i % 4]
                    ob_i += 1
                    if fo < 16:
                        if tg < 8:
                            self.act(lambda e, ob=ob, ps=ps: e.copy(
                                out=ob[:].rearrange("p (s c) -> p s c", s=8),
                                in_=ps[:].rearrange("p (c s) -> p s c", s=8)), r=[ps_r], w=[ob_r])
                            self.ld(self.uT_d[fo, :, tg, :, :], ob[:].rearrange("p (s c) -> p s c", s=8),
                                    r=[ob_r], w=[self.uT_r[fo]])
                        else:
                            self.act(lambda e, ob=ob, ps=ps: e.copy(
                                out=ob[:, 0:128].rearrange("p (s c) -> p s c", s=8),
                                in_=ps[:, 0:128].rearrange("p (c s) -> p s c", s=8)), r=[ps_r], w=[ob_r])
                            self.ld(self.uTs_d[fo], ob[:, 0:128].rearrange("p (s c) -> p s c", s=8),
                                    r=[ob_r], w=[self.uT_r[fo]])
                    else:
                        self.act(lambda e, ob=ob, ps=ps, ntok=ntok: e.activation(
                            out=ob[:, 0:ntok], in_=ps[:, 0:ntok], func=AF.Silu), r=[ps_r], w=[ob_r])
                        self.ld(self.szT_d[fo - 16, :, tok0:tok0 + ntok], ob[:, 0:ntok], r=[ob_r], w=[self.szT_r[fo - 16]])
        P.barrier()
        with ExitStack() as st:
            self.s5_scan(st, slot)
        P.barrier()
        with ExitStack() as st:
            self.s5_out(st, slot, first)

    def s5_setup(self, st0, slot):
        P = self.P
        with ExitStack() as st:
            def T(name, shape, dt=F32):
                return P.sb(st, name, shape, dt)
            ar, ar_r = T("ar", [128, 64])
            ai, ai_r = T("ai", [128, 64])
            ldt, ldt_r = T("ldt", [128, 1])
            self.ld(ar[:], self.ssm_a_re[slot], w=[ar_r])
            self.ld(ai[:], self.ssm_a_im[slot], w=[ai_r])
            self.ld(ldt[:], self.ssm_log_dt[slot], w=[ldt_r])
            dt_, dt_r = T("dt", [128, 1])
            self.act(lambda e: e.activation(out=dt_[:], in_=ldt[:], func=AF.Exp), r=[ldt_r], w=[dt_r])
            mag, mag_r = T("mag", [128, 64])
            self.act(lambda e: e.activation(out=mag[:], in_=ar[:], func=AF.Exp, scale=dt_[:, 0:1]), r=[ar_r, dt_r], w=[mag_r])
            qq, qq_r = T("qq", [128, 2, 64])
            self.dve(lambda e: e.tensor_scalar(out=qq[:, 0, :], in0=ai[:], scalar1=dt_[:, 0:1], scalar2=1.0 / TWO_PI,
                                               op0=ALU.mult, op1=ALU.mult), r=[ai_r, dt_r], w=[qq_r])
            self.dve(lambda e: e.tensor_scalar(out=qq[:, 1, :], in0=qq[:, 0, :], scalar1=0.25, scalar2=None, op0=ALU.add),
                     r=[qq_r], w=[qq_r])
            qi_, qi_r = T("qi", [128, 2, 64], I32)
            qf, qf_r = T("qf", [128, 2, 64])
            self.dve(lambda e: e.tensor_copy(out=qi_[:], in_=qq[:]), r=[qq_r], w=[qi_r])
            self.dve(lambda e: e.tensor_copy(out=qf[:], in_=qi_[:]), r=[qi_r], w=[qf_r])
            self.dve(lambda e: e.tensor_tensor(out=qq[:], in0=qq[:], in1=qf[:], op=ALU.subtract), r=[qq_r, qf_r], w=[qq_r])
            self.dve(lambda e: e.tensor_scalar(out=qq[:], in0=qq[:], scalar1=0.5, scalar2=-0.5, op0=ALU.min, op1=ALU.max),
                     r=[qq_r], w=[qq_r])
            sc, sc_r = T("sc", [128, 2, 64])
            self.act(lambda e: e.activation(out=sc[:], in_=qq[:], func=AF.Sin, scale=TWO_PI), r=[qq_r], w=[sc_r])
            Ap, Ap_r = T("Ap", [128, 9, 2, 64])
            self.pool(lambda e: e.memset(Ap[:, 0, 0, :], 1.0), w=[Ap_r])
            self.pool(lambda e: e.memset(Ap[:, 0, 1, :], 0.0), w=[Ap_r])
            self.dve(lambda e: e.tensor_tensor(out=Ap[:, 1, 0, :], in0=mag[:], in1=sc[:, 1, :], op=ALU.mult), r=[mag_r, sc_r], w=[Ap_r])
            self.dve(lambda e: e.tensor_tensor(out=Ap[:, 1, 1, :], in0=mag[:], in1=sc[:, 0, :], op=ALU.mult), r=[mag_r, sc_r], w=[Ap_r])
            t1, t1_r = T("t1", [128, 64])
            t2, t2_r = T("t2", [128, 64])

            def cmul(o_re, o_im, a_re, a_im, b_re, b_im, rr, ww):
                self.dve(lambda e: e.tensor_tensor(out=t1[:], in0=a_re, in1=b_re, op=ALU.mult), r=rr, w=[t1_r])
                self.dve(lambda e: e.tensor_tensor(out=t2[:], in0=a_im, in1=b_im, op=ALU.mult), r=rr, w=[t2_r])
                self.dve(lambda e: e.tensor_tensor(out=o_re, in0=t1[:], in1=t2[:], op=ALU.subtract), r=[t1_r, t2_r], w=ww)
                self.dve(lambda e: e.tensor_tensor(out=t1[:], in0=a_re, in1=b_im, op=ALU.mult), r=rr, w=[t1_r])
                self.dve(lambda e: e.tensor_tensor(out=t2[:], in0=a_im, in1=b_re, op=ALU.mult), r=rr, w=[t2_r])
                self.dve(lambda e: e.tensor_tensor(out=o_im, in0=t1[:], in1=t2[:], op=ALU.add), r=[t1_r, t2_r], w=ww)

            for tau in range(2, 9):
                cmul(Ap[:, tau, 0, :], Ap[:, tau, 1, :], Ap[:, tau - 1, 0, :], Ap[:, tau - 1, 1, :],
                     Ap[:, 1, 0, :], Ap[:, 1, 1, :], [Ap_r], [Ap_r])
            KC, KC_r = T("KC", [128, 64, 10, 3])
            self.dve(lambda e: e.tensor_copy(out=KC[:, :, 0, 0], in_=Ap[:, 8, 0, :]), r=[Ap_r], w=[KC_r])
            self.dve(lambda e: e.tensor_copy(out=KC[:, :, 0, 1], in_=Ap[:, 8, 1, :]), r=[Ap_r], w=[KC_r])
            for k in range(1, 10):
                cmul(KC[:, :, k, 0], KC[:, :, k, 1], KC[:, :, k - 1, 0], KC[:, :, k - 1, 1],
                     KC[:, :, k - 1, 0], KC[:, :, k - 1, 1], [KC_r], [KC_r])
            self.dve(lambda e: e.tensor_scalar(out=KC[:, :, :, 2], in0=KC[:, :, :, 1], scalar1=-1.0, scalar2=None, op0=ALU.mult),
                     r=[KC_r], w=[KC_r])
            self.ld(self.KCd, KC[:].rearrange("g p k c -> g p (k c)"), r=[KC_r], w=[self.KCd_r])
            den, den_r = T("den", [128, 64])
            self.dve(lambda e: e.tensor_tensor(out=den[:], in0=ar[:], in1=ar[:], op=ALU.mult), r=[ar_r], w=[den_r])
            self.dve(lambda e: e.tensor_tensor(out=t1[:], in0=ai[:], in1=ai[:], op=ALU.mult), r=[ai_r], w=[t1_r])
            self.dve(lambda e: e.tensor_tensor(out=den[:], in0=den[:], in1=t1[:], op=ALU.add), r=[den_r, t1_r], w=[den_r])
            self.dve(lambda e: e.reciprocal(out=den[:], in_=den[:]), r=[den_r], w=[den_r])
            zr, zr_r = T("zr", [128, 64])
            self.dve(lambda e: e.tensor_scalar(out=zr[:], in0=Ap[:, 1, 0, :], scalar1=-1.0, scalar2=None, op0=ALU.add), r=[Ap_r], w=[zr_r])
            Ff, Ff_r = T("Ff", [128, 2, 64])
            self.dve(lambda e: e.tensor_tensor(out=t1[:], in0=zr[:], in1=ar[:], op=ALU.mult), r=[zr_r, ar_r], w=[t1_r])
            self.dve(lambda e: e.tensor_tensor(out=t2[:], in0=Ap[:, 1, 1, :], in1=ai[:], op=ALU.mult), r=[Ap_r, ai_r], w=[t2_r])
            self.dve(lambda e: e.tensor_tensor(out=t1[:], in0=t1[:], in1=t2[:], op=ALU.add), r=[t1_r, t2_r], w=[t1_r])
            self.dve(lambda e: e.tensor_tensor(out=Ff[:, 0, :], in0=t1[:], in1=den[:], op=ALU.mult), r=[t1_r, den_r], w=[Ff_r])
            self.dve(lambda e: e.tensor_tensor(out=t1[:], in0=Ap[:, 1, 1, :], in1=ar[:], op=ALU.mult), r=[Ap_r, ar_r], w=[t1_r])
            self.dve(lambda e: e.tensor_tensor(out=t2[:], in0=zr[:], in1=ai[:], op=ALU.mult), r=[zr_r, ai_r], w=[t2_r])
            self.dve(lambda e: e.tensor_tensor(out=t1[:], in0=t1[:], in1=t2[:], op=ALU.subtract), r=[t1_r, t2_r], w=[t1_r])
            self.dve(lambda e: e.tensor_tensor(out=Ff[:, 1, :], in0=t1[:], in1=den[:], op=ALU.mult), r=[t1_r, den_r], w=[Ff_r])
            br, br_r = T("br", [128, 64, 16])
            bi, bi_r = T("bi", [128, 64, 16])
            self.ld(br[:], self.ssm_b_re[slot], w=[br_r])
            self.ld(bi[:], self.ssm_b_im[slot], w=[bi_r])
            brT = br[:].rearrange("g p j -> g j p")
            biT = bi[:].rearrange("g p j -> g j p")
            EE = [T("EE", [128, 16, 2, 64]) for _ in range(2)]
            u1, u1_r = T("u1", [128, 16, 64])
            u2, u2_r = T("u2", [128, 16, 64])

            def cmul_b(o, o_r, x_re, x_im, xr, a_re, a_im, a_r):
                ab_re = bc(a_re, 1, 16)
                ab_im = bc(a_im, 1, 16)
                self.dve(lambda e: e.tensor_tensor(out=u1[:], in0=x_re, in1=ab_re, op=ALU.mult), r=xr + a_r, w=[u1_r])
                self.pool(lambda e: e.tensor_tensor(out=u2[:], in0=x_im, in1=ab_im, op=ALU.mult), r=xr + a_r, w=[u2_r])
                self.dve(lambda e: e.tensor_tensor(out=o[:, :, 0, :], in0=u1[:], in1=u2[:], op=ALU.subtract), r=[u1_r, u2_r], w=[o_r])
                self.dve(lambda e: e.tensor_tensor(out=u1[:], in0=x_re, in1=ab_im, op=ALU.mult), r=xr + a_r, w=[u1_r])
                self.pool(lambda e: e.tensor_tensor(out=u2[:], in0=x_im, in1=ab_re, op=ALU.mult), r=xr + a_r, w=[u2_r])
                self.dve(lambda e: e.tensor_tensor(out=o[:, :, 1, :], in0=u1[:], in1=u2[:], op=ALU.add), r=[u1_r, u2_r], w=[o_r])

            CC, CC_r = T("CC", [128, 16, 2, 64])
            self.ld(CC[:, :, 0, :], self.ssm_c_re[slot], w=[CC_r])
            self.ld(CC[:, :, 1, :], self.ssm_c_im[slot], w=[CC_r])
            self.dve(lambda e: e.tensor_scalar(out=CC[:, :, 1, :], in0=CC[:, :, 1, :], scalar1=-1.0, scalar2=None, op0=ALU.mult),
                     r=[CC_r], w=[CC_r])
            Kg, Kg_r = T("Kg", [128, 8, 16, 16])
            tm = [T("tm", [128, 16, 128]) for _ in range(2)]
            E0, E0_r = EE[0]
            cmul_b(E0, E0_r, brT, biT, [br_r, bi_r], Ff[:, 0, :], Ff[:, 1, :], [Ff_r])
            for tau in range(8):
                Ec, Ec_r = EE[tau % 2]
                if tau > 0:
                    Epv, Epv_r = EE[(tau - 1) % 2]
                    cmul_b(Ec, Ec_r, Epv[:, :, 0, :], Epv[:, :, 1, :], [Epv_r], Ap[:, 1, 0, :], Ap[:, 1, 1, :], [Ap_r])
                self.ld(self.Wd[2 * (7 - tau):2 * (7 - tau) + 2].rearrange("ri g j p -> g j ri p"), Ec[:], r=[Ec_r], w=[self.Wd_r])
                for j in range(16):
                    tmj, tmj_r = tm[j % 2]
                    eb = bc(Ec[:, j, :, :].rearrange("g r p -> g (r p)"), 1, 16)
                    self.pool(lambda e, tmj=tmj, eb=eb: e.tensor_tensor(out=tmj[:], in0=CC[:].rearrange("g i r p -> g i (r p)"),
                                                                        in1=eb, op=ALU.mult), r=[CC_r, Ec_r], w=[tmj_r])
                    self.dve(lambda e, tmj=tmj, tau=tau, j=j: e.tensor_reduce(out=Kg[:, tau, j, :], in_=tmj[:], axis=AX.X, op=ALU.add),
                             r=[tmj_r], w=[Kg_r])
            self.ld(self.Kd.rearrange("t g j i -> g t (j i)"), Kg[:].rearrange("g t j i -> g t (j i)"), r=[Kg_r], w=[self.Kd_r])
            VV = [T("VV", [128, 2, 64, 16]) for _ in range(2)]
            CrT = CC[:, :, 0, :].rearrange("g i p -> g p i")
            nCiT = CC[:, :, 1, :].rearrange("g i p -> g p i")
            w1, w1_r = T("w1", [128, 64, 16])
            w2, w2_r = T("w2", [128, 64, 16])
            for t in range(8):
                Vc, Vc_r = VV[t % 2]
                are = bc(Ap[:, t + 1, 0, :], 2, 16)
                aim = bc(Ap[:, t + 1, 1, :], 2, 16)
                self.dve(lambda e, are=are: e.tensor_tensor(out=w1[:], in0=CrT, in1=are, op=ALU.mult), r=[CC_r, Ap_r], w=[w1_r])
                self.pool(lambda e, aim=aim: e.tensor_tensor(out=w2[:], in0=nCiT, in1=aim, op=ALU.mult), r=[CC_r, Ap_r], w=[w2_r])
                self.dve(lambda e, Vc=Vc: e.tensor_tensor(out=Vc[:, 0, :, :], in0=w1[:], in1=w2[:], op=ALU.add), r=[w1_r, w2_r], w=[Vc_r])
                self.dve(lambda e, aim=aim: e.tensor_tensor(out=w1[:], in0=CrT, in1=aim, op=ALU.mult), r=[CC_r, Ap_r], w=[w1_r])
                self.pool(lambda e, are=are: e.tensor_tensor(out=w2[:], in0=nCiT, in1=are, op=ALU.mult), r=[CC_r, Ap_r], w=[w2_r])
                self.dve(lambda e, Vc=Vc: e.tensor_tensor(out=Vc[:, 1, :, :], in0=w2[:], in1=w1[:], op=ALU.subtract), r=[w1_r, w2_r], w=[Vc_r])
                self.ld(self.Vd[2 * t:2 * t + 2].rearrange("r g p i -> g r (p i)"), Vc[:].rearrange("g r p i -> g r (p i)"),
                        r=[Vc_r], w=[self.Vd_r])
            P.barrier()

    def s5_scan(self, st, slot):
        P = self.P

        def T(name, shape, dt=F32):
            return P.sb(st, name, shape, dt)
        Kt, Kt_r = T("Kt", [128, 16, 8, 128], BF16)
        self.pool(lambda e: e.memset(Kt[:], 0.0), w=[Kt_r])
        for g8 in range(8):
            for tau in range(8):
                self.ldc(Kt[16 * g8:16 * g8 + 16, :, tau, 16 * g8:16 * g8 + 16],
                         self.Kd[tau].rearrange("(f g) j i -> g j f i", g=8)[g8], r=[self.Kd_r], w=[Kt_r])
        KC, KC_r = T("KCs", [128, 64, 30])
        self.ld(KC[:], self.KCd.rearrange("(q g) p k -> (g p) q k", g=2), r=[self.KCd_r], w=[KC_r])
        Dk, Dk_r = T("Dk", [128, 16])
        self.P.dma("sp", lambda e: e.dma_start(out=Dk[:], in_=self.ssm_d[slot].rearrange("(f p) -> p f", p=128),
                                              allow_slow_non_contiguous=True), (), [Dk_r])
        Wts = [T("Wt", [128, 16, 128], BF16) for _ in range(2)]
        Vts = [T("Vt", [128, 4, 16, 32], BF16) for _ in range(2)]
        for (t_, r_) in Wts + Vts:
            self.pool(lambda e, t_=t_: e.memset(t_[:], 0.0), w=[r_])
        uTfs = [T("uTf", [128, 8, 8, 64], BF16) for _ in range(2)]
        uTss = [T("uTs", [128, 8, N_S], BF16) for _ in range(2)]
        XA = [[T("XA", [128, 513]) for _ in range(2)] for _ in range(4)]
        XB = [[T("XB", [128, 513]) for _ in range(2)] for _ in range(4)]
        Xb = [[T("Xb", [128, 512], BF16) for _ in range(2)] for _ in range(4)]
        XS, XS_r = T("XS", [128, 4, 2, N_S, 2])
        XSb, XSb_r = T("XSb", [128, 4, 2, N_S], BF16)
        tS = [T("tS", [128, N_S]) for _ in range(2)]
        ktmp = T("ktmp", [128, 513])
        gbufs = [T("gbuf", [128, TOK], BF16) for _ in range(2)]
        ytmps = [T("ytmp", [128, 512]) for _ in range(2)]
        Fin, Fin_r = T("Fin", [128, 2, 64])
        FinS, FinS_r = T("FinS", [128, 2, 64, N_S])
        s0s = [[T("s0", [N_S, 512]) for _ in range(2)] for _ in range(2)]
        for q in range(4):
            for ri in range(2):
                self.pool(lambda e, q=q, ri=ri: e.memset(XA[q][ri][0][:, 0:1], 0.0), w=[XA[q][ri][1]])
        yi = 0
        for f in range(16):
            Wt, Wt_r = Wts[f % 2]
            Vt, Vt_r = Vts[f % 2]
            uTf, uTf_r = uTfs[f % 2]
            uTs, uTs_r = uTss[f % 2]
            gbuf, gbuf_r = gbufs[f % 2]
            self.ld(uTf[:].rearrange("p a s c -> p (a s c)"), self.uT_d[f].rearrange("p a s c -> p (a s c)"),
                    r=[self.uT_r[f]], w=[uTf_r])
            self.ld(uTs[:], self.uTs_d[f], r=[self.uT_r[f]], w=[uTs_r])
            s0 = s0s[f % 2]
            self.ld(s0[0][0][:], self.st_re[slot][:, f * 512:(f + 1) * 512], w=[s0[0][1]])
            self.ld(s0[1][0][:], self.st_im[slot][:, f * 512:(f + 1) * 512], w=[s0[1][1]])
            for g8 in range(8):
                g = 8 * f + g8
                self.ldc(Wt[16 * g8:16 * g8 + 16, :, 64 * (g8 % 2):64 * (g8 % 2) + 64],
                         self.Wd[:, g, :, :].rearrange("sr j p -> j sr p"), r=[self.Wd_r], w=[Wt_r])
            for g2 in range(2):
                for q in range(4):
                    self.ldc(Vt[64 * g2:64 * g2 + 64, q, :, 16 * g2:16 * g2 + 16],
                             self.Vd[:, 8 * f + 2 * q + g2, :, :].rearrange("tr p i -> p tr i"),
                             r=[self.Vd_r], w=[Vt_r])
            for q in range(4):
                qq = 4 * f + q
                for ri in range(2):
                    ps, ps_r = self.next_ps()
                    for s in range(8):
                        self.pe(lambda e, ps=ps, q=q, ri=ri, s=s, Wt=Wt, uTf=uTf: e.matmul(
                            ps[:].rearrange("p (a c) -> p a c", a=8), lhsT=Wt[32 * q:32 * q + 32, 2 * s + ri, :],
                            rhs=uTf[32 * q:32 * q + 32, :, s, :], start=(s == 0), stop=(s == 7),
                            tile_position=(32 * q, 0)), r=[Wt_r, uTf_r], w=[ps_r])
                    xa, xa_r = XA[q][ri]
                    self.act(lambda e, xa=xa, ps=ps: e.copy(out=xa[:, 1:513], in_=ps[:]), r=[ps_r], w=[xa_r])
                    ps, ps_r = self.next_ps()
                    for s in range(8):
                        self.pe(lambda e, ps=ps, q=q, ri=ri, s=s, Wt=Wt, uTs=uTs: e.matmul(
                            ps[:, 0:N_S], lhsT=Wt[32 * q:32 * q + 32, 2 * s + ri, :],
                            rhs=uTs[32 * q:32 * q + 32, s, :], start=(s == 0), stop=(s == 7),
                            tile_position=(32 * q, 0)), r=[Wt_r, uTs_r], w=[ps_r])
                    s0t, s0_r = s0[ri]
                    self.pe(lambda e, ps=ps, s0t=s0t, q=q: e.transpose(
                        out=ps[:, 32:32 + N_S], in_=s0t[:, q * 128:(q + 1) * 128], identity=self.ident_f[0:N_S, 0:N_S]),
                        r=[s0_r, self.ident_f_r], w=[ps_r])
                    self.act(lambda e, ps=ps, q=q, ri=ri: e.copy(out=XS[:, q, ri, :, 1], in_=ps[:, 0:N_S]), r=[ps_r], w=[XS_r])
                    self.act(lambda e, ps=ps, q=q, ri=ri: e.copy(out=XS[:, q, ri, :, 0], in_=ps[:, 32:32 + N_S]), r=[ps_r], w=[XS_r])
            for q in range(4):
                qq = 4 * f + q
                cur = XA[q]
                nxt = XB[q]
                for k in range(10):
                    d = 1 << k
                    n = 513 - d
                    cr = KC[:, qq, 3 * k:3 * k + 1]
                    ci = KC[:, qq, 3 * k + 1:3 * k + 2]
                    nci = KC[:, qq, 3 * k + 2:3 * k + 3]
                    (c_re, c_re_r), (c_im, c_im_r) = cur
                    (n_re, n_re_r), (n_im, n_im_r) = nxt
                    if q == 3:
                        tp_, tp_r = ktmp
                        self.dve(lambda e, c_re=c_re, n_re=n_re, cr=cr, d=d, n=n: e.scalar_tensor_tensor(
                            out=n_re[:, d:513], in0=c_re[:, 0:n], scalar=cr, in1=c_re[:, d:513], op0=ALU.mult, op1=ALU.add),
                            r=[c_re_r, KC_r], w=[n_re_r])
                        self.dve(lambda e, c_im=c_im, n_re=n_re, nci=nci, d=d, n=n: e.scalar_tensor_tensor(
                            out=n_re[:, d:513], in0=c_im[:, 0:n], scalar=nci, in1=n_re[:, d:513], op0=ALU.mult, op1=ALU.add),
                            r=[c_im_r, n_re_r, KC_r], w=[n_re_r])
                        for (o_, o_r, own, own_r, oth, oth_r, cc) in ((n_im, n_im_r, c_im, c_im_r, c_re, c_re_r, ci),):
                            self.pool(lambda e, o_=o_, own=own, cr=cr, d=d, n=n: e.tensor_scalar(out=o_[:, d:513], in0=own[:, 0:n], scalar1=cr, scalar2=None,
                                                                                                op0=ALU.mult), r=[own_r, KC_r], w=[o_r])
                            self.pool(lambda e, o_=o_, own=own, d=d: e.tensor_tensor(out=o_[:, d:513], in0=o_[:, d:513], in1=own[:, d:513], op=ALU.add),
                                      r=[o_r, own_r], w=[o_r])
                            self.pool(lambda e, tp_=tp_, oth=oth, cc=cc, n=n: e.tensor_scalar(out=tp_[:, 0:n], in0=oth[:, 0:n], scalar1=cc, scalar2=None,
                                                                                             op0=ALU.mult), r=[oth_r, KC_r], w=[tp_r])
                            self.pool(lambda e, o_=o_, tp_=tp_, d=d, n=n: e.tensor_tensor(out=o_[:, d:513], in0=o_[:, d:513], in1=tp_[:, 0:n], op=ALU.add),
                                      r=[o_r, tp_r], w=[o_r])
                        self.act(lambda e, c_re=c_re, n_re=n_re, d=d: e.copy(out=n_re[:, 0:d], in_=c_re[:, 0:d]), r=[c_re_r], w=[n_re_r])
                        self.act(lambda e, c_im=c_im, n_im=n_im, d=d: e.copy(out=n_im[:, 0:d], in_=c_im[:, 0:d]), r=[c_im_r], w=[n_im_r])
                        cur, nxt = nxt, cur
                        continue
                    self.dve(lambda e, c_re=c_re, n_re=n_re, cr=cr, d=d, n=n: e.scalar_tensor_tensor(
                        out=n_re[:, d:513], in0=c_re[:, 0:n], scalar=cr, in1=c_re[:, d:513], op0=ALU.mult, op1=ALU.add),
                        r=[c_re_r, KC_r], w=[n_re_r])
                    self.dve(lambda e, c_im=c_im, n_re=n_re, nci=nci, d=d, n=n: e.scalar_tensor_tensor(
                        out=n_re[:, d:513], in0=c_im[:, 0:n], scalar=nci, in1=n_re[:, d:513], op0=ALU.mult, op1=ALU.add),
                        r=[c_im_r, n_re_r, KC_r], w=[n_re_r])
                    self.dve(lambda e, c_im=c_im, n_im=n_im, cr=cr, d=d, n=n: e.scalar_tensor_tensor(
                        out=n_im[:, d:513], in0=c_im[:, 0:n], scalar=cr, in1=c_im[:, d:513], op0=ALU.mult, op1=ALU.add),
                        r=[c_im_r, KC_r], w=[n_im_r])
                    self.dve(lambda e, c_re=c_re, n_im=n_im, ci=ci, d=d, n=n: e.scalar_tensor_tensor(
                        out=n_im[:, d:513], in0=c_re[:, 0:n], scalar=ci, in1=n_im[:, d:513], op0=ALU.mult, op1=ALU.add),
                        r=[c_re_r, n_im_r, KC_r], w=[n_im_r])
                    self.act(lambda e, c_re=c_re, n_re=n_re, d=d: e.copy(out=n_re[:, 0:d], in_=c_re[:, 0:d]), r=[c_re_r], w=[n_re_r])
                    self.act(lambda e, c_im=c_im, n_im=n_im, d=d: e.copy(out=n_im[:, 0:d], in_=c_im[:, 0:d]), r=[c_im_r], w=[n_im_r])
                    cur, nxt = nxt, cur
                for ri in range(2):
                    xa, xa_r = cur[ri]
                    xb_, xb_r = Xb[q][ri]
                    self.act(lambda e, xa=xa, xb_=xb_: e.copy(out=xb_[:], in_=xa[:, 0:512]), r=[xa_r], w=[xb_r])
                    self.pool(lambda e, xa=xa, ri=ri, qq=qq: e.tensor_copy(out=Fin[:, ri, qq:qq + 1], in_=xa[:, 512:513]),
                              r=[xa_r], w=[Fin_r])
                cr = KC[:, qq, 0:1]
                ci = KC[:, qq, 1:2]
                nci = KC[:, qq, 2:3]
                (ta, ta_r), (tb, tb_r) = tS
                self.dve(lambda e, q=q, cr=cr: e.scalar_tensor_tensor(out=ta[:], in0=XS[:, q, 0, :, 0], scalar=cr, in1=XS[:, q, 0, :, 1],
                                                                      op0=ALU.mult, op1=ALU.add), r=[XS_r, KC_r], w=[ta_r])
                self.dve(lambda e, q=q, cr=cr: e.scalar_tensor_tensor(out=tb[:], in0=XS[:, q, 1, :, 0], scalar=cr, in1=XS[:, q, 1, :, 1],
                                                                      op0=ALU.mult, op1=ALU.add), r=[XS_r, KC_r], w=[tb_r])
                self.dve(lambda e, q=q, nci=nci, qq=qq: e.scalar_tensor_tensor(out=FinS[:, 0, qq, :], in0=XS[:, q, 1, :, 0], scalar=nci, in1=ta[:],
                                                                              op0=ALU.mult, op1=ALU.add), r=[XS_r, KC_r, ta_r], w=[FinS_r])
                self.dve(lambda e, q=q, ci=ci, qq=qq: e.scalar_tensor_tensor(out=FinS[:, 1, qq, :], in0=XS[:, q, 0, :, 0], scalar=ci, in1=tb[:],
                                                                             op0=ALU.mult, op1=ALU.add), r=[XS_r, KC_r, tb_r], w=[FinS_r])
                self.act(lambda e, q=q: e.copy(out=XSb[:, q, :, :], in_=XS[:, q, :, :, 0]), r=[XS_r], w=[XSb_r])
            for t in range(8):
                for smp in range(2):
                    ps, ps_r = self.next_ps()
                    nn = 512 if smp == 0 else N_S
                    first_mm = True
                    for s in range(t + 1):
                        rhs = uTf[:, :, s, :] if smp == 0 else uTs[:, s, :]
                        outp = ps[:].rearrange("p (a c) -> p a c", a=8) if smp == 0 else ps[:, 0:N_S]
                        self.pe(lambda e, outp=outp, rhs=rhs, t=t, s=s, f=f, fm=first_mm: e.matmul(
                            outp, lhsT=Kt[:, f, t - s, :], rhs=rhs, start=fm, stop=False),
                            r=[Kt_r, uTf_r, uTs_r], w=[ps_r])
                        first_mm = False
                    for q in range(4):
                        for ri in range(2):
                            rhs = Xb[q][ri][0][:, 0:512] if smp == 0 else XSb[:, q, ri, :]
                            rr = Xb[q][ri][1] if smp == 0 else XSb_r
                            last = (q == 3 and ri == 1)
                            self.pe(lambda e, ps=ps, rhs=rhs, q=q, ri=ri, t=t, Vt=Vt, nn=nn, last=last: e.matmul(
                                ps[32 * q:32 * q + 32, 0:nn], lhsT=Vt[:, q, 2 * t + ri, :], rhs=rhs, start=False, stop=last,
                                tile_position=(0, 32 * q)), r=[Vt_r, rr], w=[ps_r])
                    yt, yt_r = ytmps[yi % 2]
                    yi += 1
                    if smp == 0:
                        self.dve(lambda e, yt=yt, ps=ps, t=t, f=f, uTf=uTf: e.scalar_tensor_tensor(
                            out=yt[:].rearrange("p (a c) -> p a c", a=8), in0=uTf[:, :, t, :], scalar=Dk[:, f:f + 1],
                            in1=ps[:].rearrange("p (a c) -> p a c", a=8), op0=ALU.mult, op1=ALU.add),
                            r=[uTf_r, Dk_r, ps_r], w=[yt_r])
                        self.act(lambda e, yt=yt, gbuf=gbuf, t=t: e.activation(
                            out=gbuf[:, 0:T_P].rearrange("p (a c s) -> p a c s", a=8, s=8)[:, :, :, t],
                            in_=yt[:].rearrange("p (a c) -> p a c", a=8), func=AF.Gelu_apprx_tanh), r=[yt_r], w=[gbuf_r])
                    else:
                        self.dve(lambda e, yt=yt, ps=ps, t=t, f=f, uTs=uTs: e.scalar_tensor_tensor(
                            out=yt[:, 0:N_S], in0=uTs[:, t, :], scalar=Dk[:, f:f + 1], in1=ps[:, 0:N_S],
                            op0=ALU.mult, op1=ALU.add), r=[uTs_r, Dk_r, ps_r], w=[yt_r])
                        self.act(lambda e, yt=yt, gbuf=gbuf, t=t: e.activation(
                            out=gbuf[:, T_P:TOK].rearrange("p (n s) -> p n s", s=8)[:, :, t],
                            in_=yt[:, 0:N_S], func=AF.Gelu_apprx_tanh), r=[yt_r], w=[gbuf_r])
            self.ld(self.gT_d[f], gbuf[:], r=[gbuf_r], w=[self.gT_r[f]])
        for ri in range(2):
            ps, ps_r = self.next_ps()
            self.pe(lambda e, ps=ps, ri=ri: e.transpose(out=ps[0:64, 0:128], in_=Fin[:, ri, :], identity=self.ident_f[:]),
                    r=[Fin_r, self.ident_f_r], w=[ps_r])
            fo_, fo_r = T("fo", [64, 128])
            self.act(lambda e, ps=ps, fo_=fo_: e.copy(out=fo_[:], in_=ps[0:64, 0:128]), r=[ps_r], w=[fo_r])
            self.store((self.o_sre_p if ri == 0 else self.o_sim_p)[slot], fo_[:], r=[fo_r])
            fs_, fs_r = T("fs", [64, N_S, 128])
            for n in range(N_S):
                ps, ps_r = self.next_ps()
                self.pe(lambda e, ps=ps, ri=ri, n=n: e.transpose(out=ps[0:64, 0:128], in_=FinS[:, ri, :, n], identity=self.ident_f[:]),
                        r=[FinS_r, self.ident_f_r], w=[ps_r])
                self.act(lambda e, ps=ps, fs_=fs_, n=n: e.copy(out=fs_[:, n, :], in_=ps[0:64, 0:128]), r=[ps_r], w=[fs_r])
            self.store((self.o_sre_s if ri == 0 else self.o_sim_s)[slot].rearrange("n q c -> q n c"), fs_[:], r=[fs_r])

    def s5_out(self, st, slot, first):
        P = self.P

        def T(name, shape, dt=F32):
            return P.sb(st, name, shape, dt)
        wg, wg_r = T("wglu", [128, 16, E], BF16)
        wgv = self.ssm_w_glu[slot].rearrange("(k p) n -> p k n", p=128)
        for k in range(16):
            self.ldc(wg[:, k, :], wgv[:, k, :], w=[wg_r])
        wo, wo_r = T("wout", [128, 16, D_MODEL], BF16)
        wov = self.ssm_w_out[slot].rearrange("(k p) n -> p k n", p=128)
        for k in range(0, 16, 2):
            self.ldc(wo[:, k:k + 2, :], wov[:, k:k + 2, :], w=[wo_r])
        bg, bg_r = T("bglu", [128, 16])
        self.P.dma("sp", lambda e: e.dma_start(out=bg[:], in_=self.ssm_b_glu[slot].rearrange("(f p) -> p f", p=128),
                                              allow_slow_non_contiguous=True), (), [bg_r])
        gin, gin_r = T("gin", [128, 16, 512], BF16)
        szin, szin_r = T("szin", [128, 16, 512], BF16)
        g2T, g2T_r = T("g2T", [128, 16, 512], BF16)
        sig = [T("sig", [128, 512]) for _ in range(2)]
        hbs = [T("hb", [128, D_MODEL]) for _ in range(2)]
        hi = 0
        for tg in range(NGRP):
            tok0, ntok = grp_tok(tg)
            self.ld(gin[:, :, 0:ntok], self.gT_d[:, :, tok0:tok0 + ntok].rearrange("f p t -> p f t"), r=self.gT_r, w=[gin_r])
            self.ld(szin[:, :, 0:ntok], self.szT_d[:, :, tok0:tok0 + ntok].rearrange("f p t -> p f t"), r=self.szT_r, w=[szin_r])
            for fo in range(16):
                ps, ps_r = self.next_ps()
                for f in range(16):
                    self.pe(lambda e, ps=ps, f=f, fo=fo, ntok=ntok: e.matmul(
                        ps[:, 0:ntok], lhsT=wg[:, f, fo * 128:(fo + 1) * 128], rhs=gin[:, f, 0:ntok],
                        start=(f == 0), stop=(f == 15)), r=[wg_r, gin_r], w=[ps_r])
                sg, sg_r = sig[fo % 2]
                self.act(lambda e, ps=ps, sg=sg, fo=fo, ntok=ntok: e.activation(
                    out=sg[:, 0:ntok], in_=ps[:, 0:ntok], func=AF.Sigmoid, bias=bg[:, fo:fo + 1]), r=[ps_r, bg_r], w=[sg_r])
                self.dve(lambda e, sg=sg, fo=fo, ntok=ntok: e.tensor_tensor(
                    out=sg[:, 0:ntok], in0=sg[:, 0:ntok], in1=gin[:, fo, 0:ntok], op=ALU.mult), r=[sg_r, gin_r], w=[sg_r])
                self.pool(lambda e, sg=sg, fo=fo, ntok=ntok: e.tensor_tensor(
                    out=g2T[:, fo, 0:ntok], in0=sg[:, 0:ntok], in1=szin[:, fo, 0:ntok], op=ALU.mult), r=[sg_r, szin_r], w=[g2T_r])
            for il in range(ntok // 128):
                i = tok0 // 128 + il
                hb, hb_r = hbs[hi % 2]
                hi += 1
                self.ld(hb[:], self.h_src(first, i), r=[self.H_r[i]], w=[hb_r])
                for hh in range(2):
                    ps, ps_r = self.next_ps()
                    for f in range(16):
                        self.pe(lambda e, ps=ps, f=f, hh=hh, il=il: e.matmul(
                            ps[:], lhsT=g2T[:, f, il * 128:(il + 1) * 128], rhs=wo[:, f, hh * 512:(hh + 1) * 512],
                            start=(f == 0), stop=(f == 15)), r=[wo_r, g2T_r], w=[ps_r])
                    self.dve(lambda e, ps=ps, hb=hb, hh=hh: e.tensor_tensor(
                        out=hb[:, hh * 512:(hh + 1) * 512], in0=ps[:], in1=hb[:, hh * 512:(hh + 1) * 512], op=ALU.add),
                        r=[ps_r, hb_r], w=[hb_r])
                self.ld(self.H[i * 128:(i + 1) * 128, :], hb[:], r=[hb_r], w=[self.H_r[i]])

    def dsa_layer(self):
        P = self.P
        with ExitStack() as st:
            self.dsa_proj(st)
        P.barrier()
        if self.dsa_stage >= 2:
            with ExitStack() as st:
                self.dsa_prompt(st)
            P.barrier()
        if self.dsa_stage >= 3:
            with ExitStack() as st:
                self.dsa_sample(st)
            P.barrier()

    def dsa_proj(self, st):
        P = self.P

        def T(name, shape, dt=F32):
            return P.sb(st, name, shape, dt)
        win, win_r = T("awin", [128, 8, ATTN_IN], BF16)
        wv = self.attn_w_in.rearrange("(k p) n -> p k n", p=128)
        for k in range(8):
            self.ldc(win[:, k, :], wv[:, k, :], w=[win_r])
        gt, gt_r = T("agt", [128, D_MODEL])
        self.ld(gt[:], self.attn_norm.partition_broadcast(128), w=[gt_r])
        hts_ = [T("aht", [128, D_MODEL])] * 2
        junk, junk_r = T("ajunk", [128, D_MODEL], BF16)
        sss_ = [T("ass", [128, 1]) for _ in range(2)]
        xns_ = [T("axn", [128, D_MODEL], BF16)] * 2
        xTs = [T("axT", [128, 8, 128], BF16) for _ in range(2)]
        prs = [T("pr", [128, ATTN_IN]) for _ in range(2)]
        rps = [T("rp", [128, 2880]) for _ in range(2)]
        szbs = [T("szb", [128, E], BF16) for _ in range(2)]
        vxbs = [T("vxb", [128, 4, 65], BF16) for _ in range(2)]
        for (t_, r_) in vxbs:
            self.pool(lambda e, t_=t_: e.memset(t_[:, :, 64:65], 1.0), w=[r_])
        r1, r1_r = T("r1", [128, 36, 32])
        r2, r2_r = T("r2", [128, 36, 32])
        css = [T("cs", [128, 64]) for _ in range(2)]
        wibs = [T("wib", [128, 8]) for _ in range(2)]
        kks = [T("kk", [128, 5, 2, 64]) for _ in range(2)]
        TSs = [T("TS", [128, 25, 128], BF16) for _ in range(2)]
        TSs_s, TSs_s_r = T("TSsmp", [64, 40, 128], BF16)
        COLS = [(c0, min(512, ATTN_IN - c0)) for c0 in range(0, ATTN_IN, 512)]
        for i in range(NTILE):
            xT, xT_r = xTs[i % 2]
            pr, pr_r = prs[i % 2]
            rp, rp_r = rps[i % 2]
            szb, szb_r = szbs[i % 2]
            vxb, vxb_r = vxbs[i % 2]
            cs, cs_r = css[i % 2]
            wib, wib_r = wibs[i % 2]
            kk, kk_r = kks[i % 2]
            TS, TS_r = TSs[i % 2]
            tok0 = i * 128
            smp = (i == NTILE - 1)
            ht, ht_r = hts_[i % 2]
            ss, ss_r = sss_[i % 2]
            xn, xn_r = xns_[i % 2]
            self.norm_tile(self.H[tok0:tok0 + 128, :], self.H_r[i], gt, gt_r, ht, ht_r, junk, junk_r, ss, ss_r,
                           xn, xn_r, xT[:], xT_r)
            self.ld(cs[:], self.rope_cs[tok0:tok0 + 128, :], w=[cs_r])
            for ci, (c0, w) in enumerate(COLS):
                ps, ps_r = self.next_ps()
                for k in range(8):
                    self.pe(lambda e, ps=ps, k=k, c0=c0, w=w, xT=xT: e.matmul(
                        ps[:, 0:w], lhsT=xT[:, k, :], rhs=win[:, k, c0:c0 + w], start=(k == 0), stop=(k == 7)),
                        r=[win_r, xT_r], w=[ps_r])
                if 5 <= ci <= 8:
                    self.act(lambda e, ps=ps, szb=szb, ci=ci: e.activation(
                        out=szb[:, (ci - 5) * 512:(ci - 4) * 512], in_=ps[:], func=AF.Silu), r=[ps_r], w=[szb_r])
                else:
                    self.act(lambda e, ps=ps, pr=pr, c0=c0, w=w: e.copy(out=pr[:, c0:c0 + w], in_=ps[:, 0:w]), r=[ps_r], w=[pr_r])
            self.ld(self.sz_d[tok0:tok0 + 128, :], szb[:], r=[szb_r], w=[self.dsa_r])
            for (H, s0_, d0_) in ((36, 0, 0), (9, 4608, 2304)):
                src = pr[:, s0_:s0_ + H * 64].rearrange("p (h c d) -> p h c d", c=2, d=32)
                dst = rp[:, d0_:d0_ + H * 64].rearrange("p (h c d) -> p h c d", c=2, d=32)
                cosb = bc(cs[:, 0:32], 1, H)
                sinb = bc(cs[:, 32:64], 1, H)
                x1 = src[:, :, 0, :]
                x2 = src[:, :, 1, :]
                self.dve(lambda e, x1=x1, cosb=cosb, H=H: e.tensor_tensor(out=r1[:, 0:H, :], in0=x1, in1=cosb, op=ALU.mult), r=[pr_r, cs_r], w=[r1_r])
                self.pool(lambda e, x2=x2, sinb=sinb, H=H: e.tensor_tensor(out=r2[:, 0:H, :], in0=x2, in1=sinb, op=ALU.mult), r=[pr_r, cs_r], w=[r2_r])
                self.dve(lambda e, dst=dst, H=H: e.tensor_tensor(out=dst[:, :, 0, :], in0=r1[:, 0:H, :], in1=r2[:, 0:H, :], op=ALU.subtract),
                         r=[r1_r, r2_r], w=[rp_r])
                self.dve(lambda e, x2=x2, cosb=cosb, H=H: e.tensor_tensor(out=r1[:, 0:H, :], in0=x2, in1=cosb, op=ALU.mult), r=[pr_r, cs_r], w=[r1_r])
                self.pool(lambda e, x1=x1, sinb=sinb, H=H: e.tensor_tensor(out=r2[:, 0:H, :], in0=x1, in1=sinb, op=ALU.mult), r=[pr_r, cs_r], w=[r2_r])
                self.dve(lambda e, dst=dst, H=H: e.tensor_tensor(out=dst[:, :, 1, :], in0=r1[:, 0:H, :], in1=r2[:, 0:H, :], op=ALU.add),
                         r=[r1_r, r2_r], w=[rp_r])
            if not smp:
                self.store(self.o_k_p[tok0:tok0 + 128, :], rp[:, 2048:2304], r=[rp_r])
                self.store(self.o_v_p[tok0:tok0 + 128, :], pr[:, 2304:2560], r=[pr_r])
                self.store(self.o_kidx_p[tok0:tok0 + 128, :], rp[:, 2816:2880], r=[rp_r])
            else:
                self.store(self.o_k_s[:, :], rp[:, 2048:2304], r=[rp_r])
                self.store(self.o_v_s[:, :], pr[:, 2304:2560], r=[pr_r])
                self.store(self.o_kidx_s[:, :], rp[:, 2816:2880], r=[rp_r])
            self.dve(lambda e, wib=wib, pr=pr: e.tensor_scalar(out=wib[:], in0=pr[:, 5184:5192], scalar1=8.0 ** -1.5, scalar2=None, op0=ALU.mult),
                     r=[pr_r], w=[wib_r])
            self.ld(self.wi_d[tok0:tok0 + 128, :], wib[:], r=[wib_r], w=[self.dsa_r])
            self.pool(lambda e, vxb=vxb, pr=pr: e.tensor_copy(out=vxb[:, :, 0:64], in_=pr[:, 2304:2560].rearrange("p (g d) -> p g d", g=4)),
                      r=[pr_r], w=[vxb_r])
            self.ld(self.Vx_d[tok0:tok0 + 128, :], vxb[:].rearrange("p g d -> p (g d)"), r=[vxb_r], w=[self.dsa_r])
            self.pool(lambda e, kk=kk, rp=rp: e.tensor_copy(out=kk[:, 0:4, :, :], in_=bc(rp[:, 2048:2304].rearrange("p (g d) -> p g d", g=4), 2, 2)),
                      r=[rp_r], w=[kk_r])
            self.pool(lambda e, kk=kk, rp=rp: e.tensor_copy(out=kk[:, 4, :, :], in_=bc(rp[:, 2816:2880], 1, 2)), r=[rp_r], w=[kk_r])
            srcs = [(rp[:, 128 * a:128 * a + 128], rp_r) for a in range(16)]
            srcs += [(kk[:, g, :, :].rearrange("p a d -> p (a d)"), kk_r) for g in range(4)]
            srcs += [(rp[:, 2304 + 128 * a:2304 + 128 * a + 128], rp_r) for a in range(4)]
            srcs += [(kk[:, 4, :, :].rearrange("p a d -> p (a d)"), kk_r)]
            for b0 in range(0, 25, 4):
                nb = min(4, 25 - b0)
                ps, ps_r = self.next_ps()
                for j in range(nb):
                    src, src_r = srcs[b0 + j]
                    self.pe(lambda e, ps=ps, j=j, src=src: e.transpose(out=ps[:, j * 128:(j + 1) * 128], in_=src, identity=self.ident_f[:]),
                            r=[src_r, self.ident_f_r], w=[ps_r])
                self.act(lambda e, ps=ps, TS=TS, b0=b0, nb=nb: e.copy(out=TS[:, b0:b0 + nb, :],
                                                                    in_=ps[:, 0:nb * 128].rearrange("p (a t) -> p a t", t=128)),
                         r=[ps_r], w=[TS_r])
            self.ld(self.qT_d[:, :, tok0:tok0 + 128].rearrange("a p t -> p a t"), TS[:, 0:16, :], r=[TS_r], w=[self.dsa_r])
            self.ld(self.kT2_d[:, :, tok0:tok0 + 128].rearrange("a p t -> p a t"), TS[:, 16:20, :], r=[TS_r], w=[self.dsa_r])
            self.ld(self.qiT_d[:, :, tok0:tok0 + 128].rearrange("a p t -> p a t"), TS[:, 20:24, :], r=[TS_r], w=[self.dsa_r])
            self.ld(self.kiT2_d[:, :, tok0:tok0 + 128].rearrange("a p t -> p a t"), TS[:, 24:25, :], r=[TS_r], w=[self.dsa_r])
            if smp:
                hs = [(rp[:, 64 * h:64 * h + 64], rp_r) for h in range(32)] + [(rp[:, 2304 + 64 * h:2304 + 64 * h + 64], rp_r) for h in range(8)]
                for b0 in range(0, 40, 4):
                    ps, ps_r = self.next_ps()
                    for j in range(4):
                        src, src_r = hs[b0 + j]
                        self.pe(lambda e, ps=ps, j=j, src=src: e.transpose(out=ps[0:64, j * 128:(j + 1) * 128], in_=src, identity=self.ident_f[:]),
                                r=[src_r, self.ident_f_r], w=[ps_r])
                    self.act(lambda e, ps=ps, b0=b0: e.copy(out=TSs_s[:, b0:b0 + 4, :], in_=ps[0:64, :].rearrange("p (a t) -> p a t", t=128)),
                             r=[ps_r], w=[TSs_s_r])
                self.ld(self.qTs_d[:, :, :], TSs_s[:, 0:32, :], r=[TSs_s_r], w=[self.dsa_r])
                self.ld(self.qiTs_d[:, :, :], TSs_s[:, 32:40, :], r=[TSs_s_r], w=[self.dsa_r])

    def attn_out_tile(self, i, g2, g2_r, g2T, g2T_r, wo, wo_r, hb, hb_r):
        pt, pt_r = self.pst
        for b0 in range(0, 16, 8):
            for j in range(8):
                self.pe(lambda e, j=j, b0=b0: e.transpose(out=pt[:, j * 128:(j + 1) * 128], in_=g2[:, (b0 + j) * 128:(b0 + j + 1) * 128],
                                                          identity=self.ident_b[:]), r=[g2_r, self.ident_b_r], w=[pt_r])
            self.act(lambda e, b0=b0: e.copy(out=g2T[:, b0:b0 + 8, :], in_=pt[:].rearrange("p (a t) -> p a t", t=128)), r=[pt_r], w=[g2T_r])
        self.ld(hb[:], self.H[i * 128:(i + 1) * 128, :], r=[self.H_r[i]], w=[hb_r])
        for hh in range(2):
            ps, ps_r = self.next_ps()
            for f in range(16):
                self.pe(lambda e, ps=ps, f=f, hh=hh: e.matmul(ps[:], lhsT=g2T[:, f, :], rhs=wo[:, f, hh * 512:(hh + 1) * 512],
                                                              start=(f == 0), stop=(f == 15)), r=[wo_r, g2T_r], w=[ps_r])
            self.dve(lambda e, ps=ps, hh=hh: e.tensor_tensor(out=hb[:, hh * 512:(hh + 1) * 512], in0=ps[:], in1=hb[:, hh * 512:(hh + 1) * 512],
                                                            op=ALU.add), r=[ps_r, hb_r], w=[hb_r])
        self.ld(self.H[i * 128:(i + 1) * 128, :], hb[:], r=[hb_r], w=[self.H_r[i]])

    def topk_threshold(self, sc, sc_r, S, lo, hi, mid, cnt, sel, nsel, small_r, junk, junk_r, iters=24):
        for it in range(iters):
            self.dve(lambda e: e.tensor_scalar(out=mid[:], in0=lo[:], scalar1=hi[:, 0:1], scalar2=0.5, op0=ALU.add, op1=ALU.mult),
                     r=[small_r], w=[small_r])
            self.dve(lambda e: e.tensor_scalar(out=junk[:, 0:S], in0=sc[:, 0:S], scalar1=mid[:, 0:1], scalar2=0.0, op0=ALU.is_ge, op1=ALU.add,
                                               accum_out=cnt[:]), r=[sc_r, small_r], w=[junk_r, small_r])
            self.dve(lambda e: e.tensor_scalar(out=sel[:], in0=cnt[:], scalar1=255.5, scalar2=None, op0=ALU.is_ge), r=[small_r], w=[small_r])
            self.dve(lambda e: e.tensor_scalar(out=nsel[:], in0=cnt[:], scalar1=255.5, scalar2=None, op0=ALU.is_lt), r=[small_r], w=[small_r])
            self.dve(lambda e: e.copy_predicated(out=lo[:], mask=sel[:], data=mid[:]), r=[small_r], w=[small_r])
            self.dve(lambda e: e.copy_predicated(out=hi[:], mask=nsel[:], data=mid[:]), r=[small_r], w=[small_r])

    def dsa_prompt(self, st):
        P = self.P

        def T(name, shape, dt=F32):
            return P.sb(st, name, shape, dt)
        self.ps_lim = 5
        accs = [self.psb[5], self.psb[6]]
        kT2, kT2_r = T("kT2", [128, 4, T_P], BF16)
        for g in range(4):
            self.ld(kT2[:, g, :], self.kT2_d[g, :, 0:T_P], r=[self.dsa_r], w=[kT2_r])
        kiT2, kiT2_r = T("kiT2", [128, T_P], BF16)
        self.ld(kiT2[:], self.kiT2_d[0, :, 0:T_P], r=[self.dsa_r], w=[kiT2_r])
        Vx, Vx_r = T("Vx", [128, 32, 260], BF16)
        for k4 in range(4):
            self.ld(Vx[:, 8 * k4:8 * k4 + 8, :], self.Vx_d[1024 * k4:1024 * (k4 + 1), :].rearrange("(kt s) c -> s kt c", s=128),
                    r=[self.dsa_r], w=[Vx_r])
        wo, wo_r = T("awo", [128, 16, D_MODEL], BF16)
        self.wo_attn = (wo, wo_r)
        wov = self.attn_w_out.rearrange("(k p) n -> p k n", p=128)
        for k in range(0, 16, 2):
            self.ldc(wo[:, k:k + 2, :], wov[:, k:k + 2, :], w=[wo_r])
        cmask, cmask_r = T("cmask", [128, 128])
        self.ld(cmask[:], self.cmask_d, w=[cmask_r])
        qTs = [T("qT", [128, 16, 128], BF16) for _ in range(2)]
        qiTs = [T("qiT", [128, 4, 128], BF16) for _ in range(2)]
        wis = [T("wi", [128, 8]) for _ in range(2)]
        szs = [T("sz", [128, E], BF16) for _ in range(2)]
        hb, hb_r = T("ahb", [128, D_MODEL])
        sc, sc_r = T("sc", [128, T_P])
        tmps = [T("sctmp", [128, 512]) for _ in range(2)]
        junk, junk_r = T("scjunk", [128, T_P], BF16)
        Mb, Mb_r = T("Mb", [128, T_P], BF16)
        MTn, MTn_r = T("MTn", [128, 32, 128], BF16)
        pexps = [T("pexp", [128, 4, 128], BF16) for _ in range(4)]
        g2, g2_r = T("g2", [128, E], BF16)
        of_, of_r = T("of", [128, 4, 64])
        rc, rc_r = T("rc", [128, 4])
        g2T, g2T_r = T("g2T", [128, 16, 128], BF16)
        lo, small_r = T("lo", [128, 1])
        hi, _ = T("hi", [128, 1])
        mid, _ = T("mid", [128, 1])
        cnt, _ = T("cnt", [128, 1])
        sel, _ = T("sel", [128, 1], I32)
        nsel, _ = T("nsel", [128, 1], I32)
        pt, pt_r = self.pst
        MTns = [(MTn, MTn_r), T("MTn2", [128, 32, 128], BF16)]
        accS = [[T("accS", [128, 260]) for _ in range(2)] for _ in range(4)]
        st_ = {"pe_i": 0}

        def stage_idx(qt):
            nk = qt + 1
            S = 128 * nk
            qT, qT_r = qTs[qt % 2]
            qiT, qiT_r = qiTs[qt % 2]
            wi, wi_r = wis[qt % 2]
            sz, sz_r = szs[qt % 2]
            t0 = qt * 128
            self.ld(qT[:], self.qT_d[:, :, t0:t0 + 128].rearrange("a p t -> p a t"), r=[self.dsa_r], w=[qT_r])
            self.ld(qiT[:], self.qiT_d[:, :, t0:t0 + 128].rearrange("a p t -> p a t"), r=[self.dsa_r], w=[qiT_r])
            self.ld(wi[:], self.wi_d[t0:t0 + 128, :], r=[self.dsa_r], w=[wi_r])
            self.ld(sz[:], self.sz_d[t0:t0 + 128, :], r=[self.dsa_r], w=[sz_r])
            ti = 0
            for c0 in range(0, S, 512):
                n = min(512, S - c0)
                for h in range(8):
                    a, hf = h // 2, h % 2
                    ps, ps_r = self.next_ps()
                    self.pe(lambda e, ps=ps, a=a, hf=hf, c0=c0, n=n, qiT=qiT: e.matmul(
                        ps[:, 0:n], lhsT=qiT[64 * hf:64 * hf + 64, a, :], rhs=kiT2[64 * hf:64 * hf + 64, c0:c0 + n],
                        start=True, stop=True, tile_position=(64 * hf, 0)), r=[qiT_r, kiT2_r], w=[ps_r])
                    if h == 0:
                        self.dve(lambda e, ps=ps, c0=c0, n=n, wi=wi: e.tensor_scalar(
                            out=sc[:, c0:c0 + n], in0=ps[:, 0:n], scalar1=0.0, scalar2=wi[:, 0:1], op0=ALU.max, op1=ALU.mult),
                            r=[ps_r, wi_r], w=[sc_r])
                    else:
                        tmp, tmp_r = tmps[ti % 2]
                        ti += 1
                        self.dve(lambda e, ps=ps, n=n, wi=wi, h=h, tmp=tmp: e.tensor_scalar(
                            out=tmp[:, 0:n], in0=ps[:, 0:n], scalar1=0.0, scalar2=wi[:, h:h + 1], op0=ALU.max, op1=ALU.mult),
                            r=[ps_r, wi_r], w=[tmp_r])
                        self.pool(lambda e, c0=c0, n=n, tmp=tmp: e.tensor_tensor(
                            out=sc[:, c0:c0 + n], in0=sc[:, c0:c0 + n], in1=tmp[:, 0:n], op=ALU.add), r=[tmp_r, sc_r], w=[sc_r])
            if qt >= 2:
                self.dve(lambda e, S=S: e.tensor_reduce(out=hi[:], in_=sc[:, 0:S], axis=AX.X, op=ALU.max), r=[sc_r], w=[small_r])
                self.dve(lambda e, S=S: e.tensor_reduce(out=lo[:], in_=sc[:, 0:S], axis=AX.X, op=ALU.min), r=[sc_r], w=[small_r])
            else:
                self.pool(lambda e: e.memset(lo[:], -1e29), w=[small_r])
            self.dve(lambda e, t0=t0: e.tensor_tensor(out=sc[:, t0:t0 + 128], in0=sc[:, t0:t0 + 128], in1=cmask[:], op=ALU.add),
                     r=[sc_r, cmask_r], w=[sc_r])
            if qt >= 2:
                self.topk_threshold(sc, sc_r, S, lo, hi, mid, cnt, sel, nsel, small_r, junk, junk_r)
            self.dve(lambda e, S=S: e.tensor_scalar(out=Mb[:, 0:S], in0=sc[:, 0:S], scalar1=lo[:, 0:1], scalar2=None, op0=ALU.is_ge),
                     r=[sc_r, small_r], w=[Mb_r])

        def stage_mt(qt):
            nk = qt + 1
            MTc, MTc_r = MTns[qt % 2]
            for k0 in range(0, nk, 8):
                nb = min(8, nk - k0)
                for j in range(nb):
                    self.pe(lambda e, j=j, k0=k0: e.transpose(out=pt[:, j * 128:(j + 1) * 128], in_=Mb[:, (k0 + j) * 128:(k0 + j + 1) * 128],
                                                              identity=self.ident_b[:]), r=[Mb_r, self.ident_b_r], w=[pt_r])
                self.dve(lambda e, k0=k0, nb=nb, MTc=MTc: e.tensor_scalar(out=MTc[:, k0:k0 + nb, :], in0=pt[:, 0:nb * 128].rearrange("p (a t) -> p a t", t=128),
                                                                          scalar1=-1.0, scalar2=30000.0, op0=ALU.add, op1=ALU.mult), r=[pt_r], w=[MTc_r])

        def stage_attn(qt):
            nk = qt + 1
            qT, qT_r = qTs[qt % 2]
            sz, sz_r = szs[qt % 2]
            MTc, MTc_r = MTns[qt % 2]
            for g in range(4):
                for kt in range(nk):
                    pex = []
                    for par in range(2):
                        ps, ps_r = self.next_ps()
                        psv = ps[:].rearrange("p (a t) -> p a t", t=128)
                        self.pe(lambda e, psv=psv, par=par, g=g, kt=kt, qT=qT: e.matmul(
                            psv, lhsT=kT2[64 * par:64 * par + 64, g, kt * 128:(kt + 1) * 128], rhs=qT[64 * par:64 * par + 64, 4 * g:4 * g + 4, :],
                            start=True, stop=False, tile_position=(64 * par, 0)), r=[kT2_r, qT_r], w=[ps_r])
                        self.pe(lambda e, psv=psv, kt=kt, MTc=MTc: e.matmul(psv, lhsT=self.ident_b[:], rhs=bc(MTc[:, kt, :], 1, 4), start=False, stop=True),
                                r=[self.ident_b_r, MTc_r], w=[ps_r])
                        px, px_r = pexps[st_["pe_i"] % 4]
                        st_["pe_i"] += 1
                        self.act(lambda e, px=px, psv=psv: e.activation(out=px[:], in_=psv, func=AF.Exp, scale=0.125), r=[ps_r], w=[px_r])
                        pex.append((px, px_r))
                    for h8 in range(8):
                        par, a = h8 % 2, h8 // 2
                        acc, acc_r = accs[h8 // 4]
                        col = (h8 % 4) * 65
                        px, px_r = pex[par]
                        self.pe(lambda e, acc=acc, col=col, px=px, a=a, kt=kt, g=g, h8=h8, nk=nk: e.matmul(
                            acc[:, col:col + 65], lhsT=px[:, a, :], rhs=Vx[:, kt, g * 65:(g + 1) * 65],
                            start=(kt == 0 and h8 % 4 == 0), stop=(kt == nk - 1), skip_group_check=True), r=[px_r, Vx_r], w=[acc_r])
                for half in range(2):
                    acc, acc_r = accs[half]
                    aS, aS_r = accS[g][half]
                    self.act(lambda e, acc=acc, aS=aS: e.copy(out=aS[:], in_=acc[:, 0:260]), r=[acc_r], w=[aS_r])
            for g in range(4):
                for half in range(2):
                    aS, aS_r = accS[g][half]
                    accv = aS[:].rearrange("p (h c) -> p h c", c=65)
                    c0 = 64 * (8 * g + 4 * half)
                    self.dve(lambda e, accv=accv: e.reciprocal(out=rc[:], in_=accv[:, :, 64]), r=[aS_r], w=[rc_r])
                    self.dve(lambda e, accv=accv: e.tensor_tensor(out=of_[:], in0=accv[:, :, 0:64], in1=bc(rc[:], 2, 64), op=ALU.mult),
                             r=[aS_r, rc_r], w=[of_r])
                    self.pool(lambda e, c0=c0, sz=sz: e.tensor_tensor(out=g2[:, c0:c0 + 256], in0=of_[:].rearrange("p h d -> p (h d)"),
                                                                     in1=sz[:, c0:c0 + 256], op=ALU.mult), r=[of_r, sz_r], w=[g2_r])
            self.attn_out_tile(qt, g2, g2_r, g2T, g2T_r, wo, wo_r, hb, hb_r)

        NQ = T_P // 128
        stage_idx(0)
        stage_mt(0)
        for qt in range(NQ):
            if qt + 1 < NQ:
                stage_idx(qt + 1)
            stage_attn(qt)
            if qt + 1 < NQ:
                stage_mt(qt + 1)
        self.ps_lim = 7

    def dsa_sample(self, st):
        P = self.P

        def T(name, shape, dt=F32):
            return P.sb(st, name, shape, dt)
        self.ps_lim = 5
        acc, acc_r = self.psb[5]
        pt, pt_r = self.pst
        wo, wo_r = T("awo2", [128, 16, D_MODEL], BF16)
        wov = self.attn_w_out.rearrange("(k p) n -> p k n", p=128)
        for k in range(0, 16, 2):
            self.ldc(wo[:, k:k + 2, :], wov[:, k:k + 2, :], w=[wo_r])
        pti, pti_r = T("pti", [128, N_S * 16], I32)
        self.ld(pti[:], self.page_table.partition_broadcast(128), w=[pti_r])
        iota, iota_r = T("iota", [128, 1])
        self.ld(iota[:], self.iota_d, w=[iota_r])
        ptf, ptf_r = T("ptf", [128, N_S * 16])
        self.dve(lambda e: e.tensor_copy(out=ptf[:], in_=pti[:]), r=[pti_r], w=[ptf_r])
        self.dve(lambda e: e.tensor_scalar(out=ptf[:], in0=ptf[:], scalar1=128.0, scalar2=iota[:, 0:1], op0=ALU.mult, op1=ALU.add),
                 r=[ptf_r, iota_r], w=[ptf_r])
        idx, idx_r = T("idx", [128, N_S * 16], I32)
        self.dve(lambda e: e.tensor_copy(out=idx[:], in_=ptf[:]), r=[ptf_r], w=[idx_r])
        qTs, qTs_r = T("qTs", [64, N_S, 32, 8], BF16)
        qiTs, qiTs_r = T("qiTs", [64, N_S, 8, 8], BF16)
        for n in range(N_S):
            self.ld(qTs[:, n, :, :], self.qTs_d[:, :, 8 * n:8 * n + 8], r=[self.dsa_r], w=[qTs_r])
            self.ld(qiTs[:, n, :, :], self.qiTs_d[:, :, 8 * n:8 * n + 8], r=[self.dsa_r], w=[qiTs_r])
        kTn, kTn_r = T("kTn", [64, 4, T_S], BF16)
        self.ld(kTn[:], self.kT2_d[:, 0:64, T_P:TOK].rearrange("g p t -> p g t"), r=[self.dsa_r], w=[kTn_r])
        kiTn, kiTn_r = T("kiTn", [64, T_S], BF16)
        self.ld(kiTn[:], self.kiT2_d[0, 0:64, T_P:TOK], r=[self.dsa_r], w=[kiTn_r])
        Vxn, Vxn_r = T("Vxn", [128, 260], BF16)
        self.ld(Vxn[:], self.Vx_d[T_P:TOK, :], r=[self.dsa_r], w=[Vxn_r])
        szs_, szs_r = T("szsmp", [128, E], BF16)
        self.ld(szs_[:], self.sz_d[T_P:TOK, :], r=[self.dsa_r], w=[szs_r])
        wst, wst_r = T("wst", [64, N_S])
        for h in range(8):
            self.P.dma("sp", lambda e, h=h: e.dma_start(out=wst[8 * h:8 * h + 8, :], in_=self.wi_d[T_P:TOK, h].rearrange("(n t) -> t n", t=8),
                                                       allow_slow_non_contiguous=True), [self.dsa_r], [wst_r])
        selm, selm_r = T("selm", [64, 8])
        self.ld(selm[:], self.selm_d, w=[selm_r])
        blockm, blockm_r = T("blockm", [128, 128])
        self.ld(blockm[:], self.blockm_d, w=[blockm_r])
        cmask_s, cmask_s_r = T("cmasks", [128, 8])
        self.ld(cmask_s[:], self.cmask_s_d, w=[cmask_s_r])
        sc, sc_r = T("scs", [128, 2056])
        lo, small_r = T("slo", [128, 1])
        hi, _ = T("shi", [128, 1])
        mid, _ = T("smid", [128, 1])
        cnt, _ = T("scnt", [128, 1])
        sel, _ = T("ssel", [128, 1], I32)
        nsel, _ = T("snsel", [128, 1], I32)
        Mb, Mb_r = T("sMb", [128, 2056], BF16)
        NMT, NMT_r = T("sNMT", [128, 17, 128], BF16)
        MBn, MBn_r = T("sMBn", [128, 128], BF16)
        s1 = ExitStack()
        junk, junk_r = P.sb(s1, "sjunk", [128, 2056], BF16)
        KIgs = [P.sb(s1, "KIg", [128, 16, 64]) for _ in range(2)]
        kiTg, kiTg_r = P.sb(s1, "kiTg", [64, 16, 128], BF16)
        rls = [P.sb(s1, "rl", [64, 512]) for _ in range(2)]
        scst = [P.sb(s1, "scst", [8, 2056]) for _ in range(2)]
        ri_ = 0
        for n in range(N_S):
            KIg, KIg_r = KIgs[n % 2]
            for j in range(16):
                c = 16 * n + j
                self.P.dma("pool", lambda e, KIg=KIg, j=j, c=c: e.indirect_dma_start(
                    out=KIg[:, j, :], out_offset=None, in_=self.cache_kidx,
                    in_offset=bass.IndirectOffsetOnAxis(ap=idx[:, c:c + 1], axis=0)), [idx_r], [KIg_r])
            for b0 in range(0, 16, 4):
                ps, ps_r = self.next_ps()
                for j in range(4):
                    self.pe(lambda e, ps=ps, j=j, b0=b0, KIg=KIg: e.transpose(out=ps[0:64, j * 128:(j + 1) * 128], in_=KIg[:, b0 + j, :],
                                                                           identity=self.ident_f[:]), r=[KIg_r, self.ident_f_r], w=[ps_r])
                self.act(lambda e, ps=ps, b0=b0: e.copy(out=kiTg[:, b0:b0 + 4, :], in_=ps[0:64, :].rearrange("p (a t) -> p a t", t=128)),
                         r=[ps_r], w=[kiTg_r])
            st_, st_r = scst[n % 2]
            for c4 in range(5):
                ps, ps_r = self.next_ps()
                nn = 512 if c4 < 4 else 8
                rhs = kiTg[:, 4 * c4:4 * c4 + 4, :] if c4 < 4 else kiTn[:, 8 * n:8 * n + 8]
                outp = ps[0:64, :].rearrange("p (a t) -> p a t", t=128) if c4 < 4 else ps[0:64, 0:8]
                rr = kiTg_r if c4 < 4 else kiTn_r
                self.pe(lambda e, outp=outp, rhs=rhs, n=n: e.matmul(outp, lhsT=qiTs[:, n, :, :].rearrange("p h t -> p (h t)"), rhs=rhs, start=True, stop=True),
                        r=[qiTs_r, rr], w=[ps_r])
                rl, rl_r = rls[ri_ % 2]
                ri_ += 1
                self.dve(lambda e, ps=ps, rl=rl, nn=nn, n=n: e.tensor_scalar(out=rl[:, 0:nn], in0=ps[0:64, 0:nn], scalar1=0.0, scalar2=wst[:, n:n + 1],
                                                                           op0=ALU.max, op1=ALU.mult), r=[ps_r, wst_r], w=[rl_r])
                ps2, ps2_r = self.next_ps()
                self.pe(lambda e, ps2=ps2, rl=rl, nn=nn: e.matmul(ps2[0:8, 0:nn], lhsT=selm[:], rhs=rl[:, 0:nn], start=True, stop=True),
                        r=[selm_r, rl_r], w=[ps2_r])
                self.act(lambda e, ps2=ps2, st_=st_, c4=c4, nn=nn: e.copy(out=st_[:, 512 * c4:512 * c4 + nn], in_=ps2[0:8, 0:nn]), r=[ps2_r], w=[st_r])
            self.ld(sc[8 * n:8 * n + 8, :], st_[:], r=[st_r], w=[sc_r])
        self.dve(lambda e: e.tensor_reduce(out=hi[:], in_=sc[:], axis=AX.X, op=ALU.max), r=[sc_r], w=[small_r])
        self.dve(lambda e: e.tensor_reduce(out=lo[:], in_=sc[:], axis=AX.X, op=ALU.min), r=[sc_r], w=[small_r])
        self.dve(lambda e: e.tensor_tensor(out=sc[:, 2048:2056], in0=sc[:, 2048:2056], in1=cmask_s[:], op=ALU.add), r=[sc_r, cmask_s_r], w=[sc_r])
        self.topk_threshold(sc, sc_r, 2056, lo, hi, mid, cnt, sel, nsel, small_r, junk, junk_r)
        self.dve(lambda e: e.tensor_scalar(out=Mb[:], in0=sc[:], scalar1=lo[:, 0:1], scalar2=None, op0=ALU.is_ge), r=[sc_r, small_r], w=[Mb_r])
        self.dve(lambda e: e.tensor_tensor(out=MBn[:].rearrange("p (n t) -> p n t", t=8), in0=bc(Mb[:, 2048:2056], 1, 16),
                                           in1=blockm[:].rearrange("p (n t) -> p n t", t=8), op=ALU.mult), r=[Mb_r, blockm_r], w=[MBn_r])
        for k0 in range(0, 17, 8):
            nb = min(8, 17 - k0)
            for j in range(nb):
                src = Mb[:, (k0 + j) * 128:(k0 + j + 1) * 128] if k0 + j < 16 else MBn[:]
                self.pe(lambda e, j=j, src=src: e.transpose(out=pt[:, j * 128:(j + 1) * 128], in_=src, identity=self.ident_b[:]),
                        r=[Mb_r, MBn_r, self.ident_b_r], w=[pt_r])
            self.dve(lambda e, k0=k0, nb=nb: e.tensor_scalar(out=NMT[:, k0:k0 + nb, :], in0=pt[:, 0:nb * 128].rearrange("p (a t) -> p a t", t=128),
                                                             scalar1=-1.0, scalar2=30000.0, op0=ALU.add, op1=ALU.mult), r=[pt_r], w=[NMT_r])
        P.barrier()
        s1.close()
        Kgs = [T("Kg", [128, 16, 256]) for _ in range(2)]
        Vgs = [T("Vg", [128, 16, 256]) for _ in range(2)]
        kTg, kTg_r = T("kTg", [64, 16, 4, 128], BF16)
        Vxg, Vxg_r = T("Vxg", [128, 16, 4, 65], BF16)
        self.pool(lambda e: e.memset(Vxg[:, :, :, 64:65], 1.0), w=[Vxg_r])
        pxs = [T("spx", [128, 256], BF16) for _ in range(3)]
        rc, rc_r = T("src", [64, 4])
        osb = [T("osb", [64, 4, 64]) for _ in range(2)]
        px_i = 0
        for n in range(N_S):
            Kg, Kg_r = Kgs[n % 2]
            Vg, Vg_r = Vgs[n % 2]
            for j in range(16):
                c = 16 * n + j
                self.P.dma("pool", lambda e, Kg=Kg, j=j, c=c: e.indirect_dma_start(
                    out=Kg[:, j, :], out_offset=None, in_=self.cache_k,
                    in_offset=bass.IndirectOffsetOnAxis(ap=idx[:, c:c + 1], axis=0)), [idx_r], [Kg_r])
                self.P.dma("pool", lambda e, Vg=Vg, j=j, c=c: e.indirect_dma_start(
                    out=Vg[:, j, :], out_offset=None, in_=self.cache_v,
                    in_offset=bass.IndirectOffsetOnAxis(ap=idx[:, c:c + 1], axis=0)), [idx_r], [Vg_r])
            self.act(lambda e, Vg=Vg: e.copy(out=Vxg[:, :, :, 0:64], in_=Vg[:].rearrange("p j (g d) -> p j g d", g=4)), r=[Vg_r], w=[Vxg_r])
            for j in range(16):
                ps, ps_r = self.next_ps()
                for g in range(4):
                    self.pe(lambda e, ps=ps, j=j, g=g, Kg=Kg: e.transpose(out=ps[0:64, g * 128:(g + 1) * 128], in_=Kg[:, j, 64 * g:64 * g + 64],
                                                                        identity=self.ident_f[:]), r=[Kg_r, self.ident_f_r], w=[ps_r])
                self.act(lambda e, ps=ps, j=j: e.copy(out=kTg[:, j, :, :], in_=ps[0:64, :].rearrange("p (g t) -> p g t", t=128)), r=[ps_r], w=[kTg_r])
            for j in range(17):
                ps, ps_r = self.next_ps()
                for g in range(4):
                    lhsT = kTg[:, j, g, :] if j < 16 else kTn[:, g, :]
                    rr = kTg_r if j < 16 else kTn_r
                    self.pe(lambda e, ps=ps, g=g, lhsT=lhsT, n=n: e.matmul(
                        ps[:, 64 * g:64 * g + 64].rearrange("p (h t) -> p h t", t=8), lhsT=lhsT, rhs=qTs[:, n, 8 * g:8 * g + 8, :],
                        start=(g == 0), stop=False, skip_group_check=True), r=[rr, qTs_r], w=[ps_r])
                self.pe(lambda e, ps=ps, j=j, n=n: e.matmul(ps[:, 0:256].rearrange("p (h t) -> p h t", t=8), lhsT=self.ident_b[:],
                                                            rhs=bc(NMT[:, j, 8 * n:8 * n + 8], 1, 32), start=False, stop=True, skip_group_check=True),
                        r=[self.ident_b_r, NMT_r], w=[ps_r])
                px, px_r = pxs[px_i % 3]
                px_i += 1
                self.act(lambda e, px=px, ps=ps: e.activation(out=px[:], in_=ps[:, 0:256], func=AF.Exp, scale=0.125), r=[ps_r], w=[px_r])
                for g in range(4):
                    rhs = Vxg[:, j, g, :] if j < 16 else Vxn[:, 65 * g:65 * g + 65]
                    rr = Vxg_r if j < 16 else Vxn_r
                    self.pe(lambda e, g=g, px=px, rhs=rhs, j=j: e.matmul(acc[0:64, 65 * g:65 * g + 65], lhsT=px[:, 64 * g:64 * g + 64], rhs=rhs,
                                                                        start=(j == 0 and g == 0), stop=(j == 16), skip_group_check=True),
                            r=[px_r, rr], w=[acc_r])
            accv = acc[0:64, 0:260].rearrange("p (g c) -> p g c", c=65)
            ob, ob_r = osb[n % 2]
            self.dve(lambda e, accv=accv: e.reciprocal(out=rc[:], in_=accv[:, :, 64]), r=[acc_r], w=[rc_r])
            self.dve(lambda e, accv=accv, ob=ob: e.tensor_tensor(out=ob[:], in0=accv[:, :, 0:64], in1=bc(rc[:], 2, 64), op=ALU.mult),
                     r=[acc_r, rc_r], w=[ob_r])
            for h in range(8):
                self.ld(self.os_d[8 * n:8 * n + 8, :].rearrange("t (g h d) -> h t g d", g=4, h=8)[h], ob[8 * h:8 * h + 8, :, :],
                        r=[ob_r], w=[self.dsa_r])
        osl, osl_r = T("osl", [128, E])
        self.ld(osl[:], self.os_d, r=[self.dsa_r], w=[osl_r])
        g2, g2_r = T("sg2", [128, E], BF16)
        self.dve(lambda e: e.tensor_tensor(out=g2[:], in0=osl[:], in1=szs_[:], op=ALU.mult), r=[osl_r, szs_r], w=[g2_r])
        g2T, g2T_r = T("sg2T", [128, 16, 128], BF16)
        hb, hb_r = T("shb", [128, D_MODEL])
        self.attn_out_tile(NTILE - 1, g2, g2_r, g2T, g2T_r, wo, wo_r, hb, hb_r)
        self.ps_lim = 7

    def ml_layer(self):
        P = self.P
        with ExitStack() as st:
            self.ml_proj(st)
        P.barrier()
        with ExitStack() as st:
            self.ml_feat(st)
        P.barrier()
        if self.ml_stage >= 2:
            with ExitStack() as st:
                self.ml_chunks(st)
            P.barrier()
        if self.ml_stage >= 3:
            with ExitStack() as st:
                self.ml_sample(st)
            P.barrier()

    def ml_proj(self, st):
        P = self.P
        win, win_r = P.sb(st, "mwin", [128, 8, 2 * E], BF16)
        wv = self.ml_w_in.rearrange("(k p) n -> p k n", p=128)
        for k in range(8):
            for hf in range(2):
                self.ldc(win[:, k, hf * E:(hf + 1) * E], wv[:, k, hf * E:(hf + 1) * E], w=[win_r])
        gt, gt_r = P.sb(st, "mgt", [128, D_MODEL])
        self.ld(gt[:], self.ml_norm.partition_broadcast(128), w=[gt_r])
        hts = [P.sb(st, "mht", [128, D_MODEL]) for _ in range(2)]
        junk, junk_r = P.sb(st, "mjunk", [128, D_MODEL], BF16)
        sss = [P.sb(st, "mss", [128, 1]) for _ in range(2)]
        xns = [P.sb(st, "mxn", [128, D_MODEL], BF16) for _ in range(2)]
        xTs = [P.sb(st, "mxT", [128, 8, 512], BF16) for _ in range(2)]
        obufs = [P.sb(st, "mobuf", [128, 512], BF16) for _ in range(4)]
        utok, utok_r = P.sb(st, "utok", [128, E])
        ob_i = 0
        ti = 0
        for tg in range(NGRP):
            tok0, ntok = grp_tok(tg)
            xT, xT_r = xTs[tg % 2]
            for il in range(ntok // 128):
                i = tok0 // 128 + il
                ht, ht_r = hts[ti % 2]
                ss, ss_r = sss[ti % 2]
                xn, xn_r = xns[ti % 2]
                ti += 1
                self.norm_tile(self.H[i * 128:(i + 1) * 128, :], self.H_r[i], gt, gt_r, ht, ht_r, junk, junk_r, ss, ss_r,
                               xn, xn_r, xT[:, :, il * 128:(il + 1) * 128], xT_r)
            for fo in range(32):
                ps, ps_r = self.next_ps()
                for k in range(8):
                    self.pe(lambda e, k=k, fo=fo, ps=ps, xT=xT, ntok=ntok: e.matmul(
                        ps[:, 0:ntok], lhsT=win[:, k, fo * 128:(fo + 1) * 128], rhs=xT[:, k, 0:ntok],
                        start=(k == 0), stop=(k == 7)), r=[win_r, xT_r], w=[ps_r])
                ob, ob_r = obufs[ob_i % 4]
                ob_i += 1
                if fo < 16:
                    self.act(lambda e, ob=ob, ps=ps, ntok=ntok: e.copy(out=ob[:, 0:ntok], in_=ps[:, 0:ntok]), r=[ps_r], w=[ob_r])
                    self.ld(self.muT_d[fo, :, tok0:tok0 + ntok], ob[:, 0:ntok], r=[ob_r], w=[self.ml_r])
                else:
                    self.act(lambda e, ob=ob, ps=ps, ntok=ntok: e.activation(
                        out=ob[:, 0:ntok], in_=ps[:, 0:ntok], func=AF.Silu), r=[ps_r], w=[ob_r])
                    self.ld(self.szT_d[fo - 16, :, tok0:tok0 + ntok], ob[:, 0:ntok], r=[ob_r], w=[self.szT_r[fo - 16]])
            if tg >= 7:
                c0 = ntok - 128
                for cc in range(4):
                    ps, ps_r = self.next_ps()
                    for k in range(8):
                        self.pe(lambda e, k=k, cc=cc, ps=ps, xT=xT, c0=c0: e.matmul(
                            ps[:], lhsT=xT[:, k, c0:c0 + 128], rhs=win[:, k, cc * 512:(cc + 1) * 512],
                            start=(k == 0), stop=(k == 7)), r=[win_r, xT_r], w=[ps_r])
                    self.act(lambda e, ps=ps, cc=cc: e.copy(out=utok[:, cc * 512:(cc + 1) * 512], in_=ps[:]), r=[ps_r], w=[utok_r])
                if tg == 7:
                    self.store(self.o_mconv_p, utok[125:128, :], r=[utok_r])
                else:
                    for n in range(N_S):
                        self.store(self.o_mconv_s[n], utok[8 * n + 5:8 * n + 8, :], r=[utok_r])

    def ml_feat(self, st):
        P = self.P

        def T(name, shape, dt=F32):
            return P.sb(st, name, shape, dt)
        ws = {}
        for nm, src in (("q", self.ml_w_q), ("k", self.ml_w_k), ("v", self.ml_w_v), ("o", self.ml_w_o)):
            w_, w_r = T("mw" + nm, [128, 8, 2, 256], BF16)
            for h in range(8):
                self.ldc(w_[:, h, :, :], src[h].rearrange("(dk p) e -> p dk e", p=128), w=[w_r])
            ws[nm] = (w_, w_r)
        wg, wg_r = T("mwg", [128, 48, 16], BF16)
        self.ldc(wg[:], self.ml_w_gates.rearrange("(c p) g -> p c g", p=128), w=[wg_r])
        bo, bo_r = T("mbo", [128, E])
        self.ld(bo[:], self.ml_b_o.partition_broadcast(128), w=[bo_r])
        cw, cw_r = T("mcw", [128, 16, 5])
        for j in range(4):
            self.P.dma("sp", lambda e, j=j: e.dma_start(out=cw[:, :, j], in_=self.ml_conv_w[j].rearrange("(f p) -> p f", p=128),
                                                       allow_slow_non_contiguous=True), (), [cw_r])
        self.P.dma("sp", lambda e: e.dma_start(out=cw[:, :, 4], in_=self.ml_conv_b.rearrange("(f p) -> p f", p=128),
                                              allow_slow_non_contiguous=True), (), [cw_r])
        bgi, bg_r = T("mbgi", [8, 1])
        bgf, _ = T("mbgf", [8, 1])
        nbgf, _ = T("mnbgf", [8, 1])
        self.ld(bgi[:], self.ml_b_gates[0:8, :], w=[bg_r])
        self.ld(bgf[:], self.ml_b_gates[8:16, :], w=[bg_r])
        self.dve(lambda e: e.tensor_scalar(out=nbgf[:], in0=bgf[:], scalar1=-1.0, scalar2=None, op0=ALU.mult), r=[bg_r], w=[bg_r])
        uX, uX_r = T("uX", [128, 16, 515], BF16)
        uSc, uSc_r = T("uSc", [128, 16, T_S], BF16)
        accs = [T("cacc", [128, 512]) for _ in range(2)]
        caT, caT_r = T("caT", [128, 16, 512], BF16)
        qTg, qTg_r = T("mqT", [128, 16, 512], BF16)
        kTg, kTg_r = T("mkT", [128, 16, 512], BF16)
        vTg, vTg_r = T("mvT", [128, 16, 512], BF16)
        obufs = [T("mfob", [128, 512], BF16) for _ in range(3)]
        vxs = [T("mvx", [128, 8, 257], BF16) for _ in range(2)]
        for (t_, r_) in vxs:
            self.pool(lambda e, t_=t_: e.memset(t_[:, :, 256:257], 1.0), w=[r_])
        kts = [T("mkt", [128, E], BF16) for _ in range(2)]
        ots = [T("mot", [128, E], BF16) for _ in range(2)]
        otmp = [T("motmp", [128, 512]) for _ in range(2)]
        ig, g_r = T("g_ig", [8, 512])
        lf, _ = T("g_lf", [8, 512])
        Bc, _ = T("g_B", [8, 512])
        ones, _ = T("g_ones", [8, 512])
        aa, _ = T("g_a", [8, 512])
        GG, _ = T("g_G", [8, 512])
        nG, _ = T("g_nG", [8, 512])
        wi_, _ = T("g_wi", [8, 512])
        em, _ = T("g_em", [8, 512])
        Gp, _ = T("g_Gp", [8, 8])
        Bl, _ = T("g_Bl", [8, 1])
        Gl, _ = T("g_Gl", [8, 1])
        m0s, _ = T("g_m0", [8, N_S])
        mms, _ = T("g_mm", [8, N_S])
        self.pool(lambda e: e.memset(ones[:], 1.0), w=[g_r])
        self.pool(lambda e: e.memset(Bl[:], 0.0), w=[g_r])
        self.pool(lambda e: e.memset(Gl[:], 0.0), w=[g_r])
        self.P.dma("sp", lambda e: e.dma_start(out=m0s[:], in_=self.ml_m0.rearrange("n h -> h n"), allow_slow_non_contiguous=True), (), [g_r])
        cv, cv_r = T("mcv", [48, E])
        self.ld(cv[:], self.ml_conv0, w=[cv_r])
        ob_i = 0
        vi = 0
        for tg in range(NGRP):
            tok0, ntok = grp_tok(tg)
            smp = (tg == 8)
            if not smp:
                self.ld(uX[:, :, 3:515], self.muT_d[:, :, tok0:tok0 + 512].rearrange("f p t -> p f t"), r=[self.ml_r], w=[uX_r])
                if tg == 0:
                    self.pool(lambda e: e.memset(uX[:, :, 0:3], 0.0), w=[uX_r])
                else:
                    self.ld(uX[:, :, 0:3], self.muT_d[:, :, tok0 - 3:tok0].rearrange("f p t -> p f t"), r=[self.ml_r], w=[uX_r])
            else:
                uS = uX[:, :, 0:N_S * 11].rearrange("p f (n t) -> p f n t", t=11)
                self.ld(uSc[:], self.muT_d[:, :, T_P:TOK].rearrange("f p t -> p f t"), r=[self.ml_r], w=[uSc_r])
                for f in range(16):
                    self.ld(uS[:, f, :, 3:11], self.muT_d[f, :, T_P:TOK].rearrange("p (n t) -> p n t", t=8), r=[self.ml_r], w=[uX_r])
                for f4 in range(0, 16, 4):
                    ps, ps_r = self.next_ps()
                    for j in range(4):
                        self.pe(lambda e, ps=ps, j=j, f4=f4: e.transpose(out=ps[:, j * 48:(j + 1) * 48], in_=cv[:, (f4 + j) * 128:(f4 + j + 1) * 128],
                                                                       identity=self.ident_f[0:48, 0:48]), r=[cv_r, self.ident_f_r], w=[ps_r])
                    self.act(lambda e, ps=ps, f4=f4, uS=uS: e.copy(out=uS[:, f4:f4 + 4, :, 0:3],
                                                                  in_=ps[:, 0:192].rearrange("p (f n j) -> p f n j", f=4, j=3)), r=[ps_r], w=[uX_r])
            for f in range(16):
                acc, acc_r = accs[f % 2]
                if not smp:
                    srcs = [uX[:, f, j:j + 512] for j in range(4)]
                    accv = acc[:, 0:512]
                    cav = caT[:, f, 0:512]
                else:
                    srcs = [uS[:, f, :, j:j + 8] for j in range(4)]
                    accv = acc[:, 0:128].rearrange("p (n t) -> p n t", t=8)
                    cav = caT[:, f, 0:128].rearrange("p (n t) -> p n t", t=8)
                self.dve(lambda e, accv=accv, srcs=srcs, f=f: e.tensor_scalar(out=accv, in0=srcs[0], scalar1=cw[:, f, 0:1], scalar2=cw[:, f, 4:5],
                                                                             op0=ALU.mult, op1=ALU.add), r=[uX_r, cw_r], w=[acc_r])
                for j in range(1, 4):
                    self.dve(lambda e, accv=accv, srcs=srcs, f=f, j=j: e.scalar_tensor_tensor(out=accv, in0=srcs[j], scalar=cw[:, f, j:j + 1], in1=accv,
                                                                                          op0=ALU.mult, op1=ALU.add), r=[uX_r, cw_r, acc_r], w=[acc_r])
                self.act(lambda e, accv=accv, cav=cav: e.activation(out=cav, in_=accv, func=AF.Silu), r=[acc_r], w=[caT_r])
            self.ld(self.mca_d[:, :, tok0:tok0 + ntok].rearrange("f p t -> p f t"), caT[:, :, 0:ntok], r=[caT_r], w=[self.ml_r])

            def uview(f, lo_, n_):
                if not smp:
                    return uX[:, f, 3 + lo_:3 + lo_ + n_]
                return uSc[:, f, lo_:lo_ + n_]

            for nm, src_is_ca, dst, dst_r, dscr in (("q", True, qTg, qTg_r, self.mq_d), ("k", True, kTg, kTg_r, self.mk_d),
                                                    ("v", False, vTg, vTg_r, None)):
                w_, w_r = ws[nm]
                for h in range(8):
                    for ec in range(2):
                        ps, ps_r = self.next_ps()
                        for dk in range(2):
                            if src_is_ca:
                                rhs = caT[:, 2 * h + dk, 0:ntok]
                                outp = ps[:, 0:ntok]
                            else:
                                rhs = uview(2 * h + dk, 0, ntok)
                                outp = ps[:, 0:ntok]
                            self.pe(lambda e, outp=outp, rhs=rhs, w_=w_, h=h, dk=dk, ec=ec: e.matmul(
                                outp, lhsT=w_[:, h, dk, ec * 128:(ec + 1) * 128], rhs=rhs, start=(dk == 0), stop=(dk == 1)),
                                r=[w_r, caT_r, uX_r, uSc_r], w=[ps_r])
                        self.act(lambda e, ps=ps, dst=dst, h=h, ec=ec, ntok=ntok: e.copy(out=dst[:, 2 * h + ec, 0:ntok], in_=ps[:, 0:ntok]),
                                 r=[ps_r], w=[dst_r])
                if dscr is not None:
                    self.ld(dscr[:, :, tok0:tok0 + ntok].rearrange("f p t -> p f t"), dst[:, :, 0:ntok], r=[dst_r], w=[self.ml_r])
            for gi in range(2):
                ps, ps_r = self.next_ps()
                for c in range(48):
                    src_t, src_r = ((qTg, qTg_r), (kTg, kTg_r), (vTg, vTg_r))[c // 16]
                    self.pe(lambda e, ps=ps, c=c, gi=gi, src_t=src_t, ntok=ntok: e.matmul(
                        ps[0:8, 0:ntok], lhsT=wg[:, c, 8 * gi:8 * gi + 8], rhs=src_t[:, c % 16, 0:ntok], start=(c == 0), stop=(c == 47)),
                        r=[wg_r, src_r], w=[ps_r])
                if gi == 0:
                    self.act(lambda e, ps=ps, ntok=ntok: e.activation(out=ig[:, 0:ntok], in_=ps[0:8, 0:ntok], func=AF.Copy, bias=0.0), r=[ps_r], w=[g_r])
                    self.dve(lambda e, ntok=ntok: e.tensor_scalar(out=ig[:, 0:ntok], in0=ig[:, 0:ntok], scalar1=bgi[:, 0:1], scalar2=None, op0=ALU.add),
                             r=[g_r, bg_r], w=[g_r])
                else:
                    self.act(lambda e, ps=ps, ntok=ntok: e.activation(out=lf[:, 0:ntok], in_=ps[0:8, 0:ntok], func=AF.Exp, scale=-1.0, bias=nbgf[:, 0:1]),
                             r=[ps_r, bg_r], w=[g_r])
                    self.act(lambda e, ntok=ntok: e.activation(out=lf[:, 0:ntok], in_=lf[:, 0:ntok], func=AF.Ln, bias=1.0), r=[g_r], w=[g_r])
                    self.dve(lambda e, ntok=ntok: e.tensor_scalar(out=lf[:, 0:ntok], in0=lf[:, 0:ntok], scalar1=-1.0, scalar2=None, op0=ALU.mult),
                             r=[g_r], w=[g_r])
            G = [g_r]
            if not smp:
                self.dve(lambda e: e.tensor_tensor_scan(out=Bc[:], data0=ones[:], data1=lf[:], initial=Bl[:, 0:1], op0=ALU.mult, op1=ALU.add), r=G, w=G)
                self.dve(lambda e: e.tensor_tensor(out=aa[:], in0=ig[:], in1=Bc[:], op=ALU.subtract), r=G, w=G)
                self.dve(lambda e: e.tensor_tensor_scan(out=GG[:], data0=aa[:], data1=aa[:], initial=Gl[:, 0:1], op0=ALU.max, op1=ALU.max), r=G, w=G)
                self.dve(lambda e: e.tensor_copy(out=Gp[:, 0:1], in_=Gl[:, 0:1]), r=G, w=G)
                self.dve(lambda e: e.tensor_copy(out=Gp[:, 1:8], in_=GG[:].rearrange("p (c t) -> p c t", t=64)[:, 0:7, 63]), r=G, w=G)
                self.dve(lambda e: e.tensor_tensor(out=wi_[:].rearrange("p (c t) -> p c t", t=64), in0=bc(Gp[:], 2, 64),
                                                   in1=GG[:].rearrange("p (c t) -> p c t", t=64), op=ALU.subtract), r=G, w=G)
                self.dve(lambda e: e.tensor_copy(out=Bl[:], in_=Bc[:, 511:512]), r=G, w=G)
                self.dve(lambda e: e.tensor_copy(out=Gl[:], in_=GG[:, 511:512]), r=G, w=G)
            else:
                v3 = lambda t_: t_[:, 0:128].rearrange("p (n t) -> p n t", t=8)
                B3, l3, a3, G3, i3 = v3(Bc), v3(lf), v3(aa), v3(GG), v3(ig)
                self.dve(lambda e: e.tensor_copy(out=B3[:, :, 0], in_=l3[:, :, 0]), r=G, w=G)
                for t in range(1, 8):
                    self.dve(lambda e, t=t: e.tensor_tensor(out=B3[:, :, t], in0=B3[:, :, t - 1], in1=l3[:, :, t], op=ALU.add), r=G, w=G)
                self.dve(lambda e: e.tensor_tensor(out=aa[:, 0:128], in0=ig[:, 0:128], in1=Bc[:, 0:128], op=ALU.subtract), r=G, w=G)
                self.dve(lambda e: e.tensor_tensor(out=G3[:, :, 0], in0=a3[:, :, 0], in1=m0s[:], op=ALU.max), r=G, w=G)
                for t in range(1, 8):
                    self.dve(lambda e, t=t: e.tensor_tensor(out=G3[:, :, t], in0=G3[:, :, t - 1], in1=a3[:, :, t], op=ALU.max), r=G, w=G)
                self.dve(lambda e: e.tensor_tensor(out=v3(wi_), in0=bc(m0s[:], 2, 8), in1=G3, op=ALU.subtract), r=G, w=G)
            nt = ntok
            self.act(lambda e, nt=nt: e.activation(out=wi_[:, 0:nt], in_=wi_[:, 0:nt], func=AF.Exp), r=G, w=G)
            self.dve(lambda e, nt=nt: e.tensor_scalar(out=nG[:, 0:nt], in0=GG[:, 0:nt], scalar1=-1.0, scalar2=None, op0=ALU.mult), r=G, w=G)
            self.dve(lambda e, nt=nt: e.tensor_tensor(out=Bc[:, 0:nt], in0=Bc[:, 0:nt], in1=GG[:, 0:nt], op=ALU.add), r=G, w=G)
            self.act(lambda e, nt=nt: e.activation(out=em[:, 0:nt], in_=Bc[:, 0:nt], func=AF.Exp, scale=-1.0), r=G, w=G)
            if tg == 7:
                self.store(self.o_mm_p, Bc[:, 511:512], r=G)
            if smp:
                self.dve(lambda e: e.tensor_copy(out=mms[:], in_=Bc[:, 0:128].rearrange("p (n t) -> p n t", t=8)[:, :, 7]), r=G, w=G)
                self.store(self.o_mm_s, mms[:], r=G)
            for qi_, t_ in enumerate((aa, nG, wi_, em)):
                self.ld(self.gq_d[qi_, :, tok0:tok0 + nt], t_[:, 0:nt], r=G, w=[self.ml_r])
            for il in range(ntok // 128):
                i = tok0 // 128 + il
                vx, vx_r = vxs[vi % 2]
                kt, kt_r = kts[vi % 2]
                ot, ot_r = ots[vi % 2]
                vi += 1
                for nm in ("v", "k", "o"):
                    w_, w_r = ws[nm]
                    for h2 in range(4):
                        ps, ps_r = self.next_ps()
                        for hh in range(2):
                            h = 2 * h2 + hh
                            for dk in range(2):
                                if nm == "k":
                                    lhsT = caT[:, 2 * h + dk, il * 128:(il + 1) * 128]
                                else:
                                    lhsT = uview(2 * h + dk, il * 128, 128)
                                    if smp:
                                        lhsT = lhsT
                                self.pe(lambda e, ps=ps, lhsT=lhsT, w_=w_, h=h, hh=hh, dk=dk: e.matmul(
                                    ps[:, hh * 256:(hh + 1) * 256], lhsT=lhsT, rhs=w_[:, h, dk, :], start=(dk == 0 and hh == 0), stop=(dk == 1),
                                    skip_group_check=True), r=[w_r, caT_r, uX_r, uSc_r], w=[ps_r])
                        if nm == "v":
                            self.act(lambda e, ps=ps, vx=vx, h2=h2: e.copy(out=vx[:, 2 * h2:2 * h2 + 2, 0:256],
                                                                          in_=ps[:].rearrange("p (a e) -> p a e", a=2)), r=[ps_r], w=[vx_r])
                        elif nm == "k":
                            self.act(lambda e, ps=ps, kt=kt, h2=h2: e.activation(out=kt[:, h2 * 512:(h2 + 1) * 512], in_=ps[:], func=AF.Copy, scale=1.0 / 16.0),
                                     r=[ps_r], w=[kt_r])
                        else:
                            tmp, tmp_r = otmp[h2 % 2]
                            self.dve(lambda e, ps=ps, tmp=tmp, h2=h2: e.tensor_tensor(out=tmp[:], in0=ps[:], in1=bo[:, h2 * 512:(h2 + 1) * 512], op=ALU.add),
                                     r=[ps_r, bo_r], w=[tmp_r])
                            self.act(lambda e, tmp=tmp, ot=ot, h2=h2: e.activation(out=ot[:, h2 * 512:(h2 + 1) * 512], in_=tmp[:], func=AF.Sigmoid),
                                     r=[tmp_r], w=[ot_r])
                self.ld(self.mvt_d[i * 128:(i + 1) * 128, :], vx[:].rearrange("p h e -> p (h e)"), r=[vx_r], w=[self.ml_r])
                self.ld(self.mkt_d[i * 128:(i + 1) * 128, :], kt[:], r=[kt_r], w=[self.ml_r])
                self.ld(self.mo_d[i * 128:(i + 1) * 128, :], ot[:], r=[ot_r], w=[self.ml_r])

    def ml_post_tile(self, i, hout, hout_r, K_):
        (st6, mv, rstd, sm_r, lnw, lnw_r, skip, skip_r, wout, wout_r, ot, ot_r, caTt, caTt_r, szTt, szTt_r,
         hn3, hn3_r, g2T, g2T_r, t1, t1_r, hb, hb_r) = K_
        tok0 = i * 128
        self.ld(ot[:], self.mo_d[tok0:tok0 + 128, :], r=[self.ml_r], w=[ot_r])
        self.ld(caTt[:], self.mca_d[:, :, tok0:tok0 + 128].rearrange("f p t -> p f t"), r=[self.ml_r], w=[caTt_r])
        self.ld(szTt[:], self.szT_d[:, :, tok0:tok0 + 128].rearrange("f p t -> p f t"), r=self.szT_r, w=[szTt_r])
        for h in range(8):
            self.dve(lambda e, h=h: e.bn_stats(out=st6[:, h, :], in_=hout[:, h, :]), r=[hout_r], w=[sm_r])
            self.dve(lambda e, h=h: e.bn_aggr(out=mv[:, h, :], in_=st6[:, h, :]), r=[sm_r], w=[sm_r])
        self.dve(lambda e: e.tensor_scalar(out=rstd[:], in0=mv[:, :, 1], scalar1=1e-5, scalar2=None, op0=ALU.add), r=[sm_r], w=[sm_r])
        self.act(lambda e: e.activation(out=rstd[:], in_=rstd[:], func=AF.Sqrt), r=[sm_r], w=[sm_r])
        self.dve(lambda e: e.reciprocal(out=rstd[:], in_=rstd[:]), r=[sm_r], w=[sm_r])
        for h in range(8):
            eng = self.dve if h % 2 == 0 else self.pool
            eng(lambda e, h=h: e.tensor_scalar(out=hout[:, h, :], in0=hout[:, h, :], scalar1=mv[:, h, 0:1], scalar2=rstd[:, h:h + 1],
                                               op0=ALU.subtract, op1=ALU.mult), r=[hout_r, sm_r], w=[hout_r])
        hf = hout[:].rearrange("p h e -> p (h e)")
        self.pool(lambda e: e.tensor_tensor(out=hf, in0=hf, in1=lnw[:], op=ALU.mult), r=[hout_r, lnw_r], w=[hout_r])
        self.dve(lambda e: e.tensor_tensor(out=hn3[:], in0=hf, in1=ot[:], op=ALU.mult), r=[hout_r, ot_r], w=[hn3_r])
        pt, pt_r = self.pst
        for b0 in range(0, 16, 8):
            for j in range(8):
                self.pe(lambda e, j=j, b0=b0: e.transpose(out=pt[:, j * 128:(j + 1) * 128], in_=hn3[:, (b0 + j) * 128:(b0 + j + 1) * 128],
                                                          identity=self.ident_b[:]), r=[hn3_r, self.ident_b_r], w=[pt_r])
            self.pool(lambda e, b0=b0: e.tensor_tensor(out=t1[:], in0=caTt[:, b0:b0 + 8, :], in1=bc(skip[:, b0:b0 + 8], 2, 128), op=ALU.mult),
                      r=[caTt_r, skip_r], w=[t1_r])
            self.dve(lambda e: e.tensor_tensor(out=t1[:], in0=t1[:], in1=pt[:].rearrange("p (a t) -> p a t", t=128), op=ALU.add),
                     r=[t1_r, pt_r], w=[t1_r])
            self.pool(lambda e, b0=b0: e.tensor_tensor(out=g2T[:, b0:b0 + 8, :], in0=t1[:], in1=szTt[:, b0:b0 + 8, :], op=ALU.mult),
                      r=[t1_r, szTt_r], w=[g2T_r])
        self.ld(hb[:], self.H[tok0:tok0 + 128, :], r=[self.H_r[i]], w=[hb_r])
        for hh in range(2):
            ps, ps_r = self.next_ps()
            for f in range(16):
                self.pe(lambda e, ps=ps, f=f, hh=hh: e.matmul(ps[:], lhsT=g2T[:, f, :], rhs=wout[:, f, hh * 512:(hh + 1) * 512],
                                                              start=(f == 0), stop=(f == 15)), r=[wout_r, g2T_r], w=[ps_r])
            self.dve(lambda e, ps=ps, hh=hh: e.tensor_tensor(out=hb[:, hh * 512:(hh + 1) * 512], in0=ps[:], in1=hb[:, hh * 512:(hh + 1) * 512],
                                                            op=ALU.add), r=[ps_r, hb_r], w=[hb_r])
        self.ld(self.H[tok0:tok0 + 128, :], hb[:], r=[hb_r], w=[self.H_r[i]])

    def ml_post_alloc(self, T):
        st6, sm_r = T("st6", [128, 8, 6])
        mv, _ = T("mv", [128, 8, 2])
        rstd, _ = T("rstd", [128, 8])
        lnw, lnw_r = T("lnw", [128, E])
        self.ld(lnw[:], self.ml_ln_w.partition_broadcast(128), w=[lnw_r])
        skip, skip_r = T("mskip", [128, 16])
        self.P.dma("sp", lambda e: e.dma_start(out=skip[:], in_=self.ml_skip.rearrange("(f p) -> p f", p=128), allow_slow_non_contiguous=True),
                   (), [skip_r])
        wout, wout_r = T("mwout", [128, 16, D_MODEL], BF16)
        wov = self.ml_w_out.rearrange("(k p) n -> p k n", p=128)
        for k in range(0, 16, 2):
            self.ldc(wout[:, k:k + 2, :], wov[:, k:k + 2, :], w=[wout_r])
        ot, ot_r = T("mot2", [128, E], BF16)
        caTt, caTt_r = T("caTt", [128, 16, 128], BF16)
        szTt, szTt_r = T("szTt", [128, 16, 128], BF16)
        hn3, hn3_r = T("hn3", [128, E], BF16)
        g2T, g2T_r = T("mg2T", [128, 16, 128], BF16)
        t1, t1_r = T("mt1", [128, 8, 128])
        hb, hb_r = T("mhb", [128, D_MODEL])
        return (st6, mv, rstd, sm_r, lnw, lnw_r, skip, skip_r, wout, wout_r, ot, ot_r, caTt, caTt_r, szTt, szTt_r,
                hn3, hn3_r, g2T, g2T_r, t1, t1_r, hb, hb_r)

    def ml_chunks(self, st):
        P = self.P

        def T(name, shape, dt=F32):
            return P.sb(st, name, shape, dt)
        K_ = self.ml_post_alloc(T)
        hmask, hmask_r = T("hmask", [8, 8, 128])
        self.ld(hmask[:].rearrange("p a t -> p (a t)"), self.hmask_d, w=[hmask_r])
        ones8, ones8_r = T("ones8", [8, 128])
        self.ld(ones8[:], self.ones8_d, w=[ones8_r])
        cm64, cm64_r = T("cm64", [64, 64])
        self.ld(cm64[:], self.cm64_d, w=[cm64_r])
        C = [T("Cst", [128, 2, 257]) for _ in range(8)]
        Cb = [T("Cbf", [128, 2, 257], BF16) for _ in range(8)]
        for h in range(8):
            self.pool(lambda e, h=h: e.memset(C[h][0][:], 0.0), w=[C[h][1]])
            self.pool(lambda e, h=h: e.memset(Cb[h][0][:], 0.0), w=[Cb[h][1]])
        qTg, qTg_r = T("cqT", [128, 16, 512], BF16)
        kTg, kTg_r = T("ckT", [128, 16, 512], BF16)
        gq, gq_r = T("cgq", [8, 4, 512])
        vts = [T("vtc", [64, 8 * 257], BF16) for _ in range(2)]
        kts = [T("ktc", [64, E], BF16) for _ in range(2)]
        nGe, nGe_r = T("nGe", [8, 8, 64])
        wie, wie_r = T("wie", [8, 8, 64])
        wm, wm_r = T("wm", [64, 8, 64])
        PT, PT_r = T("PT", [64, 8, 64], BF16)
        qw, qw_r = T("qw", [128, 16, 64], BF16)
        wpv, wpv_r = T("wpv", [128, 8])
        emT, emT_r = T("emT", [128, 8])
        dn, dn_r = T("dn", [128, 1])
        kws = [T("kw", [64, 256], BF16) for _ in range(2)]
        hout, hout_r = T("hout", [128, 8, 256])
        for c in range(T_P // 64):
            cl = c % 8
            cs0 = cl * 64
            t0 = c * 64
            po = 64 * (c % 2)
            if cl == 0:
                g0 = c * 64
                self.ld(qTg[:], self.mq_d[:, :, g0:g0 + 512].rearrange("f p t -> p f t"), r=[self.ml_r], w=[qTg_r])
                self.ld(kTg[:], self.mk_d[:, :, g0:g0 + 512].rearrange("f p t -> p f t"), r=[self.ml_r], w=[kTg_r])
                self.ld(gq[:], self.gq_d[:, :, g0:g0 + 512].rearrange("q h t -> h q t"), r=[self.ml_r], w=[gq_r])
            vt, vt_r = vts[c % 2]
            kt, kt_r = kts[c % 2]
            self.ld(vt[:], self.mvt_d[t0:t0 + 64, :], r=[self.ml_r], w=[vt_r])
            self.ld(kt[:], self.mkt_d[t0:t0 + 64, :], r=[self.ml_r], w=[kt_r])
            self.dve(lambda e, cs0=cs0: e.tensor_tensor(out=nGe[:], in0=bc(gq[:, 1, cs0:cs0 + 64], 1, 8), in1=hmask[:, :, 0:64], op=ALU.mult),
                     r=[gq_r, hmask_r], w=[nGe_r])
            self.pool(lambda e, cs0=cs0: e.tensor_tensor(out=wie[:], in0=bc(gq[:, 2, cs0:cs0 + 64], 1, 8), in1=hmask[:, :, 0:64], op=ALU.mult),
                      r=[gq_r, hmask_r], w=[wie_r])
            Dps, Dps_r = self.next_ps()
            Dv = Dps[0:64, :].rearrange("p (a t) -> p a t", t=64)
            self.pe(lambda e, Dv=Dv: e.matmul(Dv, lhsT=ones8[:, 0:64], rhs=nGe[:], start=True, stop=False), r=[ones8_r, nGe_r], w=[Dps_r])
            self.pe(lambda e, Dv=Dv, cs0=cs0: e.matmul(Dv, lhsT=gq[:, 0, cs0:cs0 + 64], rhs=hmask[:, :, 0:64], start=False, stop=True),
                    r=[gq_r, hmask_r], w=[Dps_r])
            Wps, Wps_r = self.next_ps()
            Wv = Wps[:].rearrange("p (a t) -> p a t", t=64)
            self.pe(lambda e, Wv=Wv: e.matmul(Wv, lhsT=ones8[:], rhs=wie[:], start=True, stop=True), r=[ones8_r, wie_r], w=[Wps_r])
            if c % 2 == 0:
                Eps, Eps_r = self.next_ps()
                self.pe(lambda e, Eps=Eps, cs0=cs0: e.matmul(Eps[:, 0:8], lhsT=gq[:, 3, cs0:cs0 + 128], rhs=self.ident_f[0:8, 0:8], start=True, stop=True),
                        r=[gq_r, self.ident_f_r], w=[Eps_r])
                self.act(lambda e, Eps=Eps: e.copy(out=emT[:], in_=Eps[:, 0:8]), r=[Eps_r], w=[emT_r])
            self.act(lambda e, Dv=Dv: e.activation(out=wm[:], in_=Dv, func=AF.Exp), r=[Dps_r], w=[wm_r])
            self.dve(lambda e: e.tensor_tensor(out=wm[:], in0=wm[:], in1=bc(cm64[:], 1, 8), op=ALU.mult), r=[wm_r, cm64_r], w=[wm_r])
            Sps, Sps_r = self.next_ps()
            for h in range(8):
                for dk in range(2):
                    self.pe(lambda e, Sps=Sps, h=h, dk=dk, cs0=cs0: e.matmul(
                        Sps[0:64, 64 * h:64 * h + 64], lhsT=kTg[:, 2 * h + dk, cs0:cs0 + 64], rhs=qTg[:, 2 * h + dk, cs0:cs0 + 64],
                        start=(h == 0 and dk == 0), stop=(dk == 1), skip_group_check=True), r=[kTg_r, qTg_r], w=[Sps_r])
            self.dve(lambda e, Sps=Sps: e.scalar_tensor_tensor(out=PT[:], in0=Sps[0:64, :].rearrange("p (a t) -> p a t", t=64), scalar=1.0 / 16.0,
                                                               in1=wm[:], op0=ALU.mult, op1=ALU.mult), r=[Sps_r, wm_r], w=[PT_r])
            self.dve(lambda e, Wv=Wv, cs0=cs0: e.tensor_tensor(out=qw[:].rearrange("p (h k) t -> p h k t", k=2),
                                                               in0=qTg[:, :, cs0:cs0 + 64].rearrange("p (h k) t -> p h k t", k=2),
                                                               in1=bc(Wv, 2, 2), op=ALU.mult), r=[qTg_r, Wps_r], w=[qw_r])
            self.act(lambda e, Wv=Wv: e.copy(out=wpv[:], in_=Wv[:, :, 63]), r=[Wps_r], w=[wpv_r])
            for h in range(8):
                nps, nps_r = self.next_ps()
                outp = nps[po:po + 64, 0:257]
                self.pe(lambda e, outp=outp, h=h, vt=vt, po=po: e.matmul(outp, lhsT=PT[:, h, :], rhs=vt[:, 257 * h:257 * h + 257], start=True, stop=False,
                                                                        tile_position=(0, po)), r=[PT_r, vt_r], w=[nps_r])
                for dk in range(2):
                    self.pe(lambda e, outp=outp, h=h, dk=dk, po=po: e.matmul(outp, lhsT=qw[:, 2 * h + dk, :], rhs=Cb[h][0][:, dk, :], start=False, stop=(dk == 1),
                                                                            tile_position=(0, po)), r=[qw_r, Cb[h][1]], w=[nps_r])
                self.act(lambda e, nps=nps, po=po: e.activation(out=dn[po:po + 64, :], in_=nps[po:po + 64, 256:257], func=AF.Abs),
                         r=[nps_r], w=[dn_r])
                self.dve(lambda e, h=h, po=po: e.tensor_tensor(out=dn[po:po + 64, :], in0=dn[po:po + 64, :], in1=emT[po:po + 64, h:h + 1], op=ALU.max),
                         r=[dn_r, emT_r], w=[dn_r])
                self.dve(lambda e, po=po: e.reciprocal(out=dn[po:po + 64, :], in_=dn[po:po + 64, :]), r=[dn_r], w=[dn_r])
                self.act(lambda e, nps=nps, h=h, po=po: e.activation(out=hout[po:po + 64, h, :], in_=nps[po:po + 64, 0:256], func=AF.Copy,
                                                                    scale=dn[po:po + 64, 0:1]), r=[nps_r, dn_r], w=[hout_r])
                kw, kw_r = kws[h % 2]
                self.act(lambda e, kw=kw, kt=kt, h=h: e.activation(out=kw[:], in_=kt[:, 256 * h:256 * h + 256], func=AF.Copy,
                                                                  scale=wm[:, h, 63:64]), r=[kt_r, wm_r], w=[kw_r])
                for dk in range(2):
                    ups, ups_r = self.next_ps()
                    self.pe(lambda e, ups=ups, kw=kw, dk=dk, h=h, vt=vt: e.matmul(ups[:, 0:257], lhsT=kw[:, 128 * dk:128 * dk + 128],
                                                                                rhs=vt[:, 257 * h:257 * h + 257], start=True, stop=True),
                            r=[kw_r, vt_r], w=[ups_r])
                    self.dve(lambda e, ups=ups, h=h, dk=dk: e.scalar_tensor_tensor(out=C[h][0][:, dk, :], in0=C[h][0][:, dk, :], scalar=wpv[:, h:h + 1],
                                                                                  in1=ups[:, 0:257], op0=ALU.mult, op1=ALU.add),
                             r=[C[h][1], wpv_r, ups_r], w=[C[h][1]])
                self.act(lambda e, h=h: e.copy(out=Cb[h][0][:], in_=C[h][0][:]), r=[C[h][1]], w=[Cb[h][1]])
            if c % 2 == 1:
                self.ml_post_tile(c // 2, hout, hout_r, K_)
        nn, nn_r = T("nfin", [128, 16])
        for h in range(8):
            self.store(self.o_mc_p[h].rearrange("(dk p) e -> p dk e", p=128), C[h][0][:, :, 0:256], r=[C[h][1]])
            self.pool(lambda e, h=h: e.tensor_copy(out=nn[:, 2 * h:2 * h + 2], in_=C[h][0][:, :, 256]), r=[C[h][1]], w=[nn_r])
        ps, ps_r = self.next_ps()
        self.pe(lambda e, ps=ps: e.transpose(out=ps[0:16, 0:128], in_=nn[:], identity=self.ident_f[:]), r=[nn_r, self.ident_f_r], w=[ps_r])
        nt_, nt_r = T("nfinT", [16, 128])
        self.act(lambda e, ps=ps: e.copy(out=nt_[:], in_=ps[0:16, 0:128]), r=[ps_r], w=[nt_r])
        self.store(self.o_mn_p, nt_[:], r=[nt_r])

    def ml_sample(self, st):
        P = self.P

        def T(name, shape, dt=F32):
            return P.sb(st, name, shape, dt)
        K_ = self.ml_post_alloc(T)
        hmask, hmask_r = T("shmask", [8, 8, 128])
        self.ld(hmask[:].rearrange("p a t -> p (a t)"), self.hmask_d, w=[hmask_r])
        ones8, ones8_r = T("sones8", [8, 128])
        self.ld(ones8[:], self.ones8_d, w=[ones8_r])
        smask, smask_r = T("smask", [128, 128])
        self.ld(smask[:], self.smask_d, w=[smask_r])
        seqsel, seqsel_r = T("seqsel", [128, N_S])
        self.ld(seqsel[:], self.seqsel_d, w=[seqsel_r])
        qT, qT_r = T("sqT", [128, 16, T_S], BF16)
        kT, kT_r = T("skT", [128, 16, T_S], BF16)
        self.ld(qT[:], self.mq_d[:, :, T_P:TOK].rearrange("f p t -> p f t"), r=[self.ml_r], w=[qT_r])
        self.ld(kT[:], self.mk_d[:, :, T_P:TOK].rearrange("f p t -> p f t"), r=[self.ml_r], w=[kT_r])
        vt, vt_r = T("svt", [128, 8 * 257], BF16)
        kt, kt_r = T("skt", [128, E], BF16)
        self.ld(vt[:], self.mvt_d[T_P:TOK, :], r=[self.ml_r], w=[vt_r])
        self.ld(kt[:], self.mkt_d[T_P:TOK, :], r=[self.ml_r], w=[kt_r])
        gq, gq_r = T("sgq", [8, 4, T_S])
        self.ld(gq[:], self.gq_d[:, :, T_P:TOK].rearrange("q h t -> h q t"), r=[self.ml_r], w=[gq_r])
        nGe, nGe_r = T("snGe", [8, 8, 128])
        wie, wie_r = T("swie", [8, 8, 128])
        wsg, wsg_r = T("swsg", [8, 128])
        self.dve(lambda e: e.tensor_tensor(out=nGe[:], in0=bc(gq[:, 1, :], 1, 8), in1=hmask[:], op=ALU.mult), r=[gq_r, hmask_r], w=[nGe_r])
        self.pool(lambda e: e.tensor_tensor(out=wie[:], in0=bc(gq[:, 2, :], 1, 8), in1=hmask[:], op=ALU.mult), r=[gq_r, hmask_r], w=[wie_r])
        self.dve(lambda e: e.tensor_tensor(out=wsg[:].rearrange("p (n t) -> p n t", t=8), in0=gq[:, 0, :].rearrange("p (n t) -> p n t", t=8),
                                           in1=bc(gq[:, 1, :].rearrange("p (n t) -> p n t", t=8)[:, :, 7], 2, 8), op=ALU.add), r=[gq_r], w=[wsg_r])
        self.act(lambda e: e.activation(out=wsg[:], in_=wsg[:], func=AF.Exp), r=[wsg_r], w=[wsg_r])
        wm, wm_r = T("swm", [128, 8, 128])
        PT, PT_r = T("sPT", [128, 8, 128], BF16)
        qw, qw_r = T("sqw", [128, 16, 128], BF16)
        wpv, wpv_r = T("swpv", [128, 8, N_S])
        emT, emT_r = T("semT", [128, 8])
        wstT, wstT_r = T("swstT", [128, 8])
        Wsbs = [T("sWsb", [128, 512]) for _ in range(2)]
        ps, ps_r = self.next_ps()
        self.pe(lambda e, ps=ps: e.matmul(ps[:, 0:8], lhsT=gq[:, 3, :], rhs=self.ident_f[0:8, 0:8], start=True, stop=True), r=[gq_r, self.ident_f_r], w=[ps_r])
        self.act(lambda e, ps=ps: e.copy(out=emT[:], in_=ps[:, 0:8]), r=[ps_r], w=[emT_r])
        ps, ps_r = self.next_ps()
        self.pe(lambda e, ps=ps: e.matmul(ps[:, 0:8], lhsT=wsg[:], rhs=self.ident_f[0:8, 0:8], start=True, stop=True), r=[wsg_r, self.ident_f_r], w=[ps_r])
        self.act(lambda e, ps=ps: e.copy(out=wstT[:], in_=ps[:, 0:8]), r=[ps_r], w=[wstT_r])
        dstop = getattr(self, "dbg_stop", 99)
        if dstop <= 1:
            return
        for hb in range(2):
            hs = slice(4 * hb, 4 * hb + 4)
            Dps, Dps_r = self.next_ps()
            Dv = Dps[:].rearrange("p (a t) -> p a t", t=128)
            self.pe(lambda e, Dv=Dv, hs=hs: e.matmul(Dv, lhsT=ones8[:], rhs=nGe[:, hs, :], start=True, stop=False), r=[ones8_r, nGe_r], w=[Dps_r])
            self.pe(lambda e, Dv=Dv, hs=hs: e.matmul(Dv, lhsT=gq[:, 0, :], rhs=hmask[:, hs, :], start=False, stop=True), r=[gq_r, hmask_r], w=[Dps_r])
            self.act(lambda e, Dv=Dv, hs=hs: e.activation(out=wm[:, hs, :], in_=Dv, func=AF.Exp), r=[Dps_r], w=[wm_r])
            self.dve(lambda e, hs=hs: e.tensor_tensor(out=wm[:, hs, :], in0=wm[:, hs, :], in1=bc(smask[:], 1, 4), op=ALU.mult), r=[wm_r, smask_r], w=[wm_r])
            if dstop <= 1.2:
                continue
            Wps, Wps_r = self.next_ps()
            Wv = Wps[:].rearrange("p (a t) -> p a t", t=128)
            self.pe(lambda e, Wv=Wv, hs=hs: e.matmul(Wv, lhsT=ones8[:], rhs=wie[:, hs, :], start=True, stop=True), r=[ones8_r, wie_r], w=[Wps_r])
            if dstop <= 1.3:
                continue
            Wsb, Wsb_r = Wsbs[hb]
            self.act(lambda e, Wps=Wps, Wsb=Wsb: e.copy(out=Wsb[:], in_=Wps[:]), r=[Wps_r], w=[Wsb_r])
            for k2 in range(2):
                self.dve(lambda e, Wsb=Wsb, hb=hb, k2=k2: e.tensor_tensor(
                    out=qw[:, 8 * hb:8 * hb + 8, :].rearrange("p (h k) t -> p h k t", k=2)[:, :, k2, :],
                    in0=qT[:, 8 * hb:8 * hb + 8, :].rearrange("p (h k) t -> p h k t", k=2)[:, :, k2, :],
                    in1=Wsb[:].rearrange("p (a t) -> p a t", t=128), op=ALU.mult), r=[qT_r, Wsb_r], w=[qw_r])
            self.dve(lambda e, Wsb=Wsb, hs=hs: e.tensor_copy(out=wpv[:, hs, :], in_=Wsb[:].rearrange("p (a n t) -> p a n t", n=N_S, t=8)[:, :, :, 7]),
                     r=[Wsb_r], w=[wpv_r])
            if dstop <= 1.5:
                continue
            Sps, Sps_r = self.next_ps()
            for hh in range(4):
                h = 4 * hb + hh
                for dk in range(2):
                    self.pe(lambda e, Sps=Sps, hh=hh, h=h, dk=dk: e.matmul(Sps[:, 128 * hh:128 * hh + 128], lhsT=kT[:, 2 * h + dk, :], rhs=qT[:, 2 * h + dk, :],
                                                                          start=(hh == 0 and dk == 0), stop=(dk == 1), skip_group_check=True),
                            r=[kT_r, qT_r], w=[Sps_r])
            self.dve(lambda e, Sps=Sps, hs=hs: e.scalar_tensor_tensor(out=PT[:, hs, :], in0=Sps[:].rearrange("p (a t) -> p a t", t=128), scalar=1.0 / 16.0,
                                                                      in1=wm[:, hs, :], op0=ALU.mult, op1=ALU.mult), r=[Sps_r, wm_r], w=[PT_r])
        if dstop <= 2:
            return
        hout, hout_r = T("shout", [128, 8, 256])
        numacc, numacc_r = T("numacc", [128, 257])
        tot, tot_r = T("stot", [128, 257])
        dn, dn_r = T("sdn", [128, 1])
        Vexps = [T("Vexp", [128, N_S, 257], BF16) for _ in range(2)]
        kws = [T("skw", [128, 256], BF16) for _ in range(2)]
        C0x = [T("C0x", [128, 2, 257]) for _ in range(2)]
        C0b = [T("C0b", [128, 2, 257], BF16) for _ in range(2)]
        Cn = [T("Cn", [128, 2, 257]) for _ in range(2)]
        NN, NN_r = T("sNN", [128, N_S, 8, 2])
        ci = 0
        for h in range(8):
            Vexp, Vexp_r = Vexps[h % 2]
            kw, kw_r = kws[h % 2]
            self.pool(lambda e, Vexp=Vexp, h=h: e.tensor_tensor(out=Vexp[:], in0=bc(vt[:, 257 * h:257 * h + 257], 1, N_S), in1=bc(seqsel[:], 2, 257),
                                                                op=ALU.mult), r=[vt_r, seqsel_r], w=[Vexp_r])
            self.act(lambda e, kw=kw, h=h: e.activation(out=kw[:], in_=kt[:, 256 * h:256 * h + 256], func=AF.Copy, scale=wstT[:, h:h + 1]),
                     r=[kt_r, wstT_r], w=[kw_r])
            for n in range(N_S):
                cx, cx_r = C0x[ci % 2]
                cb, cb_r = C0b[ci % 2]
                cn, cn_r = Cn[ci % 2]
                ci += 1
                self.ld(cx[:, :, 0:256], self.ml_c0[n, h].rearrange("(dk p) e -> p dk e", p=128), w=[cx_r])
                self.P.dma("sp", lambda e, cx=cx, n=n, h=h: e.dma_start(out=cx[:, :, 256], in_=self.ml_n0[n, h].rearrange("(dk p) -> p dk", p=128),
                                                                     allow_slow_non_contiguous=True), (), [cx_r])
                self.pool(lambda e, cx=cx, cb=cb: e.tensor_copy(out=cb[:], in_=cx[:]), r=[cx_r], w=[cb_r])
                ips, ips_r = self.next_ps()
                for dk in range(2):
                    self.pe(lambda e, ips=ips, h=h, dk=dk, cb=cb: e.matmul(ips[:, 0:257], lhsT=qw[:, 2 * h + dk, :], rhs=cb[:, dk, :], start=(dk == 0), stop=(dk == 1)),
                            r=[qw_r, cb_r], w=[ips_r])
                if n == 0:
                    self.dve(lambda e, ips=ips, n=n: e.tensor_scalar(out=numacc[:], in0=ips[:, 0:257], scalar1=seqsel[:, n:n + 1], scalar2=None, op0=ALU.mult),
                             r=[ips_r, seqsel_r], w=[numacc_r])
                else:
                    self.dve(lambda e, ips=ips, n=n: e.scalar_tensor_tensor(out=numacc[:], in0=ips[:, 0:257], scalar=seqsel[:, n:n + 1], in1=numacc[:],
                                                                           op0=ALU.mult, op1=ALU.add), r=[ips_r, seqsel_r, numacc_r], w=[numacc_r])
                for dk in range(2):
                    ups, ups_r = self.next_ps()
                    self.pe(lambda e, ups=ups, kw=kw, dk=dk, n=n, Vexp=Vexp: e.matmul(ups[:, 0:257], lhsT=kw[:, 128 * dk:128 * dk + 128], rhs=Vexp[:, n, :],
                                                                                    start=True, stop=True), r=[kw_r, Vexp_r], w=[ups_r])
                    self.dve(lambda e, ups=ups, cn=cn, cx=cx, dk=dk, h=h, n=n: e.scalar_tensor_tensor(out=cn[:, dk, :], in0=cx[:, dk, :], scalar=wpv[:, h, n:n + 1],
                                                                                                 in1=ups[:, 0:257], op0=ALU.mult, op1=ALU.add),
                             r=[cx_r, wpv_r, ups_r], w=[cn_r])
                self.store(self.o_mc_s[n, h].rearrange("(dk p) e -> p dk e", p=128), cn[:, :, 0:256], r=[cn_r])
                self.pool(lambda e, cn=cn, n=n, h=h: e.tensor_copy(out=NN[:, n, h, :], in_=cn[:, :, 256]), r=[cn_r], w=[NN_r])
            nps, nps_r = self.next_ps()
            self.pe(lambda e, nps=nps, h=h: e.matmul(nps[:, 0:257], lhsT=PT[:, h, :], rhs=vt[:, 257 * h:257 * h + 257], start=True, stop=True),
                    r=[PT_r, vt_r], w=[nps_r])
            self.dve(lambda e, nps=nps: e.tensor_tensor(out=tot[:], in0=nps[:, 0:257], in1=numacc[:], op=ALU.add), r=[nps_r, numacc_r], w=[tot_r])
            self.act(lambda e: e.activation(out=dn[:], in_=tot[:, 256:257], func=AF.Abs), r=[tot_r], w=[dn_r])
            self.dve(lambda e, h=h: e.tensor_tensor(out=dn[:], in0=dn[:], in1=emT[:, h:h + 1], op=ALU.max), r=[dn_r, emT_r], w=[dn_r])
            self.dve(lambda e: e.reciprocal(out=dn[:], in_=dn[:]), r=[dn_r], w=[dn_r])
            self.act(lambda e, h=h: e.activation(out=hout[:, h, :], in_=tot[:, 0:256], func=AF.Copy, scale=dn[:, 0:1]), r=[tot_r, dn_r], w=[hout_r])
        if dstop <= 3:
            return
        self.ml_post_tile(NTILE - 1, hout, hout_r, K_)
        if dstop <= 4:
            return
        NNv = NN[:].rearrange("p n h k -> p (n h k)")
        ntT, ntT_r = T("sntT", [128, 2, 128])
        for b in range(2):
            ps, ps_r = self.next_ps()
            self.pe(lambda e, ps=ps, b=b: e.transpose(out=ps[:, 0:128], in_=NNv[:, 128 * b:128 * b + 128], identity=self.ident_f[:]),
                    r=[NN_r, self.ident_f_r], w=[ps_r])
            self.act(lambda e, ps=ps, b=b: e.copy(out=ntT[:, b, :], in_=ps[:, 0:128]), r=[ps_r], w=[ntT_r])
        self.store(self.o_mn_s.rearrange("(b r) p -> r b p", b=2), ntT[:], r=[ntT_r])

    def final_phase(self, from_x=False):
        P = self.P
        with ExitStack() as st:
            gt, gt_r = P.sb(st, "gtf", [128, D_MODEL])
            self.ld(gt[:], self.final_norm.partition_broadcast(128), w=[gt_r])
            hts = [P.sb(st, "htf", [128, D_MODEL]) for _ in range(3)]
            junk, junk_r = P.sb(st, "junkf", [128, D_MODEL], BF16)
            sss = [P.sb(st, "ssf", [128, 1]) for _ in range(3)]
            for i in range(NTILE):
                ht, ht_r = hts[i % 3]
                ss, ss_r = sss[i % 3]
                self.ld(ht[:], self.h_src(from_x, i), r=[self.H_r[i]], w=[ht_r])
                self.act(lambda e, ht=ht, ss=ss: e.activation(out=junk[:], in_=ht[:], func=AF.Square, accum_out=ss[:]),
                         r=[ht_r], w=[junk_r, ss_r])
                self.dve(lambda e, ss=ss: e.tensor_scalar(out=ss[:], in0=ss[:], scalar1=1.0 / D_MODEL, scalar2=1e-6,
                                                          op0=ALU.mult, op1=ALU.add), r=[ss_r], w=[ss_r])
                self.act(lambda e, ss=ss: e.activation(out=ss[:], in_=ss[:], func=AF.Sqrt), r=[ss_r], w=[ss_r])
                self.dve(lambda e, ss=ss: e.reciprocal(out=ss[:], in_=ss[:]), r=[ss_r], w=[ss_r])
                self.dve(lambda e, ht=ht, ss=ss: e.scalar_tensor_tensor(out=ht[:], in0=ht[:], scalar=ss[:, 0:1], in1=gt[:],
                                                                        op0=ALU.mult, op1=ALU.mult), r=[ht_r, ss_r, gt_r], w=[ht_r])
                self.store(self.y[i * 128:(i + 1) * 128, :], ht[:], r=[ht_r])


_CACHE = {}


def _get_prog(nlayers=4):
    if nlayers not in _CACHE:
        kb = KB(nlayers=nlayers)
        kb.build()
        _CACHE[nlayers] = kb
    return _CACHE[nlayers]


def make_in_maps(inp, kb):
    f32 = np.float32
    maps = []
    ident = np.eye(128, dtype=f32)
    for c in range(8):
        b = c % 4
        m = {}
        xs = np.asarray(inp["x_sample"][16 * c:16 * c + 16], f32).reshape(T_S, D_MODEL)
        m["xin"] = np.concatenate([np.asarray(inp["x_prompt"][b], f32), xs], axis=0)
        m["ident_f"] = ident
        m["final_norm"] = np.asarray(inp["final_norm"], f32).reshape(1, D_MODEL)
        for k in ["ssm_norm", "ssm_w_in", "ssm_a_re", "ssm_a_im", "ssm_b_re", "ssm_b_im", "ssm_c_re", "ssm_c_im",
                  "ssm_d", "ssm_w_glu", "ssm_b_glu", "ssm_w_out"]:
            m[k] = np.asarray(inp[k], f32)
        m["ssm_log_dt"] = np.asarray(inp["ssm_log_dt"], f32).reshape(2, 128, 1)
        m["st_re"] = np.asarray(inp["state_ssm_re"][:, 16 * c:16 * c + 16], f32).reshape(2, N_S, 8192)
        m["st_im"] = np.asarray(inp["state_ssm_im"][:, 16 * c:16 * c + 16], f32).reshape(2, N_S, 8192)
        m["attn_norm"] = np.asarray(inp["attn_norm"], f32).reshape(1, D_MODEL)
        m["attn_w_in"] = np.asarray(inp["attn_w_in"], f32).reshape(D_MODEL, ATTN_IN)
        m["attn_w_out"] = np.asarray(inp["attn_w_out"], f32).reshape(E, D_MODEL)
        m["rope_cs"] = rope_table()
        for k in ["mlstm_conv_w", "mlstm_conv_b", "mlstm_w_q", "mlstm_w_k", "mlstm_w_v", "mlstm_w_o", "mlstm_skip"]:
            m[k] = np.asarray(inp[k][0], f32)
        m["mlstm_norm"] = np.asarray(inp["mlstm_norm"], f32).reshape(1, D_MODEL)
        m["mlstm_w_in"] = np.asarray(inp["mlstm_w_in"][0], f32)
        m["mlstm_b_o"] = np.asarray(inp["mlstm_b_o"], f32).reshape(1, E)
        m["mlstm_w_gates"] = np.asarray(inp["mlstm_w_gates"][0], f32)
        m["mlstm_b_gates"] = np.asarray(inp["mlstm_b_gates"], f32).reshape(16, 1)
        m["mlstm_ln_w"] = np.asarray(inp["mlstm_ln_w"], f32).reshape(1, E)
        m["mlstm_w_out"] = np.asarray(inp["mlstm_w_out"][0], f32)
        m["ml_c0"] = np.asarray(inp["state_mlstm_c"][0, 16 * c:16 * c + 16], f32)
        m["ml_n0"] = np.asarray(inp["state_mlstm_n"][0, 16 * c:16 * c + 16], f32)
        m["ml_m0"] = np.asarray(inp["state_mlstm_m"][0, 16 * c:16 * c + 16], f32)
        m["ml_conv0"] = np.asarray(inp["state_mlstm_conv"][0, 16 * c:16 * c + 16], f32).reshape(N_S * 3, E)
        m["hmask"] = np.repeat(np.eye(8, dtype=f32), 128, axis=1)
        m["ones8"] = np.ones((8, 128), f32)
        m["cm64"] = (np.arange(64)[:, None] <= np.arange(64)[None, :]).astype(f32)
        m["seqsel"] = (np.arange(128)[:, None] // 8 == np.arange(N_S)[None, :]).astype(f32)
        ii = np.arange(128)
        m["smask"] = ((ii[:, None] // 8 == ii[None, :] // 8) & (ii[:, None] % 8 <= ii[None, :] % 8)).astype(f32)
        m["cmask"] = np.where(np.arange(128)[None, :] <= np.arange(128)[:, None], 0.0, -1e30).astype(f32)
        m["cmask_s"] = np.where(np.arange(8)[None, :] <= (np.arange(128) % 8)[:, None], 0.0, -1e30).astype(f32)
        m["selm"] = (np.arange(64)[:, None] % 8 == np.arange(8)[None, :]).astype(f32)
        m["blockm"] = (np.arange(128)[:, None] // 8 == np.arange(128)[None, :] // 8).astype(f32)
        m["iota_p"] = np.arange(128, dtype=f32).reshape(128, 1)
        m["page_table"] = np.asarray(inp["page_table"][16 * c:16 * c + 16], np.int32).reshape(1, N_S * 16)
        m["cache_k"] = np.asarray(inp["cache_k"], f32).reshape(2560 * 128, 256)
        m["cache_v"] = np.asarray(inp["cache_v"], f32).reshape(2560 * 128, 256)
        m["cache_kidx"] = np.asarray(inp["cache_kidx"], f32).reshape(2560 * 128, 64)
        maps.append({k: np.ascontiguousarray(v) for k, v in m.items() if k in kb.inputs})
    return maps


def rope_table():
    half = 32
    inv = (np.float32(10000.0) ** (-np.arange(half, dtype=np.float32) / np.float32(half))).astype(np.float32)
    pos = np.concatenate([np.arange(T_P), np.tile(2048 + np.arange(8), N_S)]).astype(np.float32)
    ang = (pos[:, None] * inv[None, :]).astype(np.float32)
    return np.concatenate([np.cos(ang), np.sin(ang)], axis=1).astype(np.float32)


def kernel(**inp):
    kb = _get_prog(4)
    maps = make_in_maps(inp, kb)
    res = run_bass_kernel_spmd(kb.nc, maps, core_ids=list(range(8))).results
    return assemble(res)


def assemble(res):
    f32 = np.float32

    def A(x):
        return np.ascontiguousarray(np.asarray(x, f32))
    P4 = range(4)
    C8 = range(8)
    y_p = A(np.stack([res[b]["y"][:T_P] for b in P4]))
    y_s = A(np.concatenate([res[c]["y"][T_P:].reshape(N_S, 8, D_MODEL) for c in C8]))
    sre_p = A(np.stack([res[b]["o_sre_p"].reshape(2, 128, 64) for b in P4], axis=1))
    sim_p = A(np.stack([res[b]["o_sim_p"].reshape(2, 128, 64) for b in P4], axis=1))
    sre_s = A(np.concatenate([res[c]["o_sre_s"].reshape(2, N_S, 128, 64) for c in C8], axis=1))
    sim_s = A(np.concatenate([res[c]["o_sim_s"].reshape(2, N_S, 128, 64) for c in C8], axis=1))
    k_p = A(np.stack([res[b]["o_k_p"].reshape(T_P, 4, 64) for b in P4]))[None]
    v_p = A(np.stack([res[b]["o_v_p"].reshape(T_P, 4, 64) for b in P4]))[None]
    ki_p = A(np.stack([res[b]["o_kidx_p"].reshape(T_P, 64) for b in P4]))[None]
    k_s = A(np.concatenate([res[c]["o_k_s"].reshape(N_S, 8, 4, 64) for c in C8]))[None]
    v_s = A(np.concatenate([res[c]["o_v_s"].reshape(N_S, 8, 4, 64) for c in C8]))[None]
    ki_s = A(np.concatenate([res[c]["o_kidx_s"].reshape(N_S, 8, 64) for c in C8]))[None]
    mc_p = A(np.stack([res[b]["o_mc_p"].reshape(8, 256, 256) for b in P4]))[None]
    mn_p = A(np.stack([res[b]["o_mn_p"].reshape(8, 256) for b in P4]))[None]
    mm_p = A(np.stack([res[b]["o_mm_p"].reshape(8) for b in P4]))[None]
    mv_p = A(np.stack([res[b]["o_mconv_p"].reshape(3, E) for b in P4]))[None]
    mc_s = A(np.concatenate([res[c]["o_mc_s"].reshape(N_S, 8, 256, 256) for c in C8]))[None]
    mn_s = A(np.concatenate([res[c]["o_mn_s"].reshape(N_S, 8, 256) for c in C8]))[None]
    mm_s = A(np.concatenate([res[c]["o_mm_s"].reshape(8, N_S).T for c in C8]))[None]
    mv_s = A(np.concatenate([res[c]["o_mconv_s"].reshape(N_S, 3, E) for c in C8]))[None]
    return (y_p, y_s, sre_p, sim_p, sre_s, sim_s, k_p, v_p, ki_p, k_s, v_s, ki_s,
            mc_p, mn_p, mm_p, mv_p, mc_s, mn_s, mm_s, mv_s)
```

```python
import math
import numpy as np
from contextlib import ExitStack
import concourse.bass as bass
import concourse.mybir as mybir
from concourse.bass_utils import run_bass_kernel_spmd

F32 = mybir.dt.float32
BF16 = mybir.dt.bfloat16
I32 = mybir.dt.int32
ALU = mybir.AluOpType
AF = mybir.ActivationFunctionType
AX = mybir.AxisListType

D_MODEL = 1024
E = 2048
T_P = 4096
N_S = 16
T_S = 128
TOK = T_P + T_S
NTILE = TOK // 128
NGRP = 9
TWO_PI = 2.0 * math.pi
ATTN_IN = 5192


def grp_tok(tg):
    return (tg * 512, 512) if tg < 8 else (T_P, T_S)


class Res:
    __slots__ = ("name", "w", "r")

    def __init__(self, name=""):
        self.name = name
        self.w = None
        self.r = {}


class Prog:
    ENG = ["pe", "dve", "act", "pool", "sp"]
    NS = 8

    def __init__(self, nc, stack):
        self.nc = nc
        self.q = {e: [] for e in self.ENG}
        self.cnt = {e: 0 for e in self.ENG}
        self.waited = {e: {} for e in self.ENG}
        self.ndma = {e: 0 for e in self.ENG}
        self.latest = {}
        self.sem = {}
        for e in ["pe", "dve", "act", "pool"]:
            self.sem[e] = stack.enter_context(nc.semaphore("s_" + e))
        for e in ["sp", "pool", "act"]:
            for i in range(self.NS):
                self.sem[("dma", e, i)] = stack.enter_context(nc.semaphore("d_%s_%d" % (e, i)))
        self.out_tokens = []
        self.uid = 0

    def sb(self, st, name, shape, dtype=F32):
        self.uid += 1
        t = st.enter_context(self.nc.sbuf_tensor("%s_%d" % (name, self.uid), list(shape), dtype))
        return t, Res(name)

    def ps(self, st, name, shape, dtype=F32):
        self.uid += 1
        t = st.enter_context(self.nc.psum_tensor("%s_%d" % (name, self.uid), list(shape), dtype))
        return t, Res(name)

    def _deps(self, reads, writes):
        deps = {}
        for r in reads:
            if r.w is not None:
                k, v = r.w
                if deps.get(k, 0) < v:
                    deps[k] = v
        for w in writes:
            if w.w is not None:
                k, v = w.w
                if deps.get(k, 0) < v:
                    deps[k] = v
            for k, v in w.r.items():
                if deps.get(k, 0) < v:
                    deps[k] = v
        return deps

    def _waits(self, eng, deps):
        ws = []
        wd = self.waited[eng]
        for k, v in deps.items():
            if eng == "pe" and k == "pe":
                continue
            if wd.get(k, 0) < v:
                wd[k] = v
                ws.append((self.sem[k], v))
        return ws

    def _update(self, tok, reads, writes):
        k, v = tok
        self.latest[k] = v
        for r in reads:
            if r.r.get(k, 0) < v:
                r.r[k] = v
        for w in writes:
            w.w = tok
            w.r = {}

    def op(self, eng, fn, reads=(), writes=()):
        deps = self._deps(reads, writes)
        ws = self._waits(eng, deps)
        self.cnt[eng] += 1
        tok = (eng, self.cnt[eng])
        sem = self.sem[eng]

        def emit(e, ws=ws, fn=fn, sem=sem):
            for s, v in ws:
                e.wait_ge(s, v)
            fn(e).then_inc(sem, 1)

        self.q[eng].append(emit)
        self._update(tok, reads, writes)
        return tok

    def dma(self, queue, fn, reads=(), writes=(), is_output=False):
        deps = self._deps(reads, writes)
        n = self.ndma[queue]
        self.ndma[queue] += 1
        idx = n % self.NS
        target = 16 * (n // self.NS + 1)
        key = ("dma", queue, idx)
        if target > 16 and deps.get(key, 0) < target - 16:
            deps[key] = target - 16
        ws = self._waits(queue, deps)
        sem = self.sem[key]

        def emit(e, ws=ws, fn=fn, sem=sem):
            for s, v in ws:
                e.wait_ge(s, v)
            fn(e).then_inc(sem, 16)

        self.q[queue].append(emit)
        tok = (key, target)
        self._update(tok, reads, writes)
        if is_output:
            self.out_tokens.append(tok)
        return tok

    def barrier(self):
        deps = dict(self.latest)
        for eng in self.ENG:
            ws = self._waits(eng, deps)
            if ws:
                def emit(e, ws=ws):
                    for s, v in ws:
                        e.wait_ge(s, v)
                self.q[eng].append(emit)

    def finish(self):
        self.barrier()

    def emit_all(self):
        nc = self.nc
        q = self.q
        with nc.Block() as block:
            @block.sync
            def _(e):
                for f in q["sp"]:
                    f(e)

            @block.tensor
            def _(e):
                for f in q["pe"]:
                    f(e)

            @block.vector
            def _(e):
                for f in q["dve"]:
                    f(e)

            @block.scalar
            def _(e):
                for f in q["act"]:
                    f(e)

            @block.gpsimd
            def _(e):
                for f in q["pool"]:
                    f(e)


def bc(ap, axis, n):
    a = ap.unsqueeze(axis)
    shp = list(a.shape)
    shp[axis] = n
    return a.broadcast_to(shp)


class KB:
    def __init__(self, nlayers=4, dbg=False):
        self.nlayers = nlayers
        self.dsa_stage = 3
        self.ml_stage = 3
        self.dbg = dbg
        nc = bass.Bass("TRN2", target_bir_lowering=False)
        self.nc = nc
        self.inputs = {}
        self.outputs = {}

    def din(self, name, shape, dtype=F32):
        t = self.nc.dram_tensor(name, list(shape), dtype, kind="ExternalInput").ap()
        self.inputs[name] = (tuple(shape), dtype)
        return t

    def dout(self, name, shape, dtype=F32):
        t = self.nc.dram_tensor(name, list(shape), dtype, kind="ExternalOutput").ap()
        self.outputs[name] = tuple(shape)
        return t

    def dscr(self, name, shape, dtype=F32):
        return self.nc.dram_tensor(name, list(shape), dtype, kind="Internal").ap()

    def dve(self, fn, r=(), w=()):
        return self.P.op("dve", fn, r, w)

    def act(self, fn, r=(), w=()):
        return self.P.op("act", fn, r, w)

    def pool(self, fn, r=(), w=()):
        return self.P.op("pool", fn, r, w)

    def pe(self, fn, r=(), w=()):
        return self.P.op("pe", fn, r, w)

    def ld(self, out, in_, r=(), w=(), q="sp"):
        return self.P.dma(q, lambda e: e.dma_start(out=out, in_=in_), r, w)

    def ldc(self, out, in_, r=(), w=()):
        return self.P.dma("pool", lambda e: e.dma_start(out=out, in_=in_), r, w)

    def store(self, out, in_, r=(), w=(), q="sp"):
        return self.P.dma(q, lambda e: e.dma_start(out=out, in_=in_), r, w, is_output=True)

    def next_ps(self):
        self.ps_i = (self.ps_i + 1) % self.ps_lim
        return self.psb[self.ps_i]

    def build(self):
        nc = self.nc
        self.xin = self.din("xin", [TOK, D_MODEL])
        self.ident_f_d = self.din("ident_f", [128, 128])
        self.final_norm = self.din("final_norm", [1, D_MODEL])
        self.ssm_norm = self.din("ssm_norm", [2, D_MODEL])
        self.ssm_w_in = self.din("ssm_w_in", [2, D_MODEL, 2 * E])
        self.ssm_a_re = self.din("ssm_a_re", [2, 128, 64])
        self.ssm_a_im = self.din("ssm_a_im", [2, 128, 64])
        self.ssm_log_dt = self.din("ssm_log_dt", [2, 128, 1])
        self.ssm_b_re = self.din("ssm_b_re", [2, 128, 64, 16])
        self.ssm_b_im = self.din("ssm_b_im", [2, 128, 64, 16])
        self.ssm_c_re = self.din("ssm_c_re", [2, 128, 16, 64])
        self.ssm_c_im = self.din("ssm_c_im", [2, 128, 16, 64])
        self.ssm_d = self.din("ssm_d", [2, E])
        self.ssm_w_glu = self.din("ssm_w_glu", [2, E, E])
        self.ssm_b_glu = self.din("ssm_b_glu", [2, E])
        self.ssm_w_out = self.din("ssm_w_out", [2, E, D_MODEL])
        self.st_re = self.din("st_re", [2, N_S, 8192])
        self.st_im = self.din("st_im", [2, N_S, 8192])
        self.attn_norm = self.din("attn_norm", [1, D_MODEL])
        self.attn_w_in = self.din("attn_w_in", [D_MODEL, ATTN_IN])
        self.attn_w_out = self.din("attn_w_out", [E, D_MODEL])
        self.rope_cs = self.din("rope_cs", [TOK, 64])
        self.cmask_d = self.din("cmask", [128, 128])
        self.cmask_s_d = self.din("cmask_s", [128, 8])
        self.selm_d = self.din("selm", [64, 8])
        self.blockm_d = self.din("blockm", [128, 128])
        self.iota_d = self.din("iota_p", [128, 1])
        self.page_table = self.din("page_table", [1, N_S * 16], I32)
        self.cache_k = self.din("cache_k", [2560 * 128, 256])
        self.cache_v = self.din("cache_v", [2560 * 128, 256])
        self.cache_kidx = self.din("cache_kidx", [2560 * 128, 64])
        self.os_d = self.dscr("os_d", [T_S, E])
        self.o_k_p = self.dout("o_k_p", [T_P, 256])
        self.o_v_p = self.dout("o_v_p", [T_P, 256])
        self.o_kidx_p = self.dout("o_kidx_p", [T_P, 64])
        self.o_k_s = self.dout("o_k_s", [T_S, 256])
        self.o_v_s = self.dout("o_v_s", [T_S, 256])
        self.o_kidx_s = self.dout("o_kidx_s", [T_S, 64])
        self.qT_d = self.dscr("qT_d", [16, 128, TOK], BF16)
        self.kT2_d = self.dscr("kT2_d", [4, 128, TOK], BF16)
        self.qiT_d = self.dscr("qiT_d", [4, 128, TOK], BF16)
        self.kiT2_d = self.dscr("kiT2_d", [1, 128, TOK], BF16)
        self.Vx_d = self.dscr("Vx_d", [TOK, 260], BF16)
        self.sz_d = self.dscr("sz_d", [TOK, E], BF16)
        self.wi_d = self.dscr("wi_d", [TOK, 8])
        self.qTs_d = self.dscr("qTs_d", [64, 32, T_S], BF16)
        self.qiTs_d = self.dscr("qiTs_d", [64, 8, T_S], BF16)
        self.dsa_r = Res("dsa_scratch")
        self.ml_norm = self.din("mlstm_norm", [1, D_MODEL])
        self.ml_w_in = self.din("mlstm_w_in", [D_MODEL, 2 * E])
        self.ml_conv_w = self.din("mlstm_conv_w", [4, E])
        self.ml_conv_b = self.din("mlstm_conv_b", [E])
        self.ml_w_q = self.din("mlstm_w_q", [8, 256, 256])
        self.ml_w_k = self.din("mlstm_w_k", [8, 256, 256])
        self.ml_w_v = self.din("mlstm_w_v", [8, 256, 256])
        self.ml_w_o = self.din("mlstm_w_o", [8, 256, 256])
        self.ml_b_o = self.din("mlstm_b_o", [1, E])
        self.ml_w_gates = self.din("mlstm_w_gates", [3 * E, 16])
        self.ml_b_gates = self.din("mlstm_b_gates", [16, 1])
        self.ml_ln_w = self.din("mlstm_ln_w", [1, E])
        self.ml_skip = self.din("mlstm_skip", [E])
        self.ml_w_out = self.din("mlstm_w_out", [E, D_MODEL])
        self.ml_c0 = self.din("ml_c0", [N_S, 8, 256, 256])
        self.ml_n0 = self.din("ml_n0", [N_S, 8, 256])
        self.ml_m0 = self.din("ml_m0", [N_S, 8])
        self.ml_conv0 = self.din("ml_conv0", [N_S * 3, E])
        self.hmask_d = self.din("hmask", [8, 8 * 128])
        self.ones8_d = self.din("ones8", [8, 128])
        self.cm64_d = self.din("cm64", [64, 64])
        self.seqsel_d = self.din("seqsel", [128, N_S])
        self.smask_d = self.din("smask", [128, 128])
        self.o_mc_p = self.dout("o_mc_p", [8, 256, 256])
        self.o_mn_p = self.dout("o_mn_p", [16, 128])
        self.o_mm_p = self.dout("o_mm_p", [8, 1])
        self.o_mconv_p = self.dout("o_mconv_p", [3, E])
        self.o_mc_s = self.dout("o_mc_s", [N_S, 8, 256, 256])
        self.o_mn_s = self.dout("o_mn_s", [N_S * 16, 128])
        self.o_mm_s = self.dout("o_mm_s", [8, N_S])
        self.o_mconv_s = self.dout("o_mconv_s", [N_S, 3, E])
        self.muT_d = self.dscr("muT_d", [16, 128, TOK], BF16)
        self.mca_d = self.dscr("mca_d", [16, 128, TOK], BF16)
        self.mq_d = self.dscr("mq_d", [16, 128, TOK], BF16)
        self.mk_d = self.dscr("mk_d", [16, 128, TOK], BF16)
        self.mvt_d = self.dscr("mvt_d", [TOK, 8 * 257], BF16)
        self.mkt_d = self.dscr("mkt_d", [TOK, E], BF16)
        self.mo_d = self.dscr("mo_d", [TOK, E], BF16)
        self.gq_d = self.dscr("gq_d", [4, 8, TOK])
        self.ml_r = Res("ml_scratch")
        self.y = self.dout("y", [TOK, D_MODEL])
        self.o_sre_p = self.dout("o_sre_p", [2, 64, 128])
        self.o_sim_p = self.dout("o_sim_p", [2, 64, 128])
        self.o_sre_s = self.dout("o_sre_s", [2, N_S, 64, 128])
        self.o_sim_s = self.dout("o_sim_s", [2, N_S, 64, 128])
        self.H = self.dscr("H", [TOK, D_MODEL])
        self.H_r = [Res("H%d" % i) for i in range(NTILE)]
        self.uT_d = self.dscr("uT_d", [16, 128, 8, 8, 64], BF16)
        self.uTs_d = self.dscr("uTs_d", [16, 128, 8, N_S], BF16)
        self.szT_d = self.dscr("szT_d", [16, 128, TOK], BF16)
        self.gT_d = self.dscr("gT_d", [16, 128, TOK], BF16)
        self.Wd = self.dscr("Wd", [16, 128, 16, 64])
        self.Vd = self.dscr("Vd", [16, 128, 64, 16])
        self.Kd = self.dscr("Kd", [8, 128, 16, 16])
        self.KCd = self.dscr("KCd", [128, 64, 30])
        self.uT_r = [Res() for _ in range(16)]
        self.szT_r = [Res() for _ in range(16)]
        self.gT_r = [Res() for _ in range(16)]
        self.Wd_r, self.Vd_r, self.Kd_r, self.KCd_r = Res(), Res(), Res(), Res()

        with ExitStack() as gst:
            self.P = P = Prog(nc, gst)
            self.psb = [P.ps(gst, "psb", [128, 512]) for _ in range(7)]
            self.pst = P.ps(gst, "pst", [128, 1024], BF16)
            self.ps_i = 0
            self.ps_lim = 7
            self.ident_f, self.ident_f_r = P.sb(gst, "identf", [128, 128])
            self.ident_b, self.ident_b_r = P.sb(gst, "identb", [128, 128], BF16)
            self.ld(self.ident_f[:], self.ident_f_d, w=[self.ident_f_r])
            self.ldc(self.ident_b[:], self.ident_f_d, w=[self.ident_b_r])

            kinds = [0, 1, 2, 0]
            slots = [0, 0, 0, 1]
            if getattr(self, "only", None) == "ml_sample":
                with ExitStack() as st:
                    self.ml_sample(st)
                P.barrier()
                self.nlayers = 0
            for layer in range(self.nlayers):
                if kinds[layer] == 0:
                    self.s5_layer(slots[layer], first=(layer == 0))
                elif kinds[layer] == 1:
                    self.dsa_layer()
                else:
                    self.ml_layer()
                P.barrier()
            self.final_phase(from_x=(self.nlayers == 0))
            P.finish()
            P.emit_all()
        return nc

    def norm_tile(self, src_ap, src_r, gt, gt_r, ht, ht_r, junk, junk_r, ss, ss_r, xn, xn_r, xT_out, xT_r):
        self.ld(ht[:], src_ap, r=[src_r], w=[ht_r])
        self.act(lambda e: e.activation(out=junk[:], in_=ht[:], func=AF.Square, accum_out=ss[:]),
                 r=[ht_r], w=[junk_r, ss_r])
        self.dve(lambda e: e.tensor_scalar(out=ss[:], in0=ss[:], scalar1=1.0 / D_MODEL, scalar2=1e-6,
                                           op0=ALU.mult, op1=ALU.add), r=[ss_r], w=[ss_r])
        self.act(lambda e: e.activation(out=ss[:], in_=ss[:], func=AF.Sqrt), r=[ss_r], w=[ss_r])
        self.dve(lambda e: e.reciprocal(out=ss[:], in_=ss[:]), r=[ss_r], w=[ss_r])
        self.dve(lambda e: e.scalar_tensor_tensor(out=xn[:], in0=ht[:], scalar=ss[:, 0:1], in1=gt[:],
                                                  op0=ALU.mult, op1=ALU.mult), r=[ht_r, ss_r, gt_r], w=[xn_r])
        pt, pt_r = self.pst
        for k in range(8):
            self.pe(lambda e, k=k: e.transpose(out=pt[:, k * 128:(k + 1) * 128], in_=xn[:, k * 128:(k + 1) * 128],
                                               identity=self.ident_b[:]), r=[xn_r, self.ident_b_r], w=[pt_r])
        self.act(lambda e: e.copy(out=xT_out, in_=pt[:].rearrange("p (k t) -> p k t", k=8)), r=[pt_r], w=[xT_r])

    def h_src(self, first, i):
        src = self.xin if first else self.H
        return src[i * 128:(i + 1) * 128, :]

    def s5_layer(self, slot, first):
        P = self.P
        nc = self.nc
        with ExitStack() as st:
            self.s5_setup(st, slot)
            win, win_r = P.sb(st, "win", [128, 8, 2 * E], BF16)
            wv = self.ssm_w_in[slot].rearrange("(k p) n -> p k n", p=128)
            for k in range(8):
                for hf in range(2):
                    self.ldc(win[:, k, hf * E:(hf + 1) * E], wv[:, k, hf * E:(hf + 1) * E], w=[win_r])
            gt, gt_r = P.sb(st, "gt", [128, D_MODEL])
            self.ld(gt[:], self.ssm_norm[slot:slot + 1, :].partition_broadcast(128), w=[gt_r])
            hts = [P.sb(st, "ht", [128, D_MODEL]) for _ in range(2)]
            junk, junk_r = P.sb(st, "junk", [128, D_MODEL], BF16)
            sss = [P.sb(st, "ss", [128, 1]) for _ in range(2)]
            xns = [P.sb(st, "xn", [128, D_MODEL], BF16) for _ in range(2)]
            xTs = [P.sb(st, "xT", [128, 8, 512], BF16) for _ in range(2)]
            obufs = [P.sb(st, "obuf", [128, 512], BF16) for _ in range(4)]
            ob_i = 0
            ti = 0
            for tg in range(NGRP):
                tok0, ntok = grp_tok(tg)
                xT, xT_r = xTs[tg % 2]
                for il in range(ntok // 128):
                    i = tok0 // 128 + il
                    ht, ht_r = hts[ti % 2]
                    ss, ss_r = sss[ti % 2]
                    xn, xn_r = xns[ti % 2]
                    ti += 1
                    self.norm_tile(self.h_src(first, i), self.H_r[i], gt, gt_r, ht, ht_r, junk, junk_r, ss, ss_r,
                                   xn, xn_r, xT[:, :, il * 128:(il + 1) * 128], xT_r)
                for fo in range(32):
                    ps, ps_r = self.next_ps()
                    for k in range(8):
                        self.pe(lambda e, k=k, fo=fo, ps=ps, xT=xT, ntok=ntok: e.matmul(
                            ps[:, 0:ntok], lhsT=win[:, k, fo * 128:(fo + 1) * 128], rhs=xT[:, k, 0:ntok],
                            start=(k == 0), stop=(k == 7)), r=[win_r, xT_r], w=[ps_r])
                    ob, ob_r = obufs[ob_i % 4]
                    ob_i += 1
                    if fo < 16:
                        if tg < 8:
                            self.act(lambda e, ob=ob, ps=ps: e.copy(
                                out=ob[:].rearrange("p (s c) -> p s c", s=8),
                                in_=ps[:].rearrange("p (c s) -> p s c", s=8)), r=[ps_r], w=[ob_r])
                            self.ld(self.uT_d[fo, :, tg, :, :], ob[:].rearrange("p (s c) -> p s c", s=8),
                                    r=[ob_r], w=[self.uT_r[fo]])
                        else:
                            self.act(lambda e, ob=ob, ps=ps: e.copy(
                                out=ob[:, 0:128].rearrange("p (s c) -> p s c", s=8),
                                in_=ps[:, 0:128].rearrange("p (c s) -> p s c", s=8)), r=[ps_r], w=[ob_r])
                            self.ld(self.uTs_d[fo], ob[:, 0:128].rearrange("p (s c) -> p s c", s=8),
                                    r=[ob_r], w=[self.uT_r[fo]])
                    else:
                        self.act(lambda e, ob=ob, ps=ps, ntok=ntok: e.activation(
                            out=ob[:, 0:ntok], in_=ps[:, 0:ntok], func=AF.Silu), r=[ps_r], w=[ob_r])
                        self.ld(self.szT_d[fo - 16, :, tok0:tok0 + ntok], ob[:, 0:ntok], r=[ob_r], w=[self.szT_r[fo - 16]])
        P.barrier()
        with ExitStack() as st:
            self.s5_scan(st, slot)
        P.barrier()
        with ExitStack() as st:
            self.s5_out(st, slot, first)

    def s5_setup(self, st0, slot):
        P = self.P
        with ExitStack() as st:
            def T(name, shape, dt=F32):
                return P.sb(st, name, shape, dt)
            ar, ar_r = T("ar", [128, 64])
            ai, ai_r = T("ai", [128, 64])
            ldt, ldt_r = T("ldt", [128, 1])
            self.ld(ar[:], self.ssm_a_re[slot], w=[ar_r])
            self.ld(ai[:], self.ssm_a_im[slot], w=[ai_r])
            self.ld(ldt[:], self.ssm_log_dt[slot], w=[ldt_r])
            dt_, dt_r = T("dt", [128, 1])
            self.act(lambda e: e.activation(out=dt_[:], in_=ldt[:], func=AF.Exp), r=[ldt_r], w=[dt_r])
            mag, mag_r = T("mag", [128, 64])
            self.act(lambda e: e.activation(out=mag[:], in_=ar[:], func=AF.Exp, scale=dt_[:, 0:1]), r=[ar_r, dt_r], w=[mag_r])
            qq, qq_r = T("qq", [128, 2, 64])
            self.dve(lambda e: e.tensor_scalar(out=qq[:, 0, :], in0=ai[:], scalar1=dt_[:, 0:1], scalar2=1.0 / TWO_PI,
                                               op0=ALU.mult, op1=ALU.mult), r=[ai_r, dt_r], w=[qq_r])
            self.dve(lambda e: e.tensor_scalar(out=qq[:, 1, :], in0=qq[:, 0, :], scalar1=0.25, scalar2=None, op0=ALU.add),
                     r=[qq_r], w=[qq_r])
            qi_, qi_r = T("qi", [128, 2, 64], I32)
            qf, qf_r = T("qf", [128, 2, 64])
            self.dve(lambda e: e.tensor_copy(out=qi_[:], in_=qq[:]), r=[qq_r], w=[qi_r])
            self.dve(lambda e: e.tensor_copy(out=qf[:], in_=qi_[:]), r=[qi_r], w=[qf_r])
            self.dve(lambda e: e.tensor_tensor(out=qq[:], in0=qq[:], in1=qf[:], op=ALU.subtract), r=[qq_r, qf_r], w=[qq_r])
            self.dve(lambda e: e.tensor_scalar(out=qq[:], in0=qq[:], scalar1=0.5, scalar2=-0.5, op0=ALU.min, op1=ALU.max),
                     r=[qq_r], w=[qq_r])
            sc, sc_r = T("sc", [128, 2, 64])
            self.act(lambda e: e.activation(out=sc[:], in_=qq[:], func=AF.Sin, scale=TWO_PI), r=[qq_r], w=[sc_r])
            Ap, Ap_r = T("Ap", [128, 9, 2, 64])
            self.pool(lambda e: e.memset(Ap[:, 0, 0, :], 1.0), w=[Ap_r])
            self.pool(lambda e: e.memset(Ap[:, 0, 1, :], 0.0), w=[Ap_r])
            self.dve(lambda e: e.tensor_tensor(out=Ap[:, 1, 0, :], in0=mag[:], in1=sc[:, 1, :], op=ALU.mult), r=[mag_r, sc_r], w=[Ap_r])
            self.dve(lambda e: e.tensor_tensor(out=Ap[:, 1, 1, :], in0=mag[:], in1=sc[:, 0, :], op=ALU.mult), r=[mag_r, sc_r], w=[Ap_r])
            t1, t1_r = T("t1", [128, 64])
            t2, t2_r = T("t2", [128, 64])

            def cmul(o_re, o_im, a_re, a_im, b_re, b_im, rr, ww):
                self.dve(lambda e: e.tensor_tensor(out=t1[:], in0=a_re, in1=b_re, op=ALU.mult), r=rr, w=[t1_r])
                self.dve(lambda e: e.tensor_tensor(out=t2[:], in0=a_im, in1=b_im, op=ALU.mult), r=rr, w=[t2_r])
                self.dve(lambda e: e.tensor_tensor(out=o_re, in0=t1[:], in1=t2[:], op=ALU.subtract), r=[t1_r, t2_r], w=ww)
                self.dve(lambda e: e.tensor_tensor(out=t1[:], in0=a_re, in1=b_im, op=ALU.mult), r=rr, w=[t1_r])
                self.dve(lambda e: e.tensor_tensor(out=t2[:], in0=a_im, in1=b_re, op=ALU.mult), r=rr, w=[t2_r])
                self.dve(lambda e: e.tensor_tensor(out=o_im, in0=t1[:], in1=t2[:], op=ALU.add), r=[t1_r, t2_r], w=ww)

            for tau in range(2, 9):
                cmul(Ap[:, tau, 0, :], Ap[:, tau, 1, :], Ap[:, tau - 1, 0, :], Ap[:, tau - 1, 1, :],
                     Ap[:, 1, 0, :], Ap[:, 1, 1, :], [Ap_r], [Ap_r])
            KC, KC_r = T("KC", [128, 64, 10, 3])
            self.dve(lambda e: e.tensor_copy(out=KC[:, :, 0, 0], in_=Ap[:, 8, 0, :]), r=[Ap_r], w=[KC_r])
            self.dve(lambda e: e.tensor_copy(out=KC[:, :, 0, 1], in_=Ap[:, 8, 1, :]), r=[Ap_r], w=[KC_r])
            for k in range(1, 10):
                cmul(KC[:, :, k, 0], KC[:, :, k, 1], KC[:, :, k - 1, 0], KC[:, :, k - 1, 1],
                     KC[:, :, k - 1, 0], KC[:, :, k - 1, 1], [KC_r], [KC_r])
            self.dve(lambda e: e.tensor_scalar(out=KC[:, :, :, 2], in0=KC[:, :, :, 1], scalar1=-1.0, scalar2=None, op0=ALU.mult),
                     r=[KC_r], w=[KC_r])
            self.ld(self.KCd, KC[:].rearrange("g p k c -> g p (k c)"), r=[KC_r], w=[self.KCd_r])
            den, den_r = T("den", [128, 64])
            self.dve(lambda e: e.tensor_tensor(out=den[:], in0=ar[:], in1=ar[:], op=ALU.mult), r=[ar_r], w=[den_r])
            self.dve(lambda e: e.tensor_tensor(out=t1[:], in0=ai[:], in1=ai[:], op=ALU.mult), r=[ai_r], w=[t1_r])
            self.dve(lambda e: e.tensor_tensor(out=den[:], in0=den[:], in1=t1[:], op=ALU.add), r=[den_r, t1_r], w=[den_r])
            self.dve(lambda e: e.reciprocal(out=den[:], in_=den[:]), r=[den_r], w=[den_r])
            zr, zr_r = T("zr", [128, 64])
            self.dve(lambda e: e.tensor_scalar(out=zr[:], in0=Ap[:, 1, 0, :], scalar1=-1.0, scalar2=None, op0=ALU.add), r=[Ap_r], w=[zr_r])
            Ff, Ff_r = T("Ff", [128, 2, 64])
            self.dve(lambda e: e.tensor_tensor(out=t1[:], in0=zr[:], in1=ar[:], op=ALU.mult), r=[zr_r, ar_r], w=[t1_r])
            self.dve(lambda e: e.tensor_tensor(out=t2[:], in0=Ap[:, 1, 1, :], in1=ai[:], op=ALU.mult), r=[Ap_r, ai_r], w=[t2_r])
            self.dve(lambda e: e.tensor_tensor(out=t1[:], in0=t1[:], in1=t2[:], op=ALU.add), r=[t1_r, t2_r], w=[t1_r])
            self.dve(lambda e: e.tensor_tensor(out=Ff[:, 0, :], in0=t1[:], in1=den[:], op=ALU.mult), r=[t1_r, den_r], w=[Ff_r])
            self.dve(lambda e: e.tensor_tensor(out=t1[:], in0=Ap[:, 1, 1, :], in1=ar[:], op=ALU.mult), r=[Ap_r, ar_r], w=[t1_r])
            self.dve(lambda e: e.tensor_tensor(out=t2[:], in0=zr[:], in1=ai[:], op=ALU.mult), r=[zr_r, ai_r], w=[t2_r])
            self.dve(lambda e: e.tensor_tensor(out=t1[:], in0=t1[:], in1=t2[:], op=ALU.subtract), r=[t1_r, t2_r], w=[t1_r])
            self.dve(lambda e: e.tensor_tensor(out=Ff[:, 1, :], in0=t1[:], in1=den[:], op=ALU.mult), r=[t1_r, den_r], w=[Ff_r])
            br, br_r = T("br", [128, 64, 16])
            bi, bi_r = T("bi", [128, 64, 16])
            self.ld(br[:], self.ssm_b_re[slot], w=[br_r])
            self.ld(bi[:], self.ssm_b_im[slot], w=[bi_r])
            brT = br[:].rearrange("g p j -> g j p")
            biT = bi[:].rearrange("g p j -> g j p")
            EE = [T("EE", [128, 16, 2, 64]) for _ in range(2)]
            u1, u1_r = T("u1", [128, 16, 64])
            u2, u2_r = T("u2", [128, 16, 64])

            def cmul_b(o, o_r, x_re, x_im, xr, a_re, a_im, a_r):
                ab_re = bc(a_re, 1, 16)
                ab_im = bc(a_im, 1, 16)
                self.dve(lambda e: e.tensor_tensor(out=u1[:], in0=x_re, in1=ab_re, op=ALU.mult), r=xr + a_r, w=[u1_r])
                self.pool(lambda e: e.tensor_tensor(out=u2[:], in0=x_im, in1=ab_im, op=ALU.mult), r=xr + a_r, w=[u2_r])
                self.dve(lambda e: e.tensor_tensor(out=o[:, :, 0, :], in0=u1[:], in1=u2[:], op=ALU.subtract), r=[u1_r, u2_r], w=[o_r])
                self.dve(lambda e: e.tensor_tensor(out=u1[:], in0=x_re, in1=ab_im, op=ALU.mult), r=xr + a_r, w=[u1_r])
                self.pool(lambda e: e.tensor_tensor(out=u2[:], in0=x_im, in1=ab_re, op=ALU.mult), r=xr + a_r, w=[u2_r])
                self.dve(lambda e: e.tensor_tensor(out=o[:, :, 1, :], in0=u1[:], in1=u2[:], op=ALU.add), r=[u1_r, u2_r], w=[o_r])

            CC, CC_r = T("CC", [128, 16, 2, 64])
            self.ld(CC[:, :, 0, :], self.ssm_c_re[slot], w=[CC_r])
            self.ld(CC[:, :, 1, :], self.ssm_c_im[slot], w=[CC_r])
            self.dve(lambda e: e.tensor_scalar(out=CC[:, :, 1, :], in0=CC[:, :, 1, :], scalar1=-1.0, scalar2=None, op0=ALU.mult),
                     r=[CC_r], w=[CC_r])
            Kg, Kg_r = T("Kg", [128, 8, 16, 16])
            tm = [T("tm", [128, 16, 128]) for _ in range(2)]
            E0, E0_r = EE[0]
            cmul_b(E0, E0_r, brT, biT, [br_r, bi_r], Ff[:, 0, :], Ff[:, 1, :], [Ff_r])
            for tau in range(8):
                Ec, Ec_r = EE[tau % 2]
                if tau > 0:
                    Epv, Epv_r = EE[(tau - 1) % 2]
                    cmul_b(Ec, Ec_r, Epv[:, :, 0, :], Epv[:, :, 1, :], [Epv_r], Ap[:, 1, 0, :], Ap[:, 1, 1, :], [Ap_r])
                self.ld(self.Wd[2 * (7 - tau):2 * (7 - tau) + 2].rearrange("ri g j p -> g j ri p"), Ec[:], r=[Ec_r], w=[self.Wd_r])
                for j in range(16):
                    tmj, tmj_r = tm[j % 2]
                    eb = bc(Ec[:, j, :, :].rearrange("g r p -> g (r p)"), 1, 16)
                    self.pool(lambda e, tmj=tmj, eb=eb: e.tensor_tensor(out=tmj[:], in0=CC[:].rearrange("g i r p -> g i (r p)"),
                                                                        in1=eb, op=ALU.mult), r=[CC_r, Ec_r], w=[tmj_r])
                    self.dve(lambda e, tmj=tmj, tau=tau, j=j: e.tensor_reduce(out=Kg[:, tau, j, :], in_=tmj[:], axis=AX.X, op=ALU.add),
                             r=[tmj_r], w=[Kg_r])
            self.ld(self.Kd.rearrange("t g j i -> g t (j i)"), Kg[:].rearrange("g t j i -> g t (j i)"), r=[Kg_r], w=[self.Kd_r])
            VV = [T("VV", [128, 2, 64, 16]) for _ in range(2)]
            CrT = CC[:, :, 0, :].rearrange("g i p -> g p i")
            nCiT = CC[:, :, 1, :].rearrange("g i p -> g p i")
            w1, w1_r = T("w1", [128, 64, 16])
            w2, w2_r = T("w2", [128, 64, 16])
            for t in range(8):
                Vc, Vc_r = VV[t % 2]
                are = bc(Ap[:, t + 1, 0, :], 2, 16)
                aim = bc(Ap[:, t + 1, 1, :], 2, 16)
                self.dve(lambda e, are=are: e.tensor_tensor(out=w1[:], in0=CrT, in1=are, op=ALU.mult), r=[CC_r, Ap_r], w=[w1_r])
                self.pool(lambda e, aim=aim: e.tensor_tensor(out=w2[:], in0=nCiT, in1=aim, op=ALU.mult), r=[CC_r, Ap_r], w=[w2_r])
                self.dve(lambda e, Vc=Vc: e.tensor_tensor(out=Vc[:, 0, :, :], in0=w1[:], in1=w2[:], op=ALU.add), r=[w1_r, w2_r], w=[Vc_r])
                self.dve(lambda e, aim=aim: e.tensor_tensor(out=w1[:], in0=CrT, in1=aim, op=ALU.mult), r=[CC_r, Ap_r], w=[w1_r])
                self.pool(lambda e, are=are: e.tensor_tensor(out=w2[:], in0=nCiT, in1=are, op=ALU.mult), r=[CC_r, Ap_r], w=[w2_r])
                self.dve(lambda e, Vc=Vc: e.tensor_tensor(out=Vc[:, 1, :, :], in0=w2[:], in1=w1[:], op=ALU.subtract), r=[w1_r, w2_r], w=[Vc_r])
                self.ld(self.Vd[2 * t:2 * t + 2].rearrange("r g p i -> g r (p i)"), Vc[:].rearrange("g r p i -> g r (p i)"),
                        r=[Vc_r], w=[self.Vd_r])
            P.barrier()

    def s5_scan(self, st, slot):
        P = self.P

        def T(name, shape, dt=F32):
            return P.sb(st, name, shape, dt)
        Kt, Kt_r = T("Kt", [128, 16, 8, 128], BF16)
        self.pool(lambda e: e.memset(Kt[:], 0.0), w=[Kt_r])
        for g8 in range(8):
            for tau in range(8):
                self.ldc(Kt[16 * g8:16 * g8 + 16, :, tau, 16 * g8:16 * g8 + 16],
                         self.Kd[tau].rearrange("(f g) j i -> g j f i", g=8)[g8], r=[self.Kd_r], w=[Kt_r])
        KC, KC_r = T("KCs", [128, 64, 30])
        self.ld(KC[:], self.KCd.rearrange("(q g) p k -> (g p) q k", g=2), r=[self.KCd_r], w=[KC_r])
        Dk, Dk_r = T("Dk", [128, 16])
        self.P.dma("sp", lambda e: e.dma_start(out=Dk[:], in_=self.ssm_d[slot].rearrange("(f p) -> p f", p=128),
                                              allow_slow_non_contiguous=True), (), [Dk_r])
        Wts = [T("Wt", [128, 16, 128], BF16) for _ in range(2)]
        Vts = [T("Vt", [128, 4, 16, 32], BF16) for _ in range(2)]
        for (t_, r_) in Wts + Vts:
            self.pool(lambda e, t_=t_: e.memset(t_[:], 0.0), w=[r_])
        uTfs = [T("uTf", [128, 8, 8, 64], BF16) for _ in range(2)]
        uTss = [T("uTs", [128, 8, N_S], BF16) for _ in range(2)]
        XA = [[T("XA", [128, 513]) for _ in range(2)] for _ in range(4)]
        XB = [[T("XB", [128, 513]) for _ in range(2)] for _ in range(4)]
        Xb = [[T("Xb", [128, 512], BF16) for _ in range(2)] for _ in range(4)]
        XS, XS_r = T("XS", [128, 4, 2, N_S, 2])
        XSb, XSb_r = T("XSb", [128, 4, 2, N_S], BF16)
        tS = [T("tS", [128, N_S]) for _ in range(2)]
        gbufs = [T("gbuf", [128, TOK], BF16) for _ in range(2)]
        ytmps = [T("ytmp", [128, 512]) for _ in range(2)]
        Fin, Fin_r = T("Fin", [128, 2, 64])
        FinS, FinS_r = T("FinS", [128, 2, 64, N_S])
        s0s = [[T("s0", [N_S, 512]) for _ in range(2)] for _ in range(2)]
        for q in range(4):
            for ri in range(2):
                self.pool(lambda e, q=q, ri=ri: e.memset(XA[q][ri][0][:, 0:1], 0.0), w=[XA[q][ri][1]])
        yi = 0
        for f in range(16):
            Wt, Wt_r = Wts[f % 2]
            Vt, Vt_r = Vts[f % 2]
            uTf, uTf_r = uTfs[f % 2]
            uTs, uTs_r = uTss[f % 2]
            gbuf, gbuf_r = gbufs[f % 2]
            self.ld(uTf[:].rearrange("p a s c -> p (a s c)"), self.uT_d[f].rearrange("p a s c -> p (a s c)"),
                    r=[self.uT_r[f]], w=[uTf_r])
            self.ld(uTs[:], self.uTs_d[f], r=[self.uT_r[f]], w=[uTs_r])
            s0 = s0s[f % 2]
            self.ld(s0[0][0][:], self.st_re[slot][:, f * 512:(f + 1) * 512], w=[s0[0][1]])
            self.ld(s0[1][0][:], self.st_im[slot][:, f * 512:(f + 1) * 512], w=[s0[1][1]])
            for g8 in range(8):
                g = 8 * f + g8
                self.ldc(Wt[16 * g8:16 * g8 + 16, :, 64 * (g8 % 2):64 * (g8 % 2) + 64],
                         self.Wd[:, g, :, :].rearrange("sr j p -> j sr p"), r=[self.Wd_r], w=[Wt_r])
            for g2 in range(2):
                for q in range(4):
                    self.ldc(Vt[64 * g2:64 * g2 + 64, q, :, 16 * g2:16 * g2 + 16],
                             self.Vd[:, 8 * f + 2 * q + g2, :, :].rearrange("tr p i -> p tr i"),
                             r=[self.Vd_r], w=[Vt_r])
            for q in range(4):
                qq = 4 * f + q
                for ri in range(2):
                    ps, ps_r = self.next_ps()
                    for s in range(8):
                        self.pe(lambda e, ps=ps, q=q, ri=ri, s=s, Wt=Wt, uTf=uTf: e.matmul(
                            ps[:].rearrange("p (a c) -> p a c", a=8), lhsT=Wt[32 * q:32 * q + 32, 2 * s + ri, :],
                            rhs=uTf[32 * q:32 * q + 32, :, s, :], start=(s == 0), stop=(s == 7),
                            tile_position=(32 * q, 0)), r=[Wt_r, uTf_r], w=[ps_r])
                    xa, xa_r = XA[q][ri]
                    self.act(lambda e, xa=xa, ps=ps: e.copy(out=xa[:, 1:513], in_=ps[:]), r=[ps_r], w=[xa_r])
                    ps, ps_r = self.next_ps()
                    for s in range(8):
                        self.pe(lambda e, ps=ps, q=q, ri=ri, s=s, Wt=Wt, uTs=uTs: e.matmul(
                            ps[:, 0:N_S], lhsT=Wt[32 * q:32 * q + 32, 2 * s + ri, :],
                            rhs=uTs[32 * q:32 * q + 32, s, :], start=(s == 0), stop=(s == 7),
                            tile_position=(32 * q, 0)), r=[Wt_r, uTs_r], w=[ps_r])
                    s0t, s0_r = s0[ri]
                    self.pe(lambda e, ps=ps, s0t=s0t, q=q: e.transpose(
                        out=ps[:, 32:32 + N_S], in_=s0t[:, q * 128:(q + 1) * 128], identity=self.ident_f[0:N_S, 0:N_S]),
                        r=[s0_r, self.ident_f_r], w=[ps_r])
                    self.act(lambda e, ps=ps, q=q, ri=ri: e.copy(out=XS[:, q, ri, :, 1], in_=ps[:, 0:N_S]), r=[ps_r], w=[XS_r])
                    self.act(lambda e, ps=ps, q=q, ri=ri: e.copy(out=XS[:, q, ri, :, 0], in_=ps[:, 32:32 + N_S]), r=[ps_r], w=[XS_r])
            for q in range(4):
                qq = 4 * f + q
                cur = XA[q]
                nxt = XB[q]
                for k in range(10):
                    d = 1 << k
                    n = 513 - d
                    cr = KC[:, qq, 3 * k:3 * k + 1]
                    ci = KC[:, qq, 3 * k + 1:3 * k + 2]
                    nci = KC[:, qq, 3 * k + 2:3 * k + 3]
                    (c_re, c_re_r), (c_im, c_im_r) = cur
                    (n_re, n_re_r), (n_im, n_im_r) = nxt
                    self.dve(lambda e, c_re=c_re, n_re=n_re, cr=cr, d=d, n=n: e.scalar_tensor_tensor(
                        out=n_re[:, d:513], in0=c_re[:, 0:n], scalar=cr, in1=c_re[:, d:513], op0=ALU.mult, op1=ALU.add),
                        r=[c_re_r, KC_r], w=[n_re_r])
                    self.dve(lambda e, c_im=c_im, n_re=n_re, nci=nci, d=d, n=n: e.scalar_tensor_tensor(
                        out=n_re[:, d:513], in0=c_im[:, 0:n], scalar=nci, in1=n_re[:, d:513], op0=ALU.mult, op1=ALU.add),
                        r=[c_im_r, n_re_r, KC_r], w=[n_re_r])
                    self.dve(lambda e, c_im=c_im, n_im=n_im, cr=cr, d=d, n=n: e.scalar_tensor_tensor(
                        out=n_im[:, d:513], in0=c_im[:, 0:n], scalar=cr, in1=c_im[:, d:513], op0=ALU.mult, op1=ALU.add),
                        r=[c_im_r, KC_r], w=[n_im_r])
                    self.dve(lambda e, c_re=c_re, n_im=n_im, ci=ci, d=d, n=n: e.scalar_tensor_tensor(
                        out=n_im[:, d:513], in0=c_re[:, 0:n], scalar=ci, in1=n_im[:, d:513], op0=ALU.mult, op1=ALU.add),
                        r=[c_re_r, n_im_r, KC_r], w=[n_im_r])
                    self.pool(lambda e, c_re=c_re, n_re=n_re, d=d: e.tensor_copy(out=n_re[:, 0:d], in_=c_re[:, 0:d]),
                              r=[c_re_r], w=[n_re_r])
                    self.pool(lambda e, c_im=c_im, n_im=n_im, d=d: e.tensor_copy(out=n_im[:, 0:d], in_=c_im[:, 0:d]),
                              r=[c_im_r], w=[n_im_r])
                    cur, nxt = nxt, cur
                for ri in range(2):
                    xa, xa_r = cur[ri]
                    xb_, xb_r = Xb[q][ri]
                    self.act(lambda e, xa=xa, xb_=xb_: e.copy(out=xb_[:], in_=xa[:, 0:512]), r=[xa_r], w=[xb_r])
                    self.pool(lambda e, xa=xa, ri=ri, qq=qq: e.tensor_copy(out=Fin[:, ri, qq:qq + 1], in_=xa[:, 512:513]),
                              r=[xa_r], w=[Fin_r])
                cr = KC[:, qq, 0:1]
                ci = KC[:, qq, 1:2]
                nci = KC[:, qq, 2:3]
                (ta, ta_r), (tb, tb_r) = tS
                self.dve(lambda e, q=q, cr=cr: e.scalar_tensor_tensor(out=ta[:], in0=XS[:, q, 0, :, 0], scalar=cr, in1=XS[:, q, 0, :, 1],
                                                                      op0=ALU.mult, op1=ALU.add), r=[XS_r, KC_r], w=[ta_r])
                self.dve(lambda e, q=q, cr=cr: e.scalar_tensor_tensor(out=tb[:], in0=XS[:, q, 1, :, 0], scalar=cr, in1=XS[:, q, 1, :, 1],
                                                                      op0=ALU.mult, op1=ALU.add), r=[XS_r, KC_r], w=[tb_r])
                self.dve(lambda e, q=q, nci=nci, qq=qq: e.scalar_tensor_tensor(out=FinS[:, 0, qq, :], in0=XS[:, q, 1, :, 0], scalar=nci, in1=ta[:],
                                                                              op0=ALU.mult, op1=ALU.add), r=[XS_r, KC_r, ta_r], w=[FinS_r])
                self.dve(lambda e, q=q, ci=ci, qq=qq: e.scalar_tensor_tensor(out=FinS[:, 1, qq, :], in0=XS[:, q, 0, :, 0], scalar=ci, in1=tb[:],
                                                                             op0=ALU.mult, op1=ALU.add), r=[XS_r, KC_r, tb_r], w=[FinS_r])
                self.act(lambda e, q=q: e.copy(out=XSb[:, q, :, :], in_=XS[:, q, :, :, 0]), r=[XS_r], w=[XSb_r])
            for t in range(8):
                for smp in range(2):
                    ps, ps_r = self.next_ps()
                    nn = 512 if smp == 0 else N_S
                    first_mm = True
                    for s in range(t + 1):
                        rhs = uTf[:, :, s, :] if smp == 0 else uTs[:, s, :]
                        outp = ps[:].rearrange("p (a c) -> p a c", a=8) if smp == 0 else ps[:, 0:N_S]
                        self.pe(lambda e, outp=outp, rhs=rhs, t=t, s=s, f=f, fm=first_mm: e.matmul(
                            outp, lhsT=Kt[:, f, t - s, :], rhs=rhs, start=fm, stop=False),
                            r=[Kt_r, uTf_r, uTs_r], w=[ps_r])
                        first_mm = False
                    for q in range(4):
                        for ri in range(2):
                            rhs = Xb[q][ri][0][:, 0:512] if smp == 0 else XSb[:, q, ri, :]
                            rr = Xb[q][ri][1] if smp == 0 else XSb_r
                            last = (q == 3 and ri == 1)
                            self.pe(lambda e, ps=ps, rhs=rhs, q=q, ri=ri, t=t, Vt=Vt, nn=nn, last=last: e.matmul(
                                ps[32 * q:32 * q + 32, 0:nn], lhsT=Vt[:, q, 2 * t + ri, :], rhs=rhs, start=False, stop=last,
                                tile_position=(0, 32 * q)), r=[Vt_r, rr], w=[ps_r])
                    yt, yt_r = ytmps[yi % 2]
                    yi += 1
                    if smp == 0:
                        self.dve(lambda e, yt=yt, ps=ps, t=t, f=f, uTf=uTf: e.scalar_tensor_tensor(
                            out=yt[:].rearrange("p (a c) -> p a c", a=8), in0=uTf[:, :, t, :], scalar=Dk[:, f:f + 1],
                            in1=ps[:].rearrange("p (a c) -> p a c", a=8), op0=ALU.mult, op1=ALU.add),
                            r=[uTf_r, Dk_r, ps_r], w=[yt_r])
                        self.act(lambda e, yt=yt, gbuf=gbuf, t=t: e.activation(
                            out=gbuf[:, 0:T_P].rearrange("p (a c s) -> p a c s", a=8, s=8)[:, :, :, t],
                            in_=yt[:].rearrange("p (a c) -> p a c", a=8), func=AF.Gelu_apprx_tanh), r=[yt_r], w=[gbuf_r])
                    else:
                        self.dve(lambda e, yt=yt, ps=ps, t=t, f=f, uTs=uTs: e.scalar_tensor_tensor(
                            out=yt[:, 0:N_S], in0=uTs[:, t, :], scalar=Dk[:, f:f + 1], in1=ps[:, 0:N_S],
                            op0=ALU.mult, op1=ALU.add), r=[uTs_r, Dk_r, ps_r], w=[yt_r])
                        self.act(lambda e, yt=yt, gbuf=gbuf, t=t: e.activation(
                            out=gbuf[:, T_P:TOK].rearrange("p (n s) -> p n s", s=8)[:, :, t],
                            in_=yt[:, 0:N_S], func=AF.Gelu_apprx_tanh), r=[yt_r], w=[gbuf_r])
            self.ld(self.gT_d[f], gbuf[:], r=[gbuf_r], w=[self.gT_r[f]])
        for ri in range(2):
            ps, ps_r = self.next_ps()
            self.pe(lambda e, ps=ps, ri=ri: e.transpose(out=ps[0:64, 0:128], in_=Fin[:, ri, :], identity=self.ident_f[:]),
                    r=[Fin_r, self.ident_f_r], w=[ps_r])
            fo_, fo_r = T("fo", [64, 128])
            self.act(lambda e, ps=ps, fo_=fo_: e.copy(out=fo_[:], in_=ps[0:64, 0:128]), r=[ps_r], w=[fo_r])
            self.store((self.o_sre_p if ri == 0 else self.o_sim_p)[slot], fo_[:], r=[fo_r])
            fs_, fs_r = T("fs", [64, N_S, 128])
            for n in range(N_S):
                ps, ps_r = self.next_ps()
                self.pe(lambda e, ps=ps, ri=ri, n=n: e.transpose(out=ps[0:64, 0:128], in_=FinS[:, ri, :, n], identity=self.ident_f[:]),
                        r=[FinS_r, self.ident_f_r], w=[ps_r])
                self.act(lambda e, ps=ps, fs_=fs_, n=n: e.copy(out=fs_[:, n, :], in_=ps[0:64, 0:128]), r=[ps_r], w=[fs_r])
            self.store((self.o_sre_s if ri == 0 else self.o_sim_s)[slot].rearrange("n q c -> q n c"), fs_[:], r=[fs_r])

    def s5_out(self, st, slot, first):
        P = self.P

        def T(name, shape, dt=F32):
            return P.sb(st, name, shape, dt)
        wg, wg_r = T("wglu", [128, 16, E], BF16)
        wgv = self.ssm_w_glu[slot].rearrange("(k p) n -> p k n", p=128)
        for k in range(16):
            self.ldc(wg[:, k, :], wgv[:, k, :], w=[wg_r])
        wo, wo_r = T("wout", [128, 16, D_MODEL], BF16)
        wov = self.ssm_w_out[slot].rearrange("(k p) n -> p k n", p=128)
        for k in range(0, 16, 2):
            self.ldc(wo[:, k:k + 2, :], wov[:, k:k + 2, :], w=[wo_r])
        bg, bg_r = T("bglu", [128, 16])
        self.P.dma("sp", lambda e: e.dma_start(out=bg[:], in_=self.ssm_b_glu[slot].rearrange("(f p) -> p f", p=128),
                                              allow_slow_non_contiguous=True), (), [bg_r])
        gin, gin_r = T("gin", [128, 16, 512], BF16)
        szin, szin_r = T("szin", [128, 16, 512], BF16)
        g2T, g2T_r = T("g2T", [128, 16, 512], BF16)
        sig = [T("sig", [128, 512]) for _ in range(2)]
        hbs = [T("hb", [128, D_MODEL]) for _ in range(2)]
        hi = 0
        for tg in range(NGRP):
            tok0, ntok = grp_tok(tg)
            self.ld(gin[:, :, 0:ntok], self.gT_d[:, :, tok0:tok0 + ntok].rearrange("f p t -> p f t"), r=self.gT_r, w=[gin_r])
            self.ld(szin[:, :, 0:ntok], self.szT_d[:, :, tok0:tok0 + ntok].rearrange("f p t -> p f t"), r=self.szT_r, w=[szin_r])
            for fo in range(16):
                ps, ps_r = self.next_ps()
                for f in range(16):
                    self.pe(lambda e, ps=ps, f=f, fo=fo, ntok=ntok: e.matmul(
                        ps[:, 0:ntok], lhsT=wg[:, f, fo * 128:(fo + 1) * 128], rhs=gin[:, f, 0:ntok],
                        start=(f == 0), stop=(f == 15)), r=[wg_r, gin_r], w=[ps_r])
                sg, sg_r = sig[fo % 2]
                self.act(lambda e, ps=ps, sg=sg, fo=fo, ntok=ntok: e.activation(
                    out=sg[:, 0:ntok], in_=ps[:, 0:ntok], func=AF.Sigmoid, bias=bg[:, fo:fo + 1]), r=[ps_r, bg_r], w=[sg_r])
                self.dve(lambda e, sg=sg, fo=fo, ntok=ntok: e.tensor_tensor(
                    out=sg[:, 0:ntok], in0=sg[:, 0:ntok], in1=gin[:, fo, 0:ntok], op=ALU.mult), r=[sg_r, gin_r], w=[sg_r])
                self.pool(lambda e, sg=sg, fo=fo, ntok=ntok: e.tensor_tensor(
                    out=g2T[:, fo, 0:ntok], in0=sg[:, 0:ntok], in1=szin[:, fo, 0:ntok], op=ALU.mult), r=[sg_r, szin_r], w=[g2T_r])
            for il in range(ntok // 128):
                i = tok0 // 128 + il
                hb, hb_r = hbs[hi % 2]
                hi += 1
                self.ld(hb[:], self.h_src(first, i), r=[self.H_r[i]], w=[hb_r])
                for hh in range(2):
                    ps, ps_r = self.next_ps()
                    for f in range(16):
                        self.pe(lambda e, ps=ps, f=f, hh=hh, il=il: e.matmul(
                            ps[:], lhsT=g2T[:, f, il * 128:(il + 1) * 128], rhs=wo[:, f, hh * 512:(hh + 1) * 512],
                            start=(f == 0), stop=(f == 15)), r=[wo_r, g2T_r], w=[ps_r])
                    self.dve(lambda e, ps=ps, hb=hb, hh=hh: e.tensor_tensor(
                        out=hb[:, hh * 512:(hh + 1) * 512], in0=ps[:], in1=hb[:, hh * 512:(hh + 1) * 512], op=ALU.add),
                        r=[ps_r, hb_r], w=[hb_r])
                self.ld(self.H[i * 128:(i + 1) * 128, :], hb[:], r=[hb_r], w=[self.H_r[i]])

    def dsa_layer(self):
        P = self.P
        with ExitStack() as st:
            self.dsa_proj(st)
        P.barrier()
        if self.dsa_stage >= 2:
            with ExitStack() as st:
                self.dsa_prompt(st)
            P.barrier()
        if self.dsa_stage >= 3:
            with ExitStack() as st:
                self.dsa_sample(st)
            P.barrier()

    def dsa_proj(self, st):
        P = self.P

        def T(name, shape, dt=F32):
            return P.sb(st, name, shape, dt)
        win, win_r = T("awin", [128, 8, ATTN_IN], BF16)
        wv = self.attn_w_in.rearrange("(k p) n -> p k n", p=128)
        for k in range(8):
            self.ldc(win[:, k, :], wv[:, k, :], w=[win_r])
        gt, gt_r = T("agt", [128, D_MODEL])
        self.ld(gt[:], self.attn_norm.partition_broadcast(128), w=[gt_r])
        ht, ht_r = T("aht", [128, D_MODEL])
        junk, junk_r = T("ajunk", [128, D_MODEL], BF16)
        ss, ss_r = T("ass", [128, 1])
        xn, xn_r = T("axn", [128, D_MODEL], BF16)
        xTs = [T("axT", [128, 8, 128], BF16) for _ in range(2)]
        prs = [T("pr", [128, ATTN_IN]) for _ in range(2)]
        rps = [T("rp", [128, 2880]) for _ in range(2)]
        szbs = [T("szb", [128, E], BF16) for _ in range(2)]
        vxbs = [T("vxb", [128, 4, 65], BF16) for _ in range(2)]
        for (t_, r_) in vxbs:
            self.pool(lambda e, t_=t_: e.memset(t_[:, :, 64:65], 1.0), w=[r_])
        r1, r1_r = T("r1", [128, 36, 32])
        r2, r2_r = T("r2", [128, 36, 32])
        css = [T("cs", [128, 64]) for _ in range(2)]
        wibs = [T("wib", [128, 8]) for _ in range(2)]
        kks = [T("kk", [128, 5, 2, 64]) for _ in range(2)]
        TSs = [T("TS", [128, 25, 128], BF16) for _ in range(2)]
        TSs_s, TSs_s_r = T("TSsmp", [64, 40, 128], BF16)
        COLS = [(c0, min(512, ATTN_IN - c0)) for c0 in range(0, ATTN_IN, 512)]
        for i in range(NTILE):
            xT, xT_r = xTs[i % 2]
            pr, pr_r = prs[i % 2]
            rp, rp_r = rps[i % 2]
            szb, szb_r = szbs[i % 2]
            vxb, vxb_r = vxbs[i % 2]
            cs, cs_r = css[i % 2]
            wib, wib_r = wibs[i % 2]
            kk, kk_r = kks[i % 2]
            TS, TS_r = TSs[i % 2]
            tok0 = i * 128
            smp = (i == NTILE - 1)
            self.norm_tile(self.H[tok0:tok0 + 128, :], self.H_r[i], gt, gt_r, ht, ht_r, junk, junk_r, ss, ss_r,
                           xn, xn_r, xT[:], xT_r)
            self.ld(cs[:], self.rope_cs[tok0:tok0 + 128, :], w=[cs_r])
            for ci, (c0, w) in enumerate(COLS):
                ps, ps_r = self.next_ps()
                for k in range(8):
                    self.pe(lambda e, ps=ps, k=k, c0=c0, w=w, xT=xT: e.matmul(
                        ps[:, 0:w], lhsT=xT[:, k, :], rhs=win[:, k, c0:c0 + w], start=(k == 0), stop=(k == 7)),
                        r=[win_r, xT_r], w=[ps_r])
                if 5 <= ci <= 8:
                    self.act(lambda e, ps=ps, szb=szb, ci=ci: e.activation(
                        out=szb[:, (ci - 5) * 512:(ci - 4) * 512], in_=ps[:], func=AF.Silu), r=[ps_r], w=[szb_r])
                else:
                    self.act(lambda e, ps=ps, pr=pr, c0=c0, w=w: e.copy(out=pr[:, c0:c0 + w], in_=ps[:, 0:w]), r=[ps_r], w=[pr_r])
            self.ld(self.sz_d[tok0:tok0 + 128, :], szb[:], r=[szb_r], w=[self.dsa_r])
            for (H, s0_, d0_) in ((36, 0, 0), (9, 4608, 2304)):
                src = pr[:, s0_:s0_ + H * 64].rearrange("p (h c d) -> p h c d", c=2, d=32)
                dst = rp[:, d0_:d0_ + H * 64].rearrange("p (h c d) -> p h c d", c=2, d=32)
                cosb = bc(cs[:, 0:32], 1, H)
                sinb = bc(cs[:, 32:64], 1, H)
                x1 = src[:, :, 0, :]
                x2 = src[:, :, 1, :]
                self.dve(lambda e, x1=x1, cosb=cosb, H=H: e.tensor_tensor(out=r1[:, 0:H, :], in0=x1, in1=cosb, op=ALU.mult), r=[pr_r, cs_r], w=[r1_r])
                self.pool(lambda e, x2=x2, sinb=sinb, H=H: e.tensor_tensor(out=r2[:, 0:H, :], in0=x2, in1=sinb, op=ALU.mult), r=[pr_r, cs_r], w=[r2_r])
                self.dve(lambda e, dst=dst, H=H: e.tensor_tensor(out=dst[:, :, 0, :], in0=r1[:, 0:H, :], in1=r2[:, 0:H, :], op=ALU.subtract),
                         r=[r1_r, r2_r], w=[rp_r])
                self.dve(lambda e, x2=x2, cosb=cosb, H=H: e.tensor_tensor(out=r1[:, 0:H, :], in0=x2, in1=cosb, op=ALU.mult), r=[pr_r, cs_r], w=[r1_r])
                self.pool(lambda e, x1=x1, sinb=sinb, H=H: e.tensor_tensor(out=r2[:, 0:H, :], in0=x1, in1=sinb, op=ALU.mult), r=[pr_r, cs_r], w=[r2_r])
                self.dve(lambda e, dst=dst, H=H: e.tensor_tensor(out=dst[:, :, 1, :], in0=r1[:, 0:H, :], in1=r2[:, 0:H, :], op=ALU.add),
                         r=[r1_r, r2_r], w=[rp_r])
            if not smp:
                self.store(self.o_k_p[tok0:tok0 + 128, :], rp[:, 2048:2304], r=[rp_r])
                self.store(self.o_v_p[tok0:tok0 + 128, :], pr[:, 2304:2560], r=[pr_r])
                self.store(self.o_kidx_p[tok0:tok0 + 128, :], rp[:, 2816:2880], r=[rp_r])
            else:
                self.store(self.o_k_s[:, :], rp[:, 2048:2304], r=[rp_r])
                self.store(self.o_v_s[:, :], pr[:, 2304:2560], r=[pr_r])
                self.store(self.o_kidx_s[:, :], rp[:, 2816:2880], r=[rp_r])
            self.dve(lambda e, wib=wib, pr=pr: e.tensor_scalar(out=wib[:], in0=pr[:, 5184:5192], scalar1=8.0 ** -1.5, scalar2=None, op0=ALU.mult),
                     r=[pr_r], w=[wib_r])
            self.ld(self.wi_d[tok0:tok0 + 128, :], wib[:], r=[wib_r], w=[self.dsa_r])
            self.pool(lambda e, vxb=vxb, pr=pr: e.tensor_copy(out=vxb[:, :, 0:64], in_=pr[:, 2304:2560].rearrange("p (g d) -> p g d", g=4)),
                      r=[pr_r], w=[vxb_r])
            self.ld(self.Vx_d[tok0:tok0 + 128, :], vxb[:].rearrange("p g d -> p (g d)"), r=[vxb_r], w=[self.dsa_r])
            self.pool(lambda e, kk=kk, rp=rp: e.tensor_copy(out=kk[:, 0:4, :, :], in_=bc(rp[:, 2048:2304].rearrange("p (g d) -> p g d", g=4), 2, 2)),
                      r=[rp_r], w=[kk_r])
            self.pool(lambda e, kk=kk, rp=rp: e.tensor_copy(out=kk[:, 4, :, :], in_=bc(rp[:, 2816:2880], 1, 2)), r=[rp_r], w=[kk_r])
            srcs = [(rp[:, 128 * a:128 * a + 128], rp_r) for a in range(16)]
            srcs += [(kk[:, g, :, :].rearrange("p a d -> p (a d)"), kk_r) for g in range(4)]
            srcs += [(rp[:, 2304 + 128 * a:2304 + 128 * a + 128], rp_r) for a in range(4)]
            srcs += [(kk[:, 4, :, :].rearrange("p a d -> p (a d)"), kk_r)]
            for b0 in range(0, 25, 4):
                nb = min(4, 25 - b0)
                ps, ps_r = self.next_ps()
                for j in range(nb):
                    src, src_r = srcs[b0 + j]
                    self.pe(lambda e, ps=ps, j=j, src=src: e.transpose(out=ps[:, j * 128:(j + 1) * 128], in_=src, identity=self.ident_f[:]),
                            r=[src_r, self.ident_f_r], w=[ps_r])
                self.act(lambda e, ps=ps, TS=TS, b0=b0, nb=nb: e.copy(out=TS[:, b0:b0 + nb, :],
                                                                    in_=ps[:, 0:nb * 128].rearrange("p (a t) -> p a t", t=128)),
                         r=[ps_r], w=[TS_r])
            self.ld(self.qT_d[:, :, tok0:tok0 + 128].rearrange("a p t -> p a t"), TS[:, 0:16, :], r=[TS_r], w=[self.dsa_r])
            self.ld(self.kT2_d[:, :, tok0:tok0 + 128].rearrange("a p t -> p a t"), TS[:, 16:20, :], r=[TS_r], w=[self.dsa_r])
            self.ld(self.qiT_d[:, :, tok0:tok0 + 128].rearrange("a p t -> p a t"), TS[:, 20:24, :], r=[TS_r], w=[self.dsa_r])
            self.ld(self.kiT2_d[:, :, tok0:tok0 + 128].rearrange("a p t -> p a t"), TS[:, 24:25, :], r=[TS_r], w=[self.dsa_r])
            if smp:
                hs = [(rp[:, 64 * h:64 * h + 64], rp_r) for h in range(32)] + [(rp[:, 2304 + 64 * h:2304 + 64 * h + 64], rp_r) for h in range(8)]
                for b0 in range(0, 40, 4):
                    ps, ps_r = self.next_ps()
                    for j in range(4):
                        src, src_r = hs[b0 + j]
                        self.pe(lambda e, ps=ps, j=j, src=src: e.transpose(out=ps[0:64, j * 128:(j + 1) * 128], in_=src, identity=self.ident_f[:]),
                                r=[src_r, self.ident_f_r], w=[ps_r])
                    self.act(lambda e, ps=ps, b0=b0: e.copy(out=TSs_s[:, b0:b0 + 4, :], in_=ps[0:64, :].rearrange("p (a t) -> p a t", t=128)),
                             r=[ps_r], w=[TSs_s_r])
                self.ld(self.qTs_d[:, :, :], TSs_s[:, 0:32, :], r=[TSs_s_r], w=[self.dsa_r])
                self.ld(self.qiTs_d[:, :, :], TSs_s[:, 32:40, :], r=[TSs_s_r], w=[self.dsa_r])

    def attn_out_tile(self, i, g2, g2_r, g2T, g2T_r, wo, wo_r, hb, hb_r):
        pt, pt_r = self.pst
        for b0 in range(0, 16, 8):
            for j in range(8):
                self.pe(lambda e, j=j, b0=b0: e.transpose(out=pt[:, j * 128:(j + 1) * 128], in_=g2[:, (b0 + j) * 128:(b0 + j + 1) * 128],
                                                          identity=self.ident_b[:]), r=[g2_r, self.ident_b_r], w=[pt_r])
            self.act(lambda e, b0=b0: e.copy(out=g2T[:, b0:b0 + 8, :], in_=pt[:].rearrange("p (a t) -> p a t", t=128)), r=[pt_r], w=[g2T_r])
        self.ld(hb[:], self.H[i * 128:(i + 1) * 128, :], r=[self.H_r[i]], w=[hb_r])
        for hh in range(2):
            ps, ps_r = self.next_ps()
            for f in range(16):
                self.pe(lambda e, ps=ps, f=f, hh=hh: e.matmul(ps[:], lhsT=g2T[:, f, :], rhs=wo[:, f, hh * 512:(hh + 1) * 512],
                                                              start=(f == 0), stop=(f == 15)), r=[wo_r, g2T_r], w=[ps_r])
            self.dve(lambda e, ps=ps, hh=hh: e.tensor_tensor(out=hb[:, hh * 512:(hh + 1) * 512], in0=ps[:], in1=hb[:, hh * 512:(hh + 1) * 512],
                                                            op=ALU.add), r=[ps_r, hb_r], w=[hb_r])
        self.ld(self.H[i * 128:(i + 1) * 128, :], hb[:], r=[hb_r], w=[self.H_r[i]])

    def topk_threshold(self, sc, sc_r, S, lo, hi, mid, cnt, sel, nsel, small_r, junk, junk_r, iters=24):
        for it in range(iters):
            self.dve(lambda e: e.tensor_scalar(out=mid[:], in0=lo[:], scalar1=hi[:, 0:1], scalar2=0.5, op0=ALU.add, op1=ALU.mult),
                     r=[small_r], w=[small_r])
            self.dve(lambda e: e.tensor_scalar(out=junk[:, 0:S], in0=sc[:, 0:S], scalar1=mid[:, 0:1], scalar2=0.0, op0=ALU.is_ge, op1=ALU.add,
                                               accum_out=cnt[:]), r=[sc_r, small_r], w=[junk_r, small_r])
            self.dve(lambda e: e.tensor_scalar(out=sel[:], in0=cnt[:], scalar1=255.5, scalar2=None, op0=ALU.is_ge), r=[small_r], w=[small_r])
            self.dve(lambda e: e.tensor_scalar(out=nsel[:], in0=cnt[:], scalar1=255.5, scalar2=None, op0=ALU.is_lt), r=[small_r], w=[small_r])
            self.dve(lambda e: e.copy_predicated(out=lo[:], mask=sel[:], data=mid[:]), r=[small_r], w=[small_r])
            self.dve(lambda e: e.copy_predicated(out=hi[:], mask=nsel[:], data=mid[:]), r=[small_r], w=[small_r])

    def dsa_prompt(self, st):
        P = self.P

        def T(name, shape, dt=F32):
            return P.sb(st, name, shape, dt)
        self.ps_lim = 5
        accs = [self.psb[5], self.psb[6]]
        kT2, kT2_r = T("kT2", [128, 4, T_P], BF16)
        for g in range(4):
            self.ld(kT2[:, g, :], self.kT2_d[g, :, 0:T_P], r=[self.dsa_r], w=[kT2_r])
        kiT2, kiT2_r = T("kiT2", [128, T_P], BF16)
        self.ld(kiT2[:], self.kiT2_d[0, :, 0:T_P], r=[self.dsa_r], w=[kiT2_r])
        Vx, Vx_r = T("Vx", [128, 32, 260], BF16)
        for k4 in range(4):
            self.ld(Vx[:, 8 * k4:8 * k4 + 8, :], self.Vx_d[1024 * k4:1024 * (k4 + 1), :].rearrange("(kt s) c -> s kt c", s=128),
                    r=[self.dsa_r], w=[Vx_r])
        wo, wo_r = T("awo", [128, 16, D_MODEL], BF16)
        self.wo_attn = (wo, wo_r)
        wov = self.attn_w_out.rearrange("(k p) n -> p k n", p=128)
        for k in range(0, 16, 2):
            self.ldc(wo[:, k:k + 2, :], wov[:, k:k + 2, :], w=[wo_r])
        cmask, cmask_r = T("cmask", [128, 128])
        self.ld(cmask[:], self.cmask_d, w=[cmask_r])
        qTs = [T("qT", [128, 16, 128], BF16) for _ in range(2)]
        qiTs = [T("qiT", [128, 4, 128], BF16) for _ in range(2)]
        wis = [T("wi", [128, 8]) for _ in range(2)]
        szs = [T("sz", [128, E], BF16) for _ in range(2)]
        hb, hb_r = T("ahb", [128, D_MODEL])
        sc, sc_r = T("sc", [128, T_P])
        tmps = [T("sctmp", [128, 512]) for _ in range(2)]
        junk, junk_r = T("scjunk", [128, T_P], BF16)
        Mb, Mb_r = T("Mb", [128, T_P], BF16)
        MTn, MTn_r = T("MTn", [128, 32, 128], BF16)
        pexps = [T("pexp", [128, 4, 128], BF16) for _ in range(6)]
        g2, g2_r = T("g2", [128, E], BF16)
        of_, of_r = T("of", [128, 4, 64])
        rc, rc_r = T("rc", [128, 4])
        g2T, g2T_r = T("g2T", [128, 16, 128], BF16)
        lo, small_r = T("lo", [128, 1])
        hi, _ = T("hi", [128, 1])
        mid, _ = T("mid", [128, 1])
        cnt, _ = T("cnt", [128, 1])
        sel, _ = T("sel", [128, 1], I32)
        nsel, _ = T("nsel", [128, 1], I32)
        pt, pt_r = self.pst
        MTns = [(MTn, MTn_r), T("MTn2", [128, 32, 128], BF16)]
        accS = [[T("accS", [128, 260]) for _ in range(2)] for _ in range(4)]
        st_ = {"pe_i": 0}

        def stage_idx(qt):
            nk = qt + 1
            S = 128 * nk
            qT, qT_r = qTs[qt % 2]
            qiT, qiT_r = qiTs[qt % 2]
            wi, wi_r = wis[qt % 2]
            sz, sz_r = szs[qt % 2]
            t0 = qt * 128
            self.ld(qT[:], self.qT_d[:, :, t0:t0 + 128].rearrange("a p t -> p a t"), r=[self.dsa_r], w=[qT_r])
            self.ld(qiT[:], self.qiT_d[:, :, t0:t0 + 128].rearrange("a p t -> p a t"), r=[self.dsa_r], w=[qiT_r])
            self.ld(wi[:], self.wi_d[t0:t0 + 128, :], r=[self.dsa_r], w=[wi_r])
            self.ld(sz[:], self.sz_d[t0:t0 + 128, :], r=[self.dsa_r], w=[sz_r])
            ti = 0
            for c0 in range(0, S, 512):
                n = min(512, S - c0)
                for h in range(8):
                    a, hf = h // 2, h % 2
                    ps, ps_r = self.next_ps()
                    self.pe(lambda e, ps=ps, a=a, hf=hf, c0=c0, n=n, qiT=qiT: e.matmul(
                        ps[:, 0:n], lhsT=qiT[64 * hf:64 * hf + 64, a, :], rhs=kiT2[64 * hf:64 * hf + 64, c0:c0 + n],
                        start=True, stop=True, tile_position=(64 * hf, 0)), r=[qiT_r, kiT2_r], w=[ps_r])
                    if h == 0:
                        self.dve(lambda e, ps=ps, c0=c0, n=n, wi=wi: e.tensor_scalar(
                            out=sc[:, c0:c0 + n], in0=ps[:, 0:n], scalar1=0.0, scalar2=wi[:, 0:1], op0=ALU.max, op1=ALU.mult),
                            r=[ps_r, wi_r], w=[sc_r])
                    else:
                        tmp, tmp_r = tmps[ti % 2]
                        ti += 1
                        self.dve(lambda e, ps=ps, n=n, wi=wi, h=h, tmp=tmp: e.tensor_scalar(
                            out=tmp[:, 0:n], in0=ps[:, 0:n], scalar1=0.0, scalar2=wi[:, h:h + 1], op0=ALU.max, op1=ALU.mult),
                            r=[ps_r, wi_r], w=[tmp_r])
                        self.pool(lambda e, c0=c0, n=n, tmp=tmp: e.tensor_tensor(
                            out=sc[:, c0:c0 + n], in0=sc[:, c0:c0 + n], in1=tmp[:, 0:n], op=ALU.add), r=[tmp_r, sc_r], w=[sc_r])
            if qt >= 2:
                self.dve(lambda e, S=S: e.tensor_reduce(out=hi[:], in_=sc[:, 0:S], axis=AX.X, op=ALU.max), r=[sc_r], w=[small_r])
                self.dve(lambda e, S=S: e.tensor_reduce(out=lo[:], in_=sc[:, 0:S], axis=AX.X, op=ALU.min), r=[sc_r], w=[small_r])
            else:
                self.pool(lambda e: e.memset(lo[:], -1e29), w=[small_r])
            self.dve(lambda e, t0=t0: e.tensor_tensor(out=sc[:, t0:t0 + 128], in0=sc[:, t0:t0 + 128], in1=cmask[:], op=ALU.add),
                     r=[sc_r, cmask_r], w=[sc_r])
            if qt >= 2:
                self.topk_threshold(sc, sc_r, S, lo, hi, mid, cnt, sel, nsel, small_r, junk, junk_r)
            self.dve(lambda e, S=S: e.tensor_scalar(out=Mb[:, 0:S], in0=sc[:, 0:S], scalar1=lo[:, 0:1], scalar2=None, op0=ALU.is_ge),
                     r=[sc_r, small_r], w=[Mb_r])

        def stage_mt(qt):
            nk = qt + 1
            MTc, MTc_r = MTns[qt % 2]
            for k0 in range(0, nk, 8):
                nb = min(8, nk - k0)
                for j in range(nb):
                    self.pe(lambda e, j=j, k0=k0: e.transpose(out=pt[:, j * 128:(j + 1) * 128], in_=Mb[:, (k0 + j) * 128:(k0 + j + 1) * 128],
                                                              identity=self.ident_b[:]), r=[Mb_r, self.ident_b_r], w=[pt_r])
                self.dve(lambda e, k0=k0, nb=nb, MTc=MTc: e.tensor_scalar(out=MTc[:, k0:k0 + nb, :], in0=pt[:, 0:nb * 128].rearrange("p (a t) -> p a t", t=128),
                                                                          scalar1=-1.0, scalar2=30000.0, op0=ALU.add, op1=ALU.mult), r=[pt_r], w=[MTc_r])

        def stage_attn(qt):
            nk = qt + 1
            qT, qT_r = qTs[qt % 2]
            sz, sz_r = szs[qt % 2]
            MTc, MTc_r = MTns[qt % 2]

            def emit_sm(g, kt):
                pex = []
                pss = []
                for par in range(2):
                    ps, ps_r = self.next_ps()
                    psv = ps[:].rearrange("p (a t) -> p a t", t=128)
                    self.pe(lambda e, psv=psv, par=par, g=g, kt=kt: e.matmul(
                        psv, lhsT=kT2[64 * par:64 * par + 64, g, kt * 128:(kt + 1) * 128], rhs=qT[64 * par:64 * par + 64, 4 * g:4 * g + 4, :],
                        start=True, stop=False, tile_position=(64 * par, 0)), r=[kT2_r, qT_r], w=[ps_r])
                    pss.append((ps, ps_r, psv))
                for par in range(2):
                    ps, ps_r, psv = pss[par]
                    self.pe(lambda e, psv=psv, kt=kt: e.matmul(psv, lhsT=self.ident_b[:], rhs=bc(MTc[:, kt, :], 1, 4), start=False, stop=True),
                            r=[self.ident_b_r, MTc_r], w=[ps_r])
                    px, px_r = pexps[st_["pe_i"] % len(pexps)]
                    st_["pe_i"] += 1
                    self.act(lambda e, px=px, psv=psv: e.activation(out=px[:], in_=psv, func=AF.Exp, scale=0.125), r=[ps_r], w=[px_r])
                    pex.append((px, px_r))
                return pex

            def emit_pv(g, kt, pex):
                for h8 in range(8):
                    par, a = h8 % 2, h8 // 2
                    acc, acc_r = accs[h8 // 4]
                    col = (h8 % 4) * 65
                    px, px_r = pex[par]
                    self.pe(lambda e, acc=acc, col=col, px=px, a=a, kt=kt, g=g, h8=h8: e.matmul(
                        acc[:, col:col + 65], lhsT=px[:, a, :], rhs=Vx[:, kt, g * 65:(g + 1) * 65],
                        start=(kt == 0 and h8 % 4 == 0), stop=(kt == nk - 1), skip_group_check=True), r=[px_r, Vx_r], w=[acc_r])
                if kt == nk - 1:
                    for half in range(2):
                        acc, acc_r = accs[half]
                        aS, aS_r = accS[g][half]
                        self.act(lambda e, acc=acc, aS=aS: e.copy(out=aS[:], in_=acc[:, 0:260]), r=[acc_r], w=[aS_r])

            units = [(g, kt) for g in range(4) for kt in range(nk)]
            pend = emit_sm(*units[0])
            for ui, (g, kt) in enumerate(units):
                nxt = emit_sm(*units[ui + 1]) if ui + 1 < len(units) else None
                emit_pv(g, kt, pend)
                pend = nxt
            for g in range(4):
                for half in range(2):
                    aS, aS_r = accS[g][half]
                    accv = aS[:].rearrange("p (h c) -> p h c", c=65)
                    c0 = 64 * (8 * g + 4 * half)
                    self.dve(lambda e, accv=accv: e.reciprocal(out=rc[:], in_=accv[:, :, 64]), r=[aS_r], w=[rc_r])
                    self.dve(lambda e, accv=accv: e.tensor_tensor(out=of_[:], in0=accv[:, :, 0:64], in1=bc(rc[:], 2, 64), op=ALU.mult),
                             r=[aS_r, rc_r], w=[of_r])
                    self.pool(lambda e, c0=c0, sz=sz: e.tensor_tensor(out=g2[:, c0:c0 + 256], in0=of_[:].rearrange("p h d -> p (h d)"),
                                                                     in1=sz[:, c0:c0 + 256], op=ALU.mult), r=[of_r, sz_r], w=[g2_r])
            self.attn_out_tile(qt, g2, g2_r, g2T, g2T_r, wo, wo_r, hb, hb_r)

        NQ = T_P // 128
        stage_idx(0)
        stage_mt(0)
        for qt in range(NQ):
            if qt + 1 < NQ:
                stage_idx(qt + 1)
            stage_attn(qt)
            if qt + 1 < NQ:
                stage_mt(qt + 1)
        self.ps_lim = 7

    def dsa_sample(self, st):
        P = self.P

        def T(name, shape, dt=F32):
            return P.sb(st, name, shape, dt)
        self.ps_lim = 5
        acc, acc_r = self.psb[5]
        pt, pt_r = self.pst
        wo, wo_r = T("awo2", [128, 16, D_MODEL], BF16)
        wov = self.attn_w_out.rearrange("(k p) n -> p k n", p=128)
        for k in range(0, 16, 2):
            self.ldc(wo[:, k:k + 2, :], wov[:, k:k + 2, :], w=[wo_r])
        pti, pti_r = T("pti", [128, N_S * 16], I32)
        self.ld(pti[:], self.page_table.partition_broadcast(128), w=[pti_r])
        iota, iota_r = T("iota", [128, 1])
        self.ld(iota[:], self.iota_d, w=[iota_r])
        ptf, ptf_r = T("ptf", [128, N_S * 16])
        self.dve(lambda e: e.tensor_copy(out=ptf[:], in_=pti[:]), r=[pti_r], w=[ptf_r])
        self.dve(lambda e: e.tensor_scalar(out=ptf[:], in0=ptf[:], scalar1=128.0, scalar2=iota[:, 0:1], op0=ALU.mult, op1=ALU.add),
                 r=[ptf_r, iota_r], w=[ptf_r])
        idx, idx_r = T("idx", [128, N_S * 16], I32)
        self.dve(lambda e: e.tensor_copy(out=idx[:], in_=ptf[:]), r=[ptf_r], w=[idx_r])
        qTs, qTs_r = T("qTs", [64, N_S, 32, 8], BF16)
        qiTs, qiTs_r = T("qiTs", [64, N_S, 8, 8], BF16)
        for n in range(N_S):
            self.ld(qTs[:, n, :, :], self.qTs_d[:, :, 8 * n:8 * n + 8], r=[self.dsa_r], w=[qTs_r])
            self.ld(qiTs[:, n, :, :], self.qiTs_d[:, :, 8 * n:8 * n + 8], r=[self.dsa_r], w=[qiTs_r])
        kTn, kTn_r = T("kTn", [64, 4, T_S], BF16)
        self.ld(kTn[:], self.kT2_d[:, 0:64, T_P:TOK].rearrange("g p t -> p g t"), r=[self.dsa_r], w=[kTn_r])
        kiTn, kiTn_r = T("kiTn", [64, T_S], BF16)
        self.ld(kiTn[:], self.kiT2_d[0, 0:64, T_P:TOK], r=[self.dsa_r], w=[kiTn_r])
        Vxn, Vxn_r = T("Vxn", [128, 260], BF16)
        self.ld(Vxn[:], self.Vx_d[T_P:TOK, :], r=[self.dsa_r], w=[Vxn_r])
        szs_, szs_r = T("szsmp", [128, E], BF16)
        self.ld(szs_[:], self.sz_d[T_P:TOK, :], r=[self.dsa_r], w=[szs_r])
        wst, wst_r = T("wst", [64, N_S])
        for h in range(8):
            self.P.dma("sp", lambda e, h=h: e.dma_start(out=wst[8 * h:8 * h + 8, :], in_=self.wi_d[T_P:TOK, h].rearrange("(n t) -> t n", t=8),
                                                       allow_slow_non_contiguous=True), [self.dsa_r], [wst_r])
        selm, selm_r = T("selm", [64, 8])
        self.ld(selm[:], self.selm_d, w=[selm_r])
        blockm, blockm_r = T("blockm", [128, 128])
        self.ld(blockm[:], self.blockm_d, w=[blockm_r])
        cmask_s, cmask_s_r = T("cmasks", [128, 8])
        self.ld(cmask_s[:], self.cmask_s_d, w=[cmask_s_r])
        sc, sc_r = T("scs", [128, 2056])
        lo, small_r = T("slo", [128, 1])
        hi, _ = T("shi", [128, 1])
        mid, _ = T("smid", [128, 1])
        cnt, _ = T("scnt", [128, 1])
        sel, _ = T("ssel", [128, 1], I32)
        nsel, _ = T("snsel", [128, 1], I32)
        Mb, Mb_r = T("sMb", [128, 2056], BF16)
        NMT, NMT_r = T("sNMT", [128, 17, 128], BF16)
        MBn, MBn_r = T("sMBn", [128, 128], BF16)
        s1 = ExitStack()
        junk, junk_r = P.sb(s1, "sjunk", [128, 2056], BF16)
        KIgs = [P.sb(s1, "KIg", [128, 16, 64]) for _ in range(2)]
        kiTg, kiTg_r = P.sb(s1, "kiTg", [64, 16, 128], BF16)
        rls = [P.sb(s1, "rl", [64, 512]) for _ in range(2)]
        scst = [P.sb(s1, "scst", [8, 2056]) for _ in range(2)]
        ri_ = 0
        for n in range(N_S):
            KIg, KIg_r = KIgs[n % 2]
            for j in range(16):
                c = 16 * n + j
                self.P.dma("pool", lambda e, KIg=KIg, j=j, c=c: e.indirect_dma_start(
                    out=KIg[:, j, :], out_offset=None, in_=self.cache_kidx,
                    in_offset=bass.IndirectOffsetOnAxis(ap=idx[:, c:c + 1], axis=0)), [idx_r], [KIg_r])
            for b0 in range(0, 16, 4):
                ps, ps_r = self.next_ps()
                for j in range(4):
                    self.pe(lambda e, ps=ps, j=j, b0=b0, KIg=KIg: e.transpose(out=ps[0:64, j * 128:(j + 1) * 128], in_=KIg[:, b0 + j, :],
                                                                           identity=self.ident_f[:]), r=[KIg_r, self.ident_f_r], w=[ps_r])
                self.act(lambda e, ps=ps, b0=b0: e.copy(out=kiTg[:, b0:b0 + 4, :], in_=ps[0:64, :].rearrange("p (a t) -> p a t", t=128)),
                         r=[ps_r], w=[kiTg_r])
            st_, st_r = scst[n % 2]
            for c4 in range(5):
                ps, ps_r = self.next_ps()
                nn = 512 if c4 < 4 else 8
                rhs = kiTg[:, 4 * c4:4 * c4 + 4, :] if c4 < 4 else kiTn[:, 8 * n:8 * n + 8]
                outp = ps[0:64, :].rearrange("p (a t) -> p a t", t=128) if c4 < 4 else ps[0:64, 0:8]
                rr = kiTg_r if c4 < 4 else kiTn_r
                self.pe(lambda e, outp=outp, rhs=rhs, n=n: e.matmul(outp, lhsT=qiTs[:, n, :, :].rearrange("p h t -> p (h t)"), rhs=rhs, start=True, stop=True),
                        r=[qiTs_r, rr], w=[ps_r])
                rl, rl_r = rls[ri_ % 2]
                ri_ += 1
                self.dve(lambda e, ps=ps, rl=rl, nn=nn, n=n: e.tensor_scalar(out=rl[:, 0:nn], in0=ps[0:64, 0:nn], scalar1=0.0, scalar2=wst[:, n:n + 1],
                                                                           op0=ALU.max, op1=ALU.mult), r=[ps_r, wst_r], w=[rl_r])
                ps2, ps2_r = self.next_ps()
                self.pe(lambda e, ps2=ps2, rl=rl, nn=nn: e.matmul(ps2[0:8, 0:nn], lhsT=selm[:], rhs=rl[:, 0:nn], start=True, stop=True),
                        r=[selm_r, rl_r], w=[ps2_r])
                self.act(lambda e, ps2=ps2, st_=st_, c4=c4, nn=nn: e.copy(out=st_[:, 512 * c4:512 * c4 + nn], in_=ps2[0:8, 0:nn]), r=[ps2_r], w=[st_r])
            self.ld(sc[8 * n:8 * n + 8, :], st_[:], r=[st_r], w=[sc_r])
        self.dve(lambda e: e.tensor_reduce(out=hi[:], in_=sc[:], axis=AX.X, op=ALU.max), r=[sc_r], w=[small_r])
        self.dve(lambda e: e.tensor_reduce(out=lo[:], in_=sc[:], axis=AX.X, op=ALU.min), r=[sc_r], w=[small_r])
        self.dve(lambda e: e.tensor_tensor(out=sc[:, 2048:2056], in0=sc[:, 2048:2056], in1=cmask_s[:], op=ALU.add), r=[sc_r, cmask_s_r], w=[sc_r])
        self.topk_threshold(sc, sc_r, 2056, lo, hi, mid, cnt, sel, nsel, small_r, junk, junk_r)
        self.dve(lambda e: e.tensor_scalar(out=Mb[:], in0=sc[:], scalar1=lo[:, 0:1], scalar2=None, op0=ALU.is_ge), r=[sc_r, small_r], w=[Mb_r])
        self.dve(lambda e: e.tensor_tensor(out=MBn[:].rearrange("p (n t) -> p n t", t=8), in0=bc(Mb[:, 2048:2056], 1, 16),
                                           in1=blockm[:].rearrange("p (n t) -> p n t", t=8), op=ALU.mult), r=[Mb_r, blockm_r], w=[MBn_r])
        for k0 in range(0, 17, 8):
            nb = min(8, 17 - k0)
            for j in range(nb):
                src = Mb[:, (k0 + j) * 128:(k0 + j + 1) * 128] if k0 + j < 16 else MBn[:]
                self.pe(lambda e, j=j, src=src: e.transpose(out=pt[:, j * 128:(j + 1) * 128], in_=src, identity=self.ident_b[:]),
                        r=[Mb_r, MBn_r, self.ident_b_r], w=[pt_r])
            self.dve(lambda e, k0=k0, nb=nb: e.tensor_scalar(out=NMT[:, k0:k0 + nb, :], in0=pt[:, 0:nb * 128].rearrange("p (a t) -> p a t", t=128),
                                                             scalar1=-1.0, scalar2=30000.0, op0=ALU.add, op1=ALU.mult), r=[pt_r], w=[NMT_r])
        P.barrier()
        s1.close()
        Kgs = [T("Kg", [128, 16, 256]) for _ in range(2)]
        Vgs = [T("Vg", [128, 16, 256]) for _ in range(2)]
        kTg, kTg_r = T("kTg", [64, 16, 4, 128], BF16)
        Vxg, Vxg_r = T("Vxg", [128, 16, 4, 65], BF16)
        self.pool(lambda e: e.memset(Vxg[:, :, :, 64:65], 1.0), w=[Vxg_r])
        pxs = [T("spx", [128, 256], BF16) for _ in range(3)]
        rc, rc_r = T("src", [64, 4])
        osb = [T("osb", [64, 4, 64]) for _ in range(2)]
        px_i = 0
        for n in range(N_S):
            Kg, Kg_r = Kgs[n % 2]
            Vg, Vg_r = Vgs[n % 2]
            for j in range(16):
                c = 16 * n + j
                self.P.dma("pool", lambda e, Kg=Kg, j=j, c=c: e.indirect_dma_start(
                    out=Kg[:, j, :], out_offset=None, in_=self.cache_k,
                    in_offset=bass.IndirectOffsetOnAxis(ap=idx[:, c:c + 1], axis=0)), [idx_r], [Kg_r])
                self.P.dma("pool", lambda e, Vg=Vg, j=j, c=c: e.indirect_dma_start(
                    out=Vg[:, j, :], out_offset=None, in_=self.cache_v,
                    in_offset=bass.IndirectOffsetOnAxis(ap=idx[:, c:c + 1], axis=0)), [idx_r], [Vg_r])
            self.act(lambda e, Vg=Vg: e.copy(out=Vxg[:, :, :, 0:64], in_=Vg[:].rearrange("p j (g d) -> p j g d", g=4)), r=[Vg_r], w=[Vxg_r])
            for j in range(16):
                ps, ps_r = self.next_ps()
                for g in range(4):
                    self.pe(lambda e, ps=ps, j=j, g=g, Kg=Kg: e.transpose(out=ps[0:64, g * 128:(g + 1) * 128], in_=Kg[:, j, 64 * g:64 * g + 64],
                                                                        identity=self.ident_f[:]), r=[Kg_r, self.ident_f_r], w=[ps_r])
                self.act(lambda e, ps=ps, j=j: e.copy(out=kTg[:, j, :, :], in_=ps[0:64, :].rearrange("p (g t) -> p g t", t=128)), r=[ps_r], w=[kTg_r])
            for j in range(17):
                ps, ps_r = self.next_ps()
                for g in range(4):
                    lhsT = kTg[:, j, g, :] if j < 16 else kTn[:, g, :]
                    rr = kTg_r if j < 16 else kTn_r
                    self.pe(lambda e, ps=ps, g=g, lhsT=lhsT, n=n: e.matmul(
                        ps[:, 64 * g:64 * g + 64].rearrange("p (h t) -> p h t", t=8), lhsT=lhsT, rhs=qTs[:, n, 8 * g:8 * g + 8, :],
                        start=(g == 0), stop=False, skip_group_check=True), r=[rr, qTs_r], w=[ps_r])
                self.pe(lambda e, ps=ps, j=j, n=n: e.matmul(ps[:, 0:256].rearrange("p (h t) -> p h t", t=8), lhsT=self.ident_b[:],
                                                            rhs=bc(NMT[:, j, 8 * n:8 * n + 8], 1, 32), start=False, stop=True, skip_group_check=True),
                        r=[self.ident_b_r, NMT_r], w=[ps_r])
                px, px_r = pxs[px_i % 3]
                px_i += 1
                self.act(lambda e, px=px, ps=ps: e.activation(out=px[:], in_=ps[:, 0:256], func=AF.Exp, scale=0.125), r=[ps_r], w=[px_r])
                for g in range(4):
                    rhs = Vxg[:, j, g, :] if j < 16 else Vxn[:, 65 * g:65 * g + 65]
                    rr = Vxg_r if j < 16 else Vxn_r
                    self.pe(lambda e, g=g, px=px, rhs=rhs, j=j: e.matmul(acc[0:64, 65 * g:65 * g + 65], lhsT=px[:, 64 * g:64 * g + 64], rhs=rhs,
                                                                        start=(j == 0 and g == 0), stop=(j == 16), skip_group_check=True),
                            r=[px_r, rr], w=[acc_r])
            accv = acc[0:64, 0:260].rearrange("p (g c) -> p g c", c=65)
            ob, ob_r = osb[n % 2]
            self.dve(lambda e, accv=accv: e.reciprocal(out=rc[:], in_=accv[:, :, 64]), r=[acc_r], w=[rc_r])
            self.dve(lambda e, accv=accv, ob=ob: e.tensor_tensor(out=ob[:], in0=accv[:, :, 0:64], in1=bc(rc[:], 2, 64), op=ALU.mult),
                     r=[acc_r, rc_r], w=[ob_r])
            for h in range(8):
                self.ld(self.os_d[8 * n:8 * n + 8, :].rearrange("t (g h d) -> h t g d", g=4, h=8)[h], ob[8 * h:8 * h + 8, :, :],
                        r=[ob_r], w=[self.dsa_r])
        osl, osl_r = T("osl", [128, E])
        self.ld(osl[:], self.os_d, r=[self.dsa_r], w=[osl_r])
        g2, g2_r = T("sg2", [128, E], BF16)
        self.dve(lambda e: e.tensor_tensor(out=g2[:], in0=osl[:], in1=szs_[:], op=ALU.mult), r=[osl_r, szs_r], w=[g2_r])
        g2T, g2T_r = T("sg2T", [128, 16, 128], BF16)
        hb, hb_r = T("shb", [128, D_MODEL])
        self.attn_out_tile(NTILE - 1, g2, g2_r, g2T, g2T_r, wo, wo_r, hb, hb_r)
        self.ps_lim = 7

    def ml_layer(self):
        P = self.P
        with ExitStack() as st:
            self.ml_proj(st)
        P.barrier()
        with ExitStack() as st:
            self.ml_feat(st)
        P.barrier()
        if self.ml_stage >= 2:
            with ExitStack() as st:
                self.ml_chunks(st)
            P.barrier()
        if self.ml_stage >= 3:
            with ExitStack() as st:
                self.ml_sample(st)
            P.barrier()

    def ml_proj(self, st):
        P = self.P
        win, win_r = P.sb(st, "mwin", [128, 8, 2 * E], BF16)
        wv = self.ml_w_in.rearrange("(k p) n -> p k n", p=128)
        for k in range(8):
            for hf in range(2):
                self.ldc(win[:, k, hf * E:(hf + 1) * E], wv[:, k, hf * E:(hf + 1) * E], w=[win_r])
        gt, gt_r = P.sb(st, "mgt", [128, D_MODEL])
        self.ld(gt[:], self.ml_norm.partition_broadcast(128), w=[gt_r])
        hts = [P.sb(st, "mht", [128, D_MODEL]) for _ in range(2)]
        junk, junk_r = P.sb(st, "mjunk", [128, D_MODEL], BF16)
        sss = [P.sb(st, "mss", [128, 1]) for _ in range(2)]
        xns = [P.sb(st, "mxn", [128, D_MODEL], BF16) for _ in range(2)]
        xTs = [P.sb(st, "mxT", [128, 8, 512], BF16) for _ in range(2)]
        obufs = [P.sb(st, "mobuf", [128, 512], BF16) for _ in range(4)]
        utok, utok_r = P.sb(st, "utok", [128, E])
        ob_i = 0
        ti = 0
        for tg in range(NGRP):
            tok0, ntok = grp_tok(tg)
            xT, xT_r = xTs[tg % 2]
            for il in range(ntok // 128):
                i = tok0 // 128 + il
                ht, ht_r = hts[ti % 2]
                ss, ss_r = sss[ti % 2]
                xn, xn_r = xns[ti % 2]
                ti += 1
                self.norm_tile(self.H[i * 128:(i + 1) * 128, :], self.H_r[i], gt, gt_r, ht, ht_r, junk, junk_r, ss, ss_r,
                               xn, xn_r, xT[:, :, il * 128:(il + 1) * 128], xT_r)
            for fo in range(32):
                ps, ps_r = self.next_ps()
                for k in range(8):
                    self.pe(lambda e, k=k, fo=fo, ps=ps, xT=xT, ntok=ntok: e.matmul(
                        ps[:, 0:ntok], lhsT=win[:, k, fo * 128:(fo + 1) * 128], rhs=xT[:, k, 0:ntok],
                        start=(k == 0), stop=(k == 7)), r=[win_r, xT_r], w=[ps_r])
                ob, ob_r = obufs[ob_i % 4]
                ob_i += 1
                if fo < 16:
                    self.act(lambda e, ob=ob, ps=ps, ntok=ntok: e.copy(out=ob[:, 0:ntok], in_=ps[:, 0:ntok]), r=[ps_r], w=[ob_r])
                    self.ld(self.muT_d[fo, :, tok0:tok0 + ntok], ob[:, 0:ntok], r=[ob_r], w=[self.ml_r])
                else:
                    self.act(lambda e, ob=ob, ps=ps, ntok=ntok: e.activation(
                        out=ob[:, 0:ntok], in_=ps[:, 0:ntok], func=AF.Silu), r=[ps_r], w=[ob_r])
                    self.ld(self.szT_d[fo - 16, :, tok0:tok0 + ntok], ob[:, 0:ntok], r=[ob_r], w=[self.szT_r[fo - 16]])
            if tg >= 7:
                c0 = ntok - 128
                for cc in range(4):
                    ps, ps_r = self.next_ps()
                    for k in range(8):
                        self.pe(lambda e, k=k, cc=cc, ps=ps, xT=xT, c0=c0: e.matmul(
                            ps[:], lhsT=xT[:, k, c0:c0 + 128], rhs=win[:, k, cc * 512:(cc + 1) * 512],
                            start=(k == 0), stop=(k == 7)), r=[win_r, xT_r], w=[ps_r])
                    self.act(lambda e, ps=ps, cc=cc: e.copy(out=utok[:, cc * 512:(cc + 1) * 512], in_=ps[:]), r=[ps_r], w=[utok_r])
                if tg == 7:
                    self.store(self.o_mconv_p, utok[125:128, :], r=[utok_r])
                else:
                    for n in range(N_S):
                        self.store(self.o_mconv_s[n], utok[8 * n + 5:8 * n + 8, :], r=[utok_r])

    def ml_feat(self, st):
        P = self.P

        def T(name, shape, dt=F32):
            return P.sb(st, name, shape, dt)
        ws = {}
        for nm, src in (("q", self.ml_w_q), ("k", self.ml_w_k), ("v", self.ml_w_v), ("o", self.ml_w_o)):
            w_, w_r = T("mw" + nm, [128, 8, 2, 256], BF16)
            for h in range(8):
                self.ldc(w_[:, h, :, :], src[h].rearrange("(dk p) e -> p dk e", p=128), w=[w_r])
            ws[nm] = (w_, w_r)
        wg, wg_r = T("mwg", [128, 48, 16], BF16)
        self.ldc(wg[:], self.ml_w_gates.rearrange("(c p) g -> p c g", p=128), w=[wg_r])
        bo, bo_r = T("mbo", [128, E])
        self.ld(bo[:], self.ml_b_o.partition_broadcast(128), w=[bo_r])
        cw, cw_r = T("mcw", [128, 16, 5])
        for j in range(4):
            self.P.dma("sp", lambda e, j=j: e.dma_start(out=cw[:, :, j], in_=self.ml_conv_w[j].rearrange("(f p) -> p f", p=128),
                                                       allow_slow_non_contiguous=True), (), [cw_r])
        self.P.dma("sp", lambda e: e.dma_start(out=cw[:, :, 4], in_=self.ml_conv_b.rearrange("(f p) -> p f", p=128),
                                              allow_slow_non_contiguous=True), (), [cw_r])
        bgi, bg_r = T("mbgi", [8, 1])
        bgf, _ = T("mbgf", [8, 1])
        nbgf, _ = T("mnbgf", [8, 1])
        self.ld(bgi[:], self.ml_b_gates[0:8, :], w=[bg_r])
        self.ld(bgf[:], self.ml_b_gates[8:16, :], w=[bg_r])
        self.dve(lambda e: e.tensor_scalar(out=nbgf[:], in0=bgf[:], scalar1=-1.0, scalar2=None, op0=ALU.mult), r=[bg_r], w=[bg_r])
        uX, uX_r = T("uX", [128, 16, 515], BF16)
        uSc, uSc_r = T("uSc", [128, 16, T_S], BF16)
        accs = [T("cacc", [128, 512]) for _ in range(2)]
        caT, caT_r = T("caT", [128, 16, 512], BF16)
        qTg, qTg_r = T("mqT", [128, 16, 512], BF16)
        kTg, kTg_r = T("mkT", [128, 16, 512], BF16)
        vTg, vTg_r = T("mvT", [128, 16, 512], BF16)
        obufs = [T("mfob", [128, 512], BF16) for _ in range(3)]
        vxs = [T("mvx", [128, 8, 257], BF16) for _ in range(2)]
        for (t_, r_) in vxs:
            self.pool(lambda e, t_=t_: e.memset(t_[:, :, 256:257], 1.0), w=[r_])
        kts = [T("mkt", [128, E], BF16) for _ in range(2)]
        ots = [T("mot", [128, E], BF16) for _ in range(2)]
        otmp = [T("motmp", [128, 512]) for _ in range(2)]
        ig, g_r = T("g_ig", [8, 512])
        lf, _ = T("g_lf", [8, 512])
        Bc, _ = T("g_B", [8, 512])
        ones, _ = T("g_ones", [8, 512])
        aa, _ = T("g_a", [8, 512])
        GG, _ = T("g_G", [8, 512])
        nG, _ = T("g_nG", [8, 512])
        wi_, _ = T("g_wi", [8, 512])
        em, _ = T("g_em", [8, 512])
        Gp, _ = T("g_Gp", [8, 8])
        Bl, _ = T("g_Bl", [8, 1])
        Gl, _ = T("g_Gl", [8, 1])
        m0s, _ = T("g_m0", [8, N_S])
        mms, _ = T("g_mm", [8, N_S])
        self.pool(lambda e: e.memset(ones[:], 1.0), w=[g_r])
        self.pool(lambda e: e.memset(Bl[:], 0.0), w=[g_r])
        self.pool(lambda e: e.memset(Gl[:], 0.0), w=[g_r])
        self.P.dma("sp", lambda e: e.dma_start(out=m0s[:], in_=self.ml_m0.rearrange("n h -> h n"), allow_slow_non_contiguous=True), (), [g_r])
        cv, cv_r = T("mcv", [48, E])
        self.ld(cv[:], self.ml_conv0, w=[cv_r])
        ob_i = 0
        vi = 0
        for tg in range(NGRP):
            tok0, ntok = grp_tok(tg)
            smp = (tg == 8)
            if not smp:
                self.ld(uX[:, :, 3:515], self.muT_d[:, :, tok0:tok0 + 512].rearrange("f p t -> p f t"), r=[self.ml_r], w=[uX_r])
                if tg == 0:
                    self.pool(lambda e: e.memset(uX[:, :, 0:3], 0.0), w=[uX_r])
                else:
                    self.ld(uX[:, :, 0:3], self.muT_d[:, :, tok0 - 3:tok0].rearrange("f p t -> p f t"), r=[self.ml_r], w=[uX_r])
            else:
                uS = uX[:, :, 0:N_S * 11].rearrange("p f (n t) -> p f n t", t=11)
                self.ld(uSc[:], self.muT_d[:, :, T_P:TOK].rearrange("f p t -> p f t"), r=[self.ml_r], w=[uSc_r])
                for f in range(16):
                    self.ld(uS[:, f, :, 3:11], self.muT_d[f, :, T_P:TOK].rearrange("p (n t) -> p n t", t=8), r=[self.ml_r], w=[uX_r])
                for f4 in range(0, 16, 4):
                    ps, ps_r = self.next_ps()
                    for j in range(4):
                        self.pe(lambda e, ps=ps, j=j, f4=f4: e.transpose(out=ps[:, j * 48:(j + 1) * 48], in_=cv[:, (f4 + j) * 128:(f4 + j + 1) * 128],
                                                                       identity=self.ident_f[0:48, 0:48]), r=[cv_r, self.ident_f_r], w=[ps_r])
                    self.act(lambda e, ps=ps, f4=f4, uS=uS: e.copy(out=uS[:, f4:f4 + 4, :, 0:3],
                                                                  in_=ps[:, 0:192].rearrange("p (f n j) -> p f n j", f=4, j=3)), r=[ps_r], w=[uX_r])
            for f in range(16):
                acc, acc_r = accs[f % 2]
                if not smp:
                    srcs = [uX[:, f, j:j + 512] for j in range(4)]
                    accv = acc[:, 0:512]
                    cav = caT[:, f, 0:512]
                else:
                    srcs = [uS[:, f, :, j:j + 8] for j in range(4)]
                    accv = acc[:, 0:128].rearrange("p (n t) -> p n t", t=8)
                    cav = caT[:, f, 0:128].rearrange("p (n t) -> p n t", t=8)
                self.dve(lambda e, accv=accv, srcs=srcs, f=f: e.tensor_scalar(out=accv, in0=srcs[0], scalar1=cw[:, f, 0:1], scalar2=cw[:, f, 4:5],
                                                                             op0=ALU.mult, op1=ALU.add), r=[uX_r, cw_r], w=[acc_r])
                for j in range(1, 4):
                    self.dve(lambda e, accv=accv, srcs=srcs, f=f, j=j: e.scalar_tensor_tensor(out=accv, in0=srcs[j], scalar=cw[:, f, j:j + 1], in1=accv,
                                                                                          op0=ALU.mult, op1=ALU.add), r=[uX_r, cw_r, acc_r], w=[acc_r])
                self.act(lambda e, accv=accv, cav=cav: e.activation(out=cav, in_=accv, func=AF.Silu), r=[acc_r], w=[caT_r])
            self.ld(self.mca_d[:, :, tok0:tok0 + ntok].rearrange("f p t -> p f t"), caT[:, :, 0:ntok], r=[caT_r], w=[self.ml_r])

            def uview(f, lo_, n_):
                if not smp:
                    return uX[:, f, 3 + lo_:3 + lo_ + n_]
                return uSc[:, f, lo_:lo_ + n_]

            for nm, src_is_ca, dst, dst_r, dscr in (("q", True, qTg, qTg_r, self.mq_d), ("k", True, kTg, kTg_r, self.mk_d),
                                                    ("v", False, vTg, vTg_r, None)):
                w_, w_r = ws[nm]
                for h in range(8):
                    for ec in range(2):
                        ps, ps_r = self.next_ps()
                        for dk in range(2):
                            if src_is_ca:
                                rhs = caT[:, 2 * h + dk, 0:ntok]
                                outp = ps[:, 0:ntok]
                            else:
                                rhs = uview(2 * h + dk, 0, ntok)
                                outp = ps[:, 0:ntok]
                            self.pe(lambda e, outp=outp, rhs=rhs, w_=w_, h=h, dk=dk, ec=ec: e.matmul(
                                outp, lhsT=w_[:, h, dk, ec * 128:(ec + 1) * 128], rhs=rhs, start=(dk == 0), stop=(dk == 1)),
                                r=[w_r, caT_r, uX_r, uSc_r], w=[ps_r])
                        self.act(lambda e, ps=ps, dst=dst, h=h, ec=ec, ntok=ntok: e.copy(out=dst[:, 2 * h + ec, 0:ntok], in_=ps[:, 0:ntok]),
                                 r=[ps_r], w=[dst_r])
                if dscr is not None:
                    self.ld(dscr[:, :, tok0:tok0 + ntok].rearrange("f p t -> p f t"), dst[:, :, 0:ntok], r=[dst_r], w=[self.ml_r])
            for gi in range(2):
                ps, ps_r = self.next_ps()
                for c in range(48):
                    src_t, src_r = ((qTg, qTg_r), (kTg, kTg_r), (vTg, vTg_r))[c // 16]
                    self.pe(lambda e, ps=ps, c=c, gi=gi, src_t=src_t, ntok=ntok: e.matmul(
                        ps[0:8, 0:ntok], lhsT=wg[:, c, 8 * gi:8 * gi + 8], rhs=src_t[:, c % 16, 0:ntok], start=(c == 0), stop=(c == 47)),
                        r=[wg_r, src_r], w=[ps_r])
                if gi == 0:
                    self.act(lambda e, ps=ps, ntok=ntok: e.activation(out=ig[:, 0:ntok], in_=ps[0:8, 0:ntok], func=AF.Copy, bias=0.0), r=[ps_r], w=[g_r])
                    self.dve(lambda e, ntok=ntok: e.tensor_scalar(out=ig[:, 0:ntok], in0=ig[:, 0:ntok], scalar1=bgi[:, 0:1], scalar2=None, op0=ALU.add),
                             r=[g_r, bg_r], w=[g_r])
                else:
                    self.act(lambda e, ps=ps, ntok=ntok: e.activation(out=lf[:, 0:ntok], in_=ps[0:8, 0:ntok], func=AF.Exp, scale=-1.0, bias=nbgf[:, 0:1]),
                             r=[ps_r, bg_r], w=[g_r])
                    self.act(lambda e, ntok=ntok: e.activation(out=lf[:, 0:ntok], in_=lf[:, 0:ntok], func=AF.Ln, bias=1.0), r=[g_r], w=[g_r])
                    self.dve(lambda e, ntok=ntok: e.tensor_scalar(out=lf[:, 0:ntok], in0=lf[:, 0:ntok], scalar1=-1.0, scalar2=None, op0=ALU.mult),
                             r=[g_r], w=[g_r])
            G = [g_r]
            if not smp:
                self.dve(lambda e: e.tensor_tensor_scan(out=Bc[:], data0=ones[:], data1=lf[:], initial=Bl[:, 0:1], op0=ALU.mult, op1=ALU.add), r=G, w=G)
                self.dve(lambda e: e.tensor_tensor(out=aa[:], in0=ig[:], in1=Bc[:], op=ALU.subtract), r=G, w=G)
                self.dve(lambda e: e.tensor_tensor_scan(out=GG[:], data0=aa[:], data1=aa[:], initial=Gl[:, 0:1], op0=ALU.max, op1=ALU.max), r=G, w=G)
                self.dve(lambda e: e.tensor_copy(out=Gp[:, 0:1], in_=Gl[:, 0:1]), r=G, w=G)
                self.dve(lambda e: e.tensor_copy(out=Gp[:, 1:8], in_=GG[:].rearrange("p (c t) -> p c t", t=64)[:, 0:7, 63]), r=G, w=G)
                self.dve(lambda e: e.tensor_tensor(out=wi_[:].rearrange("p (c t) -> p c t", t=64), in0=bc(Gp[:], 2, 64),
                                                   in1=GG[:].rearrange("p (c t) -> p c t", t=64), op=ALU.subtract), r=G, w=G)
                self.dve(lambda e: e.tensor_copy(out=Bl[:], in_=Bc[:, 511:512]), r=G, w=G)
                self.dve(lambda e: e.tensor_copy(out=Gl[:], in_=GG[:, 511:512]), r=G, w=G)
            else:
                v3 = lambda t_: t_[:, 0:128].rearrange("p (n t) -> p n t", t=8)
                B3, l3, a3, G3, i3 = v3(Bc), v3(lf), v3(aa), v3(GG), v3(ig)
                self.dve(lambda e: e.tensor_copy(out=B3[:, :, 0], in_=l3[:, :, 0]), r=G, w=G)
                for t in range(1, 8):
                    self.dve(lambda e, t=t: e.tensor_tensor(out=B3[:, :, t], in0=B3[:, :, t - 1], in1=l3[:, :, t], op=ALU.add), r=G, w=G)
                self.dve(lambda e: e.tensor_tensor(out=aa[:, 0:128], in0=ig[:, 0:128], in1=Bc[:, 0:128], op=ALU.subtract), r=G, w=G)
                self.dve(lambda e: e.tensor_tensor(out=G3[:, :, 0], in0=a3[:, :, 0], in1=m0s[:], op=ALU.max), r=G, w=G)
                for t in range(1, 8):
                    self.dve(lambda e, t=t: e.tensor_tensor(out=G3[:, :, t], in0=G3[:, :, t - 1], in1=a3[:, :, t], op=ALU.max), r=G, w=G)
                self.dve(lambda e: e.tensor_tensor(out=v3(wi_), in0=bc(m0s[:], 2, 8), in1=G3, op=ALU.subtract), r=G, w=G)
            nt = ntok
            self.act(lambda e, nt=nt: e.activation(out=wi_[:, 0:nt], in_=wi_[:, 0:nt], func=AF.Exp), r=G, w=G)
            self.dve(lambda e, nt=nt: e.tensor_scalar(out=nG[:, 0:nt], in0=GG[:, 0:nt], scalar1=-1.0, scalar2=None, op0=ALU.mult), r=G, w=G)
            self.dve(lambda e, nt=nt: e.tensor_tensor(out=Bc[:, 0:nt], in0=Bc[:, 0:nt], in1=GG[:, 0:nt], op=ALU.add), r=G, w=G)
            self.act(lambda e, nt=nt: e.activation(out=em[:, 0:nt], in_=Bc[:, 0:nt], func=AF.Exp, scale=-1.0), r=G, w=G)
            if tg == 7:
                self.store(self.o_mm_p, Bc[:, 511:512], r=G)
            if smp:
                self.dve(lambda e: e.tensor_copy(out=mms[:], in_=Bc[:, 0:128].rearrange("p (n t) -> p n t", t=8)[:, :, 7]), r=G, w=G)
                self.store(self.o_mm_s, mms[:], r=G)
            for qi_, t_ in enumerate((aa, nG, wi_, em)):
                self.ld(self.gq_d[qi_, :, tok0:tok0 + nt], t_[:, 0:nt], r=G, w=[self.ml_r])
            for il in range(ntok // 128):
                i = tok0 // 128 + il
                vx, vx_r = vxs[vi % 2]
                kt, kt_r = kts[vi % 2]
                ot, ot_r = ots[vi % 2]
                vi += 1
                for nm in ("v", "k", "o"):
                    w_, w_r = ws[nm]
                    for h2 in range(4):
                        ps, ps_r = self.next_ps()
                        for hh in range(2):
                            h = 2 * h2 + hh
                            for dk in range(2):
                                if nm == "k":
                                    lhsT = caT[:, 2 * h + dk, il * 128:(il + 1) * 128]
                                else:
                                    lhsT = uview(2 * h + dk, il * 128, 128)
                                    if smp:
                                        lhsT = lhsT
                                self.pe(lambda e, ps=ps, lhsT=lhsT, w_=w_, h=h, hh=hh, dk=dk: e.matmul(
                                    ps[:, hh * 256:(hh + 1) * 256], lhsT=lhsT, rhs=w_[:, h, dk, :], start=(dk == 0 and hh == 0), stop=(dk == 1),
                                    skip_group_check=True), r=[w_r, caT_r, uX_r, uSc_r], w=[ps_r])
                        if nm == "v":
                            self.act(lambda e, ps=ps, vx=vx, h2=h2: e.copy(out=vx[:, 2 * h2:2 * h2 + 2, 0:256],
                                                                          in_=ps[:].rearrange("p (a e) -> p a e", a=2)), r=[ps_r], w=[vx_r])
                        elif nm == "k":
                            self.act(lambda e, ps=ps, kt=kt, h2=h2: e.activation(out=kt[:, h2 * 512:(h2 + 1) * 512], in_=ps[:], func=AF.Copy, scale=1.0 / 16.0),
                                     r=[ps_r], w=[kt_r])
                        else:
                            tmp, tmp_r = otmp[h2 % 2]
                            self.dve(lambda e, ps=ps, tmp=tmp, h2=h2: e.tensor_tensor(out=tmp[:], in0=ps[:], in1=bo[:, h2 * 512:(h2 + 1) * 512], op=ALU.add),
                                     r=[ps_r, bo_r], w=[tmp_r])
                            self.act(lambda e, tmp=tmp, ot=ot, h2=h2: e.activation(out=ot[:, h2 * 512:(h2 + 1) * 512], in_=tmp[:], func=AF.Sigmoid),
                                     r=[tmp_r], w=[ot_r])
                self.ld(self.mvt_d[i * 128:(i + 1) * 128, :], vx[:].rearrange("p h e -> p (h e)"), r=[vx_r], w=[self.ml_r])
                self.ld(self.mkt_d[i * 128:(i + 1) * 128, :], kt[:], r=[kt_r], w=[self.ml_r])
                self.ld(self.mo_d[i * 128:(i + 1) * 128, :], ot[:], r=[ot_r], w=[self.ml_r])

    def ml_post_tile(self, i, hout, hout_r, K_):
        (st6, mv, rstd, sm_r, lnw, lnw_r, skip, skip_r, wout, wout_r, ot, ot_r, caTt, caTt_r, szTt, szTt_r,
         hn3, hn3_r, g2T, g2T_r, t1, t1_r, hb, hb_r) = K_
        tok0 = i * 128
        self.ld(ot[:], self.mo_d[tok0:tok0 + 128, :], r=[self.ml_r], w=[ot_r])
        self.ld(caTt[:], self.mca_d[:, :, tok0:tok0 + 128].rearrange("f p t -> p f t"), r=[self.ml_r], w=[caTt_r])
        self.ld(szTt[:], self.szT_d[:, :, tok0:tok0 + 128].rearrange("f p t -> p f t"), r=self.szT_r, w=[szTt_r])
        for h in range(8):
            self.dve(lambda e, h=h: e.bn_stats(out=st6[:, h, :], in_=hout[:, h, :]), r=[hout_r], w=[sm_r])
            self.dve(lambda e, h=h: e.bn_aggr(out=mv[:, h, :], in_=st6[:, h, :]), r=[sm_r], w=[sm_r])
        self.dve(lambda e: e.tensor_scalar(out=rstd[:], in0=mv[:, :, 1], scalar1=1e-5, scalar2=None, op0=ALU.add), r=[sm_r], w=[sm_r])
        self.act(lambda e: e.activation(out=rstd[:], in_=rstd[:], func=AF.Sqrt), r=[sm_r], w=[sm_r])
        self.dve(lambda e: e.reciprocal(out=rstd[:], in_=rstd[:]), r=[sm_r], w=[sm_r])
        for h in range(8):
            eng = self.dve if h % 2 == 0 else self.pool
            eng(lambda e, h=h: e.tensor_scalar(out=hout[:, h, :], in0=hout[:, h, :], scalar1=mv[:, h, 0:1], scalar2=rstd[:, h:h + 1],
                                               op0=ALU.subtract, op1=ALU.mult), r=[hout_r, sm_r], w=[hout_r])
        hf = hout[:].rearrange("p h e -> p (h e)")
        self.pool(lambda e: e.tensor_tensor(out=hf, in0=hf, in1=lnw[:], op=ALU.mult), r=[hout_r, lnw_r], w=[hout_r])
        self.dve(lambda e: e.tensor_tensor(out=hn3[:], in0=hf, in1=ot[:], op=ALU.mult), r=[hout_r, ot_r], w=[hn3_r])
        pt, pt_r = self.pst
        for b0 in range(0, 16, 8):
            for j in range(8):
                self.pe(lambda e, j=j, b0=b0: e.transpose(out=pt[:, j * 128:(j + 1) * 128], in_=hn3[:, (b0 + j) * 128:(b0 + j + 1) * 128],
                                                          identity=self.ident_b[:]), r=[hn3_r, self.ident_b_r], w=[pt_r])
            self.pool(lambda e, b0=b0: e.tensor_tensor(out=t1[:], in0=caTt[:, b0:b0 + 8, :], in1=bc(skip[:, b0:b0 + 8], 2, 128), op=ALU.mult),
                      r=[caTt_r, skip_r], w=[t1_r])
            self.dve(lambda e: e.tensor_tensor(out=t1[:], in0=t1[:], in1=pt[:].rearrange("p (a t) -> p a t", t=128), op=ALU.add),
                     r=[t1_r, pt_r], w=[t1_r])
            self.pool(lambda e, b0=b0: e.tensor_tensor(out=g2T[:, b0:b0 + 8, :], in0=t1[:], in1=szTt[:, b0:b0 + 8, :], op=ALU.mult),
                      r=[t1_r, szTt_r], w=[g2T_r])
        self.ld(hb[:], self.H[tok0:tok0 + 128, :], r=[self.H_r[i]], w=[hb_r])
        for hh in range(2):
            ps, ps_r = self.next_ps()
            for f in range(16):
                self.pe(lambda e, ps=ps, f=f, hh=hh: e.matmul(ps[:], lhsT=g2T[:, f, :], rhs=wout[:, f, hh * 512:(hh + 1) * 512],
                                                              start=(f == 0), stop=(f == 15)), r=[wout_r, g2T_r], w=[ps_r])
            self.dve(lambda e, ps=ps, hh=hh: e.tensor_tensor(out=hb[:, hh * 512:(hh + 1) * 512], in0=ps[:], in1=hb[:, hh * 512:(hh + 1) * 512],
                                                            op=ALU.add), r=[ps_r, hb_r], w=[hb_r])
        self.ld(self.H[tok0:tok0 + 128, :], hb[:], r=[hb_r], w=[self.H_r[i]])

    def ml_post_alloc(self, T):
        st6, sm_r = T("st6", [128, 8, 6])
        mv, _ = T("mv", [128, 8, 2])
        rstd, _ = T("rstd", [128, 8])
        lnw, lnw_r = T("lnw", [128, E])
        self.ld(lnw[:], self.ml_ln_w.partition_broadcast(128), w=[lnw_r])
        skip, skip_r = T("mskip", [128, 16])
        self.P.dma("sp", lambda e: e.dma_start(out=skip[:], in_=self.ml_skip.rearrange("(f p) -> p f", p=128), allow_slow_non_contiguous=True),
                   (), [skip_r])
        wout, wout_r = T("mwout", [128, 16, D_MODEL], BF16)
        wov = self.ml_w_out.rearrange("(k p) n -> p k n", p=128)
        for k in range(0, 16, 2):
            self.ldc(wout[:, k:k + 2, :], wov[:, k:k + 2, :], w=[wout_r])
        ot, ot_r = T("mot2", [128, E], BF16)
        caTt, caTt_r = T("caTt", [128, 16, 128], BF16)
        szTt, szTt_r = T("szTt", [128, 16, 128], BF16)
        hn3, hn3_r = T("hn3", [128, E], BF16)
        g2T, g2T_r = T("mg2T", [128, 16, 128], BF16)
        t1, t1_r = T("mt1", [128, 8, 128])
        hb, hb_r = T("mhb", [128, D_MODEL])
        return (st6, mv, rstd, sm_r, lnw, lnw_r, skip, skip_r, wout, wout_r, ot, ot_r, caTt, caTt_r, szTt, szTt_r,
                hn3, hn3_r, g2T, g2T_r, t1, t1_r, hb, hb_r)

    def ml_chunks(self, st):
        P = self.P

        def T(name, shape, dt=F32):
            return P.sb(st, name, shape, dt)
        K_ = self.ml_post_alloc(T)
        hmask, hmask_r = T("hmask", [8, 8, 128])
        self.ld(hmask[:].rearrange("p a t -> p (a t)"), self.hmask_d, w=[hmask_r])
        ones8, ones8_r = T("ones8", [8, 128])
        self.ld(ones8[:], self.ones8_d, w=[ones8_r])
        cm64, cm64_r = T("cm64", [64, 64])
        self.ld(cm64[:], self.cm64_d, w=[cm64_r])
        C = [T("Cst", [128, 2, 257]) for _ in range(8)]
        Cb = [T("Cbf", [128, 2, 257], BF16) for _ in range(8)]
        for h in range(8):
            self.pool(lambda e, h=h: e.memset(C[h][0][:], 0.0), w=[C[h][1]])
            self.pool(lambda e, h=h: e.memset(Cb[h][0][:], 0.0), w=[Cb[h][1]])
        qTg, qTg_r = T("cqT", [128, 16, 512], BF16)
        kTg, kTg_r = T("ckT", [128, 16, 512], BF16)
        gq, gq_r = T("cgq", [8, 4, 512])
        vts = [T("vtc", [64, 8 * 257], BF16) for _ in range(2)]
        kts = [T("ktc", [64, E], BF16) for _ in range(2)]
        nGe, nGe_r = T("nGe", [8, 8, 64])
        wie, wie_r = T("wie", [8, 8, 64])
        wm, wm_r = T("wm", [64, 8, 64])
        PT, PT_r = T("PT", [64, 8, 64], BF16)
        qw, qw_r = T("qw", [128, 16, 64], BF16)
        wpv, wpv_r = T("wpv", [128, 8])
        emT, emT_r = T("emT", [128, 8])
        dn, dn_r = T("dn", [128, 1])
        kws = [T("kw", [64, 256], BF16) for _ in range(2)]
        hout, hout_r = T("hout", [128, 8, 256])
        for c in range(T_P // 64):
            cl = c % 8
            cs0 = cl * 64
            t0 = c * 64
            po = 64 * (c % 2)
            if cl == 0:
                g0 = c * 64
                self.ld(qTg[:], self.mq_d[:, :, g0:g0 + 512].rearrange("f p t -> p f t"), r=[self.ml_r], w=[qTg_r])
                self.ld(kTg[:], self.mk_d[:, :, g0:g0 + 512].rearrange("f p t -> p f t"), r=[self.ml_r], w=[kTg_r])
                self.ld(gq[:], self.gq_d[:, :, g0:g0 + 512].rearrange("q h t -> h q t"), r=[self.ml_r], w=[gq_r])
            vt, vt_r = vts[c % 2]
            kt, kt_r = kts[c % 2]
            self.ld(vt[:], self.mvt_d[t0:t0 + 64, :], r=[self.ml_r], w=[vt_r])
            self.ld(kt[:], self.mkt_d[t0:t0 + 64, :], r=[self.ml_r], w=[kt_r])
            self.dve(lambda e, cs0=cs0: e.tensor_tensor(out=nGe[:], in0=bc(gq[:, 1, cs0:cs0 + 64], 1, 8), in1=hmask[:, :, 0:64], op=ALU.mult),
                     r=[gq_r, hmask_r], w=[nGe_r])
            self.pool(lambda e, cs0=cs0: e.tensor_tensor(out=wie[:], in0=bc(gq[:, 2, cs0:cs0 + 64], 1, 8), in1=hmask[:, :, 0:64], op=ALU.mult),
                      r=[gq_r, hmask_r], w=[wie_r])
            Dps, Dps_r = self.next_ps()
            Dv = Dps[0:64, :].rearrange("p (a t) -> p a t", t=64)
            self.pe(lambda e, Dv=Dv: e.matmul(Dv, lhsT=ones8[:, 0:64], rhs=nGe[:], start=True, stop=False), r=[ones8_r, nGe_r], w=[Dps_r])
            self.pe(lambda e, Dv=Dv, cs0=cs0: e.matmul(Dv, lhsT=gq[:, 0, cs0:cs0 + 64], rhs=hmask[:, :, 0:64], start=False, stop=True),
                    r=[gq_r, hmask_r], w=[Dps_r])
            Wps, Wps_r = self.next_ps()
            Wv = Wps[:].rearrange("p (a t) -> p a t", t=64)
            self.pe(lambda e, Wv=Wv: e.matmul(Wv, lhsT=ones8[:], rhs=wie[:], start=True, stop=True), r=[ones8_r, wie_r], w=[Wps_r])
            if c % 2 == 0:
                Eps, Eps_r = self.next_ps()
                self.pe(lambda e, Eps=Eps, cs0=cs0: e.matmul(Eps[:, 0:8], lhsT=gq[:, 3, cs0:cs0 + 128], rhs=self.ident_f[0:8, 0:8], start=True, stop=True),
                        r=[gq_r, self.ident_f_r], w=[Eps_r])
                self.act(lambda e, Eps=Eps: e.copy(out=emT[:], in_=Eps[:, 0:8]), r=[Eps_r], w=[emT_r])
            self.act(lambda e, Dv=Dv: e.activation(out=wm[:], in_=Dv, func=AF.Exp), r=[Dps_r], w=[wm_r])
            self.dve(lambda e: e.tensor_tensor(out=wm[:], in0=wm[:], in1=bc(cm64[:], 1, 8), op=ALU.mult), r=[wm_r, cm64_r], w=[wm_r])
            Sps, Sps_r = self.next_ps()
            for h in range(8):
                for dk in range(2):
                    self.pe(lambda e, Sps=Sps, h=h, dk=dk, cs0=cs0: e.matmul(
                        Sps[0:64, 64 * h:64 * h + 64], lhsT=kTg[:, 2 * h + dk, cs0:cs0 + 64], rhs=qTg[:, 2 * h + dk, cs0:cs0 + 64],
                        start=(h == 0 and dk == 0), stop=(dk == 1), skip_group_check=True), r=[kTg_r, qTg_r], w=[Sps_r])
            self.dve(lambda e, Sps=Sps: e.scalar_tensor_tensor(out=PT[:], in0=Sps[0:64, :].rearrange("p (a t) -> p a t", t=64), scalar=1.0 / 16.0,
                                                               in1=wm[:], op0=ALU.mult, op1=ALU.mult), r=[Sps_r, wm_r], w=[PT_r])
            self.dve(lambda e, Wv=Wv, cs0=cs0: e.tensor_tensor(out=qw[:].rearrange("p (h k) t -> p h k t", k=2),
                                                               in0=qTg[:, :, cs0:cs0 + 64].rearrange("p (h k) t -> p h k t", k=2),
                                                               in1=bc(Wv, 2, 2), op=ALU.mult), r=[qTg_r, Wps_r], w=[qw_r])
            self.act(lambda e, Wv=Wv: e.copy(out=wpv[:], in_=Wv[:, :, 63]), r=[Wps_r], w=[wpv_r])
            for h in range(8):
                nps, nps_r = self.next_ps()
                outp = nps[po:po + 64, 0:257]
                self.pe(lambda e, outp=outp, h=h, vt=vt, po=po: e.matmul(outp, lhsT=PT[:, h, :], rhs=vt[:, 257 * h:257 * h + 257], start=True, stop=False,
                                                                        tile_position=(0, po)), r=[PT_r, vt_r], w=[nps_r])
                for dk in range(2):
                    self.pe(lambda e, outp=outp, h=h, dk=dk, po=po: e.matmul(outp, lhsT=qw[:, 2 * h + dk, :], rhs=Cb[h][0][:, dk, :], start=False, stop=(dk == 1),
                                                                            tile_position=(0, po)), r=[qw_r, Cb[h][1]], w=[nps_r])
                self.act(lambda e, nps=nps, po=po: e.activation(out=dn[po:po + 64, :], in_=nps[po:po + 64, 256:257], func=AF.Abs),
                         r=[nps_r], w=[dn_r])
                self.dve(lambda e, h=h, po=po: e.tensor_tensor(out=dn[po:po + 64, :], in0=dn[po:po + 64, :], in1=emT[po:po + 64, h:h + 1], op=ALU.max),
                         r=[dn_r, emT_r], w=[dn_r])
                self.dve(lambda e, po=po: e.reciprocal(out=dn[po:po + 64, :], in_=dn[po:po + 64, :]), r=[dn_r], w=[dn_r])
                self.act(lambda e, nps=nps, h=h, po=po: e.activation(out=hout[po:po + 64, h, :], in_=nps[po:po + 64, 0:256], func=AF.Copy,
                                                                    scale=dn[po:po + 64, 0:1]), r=[nps_r, dn_r], w=[hout_r])
                kw, kw_r = kws[h % 2]
                self.act(lambda e, kw=kw, kt=kt, h=h: e.activation(out=kw[:], in_=kt[:, 256 * h:256 * h + 256], func=AF.Copy,
                                                                  scale=wm[:, h, 63:64]), r=[kt_r, wm_r], w=[kw_r])
                for dk in range(2):
                    ups, ups_r = self.next_ps()
                    self.pe(lambda e, ups=ups, kw=kw, dk=dk, h=h, vt=vt: e.matmul(ups[:, 0:257], lhsT=kw[:, 128 * dk:128 * dk + 128],
                                                                                rhs=vt[:, 257 * h:257 * h + 257], start=True, stop=True),
                            r=[kw_r, vt_r], w=[ups_r])
                    self.dve(lambda e, ups=ups, h=h, dk=dk: e.scalar_tensor_tensor(out=C[h][0][:, dk, :], in0=C[h][0][:, dk, :], scalar=wpv[:, h:h + 1],
                                                                                  in1=ups[:, 0:257], op0=ALU.mult, op1=ALU.add),
                             r=[C[h][1], wpv_r, ups_r], w=[C[h][1]])
                self.pool(lambda e, h=h: e.tensor_copy(out=Cb[h][0][:], in_=C[h][0][:]), r=[C[h][1]], w=[Cb[h][1]])
            if c % 2 == 1:
                self.ml_post_tile(c // 2, hout, hout_r, K_)
        nn, nn_r = T("nfin", [128, 16])
        for h in range(8):
            self.store(self.o_mc_p[h].rearrange("(dk p) e -> p dk e", p=128), C[h][0][:, :, 0:256], r=[C[h][1]])
            self.pool(lambda e, h=h: e.tensor_copy(out=nn[:, 2 * h:2 * h + 2], in_=C[h][0][:, :, 256]), r=[C[h][1]], w=[nn_r])
        ps, ps_r = self.next_ps()
        self.pe(lambda e, ps=ps: e.transpose(out=ps[0:16, 0:128], in_=nn[:], identity=self.ident_f[:]), r=[nn_r, self.ident_f_r], w=[ps_r])
        nt_, nt_r = T("nfinT", [16, 128])
        self.act(lambda e, ps=ps: e.copy(out=nt_[:], in_=ps[0:16, 0:128]), r=[ps_r], w=[nt_r])
        self.store(self.o_mn_p, nt_[:], r=[nt_r])

    def ml_sample(self, st):
        P = self.P

        def T(name, shape, dt=F32):
            return P.sb(st, name, shape, dt)
        K_ = self.ml_post_alloc(T)
        hmask, hmask_r = T("shmask", [8, 8, 128])
        self.ld(hmask[:].rearrange("p a t -> p (a t)"), self.hmask_d, w=[hmask_r])
        ones8, ones8_r = T("sones8", [8, 128])
        self.ld(ones8[:], self.ones8_d, w=[ones8_r])
        smask, smask_r = T("smask", [128, 128])
        self.ld(smask[:], self.smask_d, w=[smask_r])
        seqsel, seqsel_r = T("seqsel", [128, N_S])
        self.ld(seqsel[:], self.seqsel_d, w=[seqsel_r])
        qT, qT_r = T("sqT", [128, 16, T_S], BF16)
        kT, kT_r = T("skT", [128, 16, T_S], BF16)
        self.ld(qT[:], self.mq_d[:, :, T_P:TOK].rearrange("f p t -> p f t"), r=[self.ml_r], w=[qT_r])
        self.ld(kT[:], self.mk_d[:, :, T_P:TOK].rearrange("f p t -> p f t"), r=[self.ml_r], w=[kT_r])
        vt, vt_r = T("svt", [128, 8 * 257], BF16)
        kt, kt_r = T("skt", [128, E], BF16)
        self.ld(vt[:], self.mvt_d[T_P:TOK, :], r=[self.ml_r], w=[vt_r])
        self.ld(kt[:], self.mkt_d[T_P:TOK, :], r=[self.ml_r], w=[kt_r])
        gq, gq_r = T("sgq", [8, 4, T_S])
        self.ld(gq[:], self.gq_d[:, :, T_P:TOK].rearrange("q h t -> h q t"), r=[self.ml_r], w=[gq_r])
        nGe, nGe_r = T("snGe", [8, 8, 128])
        wie, wie_r = T("swie", [8, 8, 128])
        wsg, wsg_r = T("swsg", [8, 128])
        self.dve(lambda e: e.tensor_tensor(out=nGe[:], in0=bc(gq[:, 1, :], 1, 8), in1=hmask[:], op=ALU.mult), r=[gq_r, hmask_r], w=[nGe_r])
        self.pool(lambda e: e.tensor_tensor(out=wie[:], in0=bc(gq[:, 2, :], 1, 8), in1=hmask[:], op=ALU.mult), r=[gq_r, hmask_r], w=[wie_r])
        self.dve(lambda e: e.tensor_tensor(out=wsg[:].rearrange("p (n t) -> p n t", t=8), in0=gq[:, 0, :].rearrange("p (n t) -> p n t", t=8),
                                           in1=bc(gq[:, 1, :].rearrange("p (n t) -> p n t", t=8)[:, :, 7], 2, 8), op=ALU.add), r=[gq_r], w=[wsg_r])
        self.act(lambda e: e.activation(out=wsg[:], in_=wsg[:], func=AF.Exp), r=[wsg_r], w=[wsg_r])
        wm, wm_r = T("swm", [128, 8, 128])
        PT, PT_r = T("sPT", [128, 8, 128], BF16)
        qw, qw_r = T("sqw", [128, 16, 128], BF16)
        wpv, wpv_r = T("swpv", [128, 8, N_S])
        emT, emT_r = T("semT", [128, 8])
        wstT, wstT_r = T("swstT", [128, 8])
        Wsbs = [T("sWsb", [128, 512]) for _ in range(2)]
        ps, ps_r = self.next_ps()
        self.pe(lambda e, ps=ps: e.matmul(ps[:, 0:8], lhsT=gq[:, 3, :], rhs=self.ident_f[0:8, 0:8], start=True, stop=True), r=[gq_r, self.ident_f_r], w=[ps_r])
        self.act(lambda e, ps=ps: e.copy(out=emT[:], in_=ps[:, 0:8]), r=[ps_r], w=[emT_r])
        ps, ps_r = self.next_ps()
        self.pe(lambda e, ps=ps: e.matmul(ps[:, 0:8], lhsT=wsg[:], rhs=self.ident_f[0:8, 0:8], start=True, stop=True), r=[wsg_r, self.ident_f_r], w=[ps_r])
        self.act(lambda e, ps=ps: e.copy(out=wstT[:], in_=ps[:, 0:8]), r=[ps_r], w=[wstT_r])
        dstop = getattr(self, "dbg_stop", 99)
        if dstop <= 1:
            return
        for hb in range(2):
            hs = slice(4 * hb, 4 * hb + 4)
            Dps, Dps_r = self.next_ps()
            Dv = Dps[:].rearrange("p (a t) -> p a t", t=128)
            self.pe(lambda e, Dv=Dv, hs=hs: e.matmul(Dv, lhsT=ones8[:], rhs=nGe[:, hs, :], start=True, stop=False), r=[ones8_r, nGe_r], w=[Dps_r])
            self.pe(lambda e, Dv=Dv, hs=hs: e.matmul(Dv, lhsT=gq[:, 0, :], rhs=hmask[:, hs, :], start=False, stop=True), r=[gq_r, hmask_r], w=[Dps_r])
            self.act(lambda e, Dv=Dv, hs=hs: e.activation(out=wm[:, hs, :], in_=Dv, func=AF.Exp), r=[Dps_r], w=[wm_r])
            self.dve(lambda e, hs=hs: e.tensor_tensor(out=wm[:, hs, :], in0=wm[:, hs, :], in1=bc(smask[:], 1, 4), op=ALU.mult), r=[wm_r, smask_r], w=[wm_r])
            if dstop <= 1.2:
                continue
            Wps, Wps_r = self.next_ps()
            Wv = Wps[:].rearrange("p (a t) -> p a t", t=128)
            self.pe(lambda e, Wv=Wv, hs=hs: e.matmul(Wv, lhsT=ones8[:], rhs=wie[:, hs, :], start=True, stop=True), r=[ones8_r, wie_r], w=[Wps_r])
            if dstop <= 1.3:
                continue
            Wsb, Wsb_r = Wsbs[hb]
            self.act(lambda e, Wps=Wps, Wsb=Wsb: e.copy(out=Wsb[:], in_=Wps[:]), r=[Wps_r], w=[Wsb_r])
            for k2 in range(2):
                self.dve(lambda e, Wsb=Wsb, hb=hb, k2=k2: e.tensor_tensor(
                    out=qw[:, 8 * hb:8 * hb + 8, :].rearrange("p (h k) t -> p h k t", k=2)[:, :, k2, :],
                    in0=qT[:, 8 * hb:8 * hb + 8, :].rearrange("p (h k) t -> p h k t", k=2)[:, :, k2, :],
                    in1=Wsb[:].rearrange("p (a t) -> p a t", t=128), op=ALU.mult), r=[qT_r, Wsb_r], w=[qw_r])
            self.dve(lambda e, Wsb=Wsb, hs=hs: e.tensor_copy(out=wpv[:, hs, :], in_=Wsb[:].rearrange("p (a n t) -> p a n t", n=N_S, t=8)[:, :, :, 7]),
                     r=[Wsb_r], w=[wpv_r])
            if dstop <= 1.5:
                continue
            Sps, Sps_r = self.next_ps()
            for hh in range(4):
                h = 4 * hb + hh
                for dk in range(2):
                    self.pe(lambda e, Sps=Sps, hh=hh, h=h, dk=dk: e.matmul(Sps[:, 128 * hh:128 * hh + 128], lhsT=kT[:, 2 * h + dk, :], rhs=qT[:, 2 * h + dk, :],
                                                                          start=(hh == 0 and dk == 0), stop=(dk == 1), skip_group_check=True),
                            r=[kT_r, qT_r], w=[Sps_r])
            self.dve(lambda e, Sps=Sps, hs=hs: e.scalar_tensor_tensor(out=PT[:, hs, :], in0=Sps[:].rearrange("p (a t) -> p a t", t=128), scalar=1.0 / 16.0,
                                                                      in1=wm[:, hs, :], op0=ALU.mult, op1=ALU.mult), r=[Sps_r, wm_r], w=[PT_r])
        if dstop <= 2:
            return
        hout, hout_r = T("shout", [128, 8, 256])
        numacc, numacc_r = T("numacc", [128, 257])
        tot, tot_r = T("stot", [128, 257])
        dn, dn_r = T("sdn", [128, 1])
        Vexps = [T("Vexp", [128, N_S, 257], BF16) for _ in range(2)]
        kws = [T("skw", [128, 256], BF16) for _ in range(2)]
        C0x = [T("C0x", [128, 2, 257]) for _ in range(2)]
        C0b = [T("C0b", [128, 2, 257], BF16) for _ in range(2)]
        Cn = [T("Cn", [128, 2, 257]) for _ in range(2)]
        NN, NN_r = T("sNN", [128, N_S, 8, 2])
        ci = 0
        for h in range(8):
            Vexp, Vexp_r = Vexps[h % 2]
            kw, kw_r = kws[h % 2]
            self.pool(lambda e, Vexp=Vexp, h=h: e.tensor_tensor(out=Vexp[:], in0=bc(vt[:, 257 * h:257 * h + 257], 1, N_S), in1=bc(seqsel[:], 2, 257),
                                                                op=ALU.mult), r=[vt_r, seqsel_r], w=[Vexp_r])
            self.act(lambda e, kw=kw, h=h: e.activation(out=kw[:], in_=kt[:, 256 * h:256 * h + 256], func=AF.Copy, scale=wstT[:, h:h + 1]),
                     r=[kt_r, wstT_r], w=[kw_r])
            for n in range(N_S):
                cx, cx_r = C0x[ci % 2]
                cb, cb_r = C0b[ci % 2]
                cn, cn_r = Cn[ci % 2]
                ci += 1
                self.ld(cx[:, :, 0:256], self.ml_c0[n, h].rearrange("(dk p) e -> p dk e", p=128), w=[cx_r])
                self.P.dma("sp", lambda e, cx=cx, n=n, h=h: e.dma_start(out=cx[:, :, 256], in_=self.ml_n0[n, h].rearrange("(dk p) -> p dk", p=128),
                                                                     allow_slow_non_contiguous=True), (), [cx_r])
                self.pool(lambda e, cx=cx, cb=cb: e.tensor_copy(out=cb[:], in_=cx[:]), r=[cx_r], w=[cb_r])
                ips, ips_r = self.next_ps()
                for dk in range(2):
                    self.pe(lambda e, ips=ips, h=h, dk=dk, cb=cb: e.matmul(ips[:, 0:257], lhsT=qw[:, 2 * h + dk, :], rhs=cb[:, dk, :], start=(dk == 0), stop=(dk == 1)),
                            r=[qw_r, cb_r], w=[ips_r])
                if n == 0:
                    self.dve(lambda e, ips=ips, n=n: e.tensor_scalar(out=numacc[:], in0=ips[:, 0:257], scalar1=seqsel[:, n:n + 1], scalar2=None, op0=ALU.mult),
                             r=[ips_r, seqsel_r], w=[numacc_r])
                else:
                    self.dve(lambda e, ips=ips, n=n: e.scalar_tensor_tensor(out=numacc[:], in0=ips[:, 0:257], scalar=seqsel[:, n:n + 1], in1=numacc[:],
                                                                           op0=ALU.mult, op1=ALU.add), r=[ips_r, seqsel_r, numacc_r], w=[numacc_r])
                for dk in range(2):
                    ups, ups_r = self.next_ps()
                    self.pe(lambda e, ups=ups, kw=kw, dk=dk, n=n, Vexp=Vexp: e.matmul(ups[:, 0:257], lhsT=kw[:, 128 * dk:128 * dk + 128], rhs=Vexp[:, n, :],
                                                                                    start=True, stop=True), r=[kw_r, Vexp_r], w=[ups_r])
                    self.dve(lambda e, ups=ups, cn=cn, cx=cx, dk=dk, h=h, n=n: e.scalar_tensor_tensor(out=cn[:, dk, :], in0=cx[:, dk, :], scalar=wpv[:, h, n:n + 1],
                                                                                                 in1=ups[:, 0:257], op0=ALU.mult, op1=ALU.add),
                             r=[cx_r, wpv_r, ups_r], w=[cn_r])
                self.store(self.o_mc_s[n, h].rearrange("(dk p) e -> p dk e", p=128), cn[:, :, 0:256], r=[cn_r])
                self.pool(lambda e, cn=cn, n=n, h=h: e.tensor_copy(out=NN[:, n, h, :], in_=cn[:, :, 256]), r=[cn_r], w=[NN_r])
            nps, nps_r = self.next_ps()
            self.pe(lambda e, nps=nps, h=h: e.matmul(nps[:, 0:257], lhsT=PT[:, h, :], rhs=vt[:, 257 * h:257 * h + 257], start=True, stop=True),
                    r=[PT_r, vt_r], w=[nps_r])
            self.dve(lambda e, nps=nps: e.tensor_tensor(out=tot[:], in0=nps[:, 0:257], in1=numacc[:], op=ALU.add), r=[nps_r, numacc_r], w=[tot_r])
            self.act(lambda e: e.activation(out=dn[:], in_=tot[:, 256:257], func=AF.Abs), r=[tot_r], w=[dn_r])
            self.dve(lambda e, h=h: e.tensor_tensor(out=dn[:], in0=dn[:], in1=emT[:, h:h + 1], op=ALU.max), r=[dn_r, emT_r], w=[dn_r])
            self.dve(lambda e: e.reciprocal(out=dn[:], in_=dn[:]), r=[dn_r], w=[dn_r])
            self.act(lambda e, h=h: e.activation(out=hout[:, h, :], in_=tot[:, 0:256], func=AF.Copy, scale=dn[:, 0:1]), r=[tot_r, dn_r], w=[hout_r])
        if dstop <= 3:
            return
        self.ml_post_tile(NTILE - 1, hout, hout_r, K_)
        if dstop <= 4:
            return
        NNv = NN[:].rearrange("p n h k -> p (n h k)")
        ntT, ntT_r = T("sntT", [128, 2, 128])
        for b in range(2):
            ps, ps_r = self.next_ps()
            self.pe(lambda e, ps=ps, b=b: e.transpose(out=ps[:, 0:128], in_=NNv[:, 128 * b:128 * b + 128], identity=self.ident_f[:]),
                    r=[NN_r, self.ident_f_r], w=[ps_r])
            self.act(lambda e, ps=ps, b=b: e.copy(out=ntT[:, b, :], in_=ps[:, 0:128]), r=[ps_r], w=[ntT_r])
        self.store(self.o_mn_s.rearrange("(b r) p -> r b p", b=2), ntT[:], r=[ntT_r])

    def final_phase(self, from_x=False):
        P = self.P
        with ExitStack() as st:
            gt, gt_r = P.sb(st, "gtf", [128, D_MODEL])
            self.ld(gt[:], self.final_norm.partition_broadcast(128), w=[gt_r])
            hts = [P.sb(st, "htf", [128, D_MODEL]) for _ in range(3)]
            junk, junk_r = P.sb(st, "junkf", [128, D_MODEL], BF16)
            sss = [P.sb(st, "ssf", [128, 1]) for _ in range(3)]
            for i in range(NTILE):
                ht, ht_r = hts[i % 3]
                ss, ss_r = sss[i % 3]
                self.ld(ht[:], self.h_src(from_x, i), r=[self.H_r[i]], w=[ht_r])
                self.act(lambda e, ht=ht, ss=ss: e.activation(out=junk[:], in_=ht[:], func=AF.Square, accum_out=ss[:]),
                         r=[ht_r], w=[junk_r, ss_r])
                self.dve(lambda e, ss=ss: e.tensor_scalar(out=ss[:], in0=ss[:], scalar1=1.0 / D_MODEL, scalar2=1e-6,
                                                          op0=ALU.mult, op1=ALU.add), r=[ss_r], w=[ss_r])
                self.act(lambda e, ss=ss: e.activation(out=ss[:], in_=ss[:], func=AF.Sqrt), r=[ss_r], w=[ss_r])
                self.dve(lambda e, ss=ss: e.reciprocal(out=ss[:], in_=ss[:]), r=[ss_r], w=[ss_r])
                self.dve(lambda e, ht=ht, ss=ss: e.scalar_tensor_tensor(out=ht[:], in0=ht[:], scalar=ss[:, 0:1], in1=gt[:],
                                                                        op0=ALU.mult, op1=ALU.mult), r=[ht_r, ss_r, gt_r], w=[ht_r])
                self.store(self.y[i * 128:(i + 1) * 128, :], ht[:], r=[ht_r])


_CACHE = {}


def _get_prog(nlayers=4):
    if nlayers not in _CACHE:
        kb = KB(nlayers=nlayers)
        kb.build()
        _CACHE[nlayers] = kb
    return _CACHE[nlayers]


def make_in_maps(inp, kb):
    f32 = np.float32
    maps = []
    ident = np.eye(128, dtype=f32)
    for c in range(8):
        b = c % 4
        m = {}
        xs = np.asarray(inp["x_sample"][16 * c:16 * c + 16], f32).reshape(T_S, D_MODEL)
        m["xin"] = np.concatenate([np.asarray(inp["x_prompt"][b], f32), xs], axis=0)
        m["ident_f"] = ident
        m["final_norm"] = np.asarray(inp["final_norm"], f32).reshape(1, D_MODEL)
        for k in ["ssm_norm", "ssm_w_in", "ssm_a_re", "ssm_a_im", "ssm_b_re", "ssm_b_im", "ssm_c_re", "ssm_c_im",
                  "ssm_d", "ssm_w_glu", "ssm_b_glu", "ssm_w_out"]:
            m[k] = np.asarray(inp[k], f32)
        m["ssm_log_dt"] = np.asarray(inp["ssm_log_dt"], f32).reshape(2, 128, 1)
        m["st_re"] = np.asarray(inp["state_ssm_re"][:, 16 * c:16 * c + 16], f32).reshape(2, N_S, 8192)
        m["st_im"] = np.asarray(inp["state_ssm_im"][:, 16 * c:16 * c + 16], f32).reshape(2, N_S, 8192)
        m["attn_norm"] = np.asarray(inp["attn_norm"], f32).reshape(1, D_MODEL)
        m["attn_w_in"] = np.asarray(inp["attn_w_in"], f32).reshape(D_MODEL, ATTN_IN)
        m["attn_w_out"] = np.asarray(inp["attn_w_out"], f32).reshape(E, D_MODEL)
        m["rope_cs"] = rope_table()
        for k in ["mlstm_conv_w", "mlstm_conv_b", "mlstm_w_q", "mlstm_w_k", "mlstm_w_v", "mlstm_w_o", "mlstm_skip"]:
            m[k] = np.asarray(inp[k][0], f32)
        m["mlstm_norm"] = np.asarray(inp["mlstm_norm"], f32).reshape(1, D_MODEL)
        m["mlstm_w_in"] = np.asarray(inp["mlstm_w_in"][0], f32)
        m["mlstm_b_o"] = np.asarray(inp["mlstm_b_o"], f32).reshape(1, E)
        m["mlstm_w_gates"] = np.asarray(inp["mlstm_w_gates"][0], f32)
        m["mlstm_b_gates"] = np.asarray(inp["mlstm_b_gates"], f32).reshape(16, 1)
        m["mlstm_ln_w"] = np.asarray(inp["mlstm_ln_w"], f32).reshape(1, E)
        m["mlstm_w_out"] = np.asarray(inp["mlstm_w_out"][0], f32)
        m["ml_c0"] = np.asarray(inp["state_mlstm_c"][0, 16 * c:16 * c + 16], f32)
        m["ml_n0"] = np.asarray(inp["state_mlstm_n"][0, 16 * c:16 * c + 16], f32)
        m["ml_m0"] = np.asarray(inp["state_mlstm_m"][0, 16 * c:16 * c + 16], f32)
        m["ml_conv0"] = np.asarray(inp["state_mlstm_conv"][0, 16 * c:16 * c + 16], f32).reshape(N_S * 3, E)
        m["hmask"] = np.repeat(np.eye(8, dtype=f32), 128, axis=1)
        m["ones8"] = np.ones((8, 128), f32)
        m["cm64"] = (np.arange(64)[:, None] <= np.arange(64)[None, :]).astype(f32)
        m["seqsel"] = (np.arange(128)[:, None] // 8 == np.arange(N_S)[None, :]).astype(f32)
        ii = np.arange(128)
        m["smask"] = ((ii[:, None] // 8 == ii[None, :] // 8) & (ii[:, None] % 8 <= ii[None, :] % 8)).astype(f32)
        m["cmask"] = np.where(np.arange(128)[None, :] <= np.arange(128)[:, None], 0.0, -1e30).astype(f32)
        m["cmask_s"] = np.where(np.arange(8)[None, :] <= (np.arange(128) % 8)[:, None], 0.0, -1e30).astype(f32)
        m["selm"] = (np.arange(64)[:, None] % 8 == np.arange(8)[None, :]).astype(f32)
        m["blockm"] = (np.arange(128)[:, None] // 8 == np.arange(128)[None, :] // 8).astype(f32)
        m["iota_p"] = np.arange(128, dtype=f32).reshape(128, 1)
        m["page_table"] = np.asarray(inp["page_table"][16 * c:16 * c + 16], np.int32).reshape(1, N_S * 16)
        m["cache_k"] = np.asarray(inp["cache_k"], f32).reshape(2560 * 128, 256)
        m["cache_v"] = np.asarray(inp["cache_v"], f32).reshape(2560 * 128, 256)
        m["cache_kidx"] = np.asarray(inp["cache_kidx"], f32).reshape(2560 * 128, 64)
        maps.append({k: np.ascontiguousarray(v) for k, v in m.items() if k in kb.inputs})
    return maps


def rope_table():
    half = 32
    inv = (np.float32(10000.0) ** (-np.arange(half, dtype=np.float32) / np.float32(half))).astype(np.float32)
    pos = np.concatenate([np.arange(T_P), np.tile(2048 + np.arange(8), N_S)]).astype(np.float32)
    ang = (pos[:, None] * inv[None, :]).astype(np.float32)
    return np.concatenate([np.cos(ang), np.sin(ang)], axis=1).astype(np.float32)


def kernel(**inp):
    kb = _get_prog(4)
    maps = make_in_maps(inp, kb)
    res = run_bass_kernel_spmd(kb.nc, maps, core_ids=list(range(8))).results
    return assemble(res)


def assemble(res):
    f32 = np.float32

    def A(x):
        return np.ascontiguousarray(np.asarray(x, f32))
    P4 = range(4)
    C8 = range(8)
    y_p = A(np.stack([res[b]["y"][:T_P] for b in P4]))
    y_s = A(np.concatenate([res[c]["y"][T_P:].reshape(N_S, 8, D_MODEL) for c in C8]))
    sre_p = A(np.stack([res[b]["o_sre_p"].reshape(2, 128, 64) for b in P4], axis=1))
    sim_p = A(np.stack([res[b]["o_sim_p"].reshape(2, 128, 64) for b in P4], axis=1))
    sre_s = A(np.concatenate([res[c]["o_sre_s"].reshape(2, N_S, 128, 64) for c in C8], axis=1))
    sim_s = A(np.concatenate([res[c]["o_sim_s"].reshape(2, N_S, 128, 64) for c in C8], axis=1))
    k_p = A(np.stack([res[b]["o_k_p"].reshape(T_P, 4, 64) for b in P4]))[None]
    v_p = A(np.stack([res[b]["o_v_p"].reshape(T_P, 4, 64) for b in P4]))[None]
    ki_p = A(np.stack([res[b]["o_kidx_p"].reshape(T_P, 64) for b in P4]))[None]
    k_s = A(np.concatenate([res[c]["o_k_s"].reshape(N_S, 8, 4, 64) for c in C8]))[None]
    v_s = A(np.concatenate([res[c]["o_v_s"].reshape(N_S, 8, 4, 64) for c in C8]))[None]
    ki_s = A(np.concatenate([res[c]["o_kidx_s"].reshape(N_S, 8, 64) for c in C8]))[None]
    mc_p = A(np.stack([res[b]["o_mc_p"].reshape(8, 256, 256) for b in P4]))[None]
    mn_p = A(np.stack([res[b]["o_mn_p"].reshape(8, 256) for b in P4]))[None]
    mm_p = A(np.stack([res[b]["o_mm_p"].reshape(8) for b in P4]))[None]
    mv_p = A(np.stack([res[b]["o_mconv_p"].reshape(3, E) for b in P4]))[None]
    mc_s = A(np.concatenate([res[c]["o_mc_s"].reshape(N_S, 8, 256, 256) for c in C8]))[None]
    mn_s = A(np.concatenate([res[c]["o_mn_s"].reshape(N_S, 8, 256) for c in C8]))[None]
    mm_s = A(np.concatenate([res[c]["o_mm_s"].reshape(8, N_S).T for c in C8]))[None]
    mv_s = A(np.concatenate([res[c]["o_mconv_s"].reshape(N_S, 3, E) for c in C8]))[None]
    return (y_p, y_s, sre_p, sim_p, sre_s, sim_s, k_p, v_p, ki_p, k_s, v_s, ki_s,
            mc_p, mn_p, mm_p, mv_p, mc_s, mn_s, mm_s, mv_s)
```

```python
import math
import numpy as np
from contextlib import ExitStack
import concourse.bass as bass
import concourse.mybir as mybir
from concourse.bass_utils import run_bass_kernel_spmd

F32 = mybir.dt.float32
BF16 = mybir.dt.bfloat16
I32 = mybir.dt.int32
ALU = mybir.AluOpType
AF = mybir.ActivationFunctionType
AX = mybir.AxisListType

D_MODEL = 1024
E = 2048
T_P = 4096
N_S = 16
T_S = 128
TOK = T_P + T_S
NTILE = TOK // 128
NGRP = 9
TWO_PI = 2.0 * math.pi
ATTN_IN = 5192


def grp_tok(tg):
    return (tg * 512, 512) if tg < 8 else (T_P, T_S)


class Res:
    __slots__ = ("name", "w", "r")

    def __init__(self, name=""):
        self.name = name
        self.w = None
        self.r = {}


class Prog:
    ENG = ["pe", "dve", "act", "pool", "sp"]
    NS = 8

    def __init__(self, nc, stack):
        self.nc = nc
        self.q = {e: [] for e in self.ENG}
        self.cnt = {e: 0 for e in self.ENG}
        self.waited = {e: {} for e in self.ENG}
        self.ndma = {e: 0 for e in self.ENG}
        self.latest = {}
        self.sem = {}
        for e in ["pe", "dve", "act", "pool"]:
            self.sem[e] = stack.enter_context(nc.semaphore("s_" + e))
        for e in ["sp", "pool", "act"]:
            for i in range(self.NS):
                self.sem[("dma", e, i)] = stack.enter_context(nc.semaphore("d_%s_%d" % (e, i)))
        self.out_tokens = []
        self.uid = 0

    def sb(self, st, name, shape, dtype=F32):
        self.uid += 1
        t = st.enter_context(self.nc.sbuf_tensor("%s_%d" % (name, self.uid), list(shape), dtype))
        return t, Res(name)

    def ps(self, st, name, shape, dtype=F32):
        self.uid += 1
        t = st.enter_context(self.nc.psum_tensor("%s_%d" % (name, self.uid), list(shape), dtype))
        return t, Res(name)

    def _deps(self, reads, writes):
        deps = {}
        for r in reads:
            if r.w is not None:
                k, v = r.w
                if deps.get(k, 0) < v:
                    deps[k] = v
        for w in writes:
            if w.w is not None:
                k, v = w.w
                if deps.get(k, 0) < v:
                    deps[k] = v
            for k, v in w.r.items():
                if deps.get(k, 0) < v:
                    deps[k] = v
        return deps

    def _waits(self, eng, deps):
        ws = []
        wd = self.waited[eng]
        for k, v in deps.items():
            if eng == "pe" and k == "pe":
                continue
            if wd.get(k, 0) < v:
                wd[k] = v
                ws.append((self.sem[k], v))
        return ws

    def _update(self, tok, reads, writes):
        k, v = tok
        self.latest[k] = v
        for r in reads:
            if r.r.get(k, 0) < v:
                r.r[k] = v
        for w in writes:
            w.w = tok
            w.r = {}

    def op(self, eng, fn, reads=(), writes=()):
        deps = self._deps(reads, writes)
        ws = self._waits(eng, deps)
        self.cnt[eng] += 1
        tok = (eng, self.cnt[eng])
        sem = self.sem[eng]

        def emit(e, ws=ws, fn=fn, sem=sem):
            for s, v in ws:
                e.wait_ge(s, v)
            fn(e).then_inc(sem, 1)

        self.q[eng].append(emit)
        self._update(tok, reads, writes)
        return tok

    def dma(self, queue, fn, reads=(), writes=(), is_output=False):
        deps = self._deps(reads, writes)
        n = self.ndma[queue]
        self.ndma[queue] += 1
        idx = n % self.NS
        target = 16 * (n // self.NS + 1)
        key = ("dma", queue, idx)
        if target > 16 and deps.get(key, 0) < target - 16:
            deps[key] = target - 16
        ws = self._waits(queue, deps)
        sem = self.sem[key]

        def emit(e, ws=ws, fn=fn, sem=sem):
            for s, v in ws:
                e.wait_ge(s, v)
            fn(e).then_inc(sem, 16)

        self.q[queue].append(emit)
        tok = (key, target)
        self._update(tok, reads, writes)
        if is_output:
            self.out_tokens.append(tok)
        return tok

    def barrier(self):
        deps = dict(self.latest)
        for eng in self.ENG:
            ws = self._waits(eng, deps)
            if ws:
                def emit(e, ws=ws):
                    for s, v in ws:
                        e.wait_ge(s, v)
                self.q[eng].append(emit)

    def finish(self):
        self.barrier()

    def emit_all(self):
        nc = self.nc
        q = self.q
        with nc.Block() as block:
            @block.sync
            def _(e):
                for f in q["sp"]:
                    f(e)

            @block.tensor
            def _(e):
                for f in q["pe"]:
                    f(e)

            @block.vector
            def _(e):
                for f in q["dve"]:
                    f(e)

            @block.scalar
            def _(e):
                for f in q["act"]:
                    f(e)

            @block.gpsimd
            def _(e):
                for f in q["pool"]:
                    f(e)


def bc(ap, axis, n):
    a = ap.unsqueeze(axis)
    shp = list(a.shape)
    shp[axis] = n
    return a.broadcast_to(shp)


class KB:
    def __init__(self, nlayers=4, dbg=False):
        self.nlayers = nlayers
        self.dsa_stage = 3
        self.ml_stage = 3
        self.dbg = dbg
        nc = bass.Bass("TRN2", target_bir_lowering=False)
        self.nc = nc
        self.inputs = {}
        self.outputs = {}

    def din(self, name, shape, dtype=F32):
        t = self.nc.dram_tensor(name, list(shape), dtype, kind="ExternalInput").ap()
        self.inputs[name] = (tuple(shape), dtype)
        return t

    def dout(self, name, shape, dtype=F32):
        t = self.nc.dram_tensor(name, list(shape), dtype, kind="ExternalOutput").ap()
        self.outputs[name] = tuple(shape)
        return t

    def dscr(self, name, shape, dtype=F32):
        return self.nc.dram_tensor(name, list(shape), dtype, kind="Internal").ap()

    def dve(self, fn, r=(), w=()):
        return self.P.op("dve", fn, r, w)

    def act(self, fn, r=(), w=()):
        return self.P.op("act", fn, r, w)

    def pool(self, fn, r=(), w=()):
        return self.P.op("pool", fn, r, w)

    def pe(self, fn, r=(), w=()):
        return self.P.op("pe", fn, r, w)

    def ld(self, out, in_, r=(), w=(), q="sp"):
        return self.P.dma(q, lambda e: e.dma_start(out=out, in_=in_), r, w)

    def ldc(self, out, in_, r=(), w=()):
        return self.P.dma("pool", lambda e: e.dma_start(out=out, in_=in_), r, w)

    def store(self, out, in_, r=(), w=(), q="sp"):
        return self.P.dma(q, lambda e: e.dma_start(out=out, in_=in_), r, w, is_output=True)

    def next_ps(self):
        self.ps_i = (self.ps_i + 1) % self.ps_lim
        return self.psb[self.ps_i]

    def build(self):
        nc = self.nc
        self.xin = self.din("xin", [TOK, D_MODEL])
        self.ident_f_d = self.din("ident_f", [128, 128])
        self.final_norm = self.din("final_norm", [1, D_MODEL])
        self.ssm_norm = self.din("ssm_norm", [2, D_MODEL])
        self.ssm_w_in = self.din("ssm_w_in", [2, D_MODEL, 2 * E])
        self.ssm_a_re = self.din("ssm_a_re", [2, 128, 64])
        self.ssm_a_im = self.din("ssm_a_im", [2, 128, 64])
        self.ssm_log_dt = self.din("ssm_log_dt", [2, 128, 1])
        self.ssm_b_re = self.din("ssm_b_re", [2, 128, 64, 16])
        self.ssm_b_im = self.din("ssm_b_im", [2, 128, 64, 16])
        self.ssm_c_re = self.din("ssm_c_re", [2, 128, 16, 64])
        self.ssm_c_im = self.din("ssm_c_im", [2, 128, 16, 64])
        self.ssm_d = self.din("ssm_d", [2, E])
        self.ssm_w_glu = self.din("ssm_w_glu", [2, E, E])
        self.ssm_b_glu = self.din("ssm_b_glu", [2, E])
        self.ssm_w_out = self.din("ssm_w_out", [2, E, D_MODEL])
        self.st_re = self.din("st_re", [2, N_S, 8192])
        self.st_im = self.din("st_im", [2, N_S, 8192])
        self.attn_norm = self.din("attn_norm", [1, D_MODEL])
        self.attn_w_in = self.din("attn_w_in", [D_MODEL, ATTN_IN])
        self.attn_w_out = self.din("attn_w_out", [E, D_MODEL])
        self.rope_cs = self.din("rope_cs", [TOK, 64])
        self.cmask_d = self.din("cmask", [128, 128])
        self.cmask_s_d = self.din("cmask_s", [128, 8])
        self.selm_d = self.din("selm", [64, 8])
        self.blockm_d = self.din("blockm", [128, 128])
        self.iota_d = self.din("iota_p", [128, 1])
        self.page_table = self.din("page_table", [1, N_S * 16], I32)
        self.cache_k = self.din("cache_k", [2560 * 128, 256])
        self.cache_v = self.din("cache_v", [2560 * 128, 256])
        self.cache_kidx = self.din("cache_kidx", [2560 * 128, 64])
        self.os_d = self.dscr("os_d", [T_S, E])
        self.o_k_p = self.dout("o_k_p", [T_P, 256])
        self.o_v_p = self.dout("o_v_p", [T_P, 256])
        self.o_kidx_p = self.dout("o_kidx_p", [T_P, 64])
        self.o_k_s = self.dout("o_k_s", [T_S, 256])
        self.o_v_s = self.dout("o_v_s", [T_S, 256])
        self.o_kidx_s = self.dout("o_kidx_s", [T_S, 64])
        self.qT_d = self.dscr("qT_d", [16, 128, TOK], BF16)
        self.kT2_d = self.dscr("kT2_d", [4, 128, TOK], BF16)
        self.qiT_d = self.dscr("qiT_d", [4, 128, TOK], BF16)
        self.kiT2_d = self.dscr("kiT2_d", [1, 128, TOK], BF16)
        self.Vx_d = self.dscr("Vx_d", [TOK, 260], BF16)
        self.sz_d = self.dscr("sz_d", [TOK, E], BF16)
        self.wi_d = self.dscr("wi_d", [TOK, 8])
        self.qTs_d = self.dscr("qTs_d", [64, 32, T_S], BF16)
        self.qiTs_d = self.dscr("qiTs_d", [64, 8, T_S], BF16)
        self.dsa_r = Res("dsa_scratch")
        self.ml_norm = self.din("mlstm_norm", [1, D_MODEL])
        self.ml_w_in = self.din("mlstm_w_in", [D_MODEL, 2 * E])
        self.ml_conv_w = self.din("mlstm_conv_w", [4, E])
        self.ml_conv_b = self.din("mlstm_conv_b", [E])
        self.ml_w_q = self.din("mlstm_w_q", [8, 256, 256])
        self.ml_w_k = self.din("mlstm_w_k", [8, 256, 256])
        self.ml_w_v = self.din("mlstm_w_v", [8, 256, 256])
        self.ml_w_o = self.din("mlstm_w_o", [8, 256, 256])
        self.ml_b_o = self.din("mlstm_b_o", [1, E])
        self.ml_w_gates = self.din("mlstm_w_gates", [3 * E, 16])
        self.ml_b_gates = self.din("mlstm_b_gates", [16, 1])
        self.ml_ln_w = self.din("mlstm_ln_w", [1, E])
        self.ml_skip = self.din("mlstm_skip", [E])
        self.ml_w_out = self.din("mlstm_w_out", [E, D_MODEL])
        self.ml_c0 = self.din("ml_c0", [N_S, 8, 256, 256])
        self.ml_n0 = self.din("ml_n0", [N_S, 8, 256])
        self.ml_m0 = self.din("ml_m0", [N_S, 8])
        self.ml_conv0 = self.din("ml_conv0", [N_S * 3, E])
        self.hmask_d = self.din("hmask", [8, 8 * 128])
        self.ones8_d = self.din("ones8", [8, 128])
        self.cm64_d = self.din("cm64", [64, 64])
        self.seqsel_d = self.din("seqsel", [128, N_S])
        self.smask_d = self.din("smask", [128, 128])
        self.o_mc_p = self.dout("o_mc_p", [8, 256, 256])
        self.o_mn_p = self.dout("o_mn_p", [16, 128])
        self.o_mm_p = self.dout("o_mm_p", [8, 1])
        self.o_mconv_p = self.dout("o_mconv_p", [3, E])
        self.o_mc_s = self.dout("o_mc_s", [N_S, 8, 256, 256])
        self.o_mn_s = self.dout("o_mn_s", [N_S * 16, 128])
        self.o_mm_s = self.dout("o_mm_s", [8, N_S])
        self.o_mconv_s = self.dout("o_mconv_s", [N_S, 3, E])
        self.muT_d = self.dscr("muT_d", [16, 128, TOK], BF16)
        self.mca_d = self.dscr("mca_d", [16, 128, TOK], BF16)
        self.mq_d = self.dscr("mq_d", [16, 128, TOK], BF16)
        self.mk_d = self.dscr("mk_d", [16, 128, TOK], BF16)
        self.mvt_d = self.dscr("mvt_d", [TOK, 8 * 257], BF16)
        self.mkt_d = self.dscr("mkt_d", [TOK, E], BF16)
        self.mo_d = self.dscr("mo_d", [TOK, E], BF16)
        self.gq_d = self.dscr("gq_d", [4, 8, TOK])
        self.ml_r = Res("ml_scratch")
        self.y = self.dout("y", [TOK, D_MODEL])
        self.o_sre_p = self.dout("o_sre_p", [2, 64, 128])
        self.o_sim_p = self.dout("o_sim_p", [2, 64, 128])
        self.o_sre_s = self.dout("o_sre_s", [2, N_S, 64, 128])
        self.o_sim_s = self.dout("o_sim_s", [2, N_S, 64, 128])
        self.H = self.dscr("H", [TOK, D_MODEL])
        self.H_r = [Res("H%d" % i) for i in range(NTILE)]
        self.uT_d = self.dscr("uT_d", [16, 128, 8, 8, 64], BF16)
        self.uTs_d = self.dscr("uTs_d", [16, 128, 8, N_S], BF16)
        self.szT_d = self.dscr("szT_d", [16, 128, TOK], BF16)
        self.gT_d = self.dscr("gT_d", [16, 128, TOK], BF16)
        self.Wd = self.dscr("Wd", [16, 128, 16, 64])
        self.Vd = self.dscr("Vd", [16, 128, 64, 16])
        self.Kd = self.dscr("Kd", [8, 128, 16, 16])
        self.KCd = self.dscr("KCd", [128, 64, 30])
        self.uT_r = [Res() for _ in range(16)]
        self.szT_r = [Res() for _ in range(16)]
        self.gT_r = [Res() for _ in range(16)]
        self.Wd_r, self.Vd_r, self.Kd_r, self.KCd_r = Res(), Res(), Res(), Res()

        with ExitStack() as gst:
            self.P = P = Prog(nc, gst)
            self.psb = [P.ps(gst, "psb", [128, 512]) for _ in range(7)]
            self.pst = P.ps(gst, "pst", [128, 1024], BF16)
            self.ps_i = 0
            self.ps_lim = 7
            self.ident_f, self.ident_f_r = P.sb(gst, "identf", [128, 128])
            self.ident_b, self.ident_b_r = P.sb(gst, "identb", [128, 128], BF16)
            self.ld(self.ident_f[:], self.ident_f_d, w=[self.ident_f_r])
            self.ldc(self.ident_b[:], self.ident_f_d, w=[self.ident_b_r])

            kinds = [0, 1, 2, 0]
            slots = [0, 0, 0, 1]
            if getattr(self, "only", None) == "ml_sample":
                with ExitStack() as st:
                    self.ml_sample(st)
                P.barrier()
                self.nlayers = 0
            for layer in range(self.nlayers):
                if kinds[layer] == 0:
                    self.s5_layer(slots[layer], first=(layer == 0))
                elif kinds[layer] == 1:
                    self.dsa_layer()
                else:
                    self.ml_layer()
                P.barrier()
            self.final_phase(from_x=(self.nlayers == 0))
            P.finish()
            P.emit_all()
        return nc

    def norm_tile(self, src_ap, src_r, gt, gt_r, ht, ht_r, junk, junk_r, ss, ss_r, xn, xn_r, xT_out, xT_r):
        self.ld(ht[:], src_ap, r=[src_r], w=[ht_r])
        self.act(lambda e: e.activation(out=junk[:], in_=ht[:], func=AF.Square, accum_out=ss[:]),
                 r=[ht_r], w=[junk_r, ss_r])
        self.dve(lambda e: e.tensor_scalar(out=ss[:], in0=ss[:], scalar1=1.0 / D_MODEL, scalar2=1e-6,
                                           op0=ALU.mult, op1=ALU.add), r=[ss_r], w=[ss_r])
        self.act(lambda e: e.activation(out=ss[:], in_=ss[:], func=AF.Sqrt), r=[ss_r], w=[ss_r])
        self.dve(lambda e: e.reciprocal(out=ss[:], in_=ss[:]), r=[ss_r], w=[ss_r])
        self.dve(lambda e: e.scalar_tensor_tensor(out=xn[:], in0=ht[:], scalar=ss[:, 0:1], in1=gt[:],
                                                  op0=ALU.mult, op1=ALU.mult), r=[ht_r, ss_r, gt_r], w=[xn_r])
        pt, pt_r = self.pst
        for k in range(8):
            self.pe(lambda e, k=k: e.transpose(out=pt[:, k * 128:(k + 1) * 128], in_=xn[:, k * 128:(k + 1) * 128],
                                               identity=self.ident_b[:]), r=[xn_r, self.ident_b_r], w=[pt_r])
        self.act(lambda e: e.copy(out=xT_out, in_=pt[:].rearrange("p (k t) -> p k t", k=8)), r=[pt_r], w=[xT_r])

    def h_src(self, first, i):
        src = self.xin if first else self.H
        return src[i * 128:(i + 1) * 128, :]

    def s5_layer(self, slot, first):
        P = self.P
        nc = self.nc
        with ExitStack() as st:
            self.s5_setup(st, slot)
            win, win_r = P.sb(st, "win", [128, 8, 2 * E], BF16)
            wv = self.ssm_w_in[slot].rearrange("(k p) n -> p k n", p=128)
            for k in range(8):
                for hf in range(2):
                    self.ldc(win[:, k, hf * E:(hf + 1) * E], wv[:, k, hf * E:(hf + 1) * E], w=[win_r])
            gt, gt_r = P.sb(st, "gt", [128, D_MODEL])
            self.ld(gt[:], self.ssm_norm[slot:slot + 1, :].partition_broadcast(128), w=[gt_r])
            hts = [P.sb(st, "ht", [128, D_MODEL]) for _ in range(2)]
            junk, junk_r = P.sb(st, "junk", [128, D_MODEL], BF16)
            sss = [P.sb(st, "ss", [128, 1]) for _ in range(2)]
            xns = [P.sb(st, "xn", [128, D_MODEL], BF16) for _ in range(2)]
            xTs = [P.sb(st, "xT", [128, 8, 512], BF16) for _ in range(2)]
            obufs = [P.sb(st, "obuf", [128, 512], BF16) for _ in range(4)]
            ob_i = 0
            ti = 0
            for tg in range(NGRP):
                tok0, ntok = grp_tok(tg)
                xT, xT_r = xTs[tg % 2]
                for il in range(ntok // 128):
                    i = tok0 // 128 + il
                    ht, ht_r = hts[ti % 2]
                    ss, ss_r = sss[ti % 2]
                    xn, xn_r = xns[ti % 2]
                    ti += 1
                    self.norm_tile(self.h_src(first, i), self.H_r[i], gt, gt_r, ht, ht_r, junk, junk_r, ss, ss_r,
                                   xn, xn_r, xT[:, :, il * 128:(il + 1) * 128], xT_r)
                for fo in range(32):
                    ps, ps_r = self.next_ps()
                    for k in range(8):
                        self.pe(lambda e, k=k, fo=fo, ps=ps, xT=xT, ntok=ntok: e.matmul(
                            ps[:, 0:ntok], lhsT=win[:, k, fo * 128:(fo + 1) * 128], rhs=xT[:, k, 0:ntok],
                            start=(k == 0), stop=(k == 7)), r=[win_r, xT_r], w=[ps_r])
                    ob, ob_r = obufs[ob_i % 4]
                    ob_i += 1
                    if fo < 16:
                        if tg < 8:
                            self.act(lambda e, ob=ob, ps=ps: e.copy(
                                out=ob[:].rearrange("p (s c) -> p s c", s=8),
                                in_=ps[:].rearrange("p (c s) -> p s c", s=8)), r=[ps_r], w=[ob_r])
                            self.ld(self.uT_d[fo, :, tg, :, :], ob[:].rearrange("p (s c) -> p s c", s=8),
                                    r=[ob_r], w=[self.uT_r[fo]])
                        else:
                            self.act(lambda e, ob=ob, ps=ps: e.copy(
                                out=ob[:, 0:128].rearrange("p (s c) -> p s c", s=8),
                                in_=ps[:, 0:128].rearrange("p (c s) -> p s c", s=8)), r=[ps_r], w=[ob_r])
                            self.ld(self.uTs_d[fo], ob[:, 0:128].rearrange("p (s c) -> p s c", s=8),
                                    r=[ob_r], w=[self.uT_r[fo]])
                    else:
                        self.act(lambda e, ob=ob, ps=ps, ntok=ntok: e.activation(
                            out=ob[:, 0:ntok], in_=ps[:, 0:ntok], func=AF.Silu), r=[ps_r], w=[ob_r])
                        self.ld(self.szT_d[fo - 16, :, tok0:tok0 + ntok], ob[:, 0:ntok], r=[ob_r], w=[self.szT_r[fo - 16]])
        P.barrier()
        with ExitStack() as st:
            self.s5_scan(st, slot)
        P.barrier()
        with ExitStack() as st:
            self.s5_out(st, slot, first)

    def s5_setup(self, st0, slot):
        P = self.P
        with ExitStack() as st:
            def T(name, shape, dt=F32):
                return P.sb(st, name, shape, dt)
            ar, ar_r = T("ar", [128, 64])
            ai, ai_r = T("ai", [128, 64])
            ldt, ldt_r = T("ldt", [128, 1])
            self.ld(ar[:], self.ssm_a_re[slot], w=[ar_r])
            self.ld(ai[:], self.ssm_a_im[slot], w=[ai_r])
            self.ld(ldt[:], self.ssm_log_dt[slot], w=[ldt_r])
            dt_, dt_r = T("dt", [128, 1])
            self.act(lambda e: e.activation(out=dt_[:], in_=ldt[:], func=AF.Exp), r=[ldt_r], w=[dt_r])
            mag, mag_r = T("mag", [128, 64])
            self.act(lambda e: e.activation(out=mag[:], in_=ar[:], func=AF.Exp, scale=dt_[:, 0:1]), r=[ar_r, dt_r], w=[mag_r])
            qq, qq_r = T("qq", [128, 2, 64])
            self.dve(lambda e: e.tensor_scalar(out=qq[:, 0, :], in0=ai[:], scalar1=dt_[:, 0:1], scalar2=1.0 / TWO_PI,
                                               op0=ALU.mult, op1=ALU.mult), r=[ai_r, dt_r], w=[qq_r])
            self.dve(lambda e: e.tensor_scalar(out=qq[:, 1, :], in0=qq[:, 0, :], scalar1=0.25, scalar2=None, op0=ALU.add),
                     r=[qq_r], w=[qq_r])
            qi_, qi_r = T("qi", [128, 2, 64], I32)
            qf, qf_r = T("qf", [128, 2, 64])
            self.dve(lambda e: e.tensor_copy(out=qi_[:], in_=qq[:]), r=[qq_r], w=[qi_r])
            self.dve(lambda e: e.tensor_copy(out=qf[:], in_=qi_[:]), r=[qi_r], w=[qf_r])
            self.dve(lambda e: e.tensor_tensor(out=qq[:], in0=qq[:], in1=qf[:], op=ALU.subtract), r=[qq_r, qf_r], w=[qq_r])
            self.dve(lambda e: e.tensor_scalar(out=qq[:], in0=qq[:], scalar1=0.5, scalar2=-0.5, op0=ALU.min, op1=ALU.max),
                     r=[qq_r], w=[qq_r])
            sc, sc_r = T("sc", [128, 2, 64])
            self.act(lambda e: e.activation(out=sc[:], in_=qq[:], func=AF.Sin, scale=TWO_PI), r=[qq_r], w=[sc_r])
            Ap, Ap_r = T("Ap", [128, 9, 2, 64])
            self.pool(lambda e: e.memset(Ap[:, 0, 0, :], 1.0), w=[Ap_r])
            self.pool(lambda e: e.memset(Ap[:, 0, 1, :], 0.0), w=[Ap_r])
            self.dve(lambda e: e.tensor_tensor(out=Ap[:, 1, 0, :], in0=mag[:], in1=sc[:, 1, :], op=ALU.mult), r=[mag_r, sc_r], w=[Ap_r])
            self.dve(lambda e: e.tensor_tensor(out=Ap[:, 1, 1, :], in0=mag[:], in1=sc[:, 0, :], op=ALU.mult), r=[mag_r, sc_r], w=[Ap_r])
            t1, t1_r = T("t1", [128, 64])
            t2, t2_r = T("t2", [128, 64])

            def cmul(o_re, o_im, a_re, a_im, b_re, b_im, rr, ww):
                self.dve(lambda e: e.tensor_tensor(out=t1[:], in0=a_re, in1=b_re, op=ALU.mult), r=rr, w=[t1_r])
                self.dve(lambda e: e.tensor_tensor(out=t2[:], in0=a_im, in1=b_im, op=ALU.mult), r=rr, w=[t2_r])
                self.dve(lambda e: e.tensor_tensor(out=o_re, in0=t1[:], in1=t2[:], op=ALU.subtract), r=[t1_r, t2_r], w=ww)
                self.dve(lambda e: e.tensor_tensor(out=t1[:], in0=a_re, in1=b_im, op=ALU.mult), r=rr, w=[t1_r])
                self.dve(lambda e: e.tensor_tensor(out=t2[:], in0=a_im, in1=b_re, op=ALU.mult), r=rr, w=[t2_r])
                self.dve(lambda e: e.tensor_tensor(out=o_im, in0=t1[:], in1=t2[:], op=ALU.add), r=[t1_r, t2_r], w=ww)

            for tau in range(2, 9):
                cmul(Ap[:, tau, 0, :], Ap[:, tau, 1, :], Ap[:, tau - 1, 0, :], Ap[:, tau - 1, 1, :],
                     Ap[:, 1, 0, :], Ap[:, 1, 1, :], [Ap_r], [Ap_r])
            KC, KC_r = T("KC", [128, 64, 10, 3])
            self.dve(lambda e: e.tensor_copy(out=KC[:, :, 0, 0], in_=Ap[:, 8, 0, :]), r=[Ap_r], w=[KC_r])
            self.dve(lambda e: e.tensor_copy(out=KC[:, :, 0, 1], in_=Ap[:, 8, 1, :]), r=[Ap_r], w=[KC_r])
            for k in range(1, 10):
                cmul(KC[:, :, k, 0], KC[:, :, k, 1], KC[:, :, k - 1, 0], KC[:, :, k - 1, 1],
                     KC[:, :, k - 1, 0], KC[:, :, k - 1, 1], [KC_r], [KC_r])
            self.dve(lambda e: e.tensor_scalar(out=KC[:, :, :, 2], in0=KC[:, :, :, 1], scalar1=-1.0, scalar2=None, op0=ALU.mult),
                     r=[KC_r], w=[KC_r])
            self.ld(self.KCd, KC[:].rearrange("g p k c -> g p (k c)"), r=[KC_r], w=[self.KCd_r])
            den, den_r = T("den", [128, 64])
            self.dve(lambda e: e.tensor_tensor(out=den[:], in0=ar[:], in1=ar[:], op=ALU.mult), r=[ar_r], w=[den_r])
            self.dve(lambda e: e.tensor_tensor(out=t1[:], in0=ai[:], in1=ai[:], op=ALU.mult), r=[ai_r], w=[t1_r])
            self.dve(lambda e: e.tensor_tensor(out=den[:], in0=den[:], in1=t1[:], op=ALU.add), r=[den_r, t1_r], w=[den_r])
            self.dve(lambda e: e.reciprocal(out=den[:], in_=den[:]), r=[den_r], w=[den_r])
            zr, zr_r = T("zr", [128, 64])
            self.dve(lambda e: e.tensor_scalar(out=zr[:], in0=Ap[:, 1, 0, :], scalar1=-1.0, scalar2=None, op0=ALU.add), r=[Ap_r], w=[zr_r])
            Ff, Ff_r = T("Ff", [128, 2, 64])
            self.dve(lambda e: e.tensor_tensor(out=t1[:], in0=zr[:], in1=ar[:], op=ALU.mult), r=[zr_r, ar_r], w=[t1_r])
            self.dve(lambda e: e.tensor_tensor(out=t2[:], in0=Ap[:, 1, 1, :], in1=ai[:], op=ALU.mult), r=[Ap_r, ai_r], w=[t2_r])
            self.dve(lambda e: e.tensor_tensor(out=t1[:], in0=t1[:], in1=t2[:], op=ALU.add), r=[t1_r, t2_r], w=[t1_r])
            self.dve(lambda e: e.tensor_tensor(out=Ff[:, 0, :], in0=t1[:], in1=den[:], op=ALU.mult), r=[t1_r, den_r], w=[Ff_r])
            self.dve(lambda e: e.tensor_tensor(out=t1[:], in0=Ap[:, 1, 1, :], in1=ar[:], op=ALU.mult), r=[Ap_r, ar_r], w=[t1_r])
            self.dve(lambda e: e.tensor_tensor(out=t2[:], in0=zr[:], in1=ai[:], op=ALU.mult), r=[zr_r, ai_r], w=[t2_r])
            self.dve(lambda e: e.tensor_tensor(out=t1[:], in0=t1[:], in1=t2[:], op=ALU.subtract), r=[t1_r, t2_r], w=[t1_r])
            self.dve(lambda e: e.tensor_tensor(out=Ff[:, 1, :], in0=t1[:], in1=den[:], op=ALU.mult), r=[t1_r, den_r], w=[Ff_r])
            br, br_r = T("br", [128, 64, 16])
            bi, bi_r = T("bi", [128, 64, 16])
            self.ld(br[:], self.ssm_b_re[slot], w=[br_r])
            self.ld(bi[:], self.ssm_b_im[slot], w=[bi_r])
            brT = br[:].rearrange("g p j -> g j p")
            biT = bi[:].rearrange("g p j -> g j p")
            EE = [T("EE", [128, 16, 2, 64]) for _ in range(2)]
            u1, u1_r = T("u1", [128, 16, 64])
            u2, u2_r = T("u2", [128, 16, 64])

            def cmul_b(o, o_r, x_re, x_im, xr, a_re, a_im, a_r):
                ab_re = bc(a_re, 1, 16)
                ab_im = bc(a_im, 1, 16)
                self.dve(lambda e: e.tensor_tensor(out=u1[:], in0=x_re, in1=ab_re, op=ALU.mult), r=xr + a_r, w=[u1_r])
                self.pool(lambda e: e.tensor_tensor(out=u2[:], in0=x_im, in1=ab_im, op=ALU.mult), r=xr + a_r, w=[u2_r])
                self.dve(lambda e: e.tensor_tensor(out=o[:, :, 0, :], in0=u1[:], in1=u2[:], op=ALU.subtract), r=[u1_r, u2_r], w=[o_r])
                self.dve(lambda e: e.tensor_tensor(out=u1[:], in0=x_re, in1=ab_im, op=ALU.mult), r=xr + a_r, w=[u1_r])
                self.pool(lambda e: e.tensor_tensor(out=u2[:], in0=x_im, in1=ab_re, op=ALU.mult), r=xr + a_r, w=[u2_r])
                self.dve(lambda e: e.tensor_tensor(out=o[:, :, 1, :], in0=u1[:], in1=u2[:], op=ALU.add), r=[u1_r, u2_r], w=[o_r])

            CC, CC_r = T("CC", [128, 16, 2, 64])
            self.ld(CC[:, :, 0, :], self.ssm_c_re[slot], w=[CC_r])
            self.ld(CC[:, :, 1, :], self.ssm_c_im[slot], w=[CC_r])
            self.dve(lambda e: e.tensor_scalar(out=CC[:, :, 1, :], in0=CC[:, :, 1, :], scalar1=-1.0, scalar2=None, op0=ALU.mult),
                     r=[CC_r], w=[CC_r])
            Kg, Kg_r = T("Kg", [128, 8, 16, 16])
            tm = [T("tm", [128, 16, 128]) for _ in range(2)]
            E0, E0_r = EE[0]
            cmul_b(E0, E0_r, brT, biT, [br_r, bi_r], Ff[:, 0, :], Ff[:, 1, :], [Ff_r])
            for tau in range(8):
                Ec, Ec_r = EE[tau % 2]
                if tau > 0:
                    Epv, Epv_r = EE[(tau - 1) % 2]
                    cmul_b(Ec, Ec_r, Epv[:, :, 0, :], Epv[:, :, 1, :], [Epv_r], Ap[:, 1, 0, :], Ap[:, 1, 1, :], [Ap_r])
                self.ld(self.Wd[2 * (7 - tau):2 * (7 - tau) + 2].rearrange("ri g j p -> g j ri p"), Ec[:], r=[Ec_r], w=[self.Wd_r])
                for j in range(16):
                    tmj, tmj_r = tm[j % 2]
                    eb = bc(Ec[:, j, :, :].rearrange("g r p -> g (r p)"), 1, 16)
                    self.pool(lambda e, tmj=tmj, eb=eb: e.tensor_tensor(out=tmj[:], in0=CC[:].rearrange("g i r p -> g i (r p)"),
                                                                        in1=eb, op=ALU.mult), r=[CC_r, Ec_r], w=[tmj_r])
                    self.dve(lambda e, tmj=tmj, tau=tau, j=j: e.tensor_reduce(out=Kg[:, tau, j, :], in_=tmj[:], axis=AX.X, op=ALU.add),
                             r=[tmj_r], w=[Kg_r])
            self.ld(self.Kd.rearrange("t g j i -> g t (j i)"), Kg[:].rearrange("g t j i -> g t (j i)"), r=[Kg_r], w=[self.Kd_r])
            VV = [T("VV", [128, 2, 64, 16]) for _ in range(2)]
            CrT = CC[:, :, 0, :].rearrange("g i p -> g p i")
            nCiT = CC[:, :, 1, :].rearrange("g i p -> g p i")
            w1, w1_r = T("w1", [128, 64, 16])
            w2, w2_r = T("w2", [128, 64, 16])
            for t in range(8):
                Vc, Vc_r = VV[t % 2]
                are = bc(Ap[:, t + 1, 0, :], 2, 16)
                aim = bc(Ap[:, t + 1, 1, :], 2, 16)
                self.dve(lambda e, are=are: e.tensor_tensor(out=w1[:], in0=CrT, in1=are, op=ALU.mult), r=[CC_r, Ap_r], w=[w1_r])
                self.pool(lambda e, aim=aim: e.tensor_tensor(out=w2[:], in0=nCiT, in1=aim, op=ALU.mult), r=[CC_r, Ap_r], w=[w2_r])
                self.dve(lambda e, Vc=Vc: e.tensor_tensor(out=Vc[:, 0, :, :], in0=w1[:], in1=w2[:], op=ALU.add), r=[w1_r, w2_r], w=[Vc_r])
                self.dve(lambda e, aim=aim: e.tensor_tensor(out=w1[:], in0=CrT, in1=aim, op=ALU.mult), r=[CC_r, Ap_r], w=[w1_r])
                self.pool(lambda e, are=are: e.tensor_tensor(out=w2[:], in0=nCiT, in1=are, op=ALU.mult), r=[CC_r, Ap_r], w=[w2_r])
                self.dve(lambda e, Vc=Vc: e.tensor_tensor(out=Vc[:, 1, :, :], in0=w2[:], in1=w1[:], op=ALU.subtract), r=[w1_r, w2_r], w=[Vc_r])
                self.ld(self.Vd[2 * t:2 * t + 2].rearrange("r g p i -> g r (p i)"), Vc[:].rearrange("g r p i -> g r (p i)"),
                        r=[Vc_r], w=[self.Vd_r])
            P.barrier()

    def s5_scan(self, st, slot):
        P = self.P

        def T(name, shape, dt=F32):
            return P.sb(st, name, shape, dt)
        Kt, Kt_r = T("Kt", [128, 16, 8, 128], BF16)
        self.pool(lambda e: e.memset(Kt[:], 0.0), w=[Kt_r])
        for g8 in range(8):
            for tau in range(8):
                self.ldc(Kt[16 * g8:16 * g8 + 16, :, tau, 16 * g8:16 * g8 + 16],
                         self.Kd[tau].rearrange("(f g) j i -> g j f i", g=8)[g8], r=[self.Kd_r], w=[Kt_r])
        KC, KC_r = T("KCs", [128, 64, 30])
        self.ld(KC[:], self.KCd.rearrange("(q g) p k -> (g p) q k", g=2), r=[self.KCd_r], w=[KC_r])
        Dk, Dk_r = T("Dk", [128, 16])
        self.P.dma("sp", lambda e: e.dma_start(out=Dk[:], in_=self.ssm_d[slot].rearrange("(f p) -> p f", p=128),
                                              allow_slow_non_contiguous=True), (), [Dk_r])
        Wts = [T("Wt", [128, 16, 128], BF16) for _ in range(2)]
        Vts = [T("Vt", [128, 4, 16, 32], BF16) for _ in range(2)]
        for (t_, r_) in Wts + Vts:
            self.pool(lambda e, t_=t_: e.memset(t_[:], 0.0), w=[r_])
        uTfs = [T("uTf", [128, 8, 8, 64], BF16) for _ in range(2)]
        uTss = [T("uTs", [128, 8, N_S], BF16) for _ in range(2)]
        XA = [[T("XA", [128, 513]) for _ in range(2)] for _ in range(4)]
        XB = [[T("XB", [128, 513]) for _ in range(2)] for _ in range(4)]
        Xb = [[T("Xb", [128, 512], BF16) for _ in range(2)] for _ in range(4)]
        XS, XS_r = T("XS", [128, 4, 2, N_S, 2])
        XSb, XSb_r = T("XSb", [128, 4, 2, N_S], BF16)
        tS = [T("tS", [128, N_S]) for _ in range(2)]
        gbufs = [T("gbuf", [128, TOK], BF16) for _ in range(2)]
        ytmps = [T("ytmp", [128, 512]) for _ in range(2)]
        Fin, Fin_r = T("Fin", [128, 2, 64])
        FinS, FinS_r = T("FinS", [128, 2, 64, N_S])
        s0s = [[T("s0", [N_S, 512]) for _ in range(2)] for _ in range(2)]
        for q in range(4):
            for ri in range(2):
                self.pool(lambda e, q=q, ri=ri: e.memset(XA[q][ri][0][:, 0:1], 0.0), w=[XA[q][ri][1]])
        yi = 0
        for f in range(16):
            Wt, Wt_r = Wts[f % 2]
            Vt, Vt_r = Vts[f % 2]
            uTf, uTf_r = uTfs[f % 2]
            uTs, uTs_r = uTss[f % 2]
            gbuf, gbuf_r = gbufs[f % 2]
            self.ld(uTf[:].rearrange("p a s c -> p (a s c)"), self.uT_d[f].rearrange("p a s c -> p (a s c)"),
                    r=[self.uT_r[f]], w=[uTf_r])
            self.ld(uTs[:], self.uTs_d[f], r=[self.uT_r[f]], w=[uTs_r])
            s0 = s0s[f % 2]
            self.ld(s0[0][0][:], self.st_re[slot][:, f * 512:(f + 1) * 512], w=[s0[0][1]])
            self.ld(s0[1][0][:], self.st_im[slot][:, f * 512:(f + 1) * 512], w=[s0[1][1]])
            for g8 in range(8):
                g = 8 * f + g8
                self.ldc(Wt[16 * g8:16 * g8 + 16, :, 64 * (g8 % 2):64 * (g8 % 2) + 64],
                         self.Wd[:, g, :, :].rearrange("sr j p -> j sr p"), r=[self.Wd_r], w=[Wt_r])
            for g2 in range(2):
                for q in range(4):
                    self.ldc(Vt[64 * g2:64 * g2 + 64, q, :, 16 * g2:16 * g2 + 16],
                             self.Vd[:, 8 * f + 2 * q + g2, :, :].rearrange("tr p i -> p tr i"),
                             r=[self.Vd_r], w=[Vt_r])
            for q in range(4):
                qq = 4 * f + q
                for ri in range(2):
                    ps, ps_r = self.next_ps()
                    for s in range(8):
                        self.pe(lambda e, ps=ps, q=q, ri=ri, s=s, Wt=Wt, uTf=uTf: e.matmul(
                            ps[:].rearrange("p (a c) -> p a c", a=8), lhsT=Wt[32 * q:32 * q + 32, 2 * s + ri, :],
                            rhs=uTf[32 * q:32 * q + 32, :, s, :], start=(s == 0), stop=(s == 7),
                            tile_position=(32 * q, 0)), r=[Wt_r, uTf_r], w=[ps_r])
                    xa, xa_r = XA[q][ri]
                    self.act(lambda e, xa=xa, ps=ps: e.copy(out=xa[:, 1:513], in_=ps[:]), r=[ps_r], w=[xa_r])
                    ps, ps_r = self.next_ps()
                    for s in range(8):
                        self.pe(lambda e, ps=ps, q=q, ri=ri, s=s, Wt=Wt, uTs=uTs: e.matmul(
                            ps[:, 0:N_S], lhsT=Wt[32 * q:32 * q + 32, 2 * s + ri, :],
                            rhs=uTs[32 * q:32 * q + 32, s, :], start=(s == 0), stop=(s == 7),
                            tile_position=(32 * q, 0)), r=[Wt_r, uTs_r], w=[ps_r])
                    s0t, s0_r = s0[ri]
                    self.pe(lambda e, ps=ps, s0t=s0t, q=q: e.transpose(
                        out=ps[:, 32:32 + N_S], in_=s0t[:, q * 128:(q + 1) * 128], identity=self.ident_f[0:N_S, 0:N_S]),
                        r=[s0_r, self.ident_f_r], w=[ps_r])
                    self.act(lambda e, ps=ps, q=q, ri=ri: e.copy(out=XS[:, q, ri, :, 1], in_=ps[:, 0:N_S]), r=[ps_r], w=[XS_r])
                    self.act(lambda e, ps=ps, q=q, ri=ri: e.copy(out=XS[:, q, ri, :, 0], in_=ps[:, 32:32 + N_S]), r=[ps_r], w=[XS_r])
            for q in range(4):
                qq = 4 * f + q
                cur = XA[q]
                nxt = XB[q]
                for k in range(10):
                    d = 1 << k
                    n = 513 - d
                    cr = KC[:, qq, 3 * k:3 * k + 1]
                    ci = KC[:, qq, 3 * k + 1:3 * k + 2]
                    nci = KC[:, qq, 3 * k + 2:3 * k + 3]
                    (c_re, c_re_r), (c_im, c_im_r) = cur
                    (n_re, n_re_r), (n_im, n_im_r) = nxt
                    self.dve(lambda e, c_re=c_re, n_re=n_re, cr=cr, d=d, n=n: e.scalar_tensor_tensor(
                        out=n_re[:, d:513], in0=c_re[:, 0:n], scalar=cr, in1=c_re[:, d:513], op0=ALU.mult, op1=ALU.add),
                        r=[c_re_r, KC_r], w=[n_re_r])
                    self.dve(lambda e, c_im=c_im, n_re=n_re, nci=nci, d=d, n=n: e.scalar_tensor_tensor(
                        out=n_re[:, d:513], in0=c_im[:, 0:n], scalar=nci, in1=n_re[:, d:513], op0=ALU.mult, op1=ALU.add),
                        r=[c_im_r, n_re_r, KC_r], w=[n_re_r])
                    self.dve(lambda e, c_im=c_im, n_im=n_im, cr=cr, d=d, n=n: e.scalar_tensor_tensor(
                        out=n_im[:, d:513], in0=c_im[:, 0:n], scalar=cr, in1=c_im[:, d:513], op0=ALU.mult, op1=ALU.add),
                        r=[c_im_r, KC_r], w=[n_im_r])
                    self.dve(lambda e, c_re=c_re, n_im=n_im, ci=ci, d=d, n=n: e.scalar_tensor_tensor(
                        out=n_im[:, d:513], in0=c_re[:, 0:n], scalar=ci, in1=n_im[:, d:513], op0=ALU.mult, op1=ALU.add),
                        r=[c_re_r, n_im_r, KC_r], w=[n_im_r])
                    self.pool(lambda e, c_re=c_re, n_re=n_re, d=d: e.tensor_copy(out=n_re[:, 0:d], in_=c_re[:, 0:d]),
                              r=[c_re_r], w=[n_re_r])
                    self.pool(lambda e, c_im=c_im, n_im=n_im, d=d: e.tensor_copy(out=n_im[:, 0:d], in_=c_im[:, 0:d]),
                              r=[c_im_r], w=[n_im_r])
                    cur, nxt = nxt, cur
                for ri in range(2):
                    xa, xa_r = cur[ri]
                    xb_, xb_r = Xb[q][ri]
                    self.act(lambda e, xa=xa, xb_=xb_: e.copy(out=xb_[:], in_=xa[:, 0:512]), r=[xa_r], w=[xb_r])
                    self.pool(lambda e, xa=xa, ri=ri, qq=qq: e.tensor_copy(out=Fin[:, ri, qq:qq + 1], in_=xa[:, 512:513]),
                              r=[xa_r], w=[Fin_r])
                cr = KC[:, qq, 0:1]
                ci = KC[:, qq, 1:2]
                nci = KC[:, qq, 2:3]
                (ta, ta_r), (tb, tb_r) = tS
                self.dve(lambda e, q=q, cr=cr: e.scalar_tensor_tensor(out=ta[:], in0=XS[:, q, 0, :, 0], scalar=cr, in1=XS[:, q, 0, :, 1],
                                                                      op0=ALU.mult, op1=ALU.add), r=[XS_r, KC_r], w=[ta_r])
                self.dve(lambda e, q=q, cr=cr: e.scalar_tensor_tensor(out=tb[:], in0=XS[:, q, 1, :, 0], scalar=cr, in1=XS[:, q, 1, :, 1],
                                                                      op0=ALU.mult, op1=ALU.add), r=[XS_r, KC_r], w=[tb_r])
                self.dve(lambda e, q=q, nci=nci, qq=qq: e.scalar_tensor_tensor(out=FinS[:, 0, qq, :], in0=XS[:, q, 1, :, 0], scalar=nci, in1=ta[:],
                                                                              op0=ALU.mult, op1=ALU.add), r=[XS_r, KC_r, ta_r], w=[FinS_r])
                self.dve(lambda e, q=q, ci=ci, qq=qq: e.scalar_tensor_tensor(out=FinS[:, 1, qq, :], in0=XS[:, q, 0, :, 0], scalar=ci, in1=tb[:],
                                                                             op0=ALU.mult, op1=ALU.add), r=[XS_r, KC_r, tb_r], w=[FinS_r])
                self.act(lambda e, q=q: e.copy(out=XSb[:, q, :, :], in_=XS[:, q, :, :, 0]), r=[XS_r], w=[XSb_r])
            for t in range(8):
                for smp in range(2):
                    ps, ps_r = self.next_ps()
                    nn = 512 if smp == 0 else N_S
                    first_mm = True
                    for s in range(t + 1):
                        rhs = uTf[:, :, s, :] if smp == 0 else uTs[:, s, :]
                        outp = ps[:].rearrange("p (a c) -> p a c", a=8) if smp == 0 else ps[:, 0:N_S]
                        self.pe(lambda e, outp=outp, rhs=rhs, t=t, s=s, f=f, fm=first_mm: e.matmul(
                            outp, lhsT=Kt[:, f, t - s, :], rhs=rhs, start=fm, stop=False),
                            r=[Kt_r, uTf_r, uTs_r], w=[ps_r])
                        first_mm = False
                    for q in range(4):
                        for ri in range(2):
                            rhs = Xb[q][ri][0][:, 0:512] if smp == 0 else XSb[:, q, ri, :]
                            rr = Xb[q][ri][1] if smp == 0 else XSb_r
                            last = (q == 3 and ri == 1)
                            self.pe(lambda e, ps=ps, rhs=rhs, q=q, ri=ri, t=t, Vt=Vt, nn=nn, last=last: e.matmul(
                                ps[32 * q:32 * q + 32, 0:nn], lhsT=Vt[:, q, 2 * t + ri, :], rhs=rhs, start=False, stop=last,
                                tile_position=(0, 32 * q)), r=[Vt_r, rr], w=[ps_r])
                    yt, yt_r = ytmps[yi % 2]
                    yi += 1
                    if smp == 0:
                        self.dve(lambda e, yt=yt, ps=ps, t=t, f=f, uTf=uTf: e.scalar_tensor_tensor(
                            out=yt[:].rearrange("p (a c) -> p a c", a=8), in0=uTf[:, :, t, :], scalar=Dk[:, f:f + 1],
                            in1=ps[:].rearrange("p (a c) -> p a c", a=8), op0=ALU.mult, op1=ALU.add),
                            r=[uTf_r, Dk_r, ps_r], w=[yt_r])
                        self.act(lambda e, yt=yt, gbuf=gbuf, t=t: e.activation(
                            out=gbuf[:, 0:T_P].rearrange("p (a c s) -> p a c s", a=8, s=8)[:, :, :, t],
                            in_=yt[:].rearrange("p (a c) -> p a c", a=8), func=AF.Gelu_apprx_tanh), r=[yt_r], w=[gbuf_r])
                    else:
                        self.dve(lambda e, yt=yt, ps=ps, t=t, f=f, uTs=uTs: e.scalar_tensor_tensor(
                            out=yt[:, 0:N_S], in0=uTs[:, t, :], scalar=Dk[:, f:f + 1], in1=ps[:, 0:N_S],
                            op0=ALU.mult, op1=ALU.add), r=[uTs_r, Dk_r, ps_r], w=[yt_r])
                        self.act(lambda e, yt=yt, gbuf=gbuf, t=t: e.activation(
                            out=gbuf[:, T_P:TOK].rearrange("p (n s) -> p n s", s=8)[:, :, t],
                            in_=yt[:, 0:N_S], func=AF.Gelu_apprx_tanh), r=[yt_r], w=[gbuf_r])
            self.ld(self.gT_d[f], gbuf[:], r=[gbuf_r], w=[self.gT_r[f]])
        for ri in range(2):
            ps, ps_r = self.next_ps()
            self.pe(lambda e, ps=ps, ri=ri: e.transpose(out=ps[0:64, 0:128], in_=Fin[:, ri, :], identity=self.ident_f[:]),
                    r=[Fin_r, self.ident_f_r], w=[ps_r])
            fo_, fo_r = T("fo", [64, 128])
            self.act(lambda e, ps=ps, fo_=fo_: e.copy(out=fo_[:], in_=ps[0:64, 0:128]), r=[ps_r], w=[fo_r])
            self.store((self.o_sre_p if ri == 0 else self.o_sim_p)[slot], fo_[:], r=[fo_r])
            fs_, fs_r = T("fs", [64, N_S, 128])
            for n in range(N_S):
                ps, ps_r = self.next_ps()
                self.pe(lambda e, ps=ps, ri=ri, n=n: e.transpose(out=ps[0:64, 0:128], in_=FinS[:, ri, :, n], identity=self.ident_f[:]),
                        r=[FinS_r, self.ident_f_r], w=[ps_r])
                self.act(lambda e, ps=ps, fs_=fs_, n=n: e.copy(out=fs_[:, n, :], in_=ps[0:64, 0:128]), r=[ps_r], w=[fs_r])
            self.store((self.o_sre_s if ri == 0 else self.o_sim_s)[slot].rearrange("n q c -> q n c"), fs_[:], r=[fs_r])

    def s5_out(self, st, slot, first):
        P = self.P

        def T(name, shape, dt=F32):
            return P.sb(st, name, shape, dt)
        wg, wg_r = T("wglu", [128, 16, E], BF16)
        wgv = self.ssm_w_glu[slot].rearrange("(k p) n -> p k n", p=128)
        for k in range(16):
            self.ldc(wg[:, k, :], wgv[:, k, :], w=[wg_r])
        wo, wo_r = T("wout", [128, 16, D_MODEL], BF16)
        wov = self.ssm_w_out[slot].rearrange("(k p) n -> p k n", p=128)
        for k in range(0, 16, 2):
            self.ldc(wo[:, k:k + 2, :], wov[:, k:k + 2, :], w=[wo_r])
        bg, bg_r = T("bglu", [128, 16])
        self.P.dma("sp", lambda e: e.dma_start(out=bg[:], in_=self.ssm_b_glu[slot].rearrange("(f p) -> p f", p=128),
                                              allow_slow_non_contiguous=True), (), [bg_r])
        gin, gin_r = T("gin", [128, 16, 512], BF16)
        szin, szin_r = T("szin", [128, 16, 512], BF16)
        g2T, g2T_r = T("g2T", [128, 16, 512], BF16)
        sig = [T("sig", [128, 512]) for _ in range(2)]
        hbs = [T("hb", [128, D_MODEL]) for _ in range(2)]
        hi = 0
        for tg in range(NGRP):
            tok0, ntok = grp_tok(tg)
            self.ld(gin[:, :, 0:ntok], self.gT_d[:, :, tok0:tok0 + ntok].rearrange("f p t -> p f t"), r=self.gT_r, w=[gin_r])
            self.ld(szin[:, :, 0:ntok], self.szT_d[:, :, tok0:tok0 + ntok].rearrange("f p t -> p f t"), r=self.szT_r, w=[szin_r])
            for fo in range(16):
                ps, ps_r = self.next_ps()
                for f in range(16):
                    self.pe(lambda e, ps=ps, f=f, fo=fo, ntok=ntok: e.matmul(
                        ps[:, 0:ntok], lhsT=wg[:, f, fo * 128:(fo + 1) * 128], rhs=gin[:, f, 0:ntok],
                        start=(f == 0), stop=(f == 15)), r=[wg_r, gin_r], w=[ps_r])
                sg, sg_r = sig[fo % 2]
                self.act(lambda e, ps=ps, sg=sg, fo=fo, ntok=ntok: e.activation(
                    out=sg[:, 0:ntok], in_=ps[:, 0:ntok], func=AF.Sigmoid, bias=bg[:, fo:fo + 1]), r=[ps_r, bg_r], w=[sg_r])
                self.dve(lambda e, sg=sg, fo=fo, ntok=ntok: e.tensor_tensor(
                    out=sg[:, 0:ntok], in0=sg[:, 0:ntok], in1=gin[:, fo, 0:ntok], op=ALU.mult), r=[sg_r, gin_r], w=[sg_r])
                self.pool(lambda e, sg=sg, fo=fo, ntok=ntok: e.tensor_tensor(
                    out=g2T[:, fo, 0:ntok], in0=sg[:, 0:ntok], in1=szin[:, fo, 0:ntok], op=ALU.mult), r=[sg_r, szin_r], w=[g2T_r])
            for il in range(ntok // 128):
                i = tok0 // 128 + il
                hb, hb_r = hbs[hi % 2]
                hi += 1
                self.ld(hb[:], self.h_src(first, i), r=[self.H_r[i]], w=[hb_r])
                for hh in range(2):
                    ps, ps_r = self.next_ps()
                    for f in range(16):
                        self.pe(lambda e, ps=ps, f=f, hh=hh, il=il: e.matmul(
                            ps[:], lhsT=g2T[:, f, il * 128:(il + 1) * 128], rhs=wo[:, f, hh * 512:(hh + 1) * 512],
                            start=(f == 0), stop=(f == 15)), r=[wo_r, g2T_r], w=[ps_r])
                    self.dve(lambda e, ps=ps, hb=hb, hh=hh: e.tensor_tensor(
                        out=hb[:, hh * 512:(hh + 1) * 512], in0=ps[:], in1=hb[:, hh * 512:(hh + 1) * 512], op=ALU.add),
                        r=[ps_r, hb_r], w=[hb_r])
                self.ld(self.H[i * 128:(i + 1) * 128, :], hb[:], r=[hb_r], w=[self.H_r[i]])

    def dsa_layer(self):
        P = self.P
        with ExitStack() as st:
            self.dsa_proj(st)
        P.barrier()
        if self.dsa_stage >= 2:
            with ExitStack() as st:
                self.dsa_prompt(st)
            P.barrier()
        if self.dsa_stage >= 3:
            with ExitStack() as st:
                self.dsa_sample(st)
            P.barrier()

    def dsa_proj(self, st):
        P = self.P

        def T(name, shape, dt=F32):
            return P.sb(st, name, shape, dt)
        win, win_r = T("awin", [128, 8, ATTN_IN], BF16)
        wv = self.attn_w_in.rearrange("(k p) n -> p k n", p=128)
        for k in range(8):
            self.ldc(win[:, k, :], wv[:, k, :], w=[win_r])
        gt, gt_r = T("agt", [128, D_MODEL])
        self.ld(gt[:], self.attn_norm.partition_broadcast(128), w=[gt_r])
        ht, ht_r = T("aht", [128, D_MODEL])
        junk, junk_r = T("ajunk", [128, D_MODEL], BF16)
        ss, ss_r = T("ass", [128, 1])
        xn, xn_r = T("axn", [128, D_MODEL], BF16)
        xTs = [T("axT", [128, 8, 128], BF16) for _ in range(2)]
        prs = [T("pr", [128, ATTN_IN]) for _ in range(2)]
        rps = [T("rp", [128, 2880]) for _ in range(2)]
        szbs = [T("szb", [128, E], BF16) for _ in range(2)]
        vxbs = [T("vxb", [128, 4, 65], BF16) for _ in range(2)]
        for (t_, r_) in vxbs:
            self.pool(lambda e, t_=t_: e.memset(t_[:, :, 64:65], 1.0), w=[r_])
        r1, r1_r = T("r1", [128, 36, 32])
        r2, r2_r = T("r2", [128, 36, 32])
        css = [T("cs", [128, 64]) for _ in range(2)]
        wibs = [T("wib", [128, 8]) for _ in range(2)]
        kks = [T("kk", [128, 5, 2, 64]) for _ in range(2)]
        TSs = [T("TS", [128, 25, 128], BF16) for _ in range(2)]
        TSs_s, TSs_s_r = T("TSsmp", [64, 40, 128], BF16)
        COLS = [(c0, min(512, ATTN_IN - c0)) for c0 in range(0, ATTN_IN, 512)]
        for i in range(NTILE):
            xT, xT_r = xTs[i % 2]
            pr, pr_r = prs[i % 2]
            rp, rp_r = rps[i % 2]
            szb, szb_r = szbs[i % 2]
            vxb, vxb_r = vxbs[i % 2]
            cs, cs_r = css[i % 2]
            wib, wib_r = wibs[i % 2]
            kk, kk_r = kks[i % 2]
            TS, TS_r = TSs[i % 2]
            tok0 = i * 128
            smp = (i == NTILE - 1)
            self.norm_tile(self.H[tok0:tok0 + 128, :], self.H_r[i], gt, gt_r, ht, ht_r, junk, junk_r, ss, ss_r,
                           xn, xn_r, xT[:], xT_r)
            self.ld(cs[:], self.rope_cs[tok0:tok0 + 128, :], w=[cs_r])
            for ci, (c0, w) in enumerate(COLS):
                ps, ps_r = self.next_ps()
                for k in range(8):
                    self.pe(lambda e, ps=ps, k=k, c0=c0, w=w, xT=xT: e.matmul(
                        ps[:, 0:w], lhsT=xT[:, k, :], rhs=win[:, k, c0:c0 + w], start=(k == 0), stop=(k == 7)),
                        r=[win_r, xT_r], w=[ps_r])
                if 5 <= ci <= 8:
                    self.act(lambda e, ps=ps, szb=szb, ci=ci: e.activation(
                        out=szb[:, (ci - 5) * 512:(ci - 4) * 512], in_=ps[:], func=AF.Silu), r=[ps_r], w=[szb_r])
                else:
                    self.act(lambda e, ps=ps, pr=pr, c0=c0, w=w: e.copy(out=pr[:, c0:c0 + w], in_=ps[:, 0:w]), r=[ps_r], w=[pr_r])
            self.ld(self.sz_d[tok0:tok0 + 128, :], szb[:], r=[szb_r], w=[self.dsa_r])
            for (H, s0_, d0_) in ((36, 0, 0), (9, 4608, 2304)):
                src = pr[:, s0_:s0_ + H * 64].rearrange("p (h c d) -> p h c d", c=2, d=32)
                dst = rp[:, d0_:d0_ + H * 64].rearrange("p (h c d) -> p h c d", c=2, d=32)
                cosb = bc(cs[:, 0:32], 1, H)
                sinb = bc(cs[:, 32:64], 1, H)
                x1 = src[:, :, 0, :]
                x2 = src[:, :, 1, :]
                self.dve(lambda e, x1=x1, cosb=cosb, H=H: e.tensor_tensor(out=r1[:, 0:H, :], in0=x1, in1=cosb, op=ALU.mult), r=[pr_r, cs_r], w=[r1_r])
                self.pool(lambda e, x2=x2, sinb=sinb, H=H: e.tensor_tensor(out=r2[:, 0:H, :], in0=x2, in1=sinb, op=ALU.mult), r=[pr_r, cs_r], w=[r2_r])
                self.dve(lambda e, dst=dst, H=H: e.tensor_tensor(out=dst[:, :, 0, :], in0=r1[:, 0:H, :], in1=r2[:, 0:H, :], op=ALU.subtract),
                         r=[r1_r, r2_r], w=[rp_r])
                self.dve(lambda e, x2=x2, cosb=cosb, H=H: e.tensor_tensor(out=r1[:, 0:H, :], in0=x2, in1=cosb, op=ALU.mult), r=[pr_r, cs_r], w=[r1_r])
                self.pool(lambda e, x1=x1, sinb=sinb, H=H: e.tensor_tensor(out=r2[:, 0:H, :], in0=x1, in1=sinb, op=ALU.mult), r=[pr_r, cs_r], w=[r2_r])
                self.dve(lambda e, dst=dst, H=H: e.tensor_tensor(out=dst[:, :, 1, :], in0=r1[:, 0:H, :], in1=r2[:, 0:H, :], op=ALU.add),
                         r=[r1_r, r2_r], w=[rp_r])
            if not smp:
                self.store(self.o_k_p[tok0:tok0 + 128, :], rp[:, 2048:2304], r=[rp_r])
                self.store(self.o_v_p[tok0:tok0 + 128, :], pr[:, 2304:2560], r=[pr_r])
                self.store(self.o_kidx_p[tok0:tok0 + 128, :], rp[:, 2816:2880], r=[rp_r])
            else:
                self.store(self.o_k_s[:, :], rp[:, 2048:2304], r=[rp_r])
                self.store(self.o_v_s[:, :], pr[:, 2304:2560], r=[pr_r])
                self.store(self.o_kidx_s[:, :], rp[:, 2816:2880], r=[rp_r])
            self.dve(lambda e, wib=wib, pr=pr: e.tensor_scalar(out=wib[:], in0=pr[:, 5184:5192], scalar1=8.0 ** -1.5, scalar2=None, op0=ALU.mult),
                     r=[pr_r], w=[wib_r])
            self.ld(self.wi_d[tok0:tok0 + 128, :], wib[:], r=[wib_r], w=[self.dsa_r])
            self.pool(lambda e, vxb=vxb, pr=pr: e.tensor_copy(out=vxb[:, :, 0:64], in_=pr[:, 2304:2560].rearrange("p (g d) -> p g d", g=4)),
                      r=[pr_r], w=[vxb_r])
            self.ld(self.Vx_d[tok0:tok0 + 128, :], vxb[:].rearrange("p g d -> p (g d)"), r=[vxb_r], w=[self.dsa_r])
            self.pool(lambda e, kk=kk, rp=rp: e.tensor_copy(out=kk[:, 0:4, :, :], in_=bc(rp[:, 2048:2304].rearrange("p (g d) -> p g d", g=4), 2, 2)),
                      r=[rp_r], w=[kk_r])
            self.pool(lambda e, kk=kk, rp=rp: e.tensor_copy(out=kk[:, 4, :, :], in_=bc(rp[:, 2816:2880], 1, 2)), r=[rp_r], w=[kk_r])
            srcs = [(rp[:, 128 * a:128 * a + 128], rp_r) for a in range(16)]
            srcs += [(kk[:, g, :, :].rearrange("p a d -> p (a d)"), kk_r) for g in range(4)]
            srcs += [(rp[:, 2304 + 128 * a:2304 + 128 * a + 128], rp_r) for a in range(4)]
            srcs += [(kk[:, 4, :, :].rearrange("p a d -> p (a d)"), kk_r)]
            for b0 in range(0, 25, 4):
                nb = min(4, 25 - b0)
                ps, ps_r = self.next_ps()
                for j in range(nb):
                    src, src_r = srcs[b0 + j]
                    self.pe(lambda e, ps=ps, j=j, src=src: e.transpose(out=ps[:, j * 128:(j + 1) * 128], in_=src, identity=self.ident_f[:]),
                            r=[src_r, self.ident_f_r], w=[ps_r])
                self.act(lambda e, ps=ps, TS=TS, b0=b0, nb=nb: e.copy(out=TS[:, b0:b0 + nb, :],
                                                                    in_=ps[:, 0:nb * 128].rearrange("p (a t) -> p a t", t=128)),
                         r=[ps_r], w=[TS_r])
            self.ld(self.qT_d[:, :, tok0:tok0 + 128].rearrange("a p t -> p a t"), TS[:, 0:16, :], r=[TS_r], w=[self.dsa_r])
            self.ld(self.kT2_d[:, :, tok0:tok0 + 128].rearrange("a p t -> p a t"), TS[:, 16:20, :], r=[TS_r], w=[self.dsa_r])
            self.ld(self.qiT_d[:, :, tok0:tok0 + 128].rearrange("a p t -> p a t"), TS[:, 20:24, :], r=[TS_r], w=[self.dsa_r])
            self.ld(self.kiT2_d[:, :, tok0:tok0 + 128].rearrange("a p t -> p a t"), TS[:, 24:25, :], r=[TS_r], w=[self.dsa_r])
            if smp:
                hs = [(rp[:, 64 * h:64 * h + 64], rp_r) for h in range(32)] + [(rp[:, 2304 + 64 * h:2304 + 64 * h + 64], rp_r) for h in range(8)]
                for b0 in range(0, 40, 4):
                    ps, ps_r = self.next_ps()
                    for j in range(4):
                        src, src_r = hs[b0 + j]
                        self.pe(lambda e, ps=ps, j=j, src=src: e.transpose(out=ps[0:64, j * 128:(j + 1) * 128], in_=src, identity=self.ident_f[:]),
                                r=[src_r, self.ident_f_r], w=[ps_r])
                    self.act(lambda e, ps=ps, b0=b0: e.copy(out=TSs_s[:, b0:b0 + 4, :], in_=ps[0:64, :].rearrange("p (a t) -> p a t", t=128)),
                             r=[ps_r], w=[TSs_s_r])
                self.ld(self.qTs_d[:, :, :], TSs_s[:, 0:32, :], r=[TSs_s_r], w=[self.dsa_r])
                self.ld(self.qiTs_d[:, :, :], TSs_s[:, 32:40, :], r=[TSs_s_r], w=[self.dsa_r])

    def attn_out_tile(self, i, g2, g2_r, g2T, g2T_r, wo, wo_r, hb, hb_r):
        pt, pt_r = self.pst
        for b0 in range(0, 16, 8):
            for j in range(8):
                self.pe(lambda e, j=j, b0=b0: e.transpose(out=pt[:, j * 128:(j + 1) * 128], in_=g2[:, (b0 + j) * 128:(b0 + j + 1) * 128],
                                                          identity=self.ident_b[:]), r=[g2_r, self.ident_b_r], w=[pt_r])
            self.act(lambda e, b0=b0: e.copy(out=g2T[:, b0:b0 + 8, :], in_=pt[:].rearrange("p (a t) -> p a t", t=128)), r=[pt_r], w=[g2T_r])
        self.ld(hb[:], self.H[i * 128:(i + 1) * 128, :], r=[self.H_r[i]], w=[hb_r])
        for hh in range(2):
            ps, ps_r = self.next_ps()
            for f in range(16):
                self.pe(lambda e, ps=ps, f=f, hh=hh: e.matmul(ps[:], lhsT=g2T[:, f, :], rhs=wo[:, f, hh * 512:(hh + 1) * 512],
                                                              start=(f == 0), stop=(f == 15)), r=[wo_r, g2T_r], w=[ps_r])
            self.dve(lambda e, ps=ps, hh=hh: e.tensor_tensor(out=hb[:, hh * 512:(hh + 1) * 512], in0=ps[:], in1=hb[:, hh * 512:(hh + 1) * 512],
                                                            op=ALU.add), r=[ps_r, hb_r], w=[hb_r])
        self.ld(self.H[i * 128:(i + 1) * 128, :], hb[:], r=[hb_r], w=[self.H_r[i]])

    def topk_threshold(self, sc, sc_r, S, lo, hi, mid, cnt, sel, nsel, small_r, junk, junk_r, iters=20):
        for it in range(iters):
            self.dve(lambda e: e.tensor_scalar(out=mid[:], in0=lo[:], scalar1=hi[:, 0:1], scalar2=0.5, op0=ALU.add, op1=ALU.mult),
                     r=[small_r], w=[small_r])
            self.dve(lambda e: e.tensor_scalar(out=junk[:, 0:S], in0=sc[:, 0:S], scalar1=mid[:, 0:1], scalar2=0.0, op0=ALU.is_ge, op1=ALU.add,
                                               accum_out=cnt[:]), r=[sc_r, small_r], w=[junk_r, small_r])
            self.dve(lambda e: e.tensor_scalar(out=sel[:], in0=cnt[:], scalar1=255.5, scalar2=None, op0=ALU.is_ge), r=[small_r], w=[small_r])
            self.dve(lambda e: e.tensor_scalar(out=nsel[:], in0=cnt[:], scalar1=255.5, scalar2=None, op0=ALU.is_lt), r=[small_r], w=[small_r])
            self.dve(lambda e: e.copy_predicated(out=lo[:], mask=sel[:], data=mid[:]), r=[small_r], w=[small_r])
            self.dve(lambda e: e.copy_predicated(out=hi[:], mask=nsel[:], data=mid[:]), r=[small_r], w=[small_r])

    def dsa_prompt(self, st):
        P = self.P

        def T(name, shape, dt=F32):
            return P.sb(st, name, shape, dt)
        self.ps_lim = 5
        accs = [self.psb[5], self.psb[6]]
        kT2, kT2_r = T("kT2", [128, 4, T_P], BF16)
        for g in range(4):
            self.ld(kT2[:, g, :], self.kT2_d[g, :, 0:T_P], r=[self.dsa_r], w=[kT2_r])
        kiT2, kiT2_r = T("kiT2", [128, T_P], BF16)
        self.ld(kiT2[:], self.kiT2_d[0, :, 0:T_P], r=[self.dsa_r], w=[kiT2_r])
        Vx, Vx_r = T("Vx", [128, 32, 260], BF16)
        for k4 in range(4):
            self.ld(Vx[:, 8 * k4:8 * k4 + 8, :], self.Vx_d[1024 * k4:1024 * (k4 + 1), :].rearrange("(kt s) c -> s kt c", s=128),
                    r=[self.dsa_r], w=[Vx_r])
        wo, wo_r = T("awo", [128, 16, D_MODEL], BF16)
        self.wo_attn = (wo, wo_r)
        wov = self.attn_w_out.rearrange("(k p) n -> p k n", p=128)
        for k in range(0, 16, 2):
            self.ldc(wo[:, k:k + 2, :], wov[:, k:k + 2, :], w=[wo_r])
        cmask, cmask_r = T("cmask", [128, 128])
        self.ld(cmask[:], self.cmask_d, w=[cmask_r])
        qTs = [T("qT", [128, 16, 128], BF16) for _ in range(2)]
        qiTs = [T("qiT", [128, 4, 128], BF16) for _ in range(2)]
        wis = [T("wi", [128, 8]) for _ in range(2)]
        szs = [T("sz", [128, E], BF16) for _ in range(2)]
        hb, hb_r = T("ahb", [128, D_MODEL])
        sc, sc_r = T("sc", [128, T_P])
        tmps = [T("sctmp", [128, 512]) for _ in range(2)]
        junk, junk_r = T("scjunk", [128, T_P], BF16)
        Mb, Mb_r = T("Mb", [128, T_P], BF16)
        MTn, MTn_r = T("MTn", [128, 32, 128], BF16)
        pexps = [T("pexp", [128, 4, 128], BF16) for _ in range(6)]
        g2, g2_r = T("g2", [128, E], BF16)
        of_, of_r = T("of", [128, 4, 64])
        rc, rc_r = T("rc", [128, 4])
        g2T, g2T_r = T("g2T", [128, 16, 128], BF16)
        lo, small_r = T("lo", [128, 1])
        hi, _ = T("hi", [128, 1])
        mid, _ = T("mid", [128, 1])
        cnt, _ = T("cnt", [128, 1])
        sel, _ = T("sel", [128, 1], I32)
        nsel, _ = T("nsel", [128, 1], I32)
        pt, pt_r = self.pst
        MTns = [(MTn, MTn_r), T("MTn2", [128, 32, 128], BF16)]
        accS = [[T("accS", [128, 260]) for _ in range(2)] for _ in range(4)]
        st_ = {"pe_i": 0}

        def stage_idx(qt):
            nk = qt + 1
            S = 128 * nk
            qT, qT_r = qTs[qt % 2]
            qiT, qiT_r = qiTs[qt % 2]
            wi, wi_r = wis[qt % 2]
            sz, sz_r = szs[qt % 2]
            t0 = qt * 128
            self.ld(qT[:], self.qT_d[:, :, t0:t0 + 128].rearrange("a p t -> p a t"), r=[self.dsa_r], w=[qT_r])
            self.ld(qiT[:], self.qiT_d[:, :, t0:t0 + 128].rearrange("a p t -> p a t"), r=[self.dsa_r], w=[qiT_r])
            self.ld(wi[:], self.wi_d[t0:t0 + 128, :], r=[self.dsa_r], w=[wi_r])
            self.ld(sz[:], self.sz_d[t0:t0 + 128, :], r=[self.dsa_r], w=[sz_r])
            ti = 0
            for c0 in range(0, S, 512):
                n = min(512, S - c0)
                for h in range(8):
                    a, hf = h // 2, h % 2
                    ps, ps_r = self.next_ps()
                    self.pe(lambda e, ps=ps, a=a, hf=hf, c0=c0, n=n, qiT=qiT: e.matmul(
                        ps[:, 0:n], lhsT=qiT[64 * hf:64 * hf + 64, a, :], rhs=kiT2[64 * hf:64 * hf + 64, c0:c0 + n],
                        start=True, stop=True, tile_position=(64 * hf, 0)), r=[qiT_r, kiT2_r], w=[ps_r])
                    if h == 0:
                        self.dve(lambda e, ps=ps, c0=c0, n=n, wi=wi: e.tensor_scalar(
                            out=sc[:, c0:c0 + n], in0=ps[:, 0:n], scalar1=0.0, scalar2=wi[:, 0:1], op0=ALU.max, op1=ALU.mult),
                            r=[ps_r, wi_r], w=[sc_r])
                    else:
                        tmp, tmp_r = tmps[ti % 2]
                        ti += 1
                        self.dve(lambda e, ps=ps, n=n, wi=wi, h=h, tmp=tmp: e.tensor_scalar(
                            out=tmp[:, 0:n], in0=ps[:, 0:n], scalar1=0.0, scalar2=wi[:, h:h + 1], op0=ALU.max, op1=ALU.mult),
                            r=[ps_r, wi_r], w=[tmp_r])
                        self.pool(lambda e, c0=c0, n=n, tmp=tmp: e.tensor_tensor(
                            out=sc[:, c0:c0 + n], in0=sc[:, c0:c0 + n], in1=tmp[:, 0:n], op=ALU.add), r=[tmp_r, sc_r], w=[sc_r])
            if qt >= 2:
                self.dve(lambda e, S=S: e.tensor_reduce(out=hi[:], in_=sc[:, 0:S], axis=AX.X, op=ALU.max), r=[sc_r], w=[small_r])
                self.dve(lambda e, S=S: e.tensor_reduce(out=lo[:], in_=sc[:, 0:S], axis=AX.X, op=ALU.min), r=[sc_r], w=[small_r])
            else:
                self.pool(lambda e: e.memset(lo[:], -1e29), w=[small_r])
            self.dve(lambda e, t0=t0: e.tensor_tensor(out=sc[:, t0:t0 + 128], in0=sc[:, t0:t0 + 128], in1=cmask[:], op=ALU.add),
                     r=[sc_r, cmask_r], w=[sc_r])
            if qt >= 2:
                self.topk_threshold(sc, sc_r, S, lo, hi, mid, cnt, sel, nsel, small_r, junk, junk_r)
            self.dve(lambda e, S=S: e.tensor_scalar(out=Mb[:, 0:S], in0=sc[:, 0:S], scalar1=lo[:, 0:1], scalar2=None, op0=ALU.is_ge),
                     r=[sc_r, small_r], w=[Mb_r])

        def stage_mt(qt):
            nk = qt + 1
            MTc, MTc_r = MTns[qt % 2]
            for k0 in range(0, nk, 8):
                nb = min(8, nk - k0)
                for j in range(nb):
                    self.pe(lambda e, j=j, k0=k0: e.transpose(out=pt[:, j * 128:(j + 1) * 128], in_=Mb[:, (k0 + j) * 128:(k0 + j + 1) * 128],
                                                              identity=self.ident_b[:]), r=[Mb_r, self.ident_b_r], w=[pt_r])
                self.dve(lambda e, k0=k0, nb=nb, MTc=MTc: e.tensor_scalar(out=MTc[:, k0:k0 + nb, :], in0=pt[:, 0:nb * 128].rearrange("p (a t) -> p a t", t=128),
                                                                          scalar1=-1.0, scalar2=30000.0, op0=ALU.add, op1=ALU.mult), r=[pt_r], w=[MTc_r])

        def stage_attn(qt):
            nk = qt + 1
            qT, qT_r = qTs[qt % 2]
            sz, sz_r = szs[qt % 2]
            MTc, MTc_r = MTns[qt % 2]

            def emit_sm(g, kt):
                pex = []
                pss = []
                for par in range(2):
                    ps, ps_r = self.next_ps()
                    psv = ps[:].rearrange("p (a t) -> p a t", t=128)
                    self.pe(lambda e, psv=psv, par=par, g=g, kt=kt: e.matmul(
                        psv, lhsT=kT2[64 * par:64 * par + 64, g, kt * 128:(kt + 1) * 128], rhs=qT[64 * par:64 * par + 64, 4 * g:4 * g + 4, :],
                        start=True, stop=False, tile_position=(64 * par, 0)), r=[kT2_r, qT_r], w=[ps_r])
                    pss.append((ps, ps_r, psv))
                for par in range(2):
                    ps, ps_r, psv = pss[par]
                    self.pe(lambda e, psv=psv, kt=kt: e.matmul(psv, lhsT=self.ident_b[:], rhs=bc(MTc[:, kt, :], 1, 4), start=False, stop=True),
                            r=[self.ident_b_r, MTc_r], w=[ps_r])
                    px, px_r = pexps[st_["pe_i"] % len(pexps)]
                    st_["pe_i"] += 1
                    self.act(lambda e, px=px, psv=psv: e.activation(out=px[:], in_=psv, func=AF.Exp, scale=0.125), r=[ps_r], w=[px_r])
                    pex.append((px, px_r))
                return pex

            def emit_pv(g, kt, pex):
                for h8 in range(8):
                    par, a = h8 % 2, h8 // 2
                    acc, acc_r = accs[h8 // 4]
                    col = (h8 % 4) * 65
                    px, px_r = pex[par]
                    self.pe(lambda e, acc=acc, col=col, px=px, a=a, kt=kt, g=g, h8=h8: e.matmul(
                        acc[:, col:col + 65], lhsT=px[:, a, :], rhs=Vx[:, kt, g * 65:(g + 1) * 65],
                        start=(kt == 0 and h8 % 4 == 0), stop=(kt == nk - 1), skip_group_check=True), r=[px_r, Vx_r], w=[acc_r])
                if kt == nk - 1:
                    for half in range(2):
                        acc, acc_r = accs[half]
                        aS, aS_r = accS[g][half]
                        self.act(lambda e, acc=acc, aS=aS: e.copy(out=aS[:], in_=acc[:, 0:260]), r=[acc_r], w=[aS_r])

            units = [(g, kt) for g in range(4) for kt in range(nk)]
            pend = emit_sm(*units[0])
            for ui, (g, kt) in enumerate(units):
                nxt = emit_sm(*units[ui + 1]) if ui + 1 < len(units) else None
                emit_pv(g, kt, pend)
                pend = nxt
            for g in range(4):
                for half in range(2):
                    aS, aS_r = accS[g][half]
                    accv = aS[:].rearrange("p (h c) -> p h c", c=65)
                    c0 = 64 * (8 * g + 4 * half)
                    self.dve(lambda e, accv=accv: e.reciprocal(out=rc[:], in_=accv[:, :, 64]), r=[aS_r], w=[rc_r])
                    self.dve(lambda e, accv=accv: e.tensor_tensor(out=of_[:], in0=accv[:, :, 0:64], in1=bc(rc[:], 2, 64), op=ALU.mult),
                             r=[aS_r, rc_r], w=[of_r])
                    self.pool(lambda e, c0=c0, sz=sz: e.tensor_tensor(out=g2[:, c0:c0 + 256], in0=of_[:].rearrange("p h d -> p (h d)"),
                                                                     in1=sz[:, c0:c0 + 256], op=ALU.mult), r=[of_r, sz_r], w=[g2_r])
            self.attn_out_tile(qt, g2, g2_r, g2T, g2T_r, wo, wo_r, hb, hb_r)

        NQ = T_P // 128
        stage_idx(0)
        stage_mt(0)
        for qt in range(NQ):
            if qt + 1 < NQ:
                stage_idx(qt + 1)
            stage_attn(qt)
            if qt + 1 < NQ:
                stage_mt(qt + 1)
        self.ps_lim = 7

    def dsa_sample(self, st):
        P = self.P

        def T(name, shape, dt=F32):
            return P.sb(st, name, shape, dt)
        self.ps_lim = 5
        acc, acc_r = self.psb[5]
        pt, pt_r = self.pst
        wo, wo_r = T("awo2", [128, 16, D_MODEL], BF16)
        wov = self.attn_w_out.rearrange("(k p) n -> p k n", p=128)
        for k in range(0, 16, 2):
            self.ldc(wo[:, k:k + 2, :], wov[:, k:k + 2, :], w=[wo_r])
        pti, pti_r = T("pti", [128, N_S * 16], I32)
        self.ld(pti[:], self.page_table.partition_broadcast(128), w=[pti_r])
        iota, iota_r = T("iota", [128, 1])
        self.ld(iota[:], self.iota_d, w=[iota_r])
        ptf, ptf_r = T("ptf", [128, N_S * 16])
        self.dve(lambda e: e.tensor_copy(out=ptf[:], in_=pti[:]), r=[pti_r], w=[ptf_r])
        self.dve(lambda e: e.tensor_scalar(out=ptf[:], in0=ptf[:], scalar1=128.0, scalar2=iota[:, 0:1], op0=ALU.mult, op1=ALU.add),
                 r=[ptf_r, iota_r], w=[ptf_r])
        idx, idx_r = T("idx", [128, N_S * 16], I32)
        self.dve(lambda e: e.tensor_copy(out=idx[:], in_=ptf[:]), r=[ptf_r], w=[idx_r])
        qTs, qTs_r = T("qTs", [64, N_S, 32, 8], BF16)
        qiTs, qiTs_r = T("qiTs", [64, N_S, 8, 8], BF16)
        for n in range(N_S):
            self.ld(qTs[:, n, :, :], self.qTs_d[:, :, 8 * n:8 * n + 8], r=[self.dsa_r], w=[qTs_r])
            self.ld(qiTs[:, n, :, :], self.qiTs_d[:, :, 8 * n:8 * n + 8], r=[self.dsa_r], w=[qiTs_r])
        kTn, kTn_r = T("kTn", [64, 4, T_S], BF16)
        self.ld(kTn[:], self.kT2_d[:, 0:64, T_P:TOK].rearrange("g p t -> p g t"), r=[self.dsa_r], w=[kTn_r])
        kiTn, kiTn_r = T("kiTn", [64, T_S], BF16)
        self.ld(kiTn[:], self.kiT2_d[0, 0:64, T_P:TOK], r=[self.dsa_r], w=[kiTn_r])
        Vxn, Vxn_r = T("Vxn", [128, 260], BF16)
        self.ld(Vxn[:], self.Vx_d[T_P:TOK, :], r=[self.dsa_r], w=[Vxn_r])
        szs_, szs_r = T("szsmp", [128, E], BF16)
        self.ld(szs_[:], self.sz_d[T_P:TOK, :], r=[self.dsa_r], w=[szs_r])
        wst, wst_r = T("wst", [64, N_S])
        for h in range(8):
            self.P.dma("sp", lambda e, h=h: e.dma_start(out=wst[8 * h:8 * h + 8, :], in_=self.wi_d[T_P:TOK, h].rearrange("(n t) -> t n", t=8),
                                                       allow_slow_non_contiguous=True), [self.dsa_r], [wst_r])
        selm, selm_r = T("selm", [64, 8])
        self.ld(selm[:], self.selm_d, w=[selm_r])
        blockm, blockm_r = T("blockm", [128, 128])
        self.ld(blockm[:], self.blockm_d, w=[blockm_r])
        cmask_s, cmask_s_r = T("cmasks", [128, 8])
        self.ld(cmask_s[:], self.cmask_s_d, w=[cmask_s_r])
        sc, sc_r = T("scs", [128, 2056])
        lo, small_r = T("slo", [128, 1])
        hi, _ = T("shi", [128, 1])
        mid, _ = T("smid", [128, 1])
        cnt, _ = T("scnt", [128, 1])
        sel, _ = T("ssel", [128, 1], I32)
        nsel, _ = T("snsel", [128, 1], I32)
        Mb, Mb_r = T("sMb", [128, 2056], BF16)
        NMT, NMT_r = T("sNMT", [128, 17, 128], BF16)
        MBn, MBn_r = T("sMBn", [128, 128], BF16)
        s1 = ExitStack()
        junk, junk_r = P.sb(s1, "sjunk", [128, 2056], BF16)
        KIgs = [P.sb(s1, "KIg", [128, 16, 64]) for _ in range(2)]
        kiTg, kiTg_r = P.sb(s1, "kiTg", [64, 16, 128], BF16)
        rls = [P.sb(s1, "rl", [64, 512]) for _ in range(2)]
        scst = [P.sb(s1, "scst", [8, 2056]) for _ in range(2)]
        ri_ = 0
        for n in range(N_S):
            KIg, KIg_r = KIgs[n % 2]
            for j in range(16):
                c = 16 * n + j
                self.P.dma("pool", lambda e, KIg=KIg, j=j, c=c: e.indirect_dma_start(
                    out=KIg[:, j, :], out_offset=None, in_=self.cache_kidx,
                    in_offset=bass.IndirectOffsetOnAxis(ap=idx[:, c:c + 1], axis=0)), [idx_r], [KIg_r])
            for b0 in range(0, 16, 4):
                ps, ps_r = self.next_ps()
                for j in range(4):
                    self.pe(lambda e, ps=ps, j=j, b0=b0, KIg=KIg: e.transpose(out=ps[0:64, j * 128:(j + 1) * 128], in_=KIg[:, b0 + j, :],
                                                                           identity=self.ident_f[:]), r=[KIg_r, self.ident_f_r], w=[ps_r])
                self.act(lambda e, ps=ps, b0=b0: e.copy(out=kiTg[:, b0:b0 + 4, :], in_=ps[0:64, :].rearrange("p (a t) -> p a t", t=128)),
                         r=[ps_r], w=[kiTg_r])
            st_, st_r = scst[n % 2]
            for c4 in range(5):
                ps, ps_r = self.next_ps()
                nn = 512 if c4 < 4 else 8
                rhs = kiTg[:, 4 * c4:4 * c4 + 4, :] if c4 < 4 else kiTn[:, 8 * n:8 * n + 8]
                outp = ps[0:64, :].rearrange("p (a t) -> p a t", t=128) if c4 < 4 else ps[0:64, 0:8]
                rr = kiTg_r if c4 < 4 else kiTn_r
                self.pe(lambda e, outp=outp, rhs=rhs, n=n: e.matmul(outp, lhsT=qiTs[:, n, :, :].rearrange("p h t -> p (h t)"), rhs=rhs, start=True, stop=True),
                        r=[qiTs_r, rr], w=[ps_r])
                rl, rl_r = rls[ri_ % 2]
                ri_ += 1
                self.dve(lambda e, ps=ps, rl=rl, nn=nn, n=n: e.tensor_scalar(out=rl[:, 0:nn], in0=ps[0:64, 0:nn], scalar1=0.0, scalar2=wst[:, n:n + 1],
                                                                           op0=ALU.max, op1=ALU.mult), r=[ps_r, wst_r], w=[rl_r])
                ps2, ps2_r = self.next_ps()
                self.pe(lambda e, ps2=ps2, rl=rl, nn=nn: e.matmul(ps2[0:8, 0:nn], lhsT=selm[:], rhs=rl[:, 0:nn], start=True, stop=True),
                        r=[selm_r, rl_r], w=[ps2_r])
                self.act(lambda e, ps2=ps2, st_=st_, c4=c4, nn=nn: e.copy(out=st_[:, 512 * c4:512 * c4 + nn], in_=ps2[0:8, 0:nn]), r=[ps2_r], w=[st_r])
            self.ld(sc[8 * n:8 * n + 8, :], st_[:], r=[st_r], w=[sc_r])
        self.dve(lambda e: e.tensor_reduce(out=hi[:], in_=sc[:], axis=AX.X, op=ALU.max), r=[sc_r], w=[small_r])
        self.dve(lambda e: e.tensor_reduce(out=lo[:], in_=sc[:], axis=AX.X, op=ALU.min), r=[sc_r], w=[small_r])
        self.dve(lambda e: e.tensor_tensor(out=sc[:, 2048:2056], in0=sc[:, 2048:2056], in1=cmask_s[:], op=ALU.add), r=[sc_r, cmask_s_r], w=[sc_r])
        self.topk_threshold(sc, sc_r, 2056, lo, hi, mid, cnt, sel, nsel, small_r, junk, junk_r)
        self.dve(lambda e: e.tensor_scalar(out=Mb[:], in0=sc[:], scalar1=lo[:, 0:1], scalar2=None, op0=ALU.is_ge), r=[sc_r, small_r], w=[Mb_r])
        self.dve(lambda e: e.tensor_tensor(out=MBn[:].rearrange("p (n t) -> p n t", t=8), in0=bc(Mb[:, 2048:2056], 1, 16),
                                           in1=blockm[:].rearrange("p (n t) -> p n t", t=8), op=ALU.mult), r=[Mb_r, blockm_r], w=[MBn_r])
        for k0 in range(0, 17, 8):
            nb = min(8, 17 - k0)
            for j in range(nb):
                src = Mb[:, (k0 + j) * 128:(k0 + j + 1) * 128] if k0 + j < 16 else MBn[:]
                self.pe(lambda e, j=j, src=src: e.transpose(out=pt[:, j * 128:(j + 1) * 128], in_=src, identity=self.ident_b[:]),
                        r=[Mb_r, MBn_r, self.ident_b_r], w=[pt_r])
            self.dve(lambda e, k0=k0, nb=nb: e.tensor_scalar(out=NMT[:, k0:k0 + nb, :], in0=pt[:, 0:nb * 128].rearrange("p (a t) -> p a t", t=128),
                                                             scalar1=-1.0, scalar2=30000.0, op0=ALU.add, op1=ALU.mult), r=[pt_r], w=[NMT_r])
        P.barrier()
        s1.close()
        Kgs = [T("Kg", [128, 16, 256]) for _ in range(2)]
        Vgs = [T("Vg", [128, 16, 256]) for _ in range(2)]
        kTg, kTg_r = T("kTg", [64, 16, 4, 128], BF16)
        Vxg, Vxg_r = T("Vxg", [128, 16, 4, 65], BF16)
        self.pool(lambda e: e.memset(Vxg[:, :, :, 64:65], 1.0), w=[Vxg_r])
        pxs = [T("spx", [128, 256], BF16) for _ in range(3)]
        rc, rc_r = T("src", [64, 4])
        osb = [T("osb", [64, 4, 64]) for _ in range(2)]
        pxc = {"i": 0}
        for n in range(N_S):
            Kg, Kg_r = Kgs[n % 2]
            Vg, Vg_r = Vgs[n % 2]
            for j in range(16):
                c = 16 * n + j
                self.P.dma("pool", lambda e, Kg=Kg, j=j, c=c: e.indirect_dma_start(
                    out=Kg[:, j, :], out_offset=None, in_=self.cache_k,
                    in_offset=bass.IndirectOffsetOnAxis(ap=idx[:, c:c + 1], axis=0)), [idx_r], [Kg_r])
                self.P.dma("pool", lambda e, Vg=Vg, j=j, c=c: e.indirect_dma_start(
                    out=Vg[:, j, :], out_offset=None, in_=self.cache_v,
                    in_offset=bass.IndirectOffsetOnAxis(ap=idx[:, c:c + 1], axis=0)), [idx_r], [Vg_r])
            self.act(lambda e, Vg=Vg: e.copy(out=Vxg[:, :, :, 0:64], in_=Vg[:].rearrange("p j (g d) -> p j g d", g=4)), r=[Vg_r], w=[Vxg_r])
            for j in range(16):
                ps, ps_r = self.next_ps()
                for g in range(4):
                    self.pe(lambda e, ps=ps, j=j, g=g, Kg=Kg: e.transpose(out=ps[0:64, g * 128:(g + 1) * 128], in_=Kg[:, j, 64 * g:64 * g + 64],
                                                                        identity=self.ident_f[:]), r=[Kg_r, self.ident_f_r], w=[ps_r])
                self.act(lambda e, ps=ps, j=j: e.copy(out=kTg[:, j, :, :], in_=ps[0:64, :].rearrange("p (g t) -> p g t", t=128)), r=[ps_r], w=[kTg_r])
            def s_sm(j, n=n):
                ps, ps_r = self.next_ps()
                for g in range(4):
                    lhsT = kTg[:, j, g, :] if j < 16 else kTn[:, g, :]
                    rr = kTg_r if j < 16 else kTn_r
                    self.pe(lambda e, ps=ps, g=g, lhsT=lhsT, n=n: e.matmul(
                        ps[:, 64 * g:64 * g + 64].rearrange("p (h t) -> p h t", t=8), lhsT=lhsT, rhs=qTs[:, n, 8 * g:8 * g + 8, :],
                        start=(g == 0), stop=False, skip_group_check=True), r=[rr, qTs_r], w=[ps_r])
                self.pe(lambda e, ps=ps, j=j, n=n: e.matmul(ps[:, 0:256].rearrange("p (h t) -> p h t", t=8), lhsT=self.ident_b[:],
                                                            rhs=bc(NMT[:, j, 8 * n:8 * n + 8], 1, 32), start=False, stop=True, skip_group_check=True),
                        r=[self.ident_b_r, NMT_r], w=[ps_r])
                px, px_r = pxs[pxc["i"] % 3]
                pxc["i"] += 1
                self.act(lambda e, px=px, ps=ps: e.activation(out=px[:], in_=ps[:, 0:256], func=AF.Exp, scale=0.125), r=[ps_r], w=[px_r])
                return px, px_r

            def s_pv(j, px, px_r):
                for g in range(4):
                    rhs = Vxg[:, j, g, :] if j < 16 else Vxn[:, 65 * g:65 * g + 65]
                    rr = Vxg_r if j < 16 else Vxn_r
                    self.pe(lambda e, g=g, px=px, rhs=rhs, j=j: e.matmul(acc[0:64, 65 * g:65 * g + 65], lhsT=px[:, 64 * g:64 * g + 64], rhs=rhs,
                                                                        start=(j == 0 and g == 0), stop=(j == 16), skip_group_check=True),
                            r=[px_r, rr], w=[acc_r])

            pend = s_sm(0)
            for j in range(17):
                nxt = s_sm(j + 1) if j + 1 < 17 else None
                s_pv(j, *pend)
                pend = nxt
            accv = acc[0:64, 0:260].rearrange("p (g c) -> p g c", c=65)
            ob, ob_r = osb[n % 2]
            self.dve(lambda e, accv=accv: e.reciprocal(out=rc[:], in_=accv[:, :, 64]), r=[acc_r], w=[rc_r])
            self.dve(lambda e, accv=accv, ob=ob: e.tensor_tensor(out=ob[:], in0=accv[:, :, 0:64], in1=bc(rc[:], 2, 64), op=ALU.mult),
                     r=[acc_r, rc_r], w=[ob_r])
            for h in range(8):
                self.ld(self.os_d[8 * n:8 * n + 8, :].rearrange("t (g h d) -> h t g d", g=4, h=8)[h], ob[8 * h:8 * h + 8, :, :],
                        r=[ob_r], w=[self.dsa_r])
        osl, osl_r = T("osl", [128, E])
        self.ld(osl[:], self.os_d, r=[self.dsa_r], w=[osl_r])
        g2, g2_r = T("sg2", [128, E], BF16)
        self.dve(lambda e: e.tensor_tensor(out=g2[:], in0=osl[:], in1=szs_[:], op=ALU.mult), r=[osl_r, szs_r], w=[g2_r])
        g2T, g2T_r = T("sg2T", [128, 16, 128], BF16)
        hb, hb_r = T("shb", [128, D_MODEL])
        self.attn_out_tile(NTILE - 1, g2, g2_r, g2T, g2T_r, wo, wo_r, hb, hb_r)
        self.ps_lim = 7

    def ml_layer(self):
        P = self.P
        with ExitStack() as st:
            self.ml_proj(st)
        P.barrier()
        with ExitStack() as st:
            self.ml_feat(st)
        P.barrier()
        if self.ml_stage >= 2:
            with ExitStack() as st:
                self.ml_chunks(st)
            P.barrier()
        if self.ml_stage >= 3:
            with ExitStack() as st:
                self.ml_sample(st)
            P.barrier()

    def ml_proj(self, st):
        P = self.P
        win, win_r = P.sb(st, "mwin", [128, 8, 2 * E], BF16)
        wv = self.ml_w_in.rearrange("(k p) n -> p k n", p=128)
        for k in range(8):
            for hf in range(2):
                self.ldc(win[:, k, hf * E:(hf + 1) * E], wv[:, k, hf * E:(hf + 1) * E], w=[win_r])
        gt, gt_r = P.sb(st, "mgt", [128, D_MODEL])
        self.ld(gt[:], self.ml_norm.partition_broadcast(128), w=[gt_r])
        hts = [P.sb(st, "mht", [128, D_MODEL]) for _ in range(2)]
        junk, junk_r = P.sb(st, "mjunk", [128, D_MODEL], BF16)
        sss = [P.sb(st, "mss", [128, 1]) for _ in range(2)]
        xns = [P.sb(st, "mxn", [128, D_MODEL], BF16) for _ in range(2)]
        xTs = [P.sb(st, "mxT", [128, 8, 512], BF16) for _ in range(2)]
        obufs = [P.sb(st, "mobuf", [128, 512], BF16) for _ in range(4)]
        utok, utok_r = P.sb(st, "utok", [128, E])
        ob_i = 0
        ti = 0
        for tg in range(NGRP):
            tok0, ntok = grp_tok(tg)
            xT, xT_r = xTs[tg % 2]
            for il in range(ntok // 128):
                i = tok0 // 128 + il
                ht, ht_r = hts[ti % 2]
                ss, ss_r = sss[ti % 2]
                xn, xn_r = xns[ti % 2]
                ti += 1
                self.norm_tile(self.H[i * 128:(i + 1) * 128, :], self.H_r[i], gt, gt_r, ht, ht_r, junk, junk_r, ss, ss_r,
                               xn, xn_r, xT[:, :, il * 128:(il + 1) * 128], xT_r)
            for fo in range(32):
                ps, ps_r = self.next_ps()
                for k in range(8):
                    self.pe(lambda e, k=k, fo=fo, ps=ps, xT=xT, ntok=ntok: e.matmul(
                        ps[:, 0:ntok], lhsT=win[:, k, fo * 128:(fo + 1) * 128], rhs=xT[:, k, 0:ntok],
                        start=(k == 0), stop=(k == 7)), r=[win_r, xT_r], w=[ps_r])
                ob, ob_r = obufs[ob_i % 4]
                ob_i += 1
                if fo < 16:
                    self.act(lambda e, ob=ob, ps=ps, ntok=ntok: e.copy(out=ob[:, 0:ntok], in_=ps[:, 0:ntok]), r=[ps_r], w=[ob_r])
                    self.ld(self.muT_d[fo, :, tok0:tok0 + ntok], ob[:, 0:ntok], r=[ob_r], w=[self.ml_r])
                else:
                    self.act(lambda e, ob=ob, ps=ps, ntok=ntok: e.activation(
                        out=ob[:, 0:ntok], in_=ps[:, 0:ntok], func=AF.Silu), r=[ps_r], w=[ob_r])
                    self.ld(self.szT_d[fo - 16, :, tok0:tok0 + ntok], ob[:, 0:ntok], r=[ob_r], w=[self.szT_r[fo - 16]])
            if tg >= 7:
                c0 = ntok - 128
                for cc in range(4):
                    ps, ps_r = self.next_ps()
                    for k in range(8):
                        self.pe(lambda e, k=k, cc=cc, ps=ps, xT=xT, c0=c0: e.matmul(
                            ps[:], lhsT=xT[:, k, c0:c0 + 128], rhs=win[:, k, cc * 512:(cc + 1) * 512],
                            start=(k == 0), stop=(k == 7)), r=[win_r, xT_r], w=[ps_r])
                    self.act(lambda e, ps=ps, cc=cc: e.copy(out=utok[:, cc * 512:(cc + 1) * 512], in_=ps[:]), r=[ps_r], w=[utok_r])
                if tg == 7:
                    self.store(self.o_mconv_p, utok[125:128, :], r=[utok_r])
                else:
                    for n in range(N_S):
                        self.store(self.o_mconv_s[n], utok[8 * n + 5:8 * n + 8, :], r=[utok_r])

    def ml_feat(self, st):
        P = self.P

        def T(name, shape, dt=F32):
            return P.sb(st, name, shape, dt)
        ws = {}
        for nm, src in (("q", self.ml_w_q), ("k", self.ml_w_k), ("v", self.ml_w_v), ("o", self.ml_w_o)):
            w_, w_r = T("mw" + nm, [128, 8, 2, 256], BF16)
            for h in range(8):
                self.ldc(w_[:, h, :, :], src[h].rearrange("(dk p) e -> p dk e", p=128), w=[w_r])
            ws[nm] = (w_, w_r)
        wg, wg_r = T("mwg", [128, 48, 16], BF16)
        self.ldc(wg[:], self.ml_w_gates.rearrange("(c p) g -> p c g", p=128), w=[wg_r])
        bo, bo_r = T("mbo", [128, E])
        self.ld(bo[:], self.ml_b_o.partition_broadcast(128), w=[bo_r])
        cw, cw_r = T("mcw", [128, 16, 5])
        for j in range(4):
            self.P.dma("sp", lambda e, j=j: e.dma_start(out=cw[:, :, j], in_=self.ml_conv_w[j].rearrange("(f p) -> p f", p=128),
                                                       allow_slow_non_contiguous=True), (), [cw_r])
        self.P.dma("sp", lambda e: e.dma_start(out=cw[:, :, 4], in_=self.ml_conv_b.rearrange("(f p) -> p f", p=128),
                                              allow_slow_non_contiguous=True), (), [cw_r])
        bgi, bg_r = T("mbgi", [8, 1])
        bgf, _ = T("mbgf", [8, 1])
        nbgf, _ = T("mnbgf", [8, 1])
        self.ld(bgi[:], self.ml_b_gates[0:8, :], w=[bg_r])
        self.ld(bgf[:], self.ml_b_gates[8:16, :], w=[bg_r])
        self.dve(lambda e: e.tensor_scalar(out=nbgf[:], in0=bgf[:], scalar1=-1.0, scalar2=None, op0=ALU.mult), r=[bg_r], w=[bg_r])
        uX, uX_r = T("uX", [128, 16, 515], BF16)
        uSc, uSc_r = T("uSc", [128, 16, T_S], BF16)
        accs = [T("cacc", [128, 512]) for _ in range(2)]
        caT, caT_r = T("caT", [128, 16, 512], BF16)
        qTg, qTg_r = T("mqT", [128, 16, 512], BF16)
        kTg, kTg_r = T("mkT", [128, 16, 512], BF16)
        vTg, vTg_r = T("mvT", [128, 16, 512], BF16)
        obufs = [T("mfob", [128, 512], BF16) for _ in range(3)]
        vxs = [T("mvx", [128, 8, 257], BF16) for _ in range(2)]
        for (t_, r_) in vxs:
            self.pool(lambda e, t_=t_: e.memset(t_[:, :, 256:257], 1.0), w=[r_])
        kts = [T("mkt", [128, E], BF16) for _ in range(2)]
        ots = [T("mot", [128, E], BF16) for _ in range(2)]
        otmp = [T("motmp", [128, 512]) for _ in range(2)]
        ig, g_r = T("g_ig", [8, 512])
        lf, _ = T("g_lf", [8, 512])
        Bc, _ = T("g_B", [8, 512])
        ones, _ = T("g_ones", [8, 512])
        aa, _ = T("g_a", [8, 512])
        GG, _ = T("g_G", [8, 512])
        nG, _ = T("g_nG", [8, 512])
        wi_, _ = T("g_wi", [8, 512])
        em, _ = T("g_em", [8, 512])
        Gp, _ = T("g_Gp", [8, 8])
        Bl, _ = T("g_Bl", [8, 1])
        Gl, _ = T("g_Gl", [8, 1])
        m0s, _ = T("g_m0", [8, N_S])
        mms, _ = T("g_mm", [8, N_S])
        self.pool(lambda e: e.memset(ones[:], 1.0), w=[g_r])
        self.pool(lambda e: e.memset(Bl[:], 0.0), w=[g_r])
        self.pool(lambda e: e.memset(Gl[:], 0.0), w=[g_r])
        self.P.dma("sp", lambda e: e.dma_start(out=m0s[:], in_=self.ml_m0.rearrange("n h -> h n"), allow_slow_non_contiguous=True), (), [g_r])
        cv, cv_r = T("mcv", [48, E])
        self.ld(cv[:], self.ml_conv0, w=[cv_r])
        ob_i = 0
        vi = 0
        for tg in range(NGRP):
            tok0, ntok = grp_tok(tg)
            smp = (tg == 8)
            if not smp:
                self.ld(uX[:, :, 3:515], self.muT_d[:, :, tok0:tok0 + 512].rearrange("f p t -> p f t"), r=[self.ml_r], w=[uX_r])
                if tg == 0:
                    self.pool(lambda e: e.memset(uX[:, :, 0:3], 0.0), w=[uX_r])
                else:
                    self.ld(uX[:, :, 0:3], self.muT_d[:, :, tok0 - 3:tok0].rearrange("f p t -> p f t"), r=[self.ml_r], w=[uX_r])
            else:
                uS = uX[:, :, 0:N_S * 11].rearrange("p f (n t) -> p f n t", t=11)
                self.ld(uSc[:], self.muT_d[:, :, T_P:TOK].rearrange("f p t -> p f t"), r=[self.ml_r], w=[uSc_r])
                for f in range(16):
                    self.ld(uS[:, f, :, 3:11], self.muT_d[f, :, T_P:TOK].rearrange("p (n t) -> p n t", t=8), r=[self.ml_r], w=[uX_r])
                for f4 in range(0, 16, 4):
                    ps, ps_r = self.next_ps()
                    for j in range(4):
                        self.pe(lambda e, ps=ps, j=j, f4=f4: e.transpose(out=ps[:, j * 48:(j + 1) * 48], in_=cv[:, (f4 + j) * 128:(f4 + j + 1) * 128],
                                                                       identity=self.ident_f[0:48, 0:48]), r=[cv_r, self.ident_f_r], w=[ps_r])
                    self.act(lambda e, ps=ps, f4=f4, uS=uS: e.copy(out=uS[:, f4:f4 + 4, :, 0:3],
                                                                  in_=ps[:, 0:192].rearrange("p (f n j) -> p f n j", f=4, j=3)), r=[ps_r], w=[uX_r])
            for f in range(16):
                acc, acc_r = accs[f % 2]
                if not smp:
                    srcs = [uX[:, f, j:j + 512] for j in range(4)]
                    accv = acc[:, 0:512]
                    cav = caT[:, f, 0:512]
                else:
                    srcs = [uS[:, f, :, j:j + 8] for j in range(4)]
                    accv = acc[:, 0:128].rearrange("p (n t) -> p n t", t=8)
                    cav = caT[:, f, 0:128].rearrange("p (n t) -> p n t", t=8)
                self.dve(lambda e, accv=accv, srcs=srcs, f=f: e.tensor_scalar(out=accv, in0=srcs[0], scalar1=cw[:, f, 0:1], scalar2=cw[:, f, 4:5],
                                                                             op0=ALU.mult, op1=ALU.add), r=[uX_r, cw_r], w=[acc_r])
                for j in range(1, 4):
                    self.dve(lambda e, accv=accv, srcs=srcs, f=f, j=j: e.scalar_tensor_tensor(out=accv, in0=srcs[j], scalar=cw[:, f, j:j + 1], in1=accv,
                                                                                          op0=ALU.mult, op1=ALU.add), r=[uX_r, cw_r, acc_r], w=[acc_r])
                self.act(lambda e, accv=accv, cav=cav: e.activation(out=cav, in_=accv, func=AF.Silu), r=[acc_r], w=[caT_r])
            self.ld(self.mca_d[:, :, tok0:tok0 + ntok].rearrange("f p t -> p f t"), caT[:, :, 0:ntok], r=[caT_r], w=[self.ml_r])

            def uview(f, lo_, n_):
                if not smp:
                    return uX[:, f, 3 + lo_:3 + lo_ + n_]
                return uSc[:, f, lo_:lo_ + n_]

            for nm, src_is_ca, dst, dst_r, dscr in (("q", True, qTg, qTg_r, self.mq_d), ("k", True, kTg, kTg_r, self.mk_d),
                                                    ("v", False, vTg, vTg_r, None)):
                w_, w_r = ws[nm]
                for h in range(8):
                    for ec in range(2):
                        ps, ps_r = self.next_ps()
                        for dk in range(2):
                            if src_is_ca:
                                rhs = caT[:, 2 * h + dk, 0:ntok]
                                outp = ps[:, 0:ntok]
                            else:
                                rhs = uview(2 * h + dk, 0, ntok)
                                outp = ps[:, 0:ntok]
                            self.pe(lambda e, outp=outp, rhs=rhs, w_=w_, h=h, dk=dk, ec=ec: e.matmul(
                                outp, lhsT=w_[:, h, dk, ec * 128:(ec + 1) * 128], rhs=rhs, start=(dk == 0), stop=(dk == 1)),
                                r=[w_r, caT_r, uX_r, uSc_r], w=[ps_r])
                        self.act(lambda e, ps=ps, dst=dst, h=h, ec=ec, ntok=ntok: e.copy(out=dst[:, 2 * h + ec, 0:ntok], in_=ps[:, 0:ntok]),
                                 r=[ps_r], w=[dst_r])
                if dscr is not None:
                    self.ld(dscr[:, :, tok0:tok0 + ntok].rearrange("f p t -> p f t"), dst[:, :, 0:ntok], r=[dst_r], w=[self.ml_r])
            for gi in range(2):
                ps, ps_r = self.next_ps()
                for c in range(48):
                    src_t, src_r = ((qTg, qTg_r), (kTg, kTg_r), (vTg, vTg_r))[c // 16]
                    self.pe(lambda e, ps=ps, c=c, gi=gi, src_t=src_t, ntok=ntok: e.matmul(
                        ps[0:8, 0:ntok], lhsT=wg[:, c, 8 * gi:8 * gi + 8], rhs=src_t[:, c % 16, 0:ntok], start=(c == 0), stop=(c == 47)),
                        r=[wg_r, src_r], w=[ps_r])
                if gi == 0:
                    self.act(lambda e, ps=ps, ntok=ntok: e.activation(out=ig[:, 0:ntok], in_=ps[0:8, 0:ntok], func=AF.Copy, bias=0.0), r=[ps_r], w=[g_r])
                    self.dve(lambda e, ntok=ntok: e.tensor_scalar(out=ig[:, 0:ntok], in0=ig[:, 0:ntok], scalar1=bgi[:, 0:1], scalar2=None, op0=ALU.add),
                             r=[g_r, bg_r], w=[g_r])
                else:
                    self.act(lambda e, ps=ps, ntok=ntok: e.activation(out=lf[:, 0:ntok], in_=ps[0:8, 0:ntok], func=AF.Exp, scale=-1.0, bias=nbgf[:, 0:1]),
                             r=[ps_r, bg_r], w=[g_r])
                    self.act(lambda e, ntok=ntok: e.activation(out=lf[:, 0:ntok], in_=lf[:, 0:ntok], func=AF.Ln, bias=1.0), r=[g_r], w=[g_r])
                    self.dve(lambda e, ntok=ntok: e.tensor_scalar(out=lf[:, 0:ntok], in0=lf[:, 0:ntok], scalar1=-1.0, scalar2=None, op0=ALU.mult),
                             r=[g_r], w=[g_r])
            G = [g_r]
            if not smp:
                self.dve(lambda e: e.tensor_tensor_scan(out=Bc[:], data0=ones[:], data1=lf[:], initial=Bl[:, 0:1], op0=ALU.mult, op1=ALU.add), r=G, w=G)
                self.dve(lambda e: e.tensor_tensor(out=aa[:], in0=ig[:], in1=Bc[:], op=ALU.subtract), r=G, w=G)
                self.dve(lambda e: e.tensor_tensor_scan(out=GG[:], data0=aa[:], data1=aa[:], initial=Gl[:, 0:1], op0=ALU.max, op1=ALU.max), r=G, w=G)
                self.dve(lambda e: e.tensor_copy(out=Gp[:, 0:1], in_=Gl[:, 0:1]), r=G, w=G)
                self.dve(lambda e: e.tensor_copy(out=Gp[:, 1:8], in_=GG[:].rearrange("p (c t) -> p c t", t=64)[:, 0:7, 63]), r=G, w=G)
                self.dve(lambda e: e.tensor_tensor(out=wi_[:].rearrange("p (c t) -> p c t", t=64), in0=bc(Gp[:], 2, 64),
                                                   in1=GG[:].rearrange("p (c t) -> p c t", t=64), op=ALU.subtract), r=G, w=G)
                self.dve(lambda e: e.tensor_copy(out=Bl[:], in_=Bc[:, 511:512]), r=G, w=G)
                self.dve(lambda e: e.tensor_copy(out=Gl[:], in_=GG[:, 511:512]), r=G, w=G)
            else:
                v3 = lambda t_: t_[:, 0:128].rearrange("p (n t) -> p n t", t=8)
                B3, l3, a3, G3, i3 = v3(Bc), v3(lf), v3(aa), v3(GG), v3(ig)
                self.dve(lambda e: e.tensor_copy(out=B3[:, :, 0], in_=l3[:, :, 0]), r=G, w=G)
                for t in range(1, 8):
                    self.dve(lambda e, t=t: e.tensor_tensor(out=B3[:, :, t], in0=B3[:, :, t - 1], in1=l3[:, :, t], op=ALU.add), r=G, w=G)
                self.dve(lambda e: e.tensor_tensor(out=aa[:, 0:128], in0=ig[:, 0:128], in1=Bc[:, 0:128], op=ALU.subtract), r=G, w=G)
                self.dve(lambda e: e.tensor_tensor(out=G3[:, :, 0], in0=a3[:, :, 0], in1=m0s[:], op=ALU.max), r=G, w=G)
                for t in range(1, 8):
                    self.dve(lambda e, t=t: e.tensor_tensor(out=G3[:, :, t], in0=G3[:, :, t - 1], in1=a3[:, :, t], op=ALU.max), r=G, w=G)
                self.dve(lambda e: e.tensor_tensor(out=v3(wi_), in0=bc(m0s[:], 2, 8), in1=G3, op=ALU.subtract), r=G, w=G)
            nt = ntok
            self.act(lambda e, nt=nt: e.activation(out=wi_[:, 0:nt], in_=wi_[:, 0:nt], func=AF.Exp), r=G, w=G)
            self.dve(lambda e, nt=nt: e.tensor_scalar(out=nG[:, 0:nt], in0=GG[:, 0:nt], scalar1=-1.0, scalar2=None, op0=ALU.mult), r=G, w=G)
            self.dve(lambda e, nt=nt: e.tensor_tensor(out=Bc[:, 0:nt], in0=Bc[:, 0:nt], in1=GG[:, 0:nt], op=ALU.add), r=G, w=G)
            self.act(lambda e, nt=nt: e.activation(out=em[:, 0:nt], in_=Bc[:, 0:nt], func=AF.Exp, scale=-1.0), r=G, w=G)
            if tg == 7:
                self.store(self.o_mm_p, Bc[:, 511:512], r=G)
            if smp:
                self.dve(lambda e: e.tensor_copy(out=mms[:], in_=Bc[:, 0:128].rearrange("p (n t) -> p n t", t=8)[:, :, 7]), r=G, w=G)
                self.store(self.o_mm_s, mms[:], r=G)
            for qi_, t_ in enumerate((aa, nG, wi_, em)):
                self.ld(self.gq_d[qi_, :, tok0:tok0 + nt], t_[:, 0:nt], r=G, w=[self.ml_r])
            for il in range(ntok // 128):
                i = tok0 // 128 + il
                vx, vx_r = vxs[vi % 2]
                kt, kt_r = kts[vi % 2]
                ot, ot_r = ots[vi % 2]
                vi += 1
                for nm in ("v", "k", "o"):
                    w_, w_r = ws[nm]
                    for h2 in range(4):
                        ps, ps_r = self.next_ps()
                        for hh in range(2):
                            h = 2 * h2 + hh
                            for dk in range(2):
                                if nm == "k":
                                    lhsT = caT[:, 2 * h + dk, il * 128:(il + 1) * 128]
                                else:
                                    lhsT = uview(2 * h + dk, il * 128, 128)
                                    if smp:
                                        lhsT = lhsT
                                self.pe(lambda e, ps=ps, lhsT=lhsT, w_=w_, h=h, hh=hh, dk=dk: e.matmul(
                                    ps[:, hh * 256:(hh + 1) * 256], lhsT=lhsT, rhs=w_[:, h, dk, :], start=(dk == 0 and hh == 0), stop=(dk == 1),
                                    skip_group_check=True), r=[w_r, caT_r, uX_r, uSc_r], w=[ps_r])
                        if nm == "v":
                            self.act(lambda e, ps=ps, vx=vx, h2=h2: e.copy(out=vx[:, 2 * h2:2 * h2 + 2, 0:256],
                                                                          in_=ps[:].rearrange("p (a e) -> p a e", a=2)), r=[ps_r], w=[vx_r])
                        elif nm == "k":
                            self.act(lambda e, ps=ps, kt=kt, h2=h2: e.activation(out=kt[:, h2 * 512:(h2 + 1) * 512], in_=ps[:], func=AF.Copy, scale=1.0 / 16.0),
                                     r=[ps_r], w=[kt_r])
                        else:
                            tmp, tmp_r = otmp[h2 % 2]
                            self.dve(lambda e, ps=ps, tmp=tmp, h2=h2: e.tensor_tensor(out=tmp[:], in0=ps[:], in1=bo[:, h2 * 512:(h2 + 1) * 512], op=ALU.add),
                                     r=[ps_r, bo_r], w=[tmp_r])
                            self.act(lambda e, tmp=tmp, ot=ot, h2=h2: e.activation(out=ot[:, h2 * 512:(h2 + 1) * 512], in_=tmp[:], func=AF.Sigmoid),
                                     r=[tmp_r], w=[ot_r])
                self.ld(self.mvt_d[i * 128:(i + 1) * 128, :], vx[:].rearrange("p h e -> p (h e)"), r=[vx_r], w=[self.ml_r])
                self.ld(self.mkt_d[i * 128:(i + 1) * 128, :], kt[:], r=[kt_r], w=[self.ml_r])
                self.ld(self.mo_d[i * 128:(i + 1) * 128, :], ot[:], r=[ot_r], w=[self.ml_r])

    def ml_post_tile(self, i, hout, hout_r, K_):
        (st6, mv, rstd, sm_r, lnw, lnw_r, skip, skip_r, wout, wout_r, ot, ot_r, caTt, caTt_r, szTt, szTt_r,
         hn3, hn3_r, g2T, g2T_r, t1, t1_r, hb, hb_r) = K_
        tok0 = i * 128
        self.ld(ot[:], self.mo_d[tok0:tok0 + 128, :], r=[self.ml_r], w=[ot_r])
        self.ld(caTt[:], self.mca_d[:, :, tok0:tok0 + 128].rearrange("f p t -> p f t"), r=[self.ml_r], w=[caTt_r])
        self.ld(szTt[:], self.szT_d[:, :, tok0:tok0 + 128].rearrange("f p t -> p f t"), r=self.szT_r, w=[szTt_r])
        for h in range(8):
            self.dve(lambda e, h=h: e.bn_stats(out=st6[:, h, :], in_=hout[:, h, :]), r=[hout_r], w=[sm_r])
            self.dve(lambda e, h=h: e.bn_aggr(out=mv[:, h, :], in_=st6[:, h, :]), r=[sm_r], w=[sm_r])
        self.dve(lambda e: e.tensor_scalar(out=rstd[:], in0=mv[:, :, 1], scalar1=1e-5, scalar2=None, op0=ALU.add), r=[sm_r], w=[sm_r])
        self.act(lambda e: e.activation(out=rstd[:], in_=rstd[:], func=AF.Sqrt), r=[sm_r], w=[sm_r])
        self.dve(lambda e: e.reciprocal(out=rstd[:], in_=rstd[:]), r=[sm_r], w=[sm_r])
        for h in range(8):
            eng = self.dve if h % 2 == 0 else self.pool
            eng(lambda e, h=h: e.tensor_scalar(out=hout[:, h, :], in0=hout[:, h, :], scalar1=mv[:, h, 0:1], scalar2=rstd[:, h:h + 1],
                                               op0=ALU.subtract, op1=ALU.mult), r=[hout_r, sm_r], w=[hout_r])
        hf = hout[:].rearrange("p h e -> p (h e)")
        self.pool(lambda e: e.tensor_tensor(out=hf, in0=hf, in1=lnw[:], op=ALU.mult), r=[hout_r, lnw_r], w=[hout_r])
        self.dve(lambda e: e.tensor_tensor(out=hn3[:], in0=hf, in1=ot[:], op=ALU.mult), r=[hout_r, ot_r], w=[hn3_r])
        pt, pt_r = self.pst
        for b0 in range(0, 16, 8):
            for j in range(8):
                self.pe(lambda e, j=j, b0=b0: e.transpose(out=pt[:, j * 128:(j + 1) * 128], in_=hn3[:, (b0 + j) * 128:(b0 + j + 1) * 128],
                                                          identity=self.ident_b[:]), r=[hn3_r, self.ident_b_r], w=[pt_r])
            self.pool(lambda e, b0=b0: e.tensor_tensor(out=t1[:], in0=caTt[:, b0:b0 + 8, :], in1=bc(skip[:, b0:b0 + 8], 2, 128), op=ALU.mult),
                      r=[caTt_r, skip_r], w=[t1_r])
            self.dve(lambda e: e.tensor_tensor(out=t1[:], in0=t1[:], in1=pt[:].rearrange("p (a t) -> p a t", t=128), op=ALU.add),
                     r=[t1_r, pt_r], w=[t1_r])
            self.pool(lambda e, b0=b0: e.tensor_tensor(out=g2T[:, b0:b0 + 8, :], in0=t1[:], in1=szTt[:, b0:b0 + 8, :], op=ALU.mult),
                      r=[t1_r, szTt_r], w=[g2T_r])
        self.ld(hb[:], self.H[tok0:tok0 + 128, :], r=[self.H_r[i]], w=[hb_r])
        for hh in range(2):
            ps, ps_r = self.next_ps()
            for f in range(16):
                self.pe(lambda e, ps=ps, f=f, hh=hh: e.matmul(ps[:], lhsT=g2T[:, f, :], rhs=wout[:, f, hh * 512:(hh + 1) * 512],
                                                              start=(f == 0), stop=(f == 15)), r=[wout_r, g2T_r], w=[ps_r])
            self.dve(lambda e, ps=ps, hh=hh: e.tensor_tensor(out=hb[:, hh * 512:(hh + 1) * 512], in0=ps[:], in1=hb[:, hh * 512:(hh + 1) * 512],
                                                            op=ALU.add), r=[ps_r, hb_r], w=[hb_r])
        self.ld(self.H[tok0:tok0 + 128, :], hb[:], r=[hb_r], w=[self.H_r[i]])

    def ml_post_alloc(self, T):
        st6, sm_r = T("st6", [128, 8, 6])
        mv, _ = T("mv", [128, 8, 2])
        rstd, _ = T("rstd", [128, 8])
        lnw, lnw_r = T("lnw", [128, E])
        self.ld(lnw[:], self.ml_ln_w.partition_broadcast(128), w=[lnw_r])
        skip, skip_r = T("mskip", [128, 16])
        self.P.dma("sp", lambda e: e.dma_start(out=skip[:], in_=self.ml_skip.rearrange("(f p) -> p f", p=128), allow_slow_non_contiguous=True),
                   (), [skip_r])
        wout, wout_r = T("mwout", [128, 16, D_MODEL], BF16)
        wov = self.ml_w_out.rearrange("(k p) n -> p k n", p=128)
        for k in range(0, 16, 2):
            self.ldc(wout[:, k:k + 2, :], wov[:, k:k + 2, :], w=[wout_r])
        ot, ot_r = T("mot2", [128, E], BF16)
        caTt, caTt_r = T("caTt", [128, 16, 128], BF16)
        szTt, szTt_r = T("szTt", [128, 16, 128], BF16)
        hn3, hn3_r = T("hn3", [128, E], BF16)
        g2T, g2T_r = T("mg2T", [128, 16, 128], BF16)
        t1, t1_r = T("mt1", [128, 8, 128])
        hb, hb_r = T("mhb", [128, D_MODEL])
        return (st6, mv, rstd, sm_r, lnw, lnw_r, skip, skip_r, wout, wout_r, ot, ot_r, caTt, caTt_r, szTt, szTt_r,
                hn3, hn3_r, g2T, g2T_r, t1, t1_r, hb, hb_r)

    def ml_chunks(self, st):
        P = self.P

        def T(name, shape, dt=F32):
            return P.sb(st, name, shape, dt)
        K_ = self.ml_post_alloc(T)
        hmask, hmask_r = T("hmask", [8, 8, 128])
        self.ld(hmask[:].rearrange("p a t -> p (a t)"), self.hmask_d, w=[hmask_r])
        ones8, ones8_r = T("ones8", [8, 128])
        self.ld(ones8[:], self.ones8_d, w=[ones8_r])
        cm64, cm64_r = T("cm64", [64, 64])
        self.ld(cm64[:], self.cm64_d, w=[cm64_r])
        C = [T("Cst", [128, 2, 257]) for _ in range(8)]
        Cb = [T("Cbf", [128, 2, 257], BF16) for _ in range(8)]
        for h in range(8):
            self.pool(lambda e, h=h: e.memset(C[h][0][:], 0.0), w=[C[h][1]])
            self.pool(lambda e, h=h: e.memset(Cb[h][0][:], 0.0), w=[Cb[h][1]])
        qTg, qTg_r = T("cqT", [128, 16, 512], BF16)
        kTg, kTg_r = T("ckT", [128, 16, 512], BF16)
        gq, gq_r = T("cgq", [8, 4, 512])
        vts = [T("vtc", [64, 8 * 257], BF16) for _ in range(2)]
        kts = [T("ktc", [64, E], BF16) for _ in range(2)]
        nGe, nGe_r = T("nGe", [8, 8, 64])
        wie, wie_r = T("wie", [8, 8, 64])
        wm, wm_r = T("wm", [64, 8, 64])
        PT, PT_r = T("PT", [64, 8, 64], BF16)
        qw, qw_r = T("qw", [128, 16, 64], BF16)
        wpv, wpv_r = T("wpv", [128, 8])
        emT, emT_r = T("emT", [128, 8])
        dn, dn_r = T("dn", [128, 1])
        kws = [T("kw", [64, 256], BF16) for _ in range(8)]
        hout, hout_r = T("hout", [128, 8, 256])
        for c in range(T_P // 64):
            cl = c % 8
            cs0 = cl * 64
            t0 = c * 64
            po = 64 * (c % 2)
            if cl == 0:
                g0 = c * 64
                self.ld(qTg[:], self.mq_d[:, :, g0:g0 + 512].rearrange("f p t -> p f t"), r=[self.ml_r], w=[qTg_r])
                self.ld(kTg[:], self.mk_d[:, :, g0:g0 + 512].rearrange("f p t -> p f t"), r=[self.ml_r], w=[kTg_r])
                self.ld(gq[:], self.gq_d[:, :, g0:g0 + 512].rearrange("q h t -> h q t"), r=[self.ml_r], w=[gq_r])
            vt, vt_r = vts[c % 2]
            kt, kt_r = kts[c % 2]
            self.ld(vt[:], self.mvt_d[t0:t0 + 64, :], r=[self.ml_r], w=[vt_r])
            self.ld(kt[:], self.mkt_d[t0:t0 + 64, :], r=[self.ml_r], w=[kt_r])
            self.dve(lambda e, cs0=cs0: e.tensor_tensor(out=nGe[:], in0=bc(gq[:, 1, cs0:cs0 + 64], 1, 8), in1=hmask[:, :, 0:64], op=ALU.mult),
                     r=[gq_r, hmask_r], w=[nGe_r])
            self.pool(lambda e, cs0=cs0: e.tensor_tensor(out=wie[:], in0=bc(gq[:, 2, cs0:cs0 + 64], 1, 8), in1=hmask[:, :, 0:64], op=ALU.mult),
                      r=[gq_r, hmask_r], w=[wie_r])
            Dps, Dps_r = self.next_ps()
            Dv = Dps[0:64, :].rearrange("p (a t) -> p a t", t=64)
            self.pe(lambda e, Dv=Dv: e.matmul(Dv, lhsT=ones8[:, 0:64], rhs=nGe[:], start=True, stop=False), r=[ones8_r, nGe_r], w=[Dps_r])
            self.pe(lambda e, Dv=Dv, cs0=cs0: e.matmul(Dv, lhsT=gq[:, 0, cs0:cs0 + 64], rhs=hmask[:, :, 0:64], start=False, stop=True),
                    r=[gq_r, hmask_r], w=[Dps_r])
            Wps, Wps_r = self.next_ps()
            Wv = Wps[:].rearrange("p (a t) -> p a t", t=64)
            self.pe(lambda e, Wv=Wv: e.matmul(Wv, lhsT=ones8[:], rhs=wie[:], start=True, stop=True), r=[ones8_r, wie_r], w=[Wps_r])
            if c % 2 == 0:
                Eps, Eps_r = self.next_ps()
                self.pe(lambda e, Eps=Eps, cs0=cs0: e.matmul(Eps[:, 0:8], lhsT=gq[:, 3, cs0:cs0 + 128], rhs=self.ident_f[0:8, 0:8], start=True, stop=True),
                        r=[gq_r, self.ident_f_r], w=[Eps_r])
                self.act(lambda e, Eps=Eps: e.copy(out=emT[:], in_=Eps[:, 0:8]), r=[Eps_r], w=[emT_r])
            self.act(lambda e, Dv=Dv: e.activation(out=wm[:], in_=Dv, func=AF.Exp), r=[Dps_r], w=[wm_r])
            self.dve(lambda e: e.tensor_tensor(out=wm[:], in0=wm[:], in1=bc(cm64[:], 1, 8), op=ALU.mult), r=[wm_r, cm64_r], w=[wm_r])
            Sps, Sps_r = self.next_ps()
            for h in range(8):
                for dk in range(2):
                    self.pe(lambda e, Sps=Sps, h=h, dk=dk, cs0=cs0: e.matmul(
                        Sps[0:64, 64 * h:64 * h + 64], lhsT=kTg[:, 2 * h + dk, cs0:cs0 + 64], rhs=qTg[:, 2 * h + dk, cs0:cs0 + 64],
                        start=(h == 0 and dk == 0), stop=(dk == 1), skip_group_check=True), r=[kTg_r, qTg_r], w=[Sps_r])
            self.dve(lambda e, Sps=Sps: e.scalar_tensor_tensor(out=PT[:], in0=Sps[0:64, :].rearrange("p (a t) -> p a t", t=64), scalar=1.0 / 16.0,
                                                               in1=wm[:], op0=ALU.mult, op1=ALU.mult), r=[Sps_r, wm_r], w=[PT_r])
            self.dve(lambda e, Wv=Wv, cs0=cs0: e.tensor_tensor(out=qw[:].rearrange("p (h k) t -> p h k t", k=2),
                                                               in0=qTg[:, :, cs0:cs0 + 64].rearrange("p (h k) t -> p h k t", k=2),
                                                               in1=bc(Wv, 2, 2), op=ALU.mult), r=[qTg_r, Wps_r], w=[qw_r])
            self.act(lambda e, Wv=Wv: e.copy(out=wpv[:], in_=Wv[:, :, 63]), r=[Wps_r], w=[wpv_r])
            for h in range(8):
                kw, kw_r = kws[h]
                self.act(lambda e, kw=kw, kt=kt, h=h: e.activation(out=kw[:], in_=kt[:, 256 * h:256 * h + 256], func=AF.Copy,
                                                                  scale=wm[:, h, 63:64]), r=[kt_r, wm_r], w=[kw_r])
            for h in range(8):
                nps, nps_r = self.next_ps()
                outp = nps[po:po + 64, 0:257]
                self.pe(lambda e, outp=outp, h=h, vt=vt, po=po: e.matmul(outp, lhsT=PT[:, h, :], rhs=vt[:, 257 * h:257 * h + 257], start=True, stop=False,
                                                                        tile_position=(0, po)), r=[PT_r, vt_r], w=[nps_r])
                for dk in range(2):
                    self.pe(lambda e, outp=outp, h=h, dk=dk, po=po: e.matmul(outp, lhsT=qw[:, 2 * h + dk, :], rhs=Cb[h][0][:, dk, :], start=False, stop=(dk == 1),
                                                                            tile_position=(0, po)), r=[qw_r, Cb[h][1]], w=[nps_r])
                self.act(lambda e, nps=nps, po=po: e.activation(out=dn[po:po + 64, :], in_=nps[po:po + 64, 256:257], func=AF.Abs),
                         r=[nps_r], w=[dn_r])
                self.dve(lambda e, h=h, po=po: e.tensor_tensor(out=dn[po:po + 64, :], in0=dn[po:po + 64, :], in1=emT[po:po + 64, h:h + 1], op=ALU.max),
                         r=[dn_r, emT_r], w=[dn_r])
                self.dve(lambda e, po=po: e.reciprocal(out=dn[po:po + 64, :], in_=dn[po:po + 64, :]), r=[dn_r], w=[dn_r])
                self.act(lambda e, nps=nps, h=h, po=po: e.activation(out=hout[po:po + 64, h, :], in_=nps[po:po + 64, 0:256], func=AF.Copy,
                                                                    scale=dn[po:po + 64, 0:1]), r=[nps_r, dn_r], w=[hout_r])
                kw, kw_r = kws[h]
                for dk in range(2):
                    ups, ups_r = self.next_ps()
                    self.pe(lambda e, ups=ups, kw=kw, dk=dk, h=h, vt=vt: e.matmul(ups[:, 0:257], lhsT=kw[:, 128 * dk:128 * dk + 128],
                                                                                rhs=vt[:, 257 * h:257 * h + 257], start=True, stop=True),
                            r=[kw_r, vt_r], w=[ups_r])
                    self.dve(lambda e, ups=ups, h=h, dk=dk: e.scalar_tensor_tensor(out=C[h][0][:, dk, :], in0=C[h][0][:, dk, :], scalar=wpv[:, h:h + 1],
                                                                                  in1=ups[:, 0:257], op0=ALU.mult, op1=ALU.add),
                             r=[C[h][1], wpv_r, ups_r], w=[C[h][1]])
            for h in range(8):
                self.pool(lambda e, h=h: e.tensor_copy(out=Cb[h][0][:], in_=C[h][0][:]), r=[C[h][1]], w=[Cb[h][1]])
            if c % 2 == 1:
                self.ml_post_tile(c // 2, hout, hout_r, K_)
        nn, nn_r = T("nfin", [128, 16])
        for h in range(8):
            self.store(self.o_mc_p[h].rearrange("(dk p) e -> p dk e", p=128), C[h][0][:, :, 0:256], r=[C[h][1]])
            self.pool(lambda e, h=h: e.tensor_copy(out=nn[:, 2 * h:2 * h + 2], in_=C[h][0][:, :, 256]), r=[C[h][1]], w=[nn_r])
        ps, ps_r = self.next_ps()
        self.pe(lambda e, ps=ps: e.transpose(out=ps[0:16, 0:128], in_=nn[:], identity=self.ident_f[:]), r=[nn_r, self.ident_f_r], w=[ps_r])
        nt_, nt_r = T("nfinT", [16, 128])
        self.act(lambda e, ps=ps: e.copy(out=nt_[:], in_=ps[0:16, 0:128]), r=[ps_r], w=[nt_r])
        self.store(self.o_mn_p, nt_[:], r=[nt_r])

    def ml_sample(self, st):
        P = self.P

        def T(name, shape, dt=F32):
            return P.sb(st, name, shape, dt)
        K_ = self.ml_post_alloc(T)
        hmask, hmask_r = T("shmask", [8, 8, 128])
        self.ld(hmask[:].rearrange("p a t -> p (a t)"), self.hmask_d, w=[hmask_r])
        ones8, ones8_r = T("sones8", [8, 128])
        self.ld(ones8[:], self.ones8_d, w=[ones8_r])
        smask, smask_r = T("smask", [128, 128])
        self.ld(smask[:], self.smask_d, w=[smask_r])
        seqsel, seqsel_r = T("seqsel", [128, N_S])
        self.ld(seqsel[:], self.seqsel_d, w=[seqsel_r])
        qT, qT_r = T("sqT", [128, 16, T_S], BF16)
        kT, kT_r = T("skT", [128, 16, T_S], BF16)
        self.ld(qT[:], self.mq_d[:, :, T_P:TOK].rearrange("f p t -> p f t"), r=[self.ml_r], w=[qT_r])
        self.ld(kT[:], self.mk_d[:, :, T_P:TOK].rearrange("f p t -> p f t"), r=[self.ml_r], w=[kT_r])
        vt, vt_r = T("svt", [128, 8 * 257], BF16)
        kt, kt_r = T("skt", [128, E], BF16)
        self.ld(vt[:], self.mvt_d[T_P:TOK, :], r=[self.ml_r], w=[vt_r])
        self.ld(kt[:], self.mkt_d[T_P:TOK, :], r=[self.ml_r], w=[kt_r])
        gq, gq_r = T("sgq", [8, 4, T_S])
        self.ld(gq[:], self.gq_d[:, :, T_P:TOK].rearrange("q h t -> h q t"), r=[self.ml_r], w=[gq_r])
        nGe, nGe_r = T("snGe", [8, 8, 128])
        wie, wie_r = T("swie", [8, 8, 128])
        wsg, wsg_r = T("swsg", [8, 128])
        self.dve(lambda e: e.tensor_tensor(out=nGe[:], in0=bc(gq[:, 1, :], 1, 8), in1=hmask[:], op=ALU.mult), r=[gq_r, hmask_r], w=[nGe_r])
        self.pool(lambda e: e.tensor_tensor(out=wie[:], in0=bc(gq[:, 2, :], 1, 8), in1=hmask[:], op=ALU.mult), r=[gq_r, hmask_r], w=[wie_r])
        self.dve(lambda e: e.tensor_tensor(out=wsg[:].rearrange("p (n t) -> p n t", t=8), in0=gq[:, 0, :].rearrange("p (n t) -> p n t", t=8),
                                           in1=bc(gq[:, 1, :].rearrange("p (n t) -> p n t", t=8)[:, :, 7], 2, 8), op=ALU.add), r=[gq_r], w=[wsg_r])
        self.act(lambda e: e.activation(out=wsg[:], in_=wsg[:], func=AF.Exp), r=[wsg_r], w=[wsg_r])
        wm, wm_r = T("swm", [128, 8, 128])
        PT, PT_r = T("sPT", [128, 8, 128], BF16)
        qw, qw_r = T("sqw", [128, 16, 128], BF16)
        wpv, wpv_r = T("swpv", [128, 8, N_S])
        emT, emT_r = T("semT", [128, 8])
        wstT, wstT_r = T("swstT", [128, 8])
        Wsbs = [T("sWsb", [128, 512]) for _ in range(2)]
        ps, ps_r = self.next_ps()
        self.pe(lambda e, ps=ps: e.matmul(ps[:, 0:8], lhsT=gq[:, 3, :], rhs=self.ident_f[0:8, 0:8], start=True, stop=True), r=[gq_r, self.ident_f_r], w=[ps_r])
        self.act(lambda e, ps=ps: e.copy(out=emT[:], in_=ps[:, 0:8]), r=[ps_r], w=[emT_r])
        ps, ps_r = self.next_ps()
        self.pe(lambda e, ps=ps: e.matmul(ps[:, 0:8], lhsT=wsg[:], rhs=self.ident_f[0:8, 0:8], start=True, stop=True), r=[wsg_r, self.ident_f_r], w=[ps_r])
        self.act(lambda e, ps=ps: e.copy(out=wstT[:], in_=ps[:, 0:8]), r=[ps_r], w=[wstT_r])
        dstop = getattr(self, "dbg_stop", 99)
        if dstop <= 1:
            return
        for hb in range(2):
            hs = slice(4 * hb, 4 * hb + 4)
            Dps, Dps_r = self.next_ps()
            Dv = Dps[:].rearrange("p (a t) -> p a t", t=128)
            self.pe(lambda e, Dv=Dv, hs=hs: e.matmul(Dv, lhsT=ones8[:], rhs=nGe[:, hs, :], start=True, stop=False), r=[ones8_r, nGe_r], w=[Dps_r])
            self.pe(lambda e, Dv=Dv, hs=hs: e.matmul(Dv, lhsT=gq[:, 0, :], rhs=hmask[:, hs, :], start=False, stop=True), r=[gq_r, hmask_r], w=[Dps_r])
            self.act(lambda e, Dv=Dv, hs=hs: e.activation(out=wm[:, hs, :], in_=Dv, func=AF.Exp), r=[Dps_r], w=[wm_r])
            self.dve(lambda e, hs=hs: e.tensor_tensor(out=wm[:, hs, :], in0=wm[:, hs, :], in1=bc(smask[:], 1, 4), op=ALU.mult), r=[wm_r, smask_r], w=[wm_r])
            if dstop <= 1.2:
                continue
            Wps, Wps_r = self.next_ps()
            Wv = Wps[:].rearrange("p (a t) -> p a t", t=128)
            self.pe(lambda e, Wv=Wv, hs=hs: e.matmul(Wv, lhsT=ones8[:], rhs=wie[:, hs, :], start=True, stop=True), r=[ones8_r, wie_r], w=[Wps_r])
            if dstop <= 1.3:
                continue
            Wsb, Wsb_r = Wsbs[hb]
            self.act(lambda e, Wps=Wps, Wsb=Wsb: e.copy(out=Wsb[:], in_=Wps[:]), r=[Wps_r], w=[Wsb_r])
            for k2 in range(2):
                self.dve(lambda e, Wsb=Wsb, hb=hb, k2=k2: e.tensor_tensor(
                    out=qw[:, 8 * hb:8 * hb + 8, :].rearrange("p (h k) t -> p h k t", k=2)[:, :, k2, :],
                    in0=qT[:, 8 * hb:8 * hb + 8, :].rearrange("p (h k) t -> p h k t", k=2)[:, :, k2, :],
                    in1=Wsb[:].rearrange("p (a t) -> p a t", t=128), op=ALU.mult), r=[qT_r, Wsb_r], w=[qw_r])
            self.dve(lambda e, Wsb=Wsb, hs=hs: e.tensor_copy(out=wpv[:, hs, :], in_=Wsb[:].rearrange("p (a n t) -> p a n t", n=N_S, t=8)[:, :, :, 7]),
                     r=[Wsb_r], w=[wpv_r])
            if dstop <= 1.5:
                continue
            Sps, Sps_r = self.next_ps()
            for hh in range(4):
                h = 4 * hb + hh
                for dk in range(2):
                    self.pe(lambda e, Sps=Sps, hh=hh, h=h, dk=dk: e.matmul(Sps[:, 128 * hh:128 * hh + 128], lhsT=kT[:, 2 * h + dk, :], rhs=qT[:, 2 * h + dk, :],
                                                                          start=(hh == 0 and dk == 0), stop=(dk == 1), skip_group_check=True),
                            r=[kT_r, qT_r], w=[Sps_r])
            self.dve(lambda e, Sps=Sps, hs=hs: e.scalar_tensor_tensor(out=PT[:, hs, :], in0=Sps[:].rearrange("p (a t) -> p a t", t=128), scalar=1.0 / 16.0,
                                                                      in1=wm[:, hs, :], op0=ALU.mult, op1=ALU.mult), r=[Sps_r, wm_r], w=[PT_r])
        if dstop <= 2:
            return
        hout, hout_r = T("shout", [128, 8, 256])
        numacc, numacc_r = T("numacc", [128, 257])
        tot, tot_r = T("stot", [128, 257])
        dn, dn_r = T("sdn", [128, 1])
        Vexps = [T("Vexp", [128, N_S, 257], BF16) for _ in range(2)]
        kws = [T("skw", [128, 256], BF16) for _ in range(2)]
        C0x = [T("C0x", [128, 2, 257]) for _ in range(2)]
        C0b = [T("C0b", [128, 2, 257], BF16) for _ in range(2)]
        Cn = [T("Cn", [128, 2, 257]) for _ in range(2)]
        NN, NN_r = T("sNN", [128, N_S, 8, 2])
        ci = 0
        for h in range(8):
            Vexp, Vexp_r = Vexps[h % 2]
            kw, kw_r = kws[h % 2]
            self.pool(lambda e, Vexp=Vexp, h=h: e.tensor_tensor(out=Vexp[:], in0=bc(vt[:, 257 * h:257 * h + 257], 1, N_S), in1=bc(seqsel[:], 2, 257),
                                                                op=ALU.mult), r=[vt_r, seqsel_r], w=[Vexp_r])
            self.act(lambda e, kw=kw, h=h: e.activation(out=kw[:], in_=kt[:, 256 * h:256 * h + 256], func=AF.Copy, scale=wstT[:, h:h + 1]),
                     r=[kt_r, wstT_r], w=[kw_r])
            for n in range(N_S):
                cx, cx_r = C0x[ci % 2]
                cb, cb_r = C0b[ci % 2]
                cn, cn_r = Cn[ci % 2]
                ci += 1
                self.ld(cx[:, :, 0:256], self.ml_c0[n, h].rearrange("(dk p) e -> p dk e", p=128), w=[cx_r])
                self.P.dma("sp", lambda e, cx=cx, n=n, h=h: e.dma_start(out=cx[:, :, 256], in_=self.ml_n0[n, h].rearrange("(dk p) -> p dk", p=128),
                                                                     allow_slow_non_contiguous=True), (), [cx_r])
                self.pool(lambda e, cx=cx, cb=cb: e.tensor_copy(out=cb[:], in_=cx[:]), r=[cx_r], w=[cb_r])
                ips, ips_r = self.next_ps()
                for dk in range(2):
                    self.pe(lambda e, ips=ips, h=h, dk=dk, cb=cb: e.matmul(ips[:, 0:257], lhsT=qw[:, 2 * h + dk, :], rhs=cb[:, dk, :], start=(dk == 0), stop=(dk == 1)),
                            r=[qw_r, cb_r], w=[ips_r])
                if n == 0:
                    self.dve(lambda e, ips=ips, n=n: e.tensor_scalar(out=numacc[:], in0=ips[:, 0:257], scalar1=seqsel[:, n:n + 1], scalar2=None, op0=ALU.mult),
                             r=[ips_r, seqsel_r], w=[numacc_r])
                else:
                    self.dve(lambda e, ips=ips, n=n: e.scalar_tensor_tensor(out=numacc[:], in0=ips[:, 0:257], scalar=seqsel[:, n:n + 1], in1=numacc[:],
                                                                           op0=ALU.mult, op1=ALU.add), r=[ips_r, seqsel_r, numacc_r], w=[numacc_r])
                for dk in range(2):
                    ups, ups_r = self.next_ps()
                    self.pe(lambda e, ups=ups, kw=kw, dk=dk, n=n, Vexp=Vexp: e.matmul(ups[:, 0:257], lhsT=kw[:, 128 * dk:128 * dk + 128], rhs=Vexp[:, n, :],
                                                                                    start=True, stop=True), r=[kw_r, Vexp_r], w=[ups_r])
                    self.dve(lambda e, ups=ups, cn=cn, cx=cx, dk=dk, h=h, n=n: e.scalar_tensor_tensor(out=cn[:, dk, :], in0=cx[:, dk, :], scalar=wpv[:, h, n:n + 1],
                                                                                                 in1=ups[:, 0:257], op0=ALU.mult, op1=ALU.add),
                             r=[cx_r, wpv_r, ups_r], w=[cn_r])
                self.store(self.o_mc_s[n, h].rearrange("(dk p) e -> p dk e", p=128), cn[:, :, 0:256], r=[cn_r])
                self.pool(lambda e, cn=cn, n=n, h=h: e.tensor_copy(out=NN[:, n, h, :], in_=cn[:, :, 256]), r=[cn_r], w=[NN_r])
            nps, nps_r = self.next_ps()
            self.pe(lambda e, nps=nps, h=h: e.matmul(nps[:, 0:257], lhsT=PT[:, h, :], rhs=vt[:, 257 * h:257 * h + 257], start=True, stop=True),
                    r=[PT_r, vt_r], w=[nps_r])
            self.dve(lambda e, nps=nps: e.tensor_tensor(out=tot[:], in0=nps[:, 0:257], in1=numacc[:], op=ALU.add), r=[nps_r, numacc_r], w=[tot_r])
            self.act(lambda e: e.activation(out=dn[:], in_=tot[:, 256:257], func=AF.Abs), r=[tot_r], w=[dn_r])
            self.dve(lambda e, h=h: e.tensor_tensor(out=dn[:], in0=dn[:], in1=emT[:, h:h + 1], op=ALU.max), r=[dn_r, emT_r], w=[dn_r])
            self.dve(lambda e: e.reciprocal(out=dn[:], in_=dn[:]), r=[dn_r], w=[dn_r])
            self.act(lambda e, h=h: e.activation(out=hout[:, h, :], in_=tot[:, 0:256], func=AF.Copy, scale=dn[:, 0:1]), r=[tot_r, dn_r], w=[hout_r])
        if dstop <= 3:
            return
        self.ml_post_tile(NTILE - 1, hout, hout_r, K_)
        if dstop <= 4:
            return
        NNv = NN[:].rearrange("p n h k -> p (n h k)")
        ntT, ntT_r = T("sntT", [128, 2, 128])
        for b in range(2):
            ps, ps_r = self.next_ps()
            self.pe(lambda e, ps=ps, b=b: e.transpose(out=ps[:, 0:128], in_=NNv[:, 128 * b:128 * b + 128], identity=self.ident_f[:]),
                    r=[NN_r, self.ident_f_r], w=[ps_r])
            self.act(lambda e, ps=ps, b=b: e.copy(out=ntT[:, b, :], in_=ps[:, 0:128]), r=[ps_r], w=[ntT_r])
        self.store(self.o_mn_s.rearrange("(b r) p -> r b p", b=2), ntT[:], r=[ntT_r])

    def final_phase(self, from_x=False):
        P = self.P
        with ExitStack() as st:
            gt, gt_r = P.sb(st, "gtf", [128, D_MODEL])
            self.ld(gt[:], self.final_norm.partition_broadcast(128), w=[gt_r])
            hts = [P.sb(st, "htf", [128, D_MODEL]) for _ in range(3)]
            junk, junk_r = P.sb(st, "junkf", [128, D_MODEL], BF16)
            sss = [P.sb(st, "ssf", [128, 1]) for _ in range(3)]
            for i in range(NTILE):
                ht, ht_r = hts[i % 3]
                ss, ss_r = sss[i % 3]
                self.ld(ht[:], self.h_src(from_x, i), r=[self.H_r[i]], w=[ht_r])
                self.act(lambda e, ht=ht, ss=ss: e.activation(out=junk[:], in_=ht[:], func=AF.Square, accum_out=ss[:]),
                         r=[ht_r], w=[junk_r, ss_r])
                self.dve(lambda e, ss=ss: e.tensor_scalar(out=ss[:], in0=ss[:], scalar1=1.0 / D_MODEL, scalar2=1e-6,
                                                          op0=ALU.mult, op1=ALU.add), r=[ss_r], w=[ss_r])
                self.act(lambda e, ss=ss: e.activation(out=ss[:], in_=ss[:], func=AF.Sqrt), r=[ss_r], w=[ss_r])
                self.dve(lambda e, ss=ss: e.reciprocal(out=ss[:], in_=ss[:]), r=[ss_r], w=[ss_r])
                self.dve(lambda e, ht=ht, ss=ss: e.scalar_tensor_tensor(out=ht[:], in0=ht[:], scalar=ss[:, 0:1], in1=gt[:],
                                                                        op0=ALU.mult, op1=ALU.mult), r=[ht_r, ss_r, gt_r], w=[ht_r])
                self.store(self.y[i * 128:(i + 1) * 128, :], ht[:], r=[ht_r])


_CACHE = {}


def _get_prog(nlayers=4):
    if nlayers not in _CACHE:
        kb = KB(nlayers=nlayers)
        kb.build()
        _CACHE[nlayers] = kb
    return _CACHE[nlayers]


def make_in_maps(inp, kb):
    f32 = np.float32
    maps = []
    ident = np.eye(128, dtype=f32)
    for c in range(8):
        b = c % 4
        m = {}
        xs = np.asarray(inp["x_sample"][16 * c:16 * c + 16], f32).reshape(T_S, D_MODEL)
        m["xin"] = np.concatenate([np.asarray(inp["x_prompt"][b], f32), xs], axis=0)
        m["ident_f"] = ident
        m["final_norm"] = np.asarray(inp["final_norm"], f32).reshape(1, D_MODEL)
        for k in ["ssm_norm", "ssm_w_in", "ssm_a_re", "ssm_a_im", "ssm_b_re", "ssm_b_im", "ssm_c_re", "ssm_c_im",
                  "ssm_d", "ssm_w_glu", "ssm_b_glu", "ssm_w_out"]:
            m[k] = np.asarray(inp[k], f32)
        m["ssm_log_dt"] = np.asarray(inp["ssm_log_dt"], f32).reshape(2, 128, 1)
        m["st_re"] = np.asarray(inp["state_ssm_re"][:, 16 * c:16 * c + 16], f32).reshape(2, N_S, 8192)
        m["st_im"] = np.asarray(inp["state_ssm_im"][:, 16 * c:16 * c + 16], f32).reshape(2, N_S, 8192)
        m["attn_norm"] = np.asarray(inp["attn_norm"], f32).reshape(1, D_MODEL)
        m["attn_w_in"] = np.asarray(inp["attn_w_in"], f32).reshape(D_MODEL, ATTN_IN)
        m["attn_w_out"] = np.asarray(inp["attn_w_out"], f32).reshape(E, D_MODEL)
        m["rope_cs"] = rope_table()
        for k in ["mlstm_conv_w", "mlstm_conv_b", "mlstm_w_q", "mlstm_w_k", "mlstm_w_v", "mlstm_w_o", "mlstm_skip"]:
            m[k] = np.asarray(inp[k][0], f32)
        m["mlstm_norm"] = np.asarray(inp["mlstm_norm"], f32).reshape(1, D_MODEL)
        m["mlstm_w_in"] = np.asarray(inp["mlstm_w_in"][0], f32)
        m["mlstm_b_o"] = np.asarray(inp["mlstm_b_o"], f32).reshape(1, E)
        m["mlstm_w_gates"] = np.asarray(inp["mlstm_w_gates"][0], f32)
        m["mlstm_b_gates"] = np.asarray(inp["mlstm_b_gates"], f32).reshape(16, 1)
        m["mlstm_ln_w"] = np.asarray(inp["mlstm_ln_w"], f32).reshape(1, E)
        m["mlstm_w_out"] = np.asarray(inp["mlstm_w_out"][0], f32)
        m["ml_c0"] = np.asarray(inp["state_mlstm_c"][0, 16 * c:16 * c + 16], f32)
        m["ml_n0"] = np.asarray(inp["state_mlstm_n"][0, 16 * c:16 * c + 16], f32)
        m["ml_m0"] = np.asarray(inp["state_mlstm_m"][0, 16 * c:16 * c + 16], f32)
        m["ml_conv0"] = np.asarray(inp["state_mlstm_conv"][0, 16 * c:16 * c + 16], f32).reshape(N_S * 3, E)
        m["hmask"] = np.repeat(np.eye(8, dtype=f32), 128, axis=1)
        m["ones8"] = np.ones((8, 128), f32)
        m["cm64"] = (np.arange(64)[:, None] <= np.arange(64)[None, :]).astype(f32)
        m["seqsel"] = (np.arange(128)[:, None] // 8 == np.arange(N_S)[None, :]).astype(f32)
        ii = np.arange(128)
        m["smask"] = ((ii[:, None] // 8 == ii[None, :] // 8) & (ii[:, None] % 8 <= ii[None, :] % 8)).astype(f32)
        m["cmask"] = np.where(np.arange(128)[None, :] <= np.arange(128)[:, None], 0.0, -1e30).astype(f32)
        m["cmask_s"] = np.where(np.arange(8)[None, :] <= (np.arange(128) % 8)[:, None], 0.0, -1e30).astype(f32)
        m["selm"] = (np.arange(64)[:, None] % 8 == np.arange(8)[None, :]).astype(f32)
        m["blockm"] = (np.arange(128)[:, None] // 8 == np.arange(128)[None, :] // 8).astype(f32)
        m["iota_p"] = np.arange(128, dtype=f32).reshape(128, 1)
        m["page_table"] = np.asarray(inp["page_table"][16 * c:16 * c + 16], np.int32).reshape(1, N_S * 16)
        m["cache_k"] = np.asarray(inp["cache_k"], f32).reshape(2560 * 128, 256)
        m["cache_v"] = np.asarray(inp["cache_v"], f32).reshape(2560 * 128, 256)
        m["cache_kidx"] = np.asarray(inp["cache_kidx"], f32).reshape(2560 * 128, 64)
        maps.append({k: np.ascontiguousarray(v) for k, v in m.items() if k in kb.inputs})
    return maps


def rope_table():
    half = 32
    inv = (np.float32(10000.0) ** (-np.arange(half, dtype=np.float32) / np.float32(half))).astype(np.float32)
    pos = np.concatenate([np.arange(T_P), np.tile(2048 + np.arange(8), N_S)]).astype(np.float32)
    ang = (pos[:, None] * inv[None, :]).astype(np.float32)
    return np.concatenate([np.cos(ang), np.sin(ang)], axis=1).astype(np.float32)


def kernel(**inp):
    kb = _get_prog(4)
    maps = make_in_maps(inp, kb)
    res = run_bass_kernel_spmd(kb.nc, maps, core_ids=list(range(8))).results
    return assemble(res)


def assemble(res):
    f32 = np.float32

    def A(x):
        return np.ascontiguousarray(np.asarray(x, f32))
    P4 = range(4)
    C8 = range(8)
    y_p = A(np.stack([res[b]["y"][:T_P] for b in P4]))
    y_s = A(np.concatenate([res[c]["y"][T_P:].reshape(N_S, 8, D_MODEL) for c in C8]))
    sre_p = A(np.stack([res[b]["o_sre_p"].reshape(2, 128, 64) for b in P4], axis=1))
    sim_p = A(np.stack([res[b]["o_sim_p"].reshape(2, 128, 64) for b in P4], axis=1))
    sre_s = A(np.concatenate([res[c]["o_sre_s"].reshape(2, N_S, 128, 64) for c in C8], axis=1))
    sim_s = A(np.concatenate([res[c]["o_sim_s"].reshape(2, N_S, 128, 64) for c in C8], axis=1))
    k_p = A(np.stack([res[b]["o_k_p"].reshape(T_P, 4, 64) for b in P4]))[None]
    v_p = A(np.stack([res[b]["o_v_p"].reshape(T_P, 4, 64) for b in P4]))[None]
    ki_p = A(np.stack([res[b]["o_kidx_p"].reshape(T_P, 64) for b in P4]))[None]
    k_s = A(np.concatenate([res[c]["o_k_s"].reshape(N_S, 8, 4, 64) for c in C8]))[None]
    v_s = A(np.concatenate([res[c]["o_v_s"].reshape(N_S, 8, 4, 64) for c in C8]))[None]
    ki_s = A(np.concatenate([res[c]["o_kidx_s"].reshape(N_S, 8, 64) for c in C8]))[None]
    mc_p = A(np.stack([res[b]["o_mc_p"].reshape(8, 256, 256) for b in P4]))[None]
    mn_p = A(np.stack([res[b]["o_mn_p"].reshape(8, 256) for b in P4]))[None]
    mm_p = A(np.stack([res[b]["o_mm_p"].reshape(8) for b in P4]))[None]
    mv_p = A(np.stack([res[b]["o_mconv_p"].reshape(3, E) for b in P4]))[None]
    mc_s = A(np.concatenate([res[c]["o_mc_s"].reshape(N_S, 8, 256, 256) for c in C8]))[None]
    mn_s = A(np.concatenate([res[c]["o_mn_s"].reshape(N_S, 8, 256) for c in C8]))[None]
    mm_s = A(np.concatenate([res[c]["o_mm_s"].reshape(8, N_S).T for c in C8]))[None]
    mv_s = A(np.concatenate([res[c]["o_mconv_s"].reshape(N_S, 3, E) for c in C8]))[None]
    return (y_p, y_s, sre_p, sim_p, sre_s, sim_s, k_p, v_p, ki_p, k_s, v_s, ki_s,
            mc_p, mn_p, mm_p, mv_p, mc_s, mn_s, mm_s, mv_s)
```

```python
import math
import numpy as np
from contextlib import ExitStack
import concourse.bass as bass
import concourse.mybir as mybir
from concourse.bass_utils import run_bass_kernel_spmd

F32 = mybir.dt.float32
BF16 = mybir.dt.bfloat16
I32 = mybir.dt.int32
ALU = mybir.AluOpType
AF = mybir.ActivationFunctionType
AX = mybir.AxisListType

D_MODEL = 1024
E = 2048
T_P = 4096
N_S = 16
T_S = 128
TOK = T_P + T_S
NTILE = TOK // 128
NGRP = 9
TWO_PI = 2.0 * math.pi
ATTN_IN = 5192


def grp_tok(tg):
    return (tg * 512, 512) if tg < 8 else (T_P, T_S)


class Res:
    __slots__ = ("name", "w", "r")

    def __init__(self, name=""):
        self.name = name
        self.w = None
        self.r = {}


class Prog:
    ENG = ["pe", "dve", "act", "pool", "sp"]
    NS = 8

    def __init__(self, nc, stack):
        self.nc = nc
        self.q = {e: [] for e in self.ENG}
        self.cnt = {e: 0 for e in self.ENG}
        self.waited = {e: {} for e in self.ENG}
        self.ndma = {e: 0 for e in self.ENG}
        self.latest = {}
        self.sem = {}
        for e in ["pe", "dve", "act", "pool"]:
            self.sem[e] = stack.enter_context(nc.semaphore("s_" + e))
        for e in ["sp", "pool", "act"]:
            for i in range(self.NS):
                self.sem[("dma", e, i)] = stack.enter_context(nc.semaphore("d_%s_%d" % (e, i)))
        self.out_tokens = []
        self.uid = 0

    def sb(self, st, name, shape, dtype=F32):
        self.uid += 1
        t = st.enter_context(self.nc.sbuf_tensor("%s_%d" % (name, self.uid), list(shape), dtype))
        return t, Res(name)

    def ps(self, st, name, shape, dtype=F32):
        self.uid += 1
        t = st.enter_context(self.nc.psum_tensor("%s_%d" % (name, self.uid), list(shape), dtype))
        return t, Res(name)

    def _deps(self, reads, writes):
        deps = {}
        for r in reads:
            if r.w is not None:
                k, v = r.w
                if deps.get(k, 0) < v:
                    deps[k] = v
        for w in writes:
            if w.w is not None:
                k, v = w.w
                if deps.get(k, 0) < v:
                    deps[k] = v
            for k, v in w.r.items():
                if deps.get(k, 0) < v:
                    deps[k] = v
        return deps

    def _waits(self, eng, deps):
        ws = []
        wd = self.waited[eng]
        for k, v in deps.items():
            if eng == "pe" and k == "pe":
                continue
            if wd.get(k, 0) < v:
                wd[k] = v
                ws.append((self.sem[k], v))
        return ws

    def _update(self, tok, reads, writes):
        k, v = tok
        self.latest[k] = v
        for r in reads:
            if r.r.get(k, 0) < v:
                r.r[k] = v
        for w in writes:
            w.w = tok
            w.r = {}

    def op(self, eng, fn, reads=(), writes=()):
        deps = self._deps(reads, writes)
        ws = self._waits(eng, deps)
        self.cnt[eng] += 1
        tok = (eng, self.cnt[eng])
        sem = self.sem[eng]

        def emit(e, ws=ws, fn=fn, sem=sem):
            for s, v in ws:
                e.wait_ge(s, v)
            fn(e).then_inc(sem, 1)

        self.q[eng].append(emit)
        self._update(tok, reads, writes)
        return tok

    def dma(self, queue, fn, reads=(), writes=(), is_output=False):
        deps = self._deps(reads, writes)
        n = self.ndma[queue]
        self.ndma[queue] += 1
        idx = n % self.NS
        target = 16 * (n // self.NS + 1)
        key = ("dma", queue, idx)
        if target > 16 and deps.get(key, 0) < target - 16:
            deps[key] = target - 16
        ws = self._waits(queue, deps)
        sem = self.sem[key]

        def emit(e, ws=ws, fn=fn, sem=sem):
            for s, v in ws:
                e.wait_ge(s, v)
            fn(e).then_inc(sem, 16)

        self.q[queue].append(emit)
        tok = (key, target)
        self._update(tok, reads, writes)
        if is_output:
            self.out_tokens.append(tok)
        return tok

    def barrier(self):
        deps = dict(self.latest)
        for eng in self.ENG:
            ws = self._waits(eng, deps)
            if ws:
                def emit(e, ws=ws):
                    for s, v in ws:
                        e.wait_ge(s, v)
                self.q[eng].append(emit)

    def finish(self):
        self.barrier()

    def emit_all(self):
        nc = self.nc
        q = self.q
        with nc.Block() as block:
            @block.sync
            def _(e):
                for f in q["sp"]:
                    f(e)

            @block.tensor
            def _(e):
                for f in q["pe"]:
                    f(e)

            @block.vector
            def _(e):
                for f in q["dve"]:
                    f(e)

            @block.scalar
            def _(e):
                for f in q["act"]:
                    f(e)

            @block.gpsimd
            def _(e):
                for f in q["pool"]:
                    f(e)


def bc(ap, axis, n):
    a = ap.unsqueeze(axis)
    shp = list(a.shape)
    shp[axis] = n
    return a.broadcast_to(shp)


class KB:
    def __init__(self, nlayers=4, dbg=False):
        self.nlayers = nlayers
        self.dsa_stage = 3
        self.ml_stage = 3
        self.dbg = dbg
        nc = bass.Bass("TRN2", target_bir_lowering=False)
        self.nc = nc
        self.inputs = {}
        self.outputs = {}

    def din(self, name, shape, dtype=F32):
        t = self.nc.dram_tensor(name, list(shape), dtype, kind="ExternalInput").ap()
        self.inputs[name] = (tuple(shape), dtype)
        return t

    def dout(self, name, shape, dtype=F32):
        t = self.nc.dram_tensor(name, list(shape), dtype, kind="ExternalOutput").ap()
        self.outputs[name] = tuple(shape)
        return t

    def dscr(self, name, shape, dtype=F32):
        return self.nc.dram_tensor(name, list(shape), dtype, kind="Internal").ap()

    def dve(self, fn, r=(), w=()):
        return self.P.op("dve", fn, r, w)

    def act(self, fn, r=(), w=()):
        return self.P.op("act", fn, r, w)

    def pool(self, fn, r=(), w=()):
        return self.P.op("pool", fn, r, w)

    def pe(self, fn, r=(), w=()):
        return self.P.op("pe", fn, r, w)

    def ld(self, out, in_, r=(), w=(), q="sp"):
        return self.P.dma(q, lambda e: e.dma_start(out=out, in_=in_), r, w)

    def ldc(self, out, in_, r=(), w=()):
        return self.P.dma("pool", lambda e: e.dma_start(out=out, in_=in_), r, w)

    def store(self, out, in_, r=(), w=(), q="sp"):
        return self.P.dma(q, lambda e: e.dma_start(out=out, in_=in_), r, w, is_output=True)

    def next_ps(self):
        self.ps_i = (self.ps_i + 1) % self.ps_lim
        return self.psb[self.ps_i]

    def build(self):
        nc = self.nc
        self.xin = self.din("xin", [TOK, D_MODEL])
        self.ident_f_d = self.din("ident_f", [128, 128])
        self.final_norm = self.din("final_norm", [1, D_MODEL])
        self.ssm_norm = self.din("ssm_norm", [2, D_MODEL])
        self.ssm_w_in = self.din("ssm_w_in", [2, D_MODEL, 2 * E])
        self.ssm_a_re = self.din("ssm_a_re", [2, 128, 64])
        self.ssm_a_im = self.din("ssm_a_im", [2, 128, 64])
        self.ssm_log_dt = self.din("ssm_log_dt", [2, 128, 1])
        self.ssm_b_re = self.din("ssm_b_re", [2, 128, 64, 16])
        self.ssm_b_im = self.din("ssm_b_im", [2, 128, 64, 16])
        self.ssm_c_re = self.din("ssm_c_re", [2, 128, 16, 64])
        self.ssm_c_im = self.din("ssm_c_im", [2, 128, 16, 64])
        self.ssm_d = self.din("ssm_d", [2, E])
        self.ssm_w_glu = self.din("ssm_w_glu", [2, E, E])
        self.ssm_b_glu = self.din("ssm_b_glu", [2, E])
        self.ssm_w_out = self.din("ssm_w_out", [2, E, D_MODEL])
        self.st_re = self.din("st_re", [2, N_S, 8192])
        self.st_im = self.din("st_im", [2, N_S, 8192])
        self.attn_norm = self.din("attn_norm", [1, D_MODEL])
        self.attn_w_in = self.din("attn_w_in", [D_MODEL, ATTN_IN])
        self.attn_w_out = self.din("attn_w_out", [E, D_MODEL])
        self.rope_cs = self.din("rope_cs", [TOK, 64])
        self.cmask_d = self.din("cmask", [128, 128])
        self.cmask_s_d = self.din("cmask_s", [128, 8])
        self.selm_d = self.din("selm", [64, 8])
        self.blockm_d = self.din("blockm", [128, 128])
        self.iota_d = self.din("iota_p", [128, 1])
        self.page_table = self.din("page_table", [1, N_S * 16], I32)
        self.cache_k = self.din("cache_k", [2560 * 128, 256])
        self.cache_v = self.din("cache_v", [2560 * 128, 256])
        self.cache_kidx = self.din("cache_kidx", [2560 * 128, 64])
        self.os_d = self.dscr("os_d", [T_S, E])
        self.o_k_p = self.dout("o_k_p", [T_P, 256])
        self.o_v_p = self.dout("o_v_p", [T_P, 256])
        self.o_kidx_p = self.dout("o_kidx_p", [T_P, 64])
        self.o_k_s = self.dout("o_k_s", [T_S, 256])
        self.o_v_s = self.dout("o_v_s", [T_S, 256])
        self.o_kidx_s = self.dout("o_kidx_s", [T_S, 64])
        self.qT_d = self.dscr("qT_d", [16, 128, TOK], BF16)
        self.kT2_d = self.dscr("kT2_d", [4, 128, TOK], BF16)
        self.qiT_d = self.dscr("qiT_d", [4, 128, TOK], BF16)
        self.kiT2_d = self.dscr("kiT2_d", [1, 128, TOK], BF16)
        self.Vx_d = self.dscr("Vx_d", [TOK, 260], BF16)
        self.sz_d = self.dscr("sz_d", [TOK, E], BF16)
        self.wi_d = self.dscr("wi_d", [TOK, 8])
        self.qTs_d = self.dscr("qTs_d", [64, 32, T_S], BF16)
        self.qiTs_d = self.dscr("qiTs_d", [64, 8, T_S], BF16)
        self.dsa_r = Res("dsa_scratch")
        self.ml_norm = self.din("mlstm_norm", [1, D_MODEL])
        self.ml_w_in = self.din("mlstm_w_in", [D_MODEL, 2 * E])
        self.ml_conv_w = self.din("mlstm_conv_w", [4, E])
        self.ml_conv_b = self.din("mlstm_conv_b", [E])
        self.ml_w_q = self.din("mlstm_w_q", [8, 256, 256])
        self.ml_w_k = self.din("mlstm_w_k", [8, 256, 256])
        self.ml_w_v = self.din("mlstm_w_v", [8, 256, 256])
        self.ml_w_o = self.din("mlstm_w_o", [8, 256, 256])
        self.ml_b_o = self.din("mlstm_b_o", [1, E])
        self.ml_w_gates = self.din("mlstm_w_gates", [3 * E, 16])
        self.ml_b_gates = self.din("mlstm_b_gates", [16, 1])
        self.ml_ln_w = self.din("mlstm_ln_w", [1, E])
        self.ml_skip = self.din("mlstm_skip", [E])
        self.ml_w_out = self.din("mlstm_w_out", [E, D_MODEL])
        self.ml_c0 = self.din("ml_c0", [N_S, 8, 256, 256])
        self.ml_n0 = self.din("ml_n0", [N_S, 8, 256])
        self.ml_m0 = self.din("ml_m0", [N_S, 8])
        self.ml_conv0 = self.din("ml_conv0", [N_S * 3, E])
        self.hmask_d = self.din("hmask", [8, 8 * 128])
        self.ones8_d = self.din("ones8", [8, 128])
        self.cm64_d = self.din("cm64", [64, 64])
        self.seqsel_d = self.din("seqsel", [128, N_S])
        self.smask_d = self.din("smask", [128, 128])
        self.o_mc_p = self.dout("o_mc_p", [8, 256, 256])
        self.o_mn_p = self.dout("o_mn_p", [16, 128])
        self.o_mm_p = self.dout("o_mm_p", [8, 1])
        self.o_mconv_p = self.dout("o_mconv_p", [3, E])
        self.o_mc_s = self.dout("o_mc_s", [N_S, 8, 256, 256])
        self.o_mn_s = self.dout("o_mn_s", [N_S * 16, 128])
        self.o_mm_s = self.dout("o_mm_s", [8, N_S])
        self.o_mconv_s = self.dout("o_mconv_s", [N_S, 3, E])
        self.muT_d = self.dscr("muT_d", [16, 128, TOK], BF16)
        self.mca_d = self.dscr("mca_d", [16, 128, TOK], BF16)
        self.mq_d = self.dscr("mq_d", [16, 128, TOK], BF16)
        self.mk_d = self.dscr("mk_d", [16, 128, TOK], BF16)
        self.mvt_d = self.dscr("mvt_d", [TOK, 8 * 257], BF16)
        self.mkt_d = self.dscr("mkt_d", [TOK, E], BF16)
        self.mo_d = self.dscr("mo_d", [TOK, E], BF16)
        self.gq_d = self.dscr("gq_d", [4, 8, TOK])
        self.ml_r = Res("ml_scratch")
        self.y = self.dout("y", [TOK, D_MODEL])
        self.o_sre_p = self.dout("o_sre_p", [2, 64, 128])
        self.o_sim_p = self.dout("o_sim_p", [2, 64, 128])
        self.o_sre_s = self.dout("o_sre_s", [2, N_S, 64, 128])
        self.o_sim_s = self.dout("o_sim_s", [2, N_S, 64, 128])
        self.H = self.dscr("H", [TOK, D_MODEL])
        self.H_r = [Res("H%d" % i) for i in range(NTILE)]
        self.uT_d = self.dscr("uT_d", [16, 128, 8, 8, 64], BF16)
        self.uTs_d = self.dscr("uTs_d", [16, 128, 8, N_S], BF16)
        self.szT_d = self.dscr("szT_d", [16, 128, TOK], BF16)
        self.gT_d = self.dscr("gT_d", [16, 128, TOK], BF16)
        self.Wd = self.dscr("Wd", [16, 128, 16, 64])
        self.Vd = self.dscr("Vd", [16, 128, 64, 16])
        self.Kd = self.dscr("Kd", [8, 128, 16, 16])
        self.KCd = self.dscr("KCd", [128, 64, 30])
        self.uT_r = [Res() for _ in range(16)]
        self.szT_r = [Res() for _ in range(16)]
        self.gT_r = [Res() for _ in range(16)]
        self.Wd_r, self.Vd_r, self.Kd_r, self.KCd_r = Res(), Res(), Res(), Res()

        with ExitStack() as gst:
            self.P = P = Prog(nc, gst)
            self.psb = [P.ps(gst, "psb", [128, 512]) for _ in range(7)]
            self.pst = P.ps(gst, "pst", [128, 1024], BF16)
            self.ps_i = 0
            self.ps_lim = 7
            self.ident_f, self.ident_f_r = P.sb(gst, "identf", [128, 128])
            self.ident_b, self.ident_b_r = P.sb(gst, "identb", [128, 128], BF16)
            self.ld(self.ident_f[:], self.ident_f_d, w=[self.ident_f_r])
            self.ldc(self.ident_b[:], self.ident_f_d, w=[self.ident_b_r])

            kinds = [0, 1, 2, 0]
            slots = [0, 0, 0, 1]
            if getattr(self, "only", None) == "ml_sample":
                with ExitStack() as st:
                    self.ml_sample(st)
                P.barrier()
                self.nlayers = 0
            for layer in range(self.nlayers):
                if kinds[layer] == 0:
                    self.s5_layer(slots[layer], first=(layer == 0))
                elif kinds[layer] == 1:
                    self.dsa_layer()
                else:
                    self.ml_layer()
                P.barrier()
            self.final_phase(from_x=(self.nlayers == 0))
            P.finish()
            P.emit_all()
        return nc

    def norm_tile(self, src_ap, src_r, gt, gt_r, ht, ht_r, junk, junk_r, ss, ss_r, xn, xn_r, xT_out, xT_r):
        self.ld(ht[:], src_ap, r=[src_r], w=[ht_r])
        self.act(lambda e: e.activation(out=junk[:], in_=ht[:], func=AF.Square, accum_out=ss[:]),
                 r=[ht_r], w=[junk_r, ss_r])
        self.dve(lambda e: e.tensor_scalar(out=ss[:], in0=ss[:], scalar1=1.0 / D_MODEL, scalar2=1e-6,
                                           op0=ALU.mult, op1=ALU.add), r=[ss_r], w=[ss_r])
        self.act(lambda e: e.activation(out=ss[:], in_=ss[:], func=AF.Sqrt), r=[ss_r], w=[ss_r])
        self.dve(lambda e: e.reciprocal(out=ss[:], in_=ss[:]), r=[ss_r], w=[ss_r])
        self.dve(lambda e: e.scalar_tensor_tensor(out=xn[:], in0=ht[:], scalar=ss[:, 0:1], in1=gt[:],
                                                  op0=ALU.mult, op1=ALU.mult), r=[ht_r, ss_r, gt_r], w=[xn_r])
        pt, pt_r = self.pst
        for k in range(8):
            self.pe(lambda e, k=k: e.transpose(out=pt[:, k * 128:(k + 1) * 128], in_=xn[:, k * 128:(k + 1) * 128],
                                               identity=self.ident_b[:]), r=[xn_r, self.ident_b_r], w=[pt_r])
        self.act(lambda e: e.copy(out=xT_out, in_=pt[:].rearrange("p (k t) -> p k t", k=8)), r=[pt_r], w=[xT_r])

    def h_src(self, first, i):
        src = self.xin if first else self.H
        return src[i * 128:(i + 1) * 128, :]

    def s5_layer(self, slot, first):
        P = self.P
        nc = self.nc
        with ExitStack() as st:
            self.s5_setup(st, slot)
            win, win_r = P.sb(st, "win", [128, 8, 2 * E], BF16)
            wv = self.ssm_w_in[slot].rearrange("(k p) n -> p k n", p=128)
            for k in range(8):
                for hf in range(2):
                    self.ldc(win[:, k, hf * E:(hf + 1) * E], wv[:, k, hf * E:(hf + 1) * E], w=[win_r])
            gt, gt_r = P.sb(st, "gt", [128, D_MODEL])
            self.ld(gt[:], self.ssm_norm[slot:slot + 1, :].partition_broadcast(128), w=[gt_r])
            hts = [P.sb(st, "ht", [128, D_MODEL]) for _ in range(2)]
            junk, junk_r = P.sb(st, "junk", [128, D_MODEL], BF16)
            sss = [P.sb(st, "ss", [128, 1]) for _ in range(2)]
            xns = [P.sb(st, "xn", [128, D_MODEL], BF16) for _ in range(2)]
            xTs = [P.sb(st, "xT", [128, 8, 512], BF16) for _ in range(2)]
            obufs = [P.sb(st, "obuf", [128, 512], BF16) for _ in range(4)]
            ob_i = 0
            ti = 0
            for tg in range(NGRP):
                tok0, ntok = grp_tok(tg)
                xT, xT_r = xTs[tg % 2]
                for il in range(ntok // 128):
                    i = tok0 // 128 + il
                    ht, ht_r = hts[ti % 2]
                    ss, ss_r = sss[ti % 2]
                    xn, xn_r = xns[ti % 2]
                    ti += 1
                    self.norm_tile(self.h_src(first, i), self.H_r[i], gt, gt_r, ht, ht_r, junk, junk_r, ss, ss_r,
                                   xn, xn_r, xT[:, :, il * 128:(il + 1) * 128], xT_r)
                for fo in range(32):
                    ps, ps_r = self.next_ps()
                    for k in range(8):
                        self.pe(lambda e, k=k, fo=fo, ps=ps, xT=xT, ntok=ntok: e.matmul(
                            ps[:, 0:ntok], lhsT=win[:, k, fo * 128:(fo + 1) * 128], rhs=xT[:, k, 0:ntok],
                            start=(k == 0), stop=(k == 7)), r=[win_r, xT_r], w=[ps_r])
                    ob, ob_r = obufs[ob_i % 4]
                    ob_i += 1
                    if fo < 16:
                        if tg < 8:
                            self.act(lambda e, ob=ob, ps=ps: e.copy(
                                out=ob[:].rearrange("p (s c) -> p s c", s=8),
                                in_=ps[:].rearrange("p (c s) -> p s c", s=8)), r=[ps_r], w=[ob_r])
                            self.ld(self.uT_d[fo, :, tg, :, :], ob[:].rearrange("p (s c) -> p s c", s=8),
                                    r=[ob_r], w=[self.uT_r[fo]])
                        else:
                            self.act(lambda e, ob=ob, ps=ps: e.copy(
                                out=ob[:, 0:128].rearrange("p (s c) -> p s c", s=8),
                                in_=ps[:, 0:128].rearrange("p (c s) -> p s c", s=8)), r=[ps_r], w=[ob_r])
                            self.ld(self.uTs_d[fo], ob[:, 0:128].rearrange("p (s c) -> p s c", s=8),
                                    r=[ob_r], w=[self.uT_r[fo]])
                    else:
                        self.act(lambda e, ob=ob, ps=ps, ntok=ntok: e.activation(
                            out=ob[:, 0:ntok], in_=ps[:, 0:ntok], func=AF.Silu), r=[ps_r], w=[ob_r])
                        self.ld(self.szT_d[fo - 16, :, tok0:tok0 + ntok], ob[:, 0:ntok], r=[ob_r], w=[self.szT_r[fo - 16]])
        P.barrier()
        with ExitStack() as st:
            self.s5_scan(st, slot)
        P.barrier()
        with ExitStack() as st:
            self.s5_out(st, slot, first)

    def s5_setup(self, st0, slot):
        P = self.P
        with ExitStack() as st:
            def T(name, shape, dt=F32):
                return P.sb(st, name, shape, dt)
            ar, ar_r = T("ar", [128, 64])
            ai, ai_r = T("ai", [128, 64])
            ldt, ldt_r = T("ldt", [128, 1])
            self.ld(ar[:], self.ssm_a_re[slot], w=[ar_r])
            self.ld(ai[:], self.ssm_a_im[slot], w=[ai_r])
            self.ld(ldt[:], self.ssm_log_dt[slot], w=[ldt_r])
            dt_, dt_r = T("dt", [128, 1])
            self.act(lambda e: e.activation(out=dt_[:], in_=ldt[:], func=AF.Exp), r=[ldt_r], w=[dt_r])
            mag, mag_r = T("mag", [128, 64])
            self.act(lambda e: e.activation(out=mag[:], in_=ar[:], func=AF.Exp, scale=dt_[:, 0:1]), r=[ar_r, dt_r], w=[mag_r])
            qq, qq_r = T("qq", [128, 2, 64])
            self.dve(lambda e: e.tensor_scalar(out=qq[:, 0, :], in0=ai[:], scalar1=dt_[:, 0:1], scalar2=1.0 / TWO_PI,
                                               op0=ALU.mult, op1=ALU.mult), r=[ai_r, dt_r], w=[qq_r])
            self.dve(lambda e: e.tensor_scalar(out=qq[:, 1, :], in0=qq[:, 0, :], scalar1=0.25, scalar2=None, op0=ALU.add),
                     r=[qq_r], w=[qq_r])
            qi_, qi_r = T("qi", [128, 2, 64], I32)
            qf, qf_r = T("qf", [128, 2, 64])
            self.dve(lambda e: e.tensor_copy(out=qi_[:], in_=qq[:]), r=[qq_r], w=[qi_r])
            self.dve(lambda e: e.tensor_copy(out=qf[:], in_=qi_[:]), r=[qi_r], w=[qf_r])
            self.dve(lambda e: e.tensor_tensor(out=qq[:], in0=qq[:], in1=qf[:], op=ALU.subtract), r=[qq_r, qf_r], w=[qq_r])
            self.dve(lambda e: e.tensor_scalar(out=qq[:], in0=qq[:], scalar1=0.5, scalar2=-0.5, op0=ALU.min, op1=ALU.max),
                     r=[qq_r], w=[qq_r])
            sc, sc_r = T("sc", [128, 2, 64])
            self.act(lambda e: e.activation(out=sc[:], in_=qq[:], func=AF.Sin, scale=TWO_PI), r=[qq_r], w=[sc_r])
            Ap, Ap_r = T("Ap", [128, 9, 2, 64])
            self.pool(lambda e: e.memset(Ap[:, 0, 0, :], 1.0), w=[Ap_r])
            self.pool(lambda e: e.memset(Ap[:, 0, 1, :], 0.0), w=[Ap_r])
            self.dve(lambda e: e.tensor_tensor(out=Ap[:, 1, 0, :], in0=mag[:], in1=sc[:, 1, :], op=ALU.mult), r=[mag_r, sc_r], w=[Ap_r])
            self.dve(lambda e: e.tensor_tensor(out=Ap[:, 1, 1, :], in0=mag[:], in1=sc[:, 0, :], op=ALU.mult), r=[mag_r, sc_r], w=[Ap_r])
            t1, t1_r = T("t1", [128, 64])
            t2, t2_r = T("t2", [128, 64])

            def cmul(o_re, o_im, a_re, a_im, b_re, b_im, rr, ww):
                self.dve(lambda e: e.tensor_tensor(out=t1[:], in0=a_re, in1=b_re, op=ALU.mult), r=rr, w=[t1_r])
                self.dve(lambda e: e.tensor_tensor(out=t2[:], in0=a_im, in1=b_im, op=ALU.mult), r=rr, w=[t2_r])
                self.dve(lambda e: e.tensor_tensor(out=o_re, in0=t1[:], in1=t2[:], op=ALU.subtract), r=[t1_r, t2_r], w=ww)
                self.dve(lambda e: e.tensor_tensor(out=t1[:], in0=a_re, in1=b_im, op=ALU.mult), r=rr, w=[t1_r])
                self.dve(lambda e: e.tensor_tensor(out=t2[:], in0=a_im, in1=b_re, op=ALU.mult), r=rr, w=[t2_r])
                self.dve(lambda e: e.tensor_tensor(out=o_im, in0=t1[:], in1=t2[:], op=ALU.add), r=[t1_r, t2_r], w=ww)

            for tau in range(2, 9):
                cmul(Ap[:, tau, 0, :], Ap[:, tau, 1, :], Ap[:, tau - 1, 0, :], Ap[:, tau - 1, 1, :],
                     Ap[:, 1, 0, :], Ap[:, 1, 1, :], [Ap_r], [Ap_r])
            KC, KC_r = T("KC", [128, 64, 10, 3])
            self.dve(lambda e: e.tensor_copy(out=KC[:, :, 0, 0], in_=Ap[:, 8, 0, :]), r=[Ap_r], w=[KC_r])
            self.dve(lambda e: e.tensor_copy(out=KC[:, :, 0, 1], in_=Ap[:, 8, 1, :]), r=[Ap_r], w=[KC_r])
            for k in range(1, 10):
                cmul(KC[:, :, k, 0], KC[:, :, k, 1], KC[:, :, k - 1, 0], KC[:, :, k - 1, 1],
                     KC[:, :, k - 1, 0], KC[:, :, k - 1, 1], [KC_r], [KC_r])
            self.dve(lambda e: e.tensor_scalar(out=KC[:, :, :, 2], in0=KC[:, :, :, 1], scalar1=-1.0, scalar2=None, op0=ALU.mult),
                     r=[KC_r], w=[KC_r])
            self.ld(self.KCd, KC[:].rearrange("g p k c -> g p (k c)"), r=[KC_r], w=[self.KCd_r])
            den, den_r = T("den", [128, 64])
            self.dve(lambda e: e.tensor_tensor(out=den[:], in0=ar[:], in1=ar[:], op=ALU.mult), r=[ar_r], w=[den_r])
            self.dve(lambda e: e.tensor_tensor(out=t1[:], in0=ai[:], in1=ai[:], op=ALU.mult), r=[ai_r], w=[t1_r])
            self.dve(lambda e: e.tensor_tensor(out=den[:], in0=den[:], in1=t1[:], op=ALU.add), r=[den_r, t1_r], w=[den_r])
            self.dve(lambda e: e.reciprocal(out=den[:], in_=den[:]), r=[den_r], w=[den_r])
            zr, zr_r = T("zr", [128, 64])
            self.dve(lambda e: e.tensor_scalar(out=zr[:], in0=Ap[:, 1, 0, :], scalar1=-1.0, scalar2=None, op0=ALU.add), r=[Ap_r], w=[zr_r])
            Ff, Ff_r = T("Ff", [128, 2, 64])
            self.dve(lambda e: e.tensor_tensor(out=t1[:], in0=zr[:], in1=ar[:], op=ALU.mult), r=[zr_r, ar_r], w=[t1_r])
            self.dve(lambda e: e.tensor_tensor(out=t2[:], in0=Ap[:, 1, 1, :], in1=ai[:], op=ALU.mult), r=[Ap_r, ai_r], w=[t2_r])
            self.dve(lambda e: e.tensor_tensor(out=t1[:], in0=t1[:], in1=t2[:], op=ALU.add), r=[t1_r, t2_r], w=[t1_r])
            self.dve(lambda e: e.tensor_tensor(out=Ff[:, 0, :], in0=t1[:], in1=den[:], op=ALU.mult), r=[t1_r, den_r], w=[Ff_r])
            self.dve(lambda e: e.tensor_tensor(out=t1[:], in0=Ap[:, 1, 1, :], in1=ar[:], op=ALU.mult), r=[Ap_r, ar_r], w=[t1_r])
            self.dve(lambda e: e.tensor_tensor(out=t2[:], in0=zr[:], in1=ai[:], op=ALU.mult), r=[zr_r, ai_r], w=[t2_r])
            self.dve(lambda e: e.tensor_tensor(out=t1[:], in0=t1[:], in1=t2[:], op=ALU.subtract), r=[t1_r, t2_r], w=[t1_r])
            self.dve(lambda e: e.tensor_tensor(out=Ff[:, 1, :], in0=t1[:], in1=den[:], op=ALU.mult), r=[t1_r, den_r], w=[Ff_r])
            br, br_r = T("br", [128, 64, 16])
            bi, bi_r = T("bi", [128, 64, 16])
            self.ld(br[:], self.ssm_b_re[slot], w=[br_r])
            self.ld(bi[:], self.ssm_b_im[slot], w=[bi_r])
            brT = br[:].rearrange("g p j -> g j p")
            biT = bi[:].rearrange("g p j -> g j p")
            EE = [T("EE", [128, 16, 2, 64]) for _ in range(2)]
            u1, u1_r = T("u1", [128, 16, 64])
            u2, u2_r = T("u2", [128, 16, 64])

            def cmul_b(o, o_r, x_re, x_im, xr, a_re, a_im, a_r):
                ab_re = bc(a_re, 1, 16)
                ab_im = bc(a_im, 1, 16)
                self.dve(lambda e: e.tensor_tensor(out=u1[:], in0=x_re, in1=ab_re, op=ALU.mult), r=xr + a_r, w=[u1_r])
                self.pool(lambda e: e.tensor_tensor(out=u2[:], in0=x_im, in1=ab_im, op=ALU.mult), r=xr + a_r, w=[u2_r])
                self.dve(lambda e: e.tensor_tensor(out=o[:, :, 0, :], in0=u1[:], in1=u2[:], op=ALU.subtract), r=[u1_r, u2_r], w=[o_r])
                self.dve(lambda e: e.tensor_tensor(out=u1[:], in0=x_re, in1=ab_im, op=ALU.mult), r=xr + a_r, w=[u1_r])
                self.pool(lambda e: e.tensor_tensor(out=u2[:], in0=x_im, in1=ab_re, op=ALU.mult), r=xr + a_r, w=[u2_r])
                self.dve(lambda e: e.tensor_tensor(out=o[:, :, 1, :], in0=u1[:], in1=u2[:], op=ALU.add), r=[u1_r, u2_r], w=[o_r])

            CC, CC_r = T("CC", [128, 16, 2, 64])
            self.ld(CC[:, :, 0, :], self.ssm_c_re[slot], w=[CC_r])
            self.ld(CC[:, :, 1, :], self.ssm_c_im[slot], w=[CC_r])
            self.dve(lambda e: e.tensor_scalar(out=CC[:, :, 1, :], in0=CC[:, :, 1, :], scalar1=-1.0, scalar2=None, op0=ALU.mult),
                     r=[CC_r], w=[CC_r])
            Kg, Kg_r = T("Kg", [128, 8, 16, 16])
            tm = [T("tm", [128, 16, 128]) for _ in range(2)]
            E0, E0_r = EE[0]
            cmul_b(E0, E0_r, brT, biT, [br_r, bi_r], Ff[:, 0, :], Ff[:, 1, :], [Ff_r])
            for tau in range(8):
                Ec, Ec_r = EE[tau % 2]
                if tau > 0:
                    Epv, Epv_r = EE[(tau - 1) % 2]
                    cmul_b(Ec, Ec_r, Epv[:, :, 0, :], Epv[:, :, 1, :], [Epv_r], Ap[:, 1, 0, :], Ap[:, 1, 1, :], [Ap_r])
                self.ld(self.Wd[2 * (7 - tau):2 * (7 - tau) + 2].rearrange("ri g j p -> g j ri p"), Ec[:], r=[Ec_r], w=[self.Wd_r])
                for j in range(16):
                    tmj, tmj_r = tm[j % 2]
                    eb = bc(Ec[:, j, :, :].rearrange("g r p -> g (r p)"), 1, 16)
                    self.pool(lambda e, tmj=tmj, eb=eb: e.tensor_tensor(out=tmj[:], in0=CC[:].rearrange("g i r p -> g i (r p)"),
                                                                        in1=eb, op=ALU.mult), r=[CC_r, Ec_r], w=[tmj_r])
                    self.dve(lambda e, tmj=tmj, tau=tau, j=j: e.tensor_reduce(out=Kg[:, tau, j, :], in_=tmj[:], axis=AX.X, op=ALU.add),
                             r=[tmj_r], w=[Kg_r])
            self.ld(self.Kd.rearrange("t g j i -> g t (j i)"), Kg[:].rearrange("g t j i -> g t (j i)"), r=[Kg_r], w=[self.Kd_r])
            VV = [T("VV", [128, 2, 64, 16]) for _ in range(2)]
            CrT = CC[:, :, 0, :].rearrange("g i p -> g p i")
            nCiT = CC[:, :, 1, :].rearrange("g i p -> g p i")
            w1, w1_r = T("w1", [128, 64, 16])
            w2, w2_r = T("w2", [128, 64, 16])
            for t in range(8):
                Vc, Vc_r = VV[t % 2]
                are = bc(Ap[:, t + 1, 0, :], 2, 16)
                aim = bc(Ap[:, t + 1, 1, :], 2, 16)
                self.dve(lambda e, are=are: e.tensor_tensor(out=w1[:], in0=CrT, in1=are, op=ALU.mult), r=[CC_r, Ap_r], w=[w1_r])
                self.pool(lambda e, aim=aim: e.tensor_tensor(out=w2[:], in0=nCiT, in1=aim, op=ALU.mult), r=[CC_r, Ap_r], w=[w2_r])
                self.dve(lambda e, Vc=Vc: e.tensor_tensor(out=Vc[:, 0, :, :], in0=w1[:], in1=w2[:], op=ALU.add), r=[w1_r, w2_r], w=[Vc_r])
                self.dve(lambda e, aim=aim: e.tensor_tensor(out=w1[:], in0=CrT, in1=aim, op=ALU.mult), r=[CC_r, Ap_r], w=[w1_r])
                self.pool(lambda e, are=are: e.tensor_tensor(out=w2[:], in0=nCiT, in1=are, op=ALU.mult), r=[CC_r, Ap_r], w=[w2_r])
                self.dve(lambda e, Vc=Vc: e.tensor_tensor(out=Vc[:, 1, :, :], in0=w2[:], in1=w1[:], op=ALU.subtract), r=[w1_r, w2_r], w=[Vc_r])
                self.ld(self.Vd[2 * t:2 * t + 2].rearrange("r g p i -> g r (p i)"), Vc[:].rearrange("g r p i -> g r (p i)"),
                        r=[Vc_r], w=[self.Vd_r])
            P.barrier()

    def s5_scan(self, st, slot):
        P = self.P

        def T(name, shape, dt=F32):
            return P.sb(st, name, shape, dt)
        Kt, Kt_r = T("Kt", [128, 16, 8, 128], BF16)
        self.pool(lambda e: e.memset(Kt[:], 0.0), w=[Kt_r])
        for g8 in range(8):
            for tau in range(8):
                self.ldc(Kt[16 * g8:16 * g8 + 16, :, tau, 16 * g8:16 * g8 + 16],
                         self.Kd[tau].rearrange("(f g) j i -> g j f i", g=8)[g8], r=[self.Kd_r], w=[Kt_r])
        KC, KC_r = T("KCs", [128, 64, 30])
        self.ld(KC[:], self.KCd.rearrange("(q g) p k -> (g p) q k", g=2), r=[self.KCd_r], w=[KC_r])
        Dk, Dk_r = T("Dk", [128, 16])
        self.P.dma("sp", lambda e: e.dma_start(out=Dk[:], in_=self.ssm_d[slot].rearrange("(f p) -> p f", p=128),
                                              allow_slow_non_contiguous=True), (), [Dk_r])
        Wts = [T("Wt", [128, 16, 128], BF16) for _ in range(2)]
        Vts = [T("Vt", [128, 4, 16, 32], BF16) for _ in range(2)]
        for (t_, r_) in Wts + Vts:
            self.pool(lambda e, t_=t_: e.memset(t_[:], 0.0), w=[r_])
        uTfs = [T("uTf", [128, 8, 8, 64], BF16) for _ in range(2)]
        uTss = [T("uTs", [128, 8, N_S], BF16) for _ in range(2)]
        XA = [[T("XA", [128, 513]) for _ in range(2)] for _ in range(4)]
        XB = [[T("XB", [128, 513]) for _ in range(2)] for _ in range(4)]
        Xb = [[T("Xb", [128, 512], BF16) for _ in range(2)] for _ in range(4)]
        XS, XS_r = T("XS", [128, 4, 2, N_S, 2])
        XSb, XSb_r = T("XSb", [128, 4, 2, N_S], BF16)
        tS = [T("tS", [128, N_S]) for _ in range(2)]
        gbufs = [T("gbuf", [128, TOK], BF16) for _ in range(2)]
        ytmps = [T("ytmp", [128, 512]) for _ in range(2)]
        Fin, Fin_r = T("Fin", [128, 2, 64])
        FinS, FinS_r = T("FinS", [128, 2, 64, N_S])
        s0s = [[T("s0", [N_S, 512]) for _ in range(2)] for _ in range(2)]
        for q in range(4):
            for ri in range(2):
                self.pool(lambda e, q=q, ri=ri: e.memset(XA[q][ri][0][:, 0:1], 0.0), w=[XA[q][ri][1]])
        yi = 0
        for f in range(16):
            Wt, Wt_r = Wts[f % 2]
            Vt, Vt_r = Vts[f % 2]
            uTf, uTf_r = uTfs[f % 2]
            uTs, uTs_r = uTss[f % 2]
            gbuf, gbuf_r = gbufs[f % 2]
            self.ld(uTf[:].rearrange("p a s c -> p (a s c)"), self.uT_d[f].rearrange("p a s c -> p (a s c)"),
                    r=[self.uT_r[f]], w=[uTf_r])
            self.ld(uTs[:], self.uTs_d[f], r=[self.uT_r[f]], w=[uTs_r])
            s0 = s0s[f % 2]
            self.ld(s0[0][0][:], self.st_re[slot][:, f * 512:(f + 1) * 512], w=[s0[0][1]])
            self.ld(s0[1][0][:], self.st_im[slot][:, f * 512:(f + 1) * 512], w=[s0[1][1]])
            for g8 in range(8):
                g = 8 * f + g8
                self.ldc(Wt[16 * g8:16 * g8 + 16, :, 64 * (g8 % 2):64 * (g8 % 2) + 64],
                         self.Wd[:, g, :, :].rearrange("sr j p -> j sr p"), r=[self.Wd_r], w=[Wt_r])
            for g2 in range(2):
                for q in range(4):
                    self.ldc(Vt[64 * g2:64 * g2 + 64, q, :, 16 * g2:16 * g2 + 16],
                             self.Vd[:, 8 * f + 2 * q + g2, :, :].rearrange("tr p i -> p tr i"),
                             r=[self.Vd_r], w=[Vt_r])
            for q in range(4):
                qq = 4 * f + q
                for ri in range(2):
                    ps, ps_r = self.next_ps()
                    for s in range(8):
                        self.pe(lambda e, ps=ps, q=q, ri=ri, s=s, Wt=Wt, uTf=uTf: e.matmul(
                            ps[:].rearrange("p (a c) -> p a c", a=8), lhsT=Wt[32 * q:32 * q + 32, 2 * s + ri, :],
                            rhs=uTf[32 * q:32 * q + 32, :, s, :], start=(s == 0), stop=(s == 7),
                            tile_position=(32 * q, 0)), r=[Wt_r, uTf_r], w=[ps_r])
                    xa, xa_r = XA[q][ri]
                    self.act(lambda e, xa=xa, ps=ps: e.copy(out=xa[:, 1:513], in_=ps[:]), r=[ps_r], w=[xa_r])
                    ps, ps_r = self.next_ps()
                    for s in range(8):
                        self.pe(lambda e, ps=ps, q=q, ri=ri, s=s, Wt=Wt, uTs=uTs: e.matmul(
                            ps[:, 0:N_S], lhsT=Wt[32 * q:32 * q + 32, 2 * s + ri, :],
                            rhs=uTs[32 * q:32 * q + 32, s, :], start=(s == 0), stop=(s == 7),
                            tile_position=(32 * q, 0)), r=[Wt_r, uTs_r], w=[ps_r])
                    s0t, s0_r = s0[ri]
                    self.pe(lambda e, ps=ps, s0t=s0t, q=q: e.transpose(
                        out=ps[:, 32:32 + N_S], in_=s0t[:, q * 128:(q + 1) * 128], identity=self.ident_f[0:N_S, 0:N_S]),
                        r=[s0_r, self.ident_f_r], w=[ps_r])
                    self.act(lambda e, ps=ps, q=q, ri=ri: e.copy(out=XS[:, q, ri, :, 1], in_=ps[:, 0:N_S]), r=[ps_r], w=[XS_r])
                    self.act(lambda e, ps=ps, q=q, ri=ri: e.copy(out=XS[:, q, ri, :, 0], in_=ps[:, 32:32 + N_S]), r=[ps_r], w=[XS_r])
            for q in range(4):
                qq = 4 * f + q
                cur = XA[q]
                nxt = XB[q]
                for k in range(9):
                    d = 1 << k
                    n = 513 - d
                    cr = KC[:, qq, 3 * k:3 * k + 1]
                    ci = KC[:, qq, 3 * k + 1:3 * k + 2]
                    nci = KC[:, qq, 3 * k + 2:3 * k + 3]
                    (c_re, c_re_r), (c_im, c_im_r) = cur
                    (n_re, n_re_r), (n_im, n_im_r) = nxt
                    self.dve(lambda e, c_re=c_re, n_re=n_re, cr=cr, d=d, n=n: e.scalar_tensor_tensor(
                        out=n_re[:, d:513], in0=c_re[:, 0:n], scalar=cr, in1=c_re[:, d:513], op0=ALU.mult, op1=ALU.add),
                        r=[c_re_r, KC_r], w=[n_re_r])
                    self.dve(lambda e, c_im=c_im, n_re=n_re, nci=nci, d=d, n=n: e.scalar_tensor_tensor(
                        out=n_re[:, d:513], in0=c_im[:, 0:n], scalar=nci, in1=n_re[:, d:513], op0=ALU.mult, op1=ALU.add),
                        r=[c_im_r, n_re_r, KC_r], w=[n_re_r])
                    self.dve(lambda e, c_im=c_im, n_im=n_im, cr=cr, d=d, n=n: e.scalar_tensor_tensor(
                        out=n_im[:, d:513], in0=c_im[:, 0:n], scalar=cr, in1=c_im[:, d:513], op0=ALU.mult, op1=ALU.add),
                        r=[c_im_r, KC_r], w=[n_im_r])
                    self.dve(lambda e, c_re=c_re, n_im=n_im, ci=ci, d=d, n=n: e.scalar_tensor_tensor(
                        out=n_im[:, d:513], in0=c_re[:, 0:n], scalar=ci, in1=n_im[:, d:513], op0=ALU.mult, op1=ALU.add),
                        r=[c_re_r, n_im_r, KC_r], w=[n_im_r])
                    self.pool(lambda e, c_re=c_re, n_re=n_re, d=d: e.tensor_copy(out=n_re[:, 0:d], in_=c_re[:, 0:d]),
                              r=[c_re_r], w=[n_re_r])
                    self.pool(lambda e, c_im=c_im, n_im=n_im, d=d: e.tensor_copy(out=n_im[:, 0:d], in_=c_im[:, 0:d]),
                              r=[c_im_r], w=[n_im_r])
                    cur, nxt = nxt, cur
                for ri in range(2):
                    xa, xa_r = cur[ri]
                    xb_, xb_r = Xb[q][ri]
                    self.act(lambda e, xa=xa, xb_=xb_: e.copy(out=xb_[:], in_=xa[:, 0:512]), r=[xa_r], w=[xb_r])
                    self.pool(lambda e, xa=xa, ri=ri, qq=qq: e.tensor_copy(out=Fin[:, ri, qq:qq + 1], in_=xa[:, 512:513]),
                              r=[xa_r], w=[Fin_r])
                cr = KC[:, qq, 0:1]
                ci = KC[:, qq, 1:2]
                nci = KC[:, qq, 2:3]
                (ta, ta_r), (tb, tb_r) = tS
                self.dve(lambda e, q=q, cr=cr: e.scalar_tensor_tensor(out=ta[:], in0=XS[:, q, 0, :, 0], scalar=cr, in1=XS[:, q, 0, :, 1],
                                                                      op0=ALU.mult, op1=ALU.add), r=[XS_r, KC_r], w=[ta_r])
                self.dve(lambda e, q=q, cr=cr: e.scalar_tensor_tensor(out=tb[:], in0=XS[:, q, 1, :, 0], scalar=cr, in1=XS[:, q, 1, :, 1],
                                                                      op0=ALU.mult, op1=ALU.add), r=[XS_r, KC_r], w=[tb_r])
                self.dve(lambda e, q=q, nci=nci, qq=qq: e.scalar_tensor_tensor(out=FinS[:, 0, qq, :], in0=XS[:, q, 1, :, 0], scalar=nci, in1=ta[:],
                                                                              op0=ALU.mult, op1=ALU.add), r=[XS_r, KC_r, ta_r], w=[FinS_r])
                self.dve(lambda e, q=q, ci=ci, qq=qq: e.scalar_tensor_tensor(out=FinS[:, 1, qq, :], in0=XS[:, q, 0, :, 0], scalar=ci, in1=tb[:],
                                                                             op0=ALU.mult, op1=ALU.add), r=[XS_r, KC_r, tb_r], w=[FinS_r])
                self.act(lambda e, q=q: e.copy(out=XSb[:, q, :, :], in_=XS[:, q, :, :, 0]), r=[XS_r], w=[XSb_r])
            for t in range(8):
                for smp in range(2):
                    ps, ps_r = self.next_ps()
                    nn = 512 if smp == 0 else N_S
                    first_mm = True
                    for s in range(t + 1):
                        rhs = uTf[:, :, s, :] if smp == 0 else uTs[:, s, :]
                        outp = ps[:].rearrange("p (a c) -> p a c", a=8) if smp == 0 else ps[:, 0:N_S]
                        self.pe(lambda e, outp=outp, rhs=rhs, t=t, s=s, f=f, fm=first_mm: e.matmul(
                            outp, lhsT=Kt[:, f, t - s, :], rhs=rhs, start=fm, stop=False),
                            r=[Kt_r, uTf_r, uTs_r], w=[ps_r])
                        first_mm = False
                    for q in range(4):
                        for ri in range(2):
                            rhs = Xb[q][ri][0][:, 0:512] if smp == 0 else XSb[:, q, ri, :]
                            rr = Xb[q][ri][1] if smp == 0 else XSb_r
                            last = (q == 3 and ri == 1)
                            self.pe(lambda e, ps=ps, rhs=rhs, q=q, ri=ri, t=t, Vt=Vt, nn=nn, last=last: e.matmul(
                                ps[32 * q:32 * q + 32, 0:nn], lhsT=Vt[:, q, 2 * t + ri, :], rhs=rhs, start=False, stop=last,
                                tile_position=(0, 32 * q)), r=[Vt_r, rr], w=[ps_r])
                    yt, yt_r = ytmps[yi % 2]
                    yi += 1
                    if smp == 0:
                        self.dve(lambda e, yt=yt, ps=ps, t=t, f=f, uTf=uTf: e.scalar_tensor_tensor(
                            out=yt[:].rearrange("p (a c) -> p a c", a=8), in0=uTf[:, :, t, :], scalar=Dk[:, f:f + 1],
                            in1=ps[:].rearrange("p (a c) -> p a c", a=8), op0=ALU.mult, op1=ALU.add),
                            r=[uTf_r, Dk_r, ps_r], w=[yt_r])
                        self.act(lambda e, yt=yt, gbuf=gbuf, t=t: e.activation(
                            out=gbuf[:, 0:T_P].rearrange("p (a c s) -> p a c s", a=8, s=8)[:, :, :, t],
                            in_=yt[:].rearrange("p (a c) -> p a c", a=8), func=AF.Gelu_apprx_tanh), r=[yt_r], w=[gbuf_r])
                    else:
                        self.dve(lambda e, yt=yt, ps=ps, t=t, f=f, uTs=uTs: e.scalar_tensor_tensor(
                            out=yt[:, 0:N_S], in0=uTs[:, t, :], scalar=Dk[:, f:f + 1], in1=ps[:, 0:N_S],
                            op0=ALU.mult, op1=ALU.add), r=[uTs_r, Dk_r, ps_r], w=[yt_r])
                        self.act(lambda e, yt=yt, gbuf=gbuf, t=t: e.activation(
                            out=gbuf[:, T_P:TOK].rearrange("p (n s) -> p n s", s=8)[:, :, t],
                            in_=yt[:, 0:N_S], func=AF.Gelu_apprx_tanh), r=[yt_r], w=[gbuf_r])
            self.ld(self.gT_d[f], gbuf[:], r=[gbuf_r], w=[self.gT_r[f]])
        for ri in range(2):
            ps, ps_r = self.next_ps()
            self.pe(lambda e, ps=ps, ri=ri: e.transpose(out=ps[0:64, 0:128], in_=Fin[:, ri, :], identity=self.ident_f[:]),
                    r=[Fin_r, self.ident_f_r], w=[ps_r])
            fo_, fo_r = T("fo", [64, 128])
            self.act(lambda e, ps=ps, fo_=fo_: e.copy(out=fo_[:], in_=ps[0:64, 0:128]), r=[ps_r], w=[fo_r])
            self.store((self.o_sre_p if ri == 0 else self.o_sim_p)[slot], fo_[:], r=[fo_r])
            fs_, fs_r = T("fs", [64, N_S, 128])
            for n in range(N_S):
                ps, ps_r = self.next_ps()
                self.pe(lambda e, ps=ps, ri=ri, n=n: e.transpose(out=ps[0:64, 0:128], in_=FinS[:, ri, :, n], identity=self.ident_f[:]),
                        r=[FinS_r, self.ident_f_r], w=[ps_r])
                self.act(lambda e, ps=ps, fs_=fs_, n=n: e.copy(out=fs_[:, n, :], in_=ps[0:64, 0:128]), r=[ps_r], w=[fs_r])
            self.store((self.o_sre_s if ri == 0 else self.o_sim_s)[slot].rearrange("n q c -> q n c"), fs_[:], r=[fs_r])

    def s5_out(self, st, slot, first):
        P = self.P

        def T(name, shape, dt=F32):
            return P.sb(st, name, shape, dt)
        wg, wg_r = T("wglu", [128, 16, E], BF16)
        wgv = self.ssm_w_glu[slot].rearrange("(k p) n -> p k n", p=128)
        for k in range(16):
            self.ldc(wg[:, k, :], wgv[:, k, :], w=[wg_r])
        wo, wo_r = T("wout", [128, 16, D_MODEL], BF16)
        wov = self.ssm_w_out[slot].rearrange("(k p) n -> p k n", p=128)
        for k in range(0, 16, 2):
            self.ldc(wo[:, k:k + 2, :], wov[:, k:k + 2, :], w=[wo_r])
        bg, bg_r = T("bglu", [128, 16])
        self.P.dma("sp", lambda e: e.dma_start(out=bg[:], in_=self.ssm_b_glu[slot].rearrange("(f p) -> p f", p=128),
                                              allow_slow_non_contiguous=True), (), [bg_r])
        gin, gin_r = T("gin", [128, 16, 512], BF16)
        szin, szin_r = T("szin", [128, 16, 512], BF16)
        g2T, g2T_r = T("g2T", [128, 16, 512], BF16)
        sig = [T("sig", [128, 512]) for _ in range(2)]
        hbs = [T("hb", [128, D_MODEL]) for _ in range(2)]
        hi = 0
        for tg in range(NGRP):
            tok0, ntok = grp_tok(tg)
            self.ld(gin[:, :, 0:ntok], self.gT_d[:, :, tok0:tok0 + ntok].rearrange("f p t -> p f t"), r=self.gT_r, w=[gin_r])
            self.ld(szin[:, :, 0:ntok], self.szT_d[:, :, tok0:tok0 + ntok].rearrange("f p t -> p f t"), r=self.szT_r, w=[szin_r])
            for fo in range(16):
                ps, ps_r = self.next_ps()
                for f in range(16):
                    self.pe(lambda e, ps=ps, f=f, fo=fo, ntok=ntok: e.matmul(
                        ps[:, 0:ntok], lhsT=wg[:, f, fo * 128:(fo + 1) * 128], rhs=gin[:, f, 0:ntok],
                        start=(f == 0), stop=(f == 15)), r=[wg_r, gin_r], w=[ps_r])
                sg, sg_r = sig[fo % 2]
                self.act(lambda e, ps=ps, sg=sg, fo=fo, ntok=ntok: e.activation(
                    out=sg[:, 0:ntok], in_=ps[:, 0:ntok], func=AF.Sigmoid, bias=bg[:, fo:fo + 1]), r=[ps_r, bg_r], w=[sg_r])
                self.dve(lambda e, sg=sg, fo=fo, ntok=ntok: e.tensor_tensor(
                    out=sg[:, 0:ntok], in0=sg[:, 0:ntok], in1=gin[:, fo, 0:ntok], op=ALU.mult), r=[sg_r, gin_r], w=[sg_r])
                self.pool(lambda e, sg=sg, fo=fo, ntok=ntok: e.tensor_tensor(
                    out=g2T[:, fo, 0:ntok], in0=sg[:, 0:ntok], in1=szin[:, fo, 0:ntok], op=ALU.mult), r=[sg_r, szin_r], w=[g2T_r])
            for il in range(ntok // 128):
                i = tok0 // 128 + il
                hb, hb_r = hbs[hi % 2]
                hi += 1
                self.ld(hb[:], self.h_src(first, i), r=[self.H_r[i]], w=[hb_r])
                for hh in range(2):
                    ps, ps_r = self.next_ps()
                    for f in range(16):
                        self.pe(lambda e, ps=ps, f=f, hh=hh, il=il: e.matmul(
                            ps[:], lhsT=g2T[:, f, il * 128:(il + 1) * 128], rhs=wo[:, f, hh * 512:(hh + 1) * 512],
                            start=(f == 0), stop=(f == 15)), r=[wo_r, g2T_r], w=[ps_r])
                    self.dve(lambda e, ps=ps, hb=hb, hh=hh: e.tensor_tensor(
                        out=hb[:, hh * 512:(hh + 1) * 512], in0=ps[:], in1=hb[:, hh * 512:(hh + 1) * 512], op=ALU.add),
                        r=[ps_r, hb_r], w=[hb_r])
                self.ld(self.H[i * 128:(i + 1) * 128, :], hb[:], r=[hb_r], w=[self.H_r[i]])

    def dsa_layer(self):
        P = self.P
        with ExitStack() as st:
            self.dsa_proj(st)
        P.barrier()
        if self.dsa_stage >= 2:
            with ExitStack() as st:
                self.dsa_prompt(st)
            P.barrier()
        if self.dsa_stage >= 3:
            with ExitStack() as st:
                self.dsa_sample(st)
            P.barrier()

    def dsa_proj(self, st):
        P = self.P

        def T(name, shape, dt=F32):
            return P.sb(st, name, shape, dt)
        win, win_r = T("awin", [128, 8, ATTN_IN], BF16)
        wv = self.attn_w_in.rearrange("(k p) n -> p k n", p=128)
        for k in range(8):
            self.ldc(win[:, k, :], wv[:, k, :], w=[win_r])
        gt, gt_r = T("agt", [128, D_MODEL])
        self.ld(gt[:], self.attn_norm.partition_broadcast(128), w=[gt_r])
        ht, ht_r = T("aht", [128, D_MODEL])
        junk, junk_r = T("ajunk", [128, D_MODEL], BF16)
        ss, ss_r = T("ass", [128, 1])
        xn, xn_r = T("axn", [128, D_MODEL], BF16)
        xTs = [T("axT", [128, 8, 128], BF16) for _ in range(2)]
        prs = [T("pr", [128, ATTN_IN]) for _ in range(2)]
        rps = [T("rp", [128, 2880]) for _ in range(2)]
        szbs = [T("szb", [128, E], BF16) for _ in range(2)]
        vxbs = [T("vxb", [128, 4, 65], BF16) for _ in range(2)]
        for (t_, r_) in vxbs:
            self.pool(lambda e, t_=t_: e.memset(t_[:, :, 64:65], 1.0), w=[r_])
        r1, r1_r = T("r1", [128, 36, 32])
        r2, r2_r = T("r2", [128, 36, 32])
        css = [T("cs", [128, 64]) for _ in range(2)]
        wibs = [T("wib", [128, 8]) for _ in range(2)]
        kks = [T("kk", [128, 5, 2, 64]) for _ in range(2)]
        TSs = [T("TS", [128, 25, 128], BF16) for _ in range(2)]
        TSs_s, TSs_s_r = T("TSsmp", [64, 40, 128], BF16)
        COLS = [(c0, min(512, ATTN_IN - c0)) for c0 in range(0, ATTN_IN, 512)]
        for i in range(NTILE):
            xT, xT_r = xTs[i % 2]
            pr, pr_r = prs[i % 2]
            rp, rp_r = rps[i % 2]
            szb, szb_r = szbs[i % 2]
            vxb, vxb_r = vxbs[i % 2]
            cs, cs_r = css[i % 2]
            wib, wib_r = wibs[i % 2]
            kk, kk_r = kks[i % 2]
            TS, TS_r = TSs[i % 2]
            tok0 = i * 128
            smp = (i == NTILE - 1)
            self.norm_tile(self.H[tok0:tok0 + 128, :], self.H_r[i], gt, gt_r, ht, ht_r, junk, junk_r, ss, ss_r,
                           xn, xn_r, xT[:], xT_r)
            self.ld(cs[:], self.rope_cs[tok0:tok0 + 128, :], w=[cs_r])
            for ci, (c0, w) in enumerate(COLS):
                ps, ps_r = self.next_ps()
                for k in range(8):
                    self.pe(lambda e, ps=ps, k=k, c0=c0, w=w, xT=xT: e.matmul(
                        ps[:, 0:w], lhsT=xT[:, k, :], rhs=win[:, k, c0:c0 + w], start=(k == 0), stop=(k == 7)),
                        r=[win_r, xT_r], w=[ps_r])
                if 5 <= ci <= 8:
                    self.act(lambda e, ps=ps, szb=szb, ci=ci: e.activation(
                        out=szb[:, (ci - 5) * 512:(ci - 4) * 512], in_=ps[:], func=AF.Silu), r=[ps_r], w=[szb_r])
                else:
                    self.act(lambda e, ps=ps, pr=pr, c0=c0, w=w: e.copy(out=pr[:, c0:c0 + w], in_=ps[:, 0:w]), r=[ps_r], w=[pr_r])
            self.ld(self.sz_d[tok0:tok0 + 128, :], szb[:], r=[szb_r], w=[self.dsa_r])
            for (H, s0_, d0_) in ((36, 0, 0), (9, 4608, 2304)):
                src = pr[:, s0_:s0_ + H * 64].rearrange("p (h c d) -> p h c d", c=2, d=32)
                dst = rp[:, d0_:d0_ + H * 64].rearrange("p (h c d) -> p h c d", c=2, d=32)
                cosb = bc(cs[:, 0:32], 1, H)
                sinb = bc(cs[:, 32:64], 1, H)
                x1 = src[:, :, 0, :]
                x2 = src[:, :, 1, :]
                self.dve(lambda e, x1=x1, cosb=cosb, H=H: e.tensor_tensor(out=r1[:, 0:H, :], in0=x1, in1=cosb, op=ALU.mult), r=[pr_r, cs_r], w=[r1_r])
                self.pool(lambda e, x2=x2, sinb=sinb, H=H: e.tensor_tensor(out=r2[:, 0:H, :], in0=x2, in1=sinb, op=ALU.mult), r=[pr_r, cs_r], w=[r2_r])
                self.dve(lambda e, dst=dst, H=H: e.tensor_tensor(out=dst[:, :, 0, :], in0=r1[:, 0:H, :], in1=r2[:, 0:H, :], op=ALU.subtract),
                         r=[r1_r, r2_r], w=[rp_r])
                self.dve(lambda e, x2=x2, cosb=cosb, H=H: e.tensor_tensor(out=r1[:, 0:H, :], in0=x2, in1=cosb, op=ALU.mult), r=[pr_r, cs_r], w=[r1_r])
                self.pool(lambda e, x1=x1, sinb=sinb, H=H: e.tensor_tensor(out=r2[:, 0:H, :], in0=x1, in1=sinb, op=ALU.mult), r=[pr_r, cs_r], w=[r2_r])
                self.dve(lambda e, dst=dst, H=H: e.tensor_tensor(out=dst[:, :, 1, :], in0=r1[:, 0:H, :], in1=r2[:, 0:H, :], op=ALU.add),
                         r=[r1_r, r2_r], w=[rp_r])
            if not smp:
                self.store(self.o_k_p[tok0:tok0 + 128, :], rp[:, 2048:2304], r=[rp_r])
                self.store(self.o_v_p[tok0:tok0 + 128, :], pr[:, 2304:2560], r=[pr_r])
                self.store(self.o_kidx_p[tok0:tok0 + 128, :], rp[:, 2816:2880], r=[rp_r])
            else:
                self.store(self.o_k_s[:, :], rp[:, 2048:2304], r=[rp_r])
                self.store(self.o_v_s[:, :], pr[:, 2304:2560], r=[pr_r])
                self.store(self.o_kidx_s[:, :], rp[:, 2816:2880], r=[rp_r])
            self.dve(lambda e, wib=wib, pr=pr: e.tensor_scalar(out=wib[:], in0=pr[:, 5184:5192], scalar1=8.0 ** -1.5, scalar2=None, op0=ALU.mult),
                     r=[pr_r], w=[wib_r])
            self.ld(self.wi_d[tok0:tok0 + 128, :], wib[:], r=[wib_r], w=[self.dsa_r])
            self.pool(lambda e, vxb=vxb, pr=pr: e.tensor_copy(out=vxb[:, :, 0:64], in_=pr[:, 2304:2560].rearrange("p (g d) -> p g d", g=4)),
                      r=[pr_r], w=[vxb_r])
            self.ld(self.Vx_d[tok0:tok0 + 128, :], vxb[:].rearrange("p g d -> p (g d)"), r=[vxb_r], w=[self.dsa_r])
            self.pool(lambda e, kk=kk, rp=rp: e.tensor_copy(out=kk[:, 0:4, :, :], in_=bc(rp[:, 2048:2304].rearrange("p (g d) -> p g d", g=4), 2, 2)),
                      r=[rp_r], w=[kk_r])
            self.pool(lambda e, kk=kk, rp=rp: e.tensor_copy(out=kk[:, 4, :, :], in_=bc(rp[:, 2816:2880], 1, 2)), r=[rp_r], w=[kk_r])
            srcs = [(rp[:, 128 * a:128 * a + 128], rp_r) for a in range(16)]
            srcs += [(kk[:, g, :, :].rearrange("p a d -> p (a d)"), kk_r) for g in range(4)]
            srcs += [(rp[:, 2304 + 128 * a:2304 + 128 * a + 128], rp_r) for a in range(4)]
            srcs += [(kk[:, 4, :, :].rearrange("p a d -> p (a d)"), kk_r)]
            for b0 in range(0, 25, 4):
                nb = min(4, 25 - b0)
                ps, ps_r = self.next_ps()
                for j in range(nb):
                    src, src_r = srcs[b0 + j]
                    self.pe(lambda e, ps=ps, j=j, src=src: e.transpose(out=ps[:, j * 128:(j + 1) * 128], in_=src, identity=self.ident_f[:]),
                            r=[src_r, self.ident_f_r], w=[ps_r])
                self.act(lambda e, ps=ps, TS=TS, b0=b0, nb=nb: e.copy(out=TS[:, b0:b0 + nb, :],
                                                                    in_=ps[:, 0:nb * 128].rearrange("p (a t) -> p a t", t=128)),
                         r=[ps_r], w=[TS_r])
            self.ld(self.qT_d[:, :, tok0:tok0 + 128].rearrange("a p t -> p a t"), TS[:, 0:16, :], r=[TS_r], w=[self.dsa_r])
            self.ld(self.kT2_d[:, :, tok0:tok0 + 128].rearrange("a p t -> p a t"), TS[:, 16:20, :], r=[TS_r], w=[self.dsa_r])
            self.ld(self.qiT_d[:, :, tok0:tok0 + 128].rearrange("a p t -> p a t"), TS[:, 20:24, :], r=[TS_r], w=[self.dsa_r])
            self.ld(self.kiT2_d[:, :, tok0:tok0 + 128].rearrange("a p t -> p a t"), TS[:, 24:25, :], r=[TS_r], w=[self.dsa_r])
            if smp:
                hs = [(rp[:, 64 * h:64 * h + 64], rp_r) for h in range(32)] + [(rp[:, 2304 + 64 * h:2304 + 64 * h + 64], rp_r) for h in range(8)]
                for b0 in range(0, 40, 4):
                    ps, ps_r = self.next_ps()
                    for j in range(4):
                        src, src_r = hs[b0 + j]
                        self.pe(lambda e, ps=ps, j=j, src=src: e.transpose(out=ps[0:64, j * 128:(j + 1) * 128], in_=src, identity=self.ident_f[:]),
                                r=[src_r, self.ident_f_r], w=[ps_r])
                    self.act(lambda e, ps=ps, b0=b0: e.copy(out=TSs_s[:, b0:b0 + 4, :], in_=ps[0:64, :].rearrange("p (a t) -> p a t", t=128)),
                             r=[ps_r], w=[TSs_s_r])
                self.ld(self.qTs_d[:, :, :], TSs_s[:, 0:32, :], r=[TSs_s_r], w=[self.dsa_r])
                self.ld(self.qiTs_d[:, :, :], TSs_s[:, 32:40, :], r=[TSs_s_r], w=[self.dsa_r])

    def attn_out_tile(self, i, g2, g2_r, g2T, g2T_r, wo, wo_r, hb, hb_r):
        pt, pt_r = self.pst
        for b0 in range(0, 16, 8):
            for j in range(8):
                self.pe(lambda e, j=j, b0=b0: e.transpose(out=pt[:, j * 128:(j + 1) * 128], in_=g2[:, (b0 + j) * 128:(b0 + j + 1) * 128],
                                                          identity=self.ident_b[:]), r=[g2_r, self.ident_b_r], w=[pt_r])
            self.act(lambda e, b0=b0: e.copy(out=g2T[:, b0:b0 + 8, :], in_=pt[:].rearrange("p (a t) -> p a t", t=128)), r=[pt_r], w=[g2T_r])
        self.ld(hb[:], self.H[i * 128:(i + 1) * 128, :], r=[self.H_r[i]], w=[hb_r])
        for hh in range(2):
            ps, ps_r = self.next_ps()
            for f in range(16):
                self.pe(lambda e, ps=ps, f=f, hh=hh: e.matmul(ps[:], lhsT=g2T[:, f, :], rhs=wo[:, f, hh * 512:(hh + 1) * 512],
                                                              start=(f == 0), stop=(f == 15)), r=[wo_r, g2T_r], w=[ps_r])
            self.dve(lambda e, ps=ps, hh=hh: e.tensor_tensor(out=hb[:, hh * 512:(hh + 1) * 512], in0=ps[:], in1=hb[:, hh * 512:(hh + 1) * 512],
                                                            op=ALU.add), r=[ps_r, hb_r], w=[hb_r])
        self.ld(self.H[i * 128:(i + 1) * 128, :], hb[:], r=[hb_r], w=[self.H_r[i]])

    def topk_threshold(self, sc, sc_r, S, lo, hi, mid, cnt, sel, nsel, small_r, junk, junk_r, iters=20):
        for it in range(iters):
            self.dve(lambda e: e.tensor_scalar(out=mid[:], in0=lo[:], scalar1=hi[:, 0:1], scalar2=0.5, op0=ALU.add, op1=ALU.mult),
                     r=[small_r], w=[small_r])
            self.dve(lambda e: e.tensor_scalar(out=junk[:, 0:S], in0=sc[:, 0:S], scalar1=mid[:, 0:1], scalar2=0.0, op0=ALU.is_ge, op1=ALU.add,
                                               accum_out=cnt[:]), r=[sc_r, small_r], w=[junk_r, small_r])
            self.dve(lambda e: e.tensor_scalar(out=sel[:], in0=cnt[:], scalar1=255.5, scalar2=None, op0=ALU.is_ge), r=[small_r], w=[small_r])
            self.dve(lambda e: e.tensor_scalar(out=nsel[:], in0=cnt[:], scalar1=255.5, scalar2=None, op0=ALU.is_lt), r=[small_r], w=[small_r])
            self.dve(lambda e: e.copy_predicated(out=lo[:], mask=sel[:], data=mid[:]), r=[small_r], w=[small_r])
            self.dve(lambda e: e.copy_predicated(out=hi[:], mask=nsel[:], data=mid[:]), r=[small_r], w=[small_r])

    def dsa_prompt(self, st):
        P = self.P

        def T(name, shape, dt=F32):
            return P.sb(st, name, shape, dt)
        self.ps_lim = 5
        accs = [self.psb[5], self.psb[6]]
        kT2, kT2_r = T("kT2", [128, 4, T_P], BF16)
        for g in range(4):
            self.ld(kT2[:, g, :], self.kT2_d[g, :, 0:T_P], r=[self.dsa_r], w=[kT2_r])
        kiT2, kiT2_r = T("kiT2", [128, T_P], BF16)
        self.ld(kiT2[:], self.kiT2_d[0, :, 0:T_P], r=[self.dsa_r], w=[kiT2_r])
        Vx, Vx_r = T("Vx", [128, 32, 260], BF16)
        for k4 in range(4):
            self.ld(Vx[:, 8 * k4:8 * k4 + 8, :], self.Vx_d[1024 * k4:1024 * (k4 + 1), :].rearrange("(kt s) c -> s kt c", s=128),
                    r=[self.dsa_r], w=[Vx_r])
        wo, wo_r = T("awo", [128, 16, D_MODEL], BF16)
        self.wo_attn = (wo, wo_r)
        wov = self.attn_w_out.rearrange("(k p) n -> p k n", p=128)
        for k in range(0, 16, 2):
            self.ldc(wo[:, k:k + 2, :], wov[:, k:k + 2, :], w=[wo_r])
        cmask, cmask_r = T("cmask", [128, 128])
        self.ld(cmask[:], self.cmask_d, w=[cmask_r])
        qTs = [T("qT", [128, 16, 128], BF16) for _ in range(2)]
        qiTs = [T("qiT", [128, 4, 128], BF16) for _ in range(2)]
        wis = [T("wi", [128, 8]) for _ in range(2)]
        szs = [T("sz", [128, E], BF16) for _ in range(2)]
        hb, hb_r = T("ahb", [128, D_MODEL])
        sc, sc_r = T("sc", [128, T_P])
        tmps = [T("sctmp", [128, 512]) for _ in range(2)]
        junk, junk_r = T("scjunk", [128, T_P], BF16)
        Mb, Mb_r = T("Mb", [128, T_P], BF16)
        MTn, MTn_r = T("MTn", [128, 32, 128], BF16)
        pexps = [T("pexp", [128, 4, 128], BF16) for _ in range(6)]
        g2, g2_r = T("g2", [128, E], BF16)
        of_, of_r = T("of", [128, 4, 64])
        rc, rc_r = T("rc", [128, 4])
        g2T, g2T_r = T("g2T", [128, 16, 128], BF16)
        lo, small_r = T("lo", [128, 1])
        hi, _ = T("hi", [128, 1])
        mid, _ = T("mid", [128, 1])
        cnt, _ = T("cnt", [128, 1])
        sel, _ = T("sel", [128, 1], I32)
        nsel, _ = T("nsel", [128, 1], I32)
        pt, pt_r = self.pst
        MTns = [(MTn, MTn_r), T("MTn2", [128, 32, 128], BF16)]
        accS = [[T("accS", [128, 260]) for _ in range(2)] for _ in range(4)]
        st_ = {"pe_i": 0}

        def stage_idx(qt):
            nk = qt + 1
            S = 128 * nk
            qT, qT_r = qTs[qt % 2]
            qiT, qiT_r = qiTs[qt % 2]
            wi, wi_r = wis[qt % 2]
            sz, sz_r = szs[qt % 2]
            t0 = qt * 128
            self.ld(qT[:], self.qT_d[:, :, t0:t0 + 128].rearrange("a p t -> p a t"), r=[self.dsa_r], w=[qT_r])
            self.ld(qiT[:], self.qiT_d[:, :, t0:t0 + 128].rearrange("a p t -> p a t"), r=[self.dsa_r], w=[qiT_r])
            self.ld(wi[:], self.wi_d[t0:t0 + 128, :], r=[self.dsa_r], w=[wi_r])
            self.ld(sz[:], self.sz_d[t0:t0 + 128, :], r=[self.dsa_r], w=[sz_r])
            ti = 0
            for c0 in range(0, S, 512):
                n = min(512, S - c0)
                for h in range(8):
                    a, hf = h // 2, h % 2
                    ps, ps_r = self.next_ps()
                    self.pe(lambda e, ps=ps, a=a, hf=hf, c0=c0, n=n, qiT=qiT: e.matmul(
                        ps[:, 0:n], lhsT=qiT[64 * hf:64 * hf + 64, a, :], rhs=kiT2[64 * hf:64 * hf + 64, c0:c0 + n],
                        start=True, stop=True, tile_position=(64 * hf, 0)), r=[qiT_r, kiT2_r], w=[ps_r])
                    if h == 0:
                        self.dve(lambda e, ps=ps, c0=c0, n=n, wi=wi: e.tensor_scalar(
                            out=sc[:, c0:c0 + n], in0=ps[:, 0:n], scalar1=0.0, scalar2=wi[:, 0:1], op0=ALU.max, op1=ALU.mult),
                            r=[ps_r, wi_r], w=[sc_r])
                    else:
                        tmp, tmp_r = tmps[ti % 2]
                        ti += 1
                        self.dve(lambda e, ps=ps, n=n, wi=wi, h=h, tmp=tmp: e.tensor_scalar(
                            out=tmp[:, 0:n], in0=ps[:, 0:n], scalar1=0.0, scalar2=wi[:, h:h + 1], op0=ALU.max, op1=ALU.mult),
                            r=[ps_r, wi_r], w=[tmp_r])
                        self.pool(lambda e, c0=c0, n=n, tmp=tmp: e.tensor_tensor(
                            out=sc[:, c0:c0 + n], in0=sc[:, c0:c0 + n], in1=tmp[:, 0:n], op=ALU.add), r=[tmp_r, sc_r], w=[sc_r])
            if qt >= 2:
                self.dve(lambda e, S=S: e.tensor_reduce(out=hi[:], in_=sc[:, 0:S], axis=AX.X, op=ALU.max), r=[sc_r], w=[small_r])
                self.dve(lambda e, S=S: e.tensor_reduce(out=lo[:], in_=sc[:, 0:S], axis=AX.X, op=ALU.min), r=[sc_r], w=[small_r])
            else:
                self.pool(lambda e: e.memset(lo[:], -1e29), w=[small_r])
            self.dve(lambda e, t0=t0: e.tensor_tensor(out=sc[:, t0:t0 + 128], in0=sc[:, t0:t0 + 128], in1=cmask[:], op=ALU.add),
                     r=[sc_r, cmask_r], w=[sc_r])
            if qt >= 2:
                self.topk_threshold(sc, sc_r, S, lo, hi, mid, cnt, sel, nsel, small_r, junk, junk_r)
            self.dve(lambda e, S=S: e.tensor_scalar(out=Mb[:, 0:S], in0=sc[:, 0:S], scalar1=lo[:, 0:1], scalar2=None, op0=ALU.is_ge),
                     r=[sc_r, small_r], w=[Mb_r])

        def stage_mt(qt):
            nk = qt + 1
            MTc, MTc_r = MTns[qt % 2]
            for k0 in range(0, nk, 8):
                nb = min(8, nk - k0)
                for j in range(nb):
                    self.pe(lambda e, j=j, k0=k0: e.transpose(out=pt[:, j * 128:(j + 1) * 128], in_=Mb[:, (k0 + j) * 128:(k0 + j + 1) * 128],
                                                              identity=self.ident_b[:]), r=[Mb_r, self.ident_b_r], w=[pt_r])
                self.dve(lambda e, k0=k0, nb=nb, MTc=MTc: e.tensor_scalar(out=MTc[:, k0:k0 + nb, :], in0=pt[:, 0:nb * 128].rearrange("p (a t) -> p a t", t=128),
                                                                          scalar1=-1.0, scalar2=30000.0, op0=ALU.add, op1=ALU.mult), r=[pt_r], w=[MTc_r])

        def stage_attn(qt):
            nk = qt + 1
            qT, qT_r = qTs[qt % 2]
            sz, sz_r = szs[qt % 2]
            MTc, MTc_r = MTns[qt % 2]

            def emit_sm(g, kt):
                pex = []
                pss = []
                for par in range(2):
                    ps, ps_r = self.next_ps()
                    psv = ps[:].rearrange("p (a t) -> p a t", t=128)
                    self.pe(lambda e, psv=psv, par=par, g=g, kt=kt: e.matmul(
                        psv, lhsT=kT2[64 * par:64 * par + 64, g, kt * 128:(kt + 1) * 128], rhs=qT[64 * par:64 * par + 64, 4 * g:4 * g + 4, :],
                        start=True, stop=False, tile_position=(64 * par, 0)), r=[kT2_r, qT_r], w=[ps_r])
                    pss.append((ps, ps_r, psv))
                for par in range(2):
                    ps, ps_r, psv = pss[par]
                    self.pe(lambda e, psv=psv, kt=kt: e.matmul(psv, lhsT=self.ident_b[:], rhs=bc(MTc[:, kt, :], 1, 4), start=False, stop=True),
                            r=[self.ident_b_r, MTc_r], w=[ps_r])
                    px, px_r = pexps[st_["pe_i"] % len(pexps)]
                    st_["pe_i"] += 1
                    self.act(lambda e, px=px, psv=psv: e.activation(out=px[:], in_=psv, func=AF.Exp, scale=0.125), r=[ps_r], w=[px_r])
                    pex.append((px, px_r))
                return pex

            def emit_pv(g, kt, pex):
                for h8 in range(8):
                    par, a = h8 % 2, h8 // 2
                    acc, acc_r = accs[h8 // 4]
                    col = (h8 % 4) * 65
                    px, px_r = pex[par]
                    self.pe(lambda e, acc=acc, col=col, px=px, a=a, kt=kt, g=g, h8=h8: e.matmul(
                        acc[:, col:col + 65], lhsT=px[:, a, :], rhs=Vx[:, kt, g * 65:(g + 1) * 65],
                        start=(kt == 0 and h8 % 4 == 0), stop=(kt == nk - 1), skip_group_check=True), r=[px_r, Vx_r], w=[acc_r])
                if kt == nk - 1:
                    for half in range(2):
                        acc, acc_r = accs[half]
                        aS, aS_r = accS[g][half]
                        self.act(lambda e, acc=acc, aS=aS: e.copy(out=aS[:], in_=acc[:, 0:260]), r=[acc_r], w=[aS_r])

            units = [(g, kt) for g in range(4) for kt in range(nk)]
            pend = emit_sm(*units[0])
            for ui, (g, kt) in enumerate(units):
                nxt = emit_sm(*units[ui + 1]) if ui + 1 < len(units) else None
                emit_pv(g, kt, pend)
                pend = nxt
            for g in range(4):
                for half in range(2):
                    aS, aS_r = accS[g][half]
                    accv = aS[:].rearrange("p (h c) -> p h c", c=65)
                    c0 = 64 * (8 * g + 4 * half)
                    self.dve(lambda e, accv=accv: e.reciprocal(out=rc[:], in_=accv[:, :, 64]), r=[aS_r], w=[rc_r])
                    self.dve(lambda e, accv=accv: e.tensor_tensor(out=of_[:], in0=accv[:, :, 0:64], in1=bc(rc[:], 2, 64), op=ALU.mult),
                             r=[aS_r, rc_r], w=[of_r])
                    self.pool(lambda e, c0=c0, sz=sz: e.tensor_tensor(out=g2[:, c0:c0 + 256], in0=of_[:].rearrange("p h d -> p (h d)"),
                                                                     in1=sz[:, c0:c0 + 256], op=ALU.mult), r=[of_r, sz_r], w=[g2_r])
            self.attn_out_tile(qt, g2, g2_r, g2T, g2T_r, wo, wo_r, hb, hb_r)

        NQ = T_P // 128
        stage_idx(0)
        stage_mt(0)
        for qt in range(NQ):
            if qt + 1 < NQ:
                stage_idx(qt + 1)
            stage_attn(qt)
            if qt + 1 < NQ:
                stage_mt(qt + 1)
        self.ps_lim = 7

    def dsa_sample(self, st):
        P = self.P

        def T(name, shape, dt=F32):
            return P.sb(st, name, shape, dt)
        self.ps_lim = 5
        acc, acc_r = self.psb[5]
        pt, pt_r = self.pst
        wo, wo_r = T("awo2", [128, 16, D_MODEL], BF16)
        wov = self.attn_w_out.rearrange("(k p) n -> p k n", p=128)
        for k in range(0, 16, 2):
            self.ldc(wo[:, k:k + 2, :], wov[:, k:k + 2, :], w=[wo_r])
        pti, pti_r = T("pti", [128, N_S * 16], I32)
        self.ld(pti[:], self.page_table.partition_broadcast(128), w=[pti_r])
        iota, iota_r = T("iota", [128, 1])
        self.ld(iota[:], self.iota_d, w=[iota_r])
        ptf, ptf_r = T("ptf", [128, N_S * 16])
        self.dve(lambda e: e.tensor_copy(out=ptf[:], in_=pti[:]), r=[pti_r], w=[ptf_r])
        self.dve(lambda e: e.tensor_scalar(out=ptf[:], in0=ptf[:], scalar1=128.0, scalar2=iota[:, 0:1], op0=ALU.mult, op1=ALU.add),
                 r=[ptf_r, iota_r], w=[ptf_r])
        idx, idx_r = T("idx", [128, N_S * 16], I32)
        self.dve(lambda e: e.tensor_copy(out=idx[:], in_=ptf[:]), r=[ptf_r], w=[idx_r])
        qTs, qTs_r = T("qTs", [64, N_S, 32, 8], BF16)
        qiTs, qiTs_r = T("qiTs", [64, N_S, 8, 8], BF16)
        for n in range(N_S):
            self.ld(qTs[:, n, :, :], self.qTs_d[:, :, 8 * n:8 * n + 8], r=[self.dsa_r], w=[qTs_r])
            self.ld(qiTs[:, n, :, :], self.qiTs_d[:, :, 8 * n:8 * n + 8], r=[self.dsa_r], w=[qiTs_r])
        kTn, kTn_r = T("kTn", [64, 4, T_S], BF16)
        self.ld(kTn[:], self.kT2_d[:, 0:64, T_P:TOK].rearrange("g p t -> p g t"), r=[self.dsa_r], w=[kTn_r])
        kiTn, kiTn_r = T("kiTn", [64, T_S], BF16)
        self.ld(kiTn[:], self.kiT2_d[0, 0:64, T_P:TOK], r=[self.dsa_r], w=[kiTn_r])
        Vxn, Vxn_r = T("Vxn", [128, 260], BF16)
        self.ld(Vxn[:], self.Vx_d[T_P:TOK, :], r=[self.dsa_r], w=[Vxn_r])
        szs_, szs_r = T("szsmp", [128, E], BF16)
        self.ld(szs_[:], self.sz_d[T_P:TOK, :], r=[self.dsa_r], w=[szs_r])
        wst, wst_r = T("wst", [64, N_S])
        for h in range(8):
            self.P.dma("sp", lambda e, h=h: e.dma_start(out=wst[8 * h:8 * h + 8, :], in_=self.wi_d[T_P:TOK, h].rearrange("(n t) -> t n", t=8),
                                                       allow_slow_non_contiguous=True), [self.dsa_r], [wst_r])
        selm, selm_r = T("selm", [64, 8])
        self.ld(selm[:], self.selm_d, w=[selm_r])
        blockm, blockm_r = T("blockm", [128, 128])
        self.ld(blockm[:], self.blockm_d, w=[blockm_r])
        cmask_s, cmask_s_r = T("cmasks", [128, 8])
        self.ld(cmask_s[:], self.cmask_s_d, w=[cmask_s_r])
        sc, sc_r = T("scs", [128, 2056])
        lo, small_r = T("slo", [128, 1])
        hi, _ = T("shi", [128, 1])
        mid, _ = T("smid", [128, 1])
        cnt, _ = T("scnt", [128, 1])
        sel, _ = T("ssel", [128, 1], I32)
        nsel, _ = T("snsel", [128, 1], I32)
        Mb, Mb_r = T("sMb", [128, 2056], BF16)
        NMT, NMT_r = T("sNMT", [128, 17, 128], BF16)
        MBn, MBn_r = T("sMBn", [128, 128], BF16)
        s1 = ExitStack()
        junk, junk_r = P.sb(s1, "sjunk", [128, 2056], BF16)
        KIgs = [P.sb(s1, "KIg", [128, 16, 64]) for _ in range(2)]
        kiTg, kiTg_r = P.sb(s1, "kiTg", [64, 16, 128], BF16)
        rls = [P.sb(s1, "rl", [64, 512]) for _ in range(2)]
        scst = [P.sb(s1, "scst", [8, 2056]) for _ in range(2)]
        ri_ = 0
        for n in range(N_S):
            KIg, KIg_r = KIgs[n % 2]
            for j in range(16):
                c = 16 * n + j
                self.P.dma("pool", lambda e, KIg=KIg, j=j, c=c: e.indirect_dma_start(
                    out=KIg[:, j, :], out_offset=None, in_=self.cache_kidx,
                    in_offset=bass.IndirectOffsetOnAxis(ap=idx[:, c:c + 1], axis=0)), [idx_r], [KIg_r])
            for b0 in range(0, 16, 4):
                ps, ps_r = self.next_ps()
                for j in range(4):
                    self.pe(lambda e, ps=ps, j=j, b0=b0, KIg=KIg: e.transpose(out=ps[0:64, j * 128:(j + 1) * 128], in_=KIg[:, b0 + j, :],
                                                                           identity=self.ident_f[:]), r=[KIg_r, self.ident_f_r], w=[ps_r])
                self.act(lambda e, ps=ps, b0=b0: e.copy(out=kiTg[:, b0:b0 + 4, :], in_=ps[0:64, :].rearrange("p (a t) -> p a t", t=128)),
                         r=[ps_r], w=[kiTg_r])
            st_, st_r = scst[n % 2]
            for c4 in range(5):
                ps, ps_r = self.next_ps()
                nn = 512 if c4 < 4 else 8
                rhs = kiTg[:, 4 * c4:4 * c4 + 4, :] if c4 < 4 else kiTn[:, 8 * n:8 * n + 8]
                outp = ps[0:64, :].rearrange("p (a t) -> p a t", t=128) if c4 < 4 else ps[0:64, 0:8]
                rr = kiTg_r if c4 < 4 else kiTn_r
                self.pe(lambda e, outp=outp, rhs=rhs, n=n: e.matmul(outp, lhsT=qiTs[:, n, :, :].rearrange("p h t -> p (h t)"), rhs=rhs, start=True, stop=True),
                        r=[qiTs_r, rr], w=[ps_r])
                rl, rl_r = rls[ri_ % 2]
                ri_ += 1
                self.dve(lambda e, ps=ps, rl=rl, nn=nn, n=n: e.tensor_scalar(out=rl[:, 0:nn], in0=ps[0:64, 0:nn], scalar1=0.0, scalar2=wst[:, n:n + 1],
                                                                           op0=ALU.max, op1=ALU.mult), r=[ps_r, wst_r], w=[rl_r])
                ps2, ps2_r = self.next_ps()
                self.pe(lambda e, ps2=ps2, rl=rl, nn=nn: e.matmul(ps2[0:8, 0:nn], lhsT=selm[:], rhs=rl[:, 0:nn], start=True, stop=True),
                        r=[selm_r, rl_r], w=[ps2_r])
                self.act(lambda e, ps2=ps2, st_=st_, c4=c4, nn=nn: e.copy(out=st_[:, 512 * c4:512 * c4 + nn], in_=ps2[0:8, 0:nn]), r=[ps2_r], w=[st_r])
            self.ld(sc[8 * n:8 * n + 8, :], st_[:], r=[st_r], w=[sc_r])
        self.dve(lambda e: e.tensor_reduce(out=hi[:], in_=sc[:], axis=AX.X, op=ALU.max), r=[sc_r], w=[small_r])
        self.dve(lambda e: e.tensor_reduce(out=lo[:], in_=sc[:], axis=AX.X, op=ALU.min), r=[sc_r], w=[small_r])
        self.dve(lambda e: e.tensor_tensor(out=sc[:, 2048:2056], in0=sc[:, 2048:2056], in1=cmask_s[:], op=ALU.add), r=[sc_r, cmask_s_r], w=[sc_r])
        self.topk_threshold(sc, sc_r, 2056, lo, hi, mid, cnt, sel, nsel, small_r, junk, junk_r)
        self.dve(lambda e: e.tensor_scalar(out=Mb[:], in0=sc[:], scalar1=lo[:, 0:1], scalar2=None, op0=ALU.is_ge), r=[sc_r, small_r], w=[Mb_r])
        self.dve(lambda e: e.tensor_tensor(out=MBn[:].rearrange("p (n t) -> p n t", t=8), in0=bc(Mb[:, 2048:2056], 1, 16),
                                           in1=blockm[:].rearrange("p (n t) -> p n t", t=8), op=ALU.mult), r=[Mb_r, blockm_r], w=[MBn_r])
        for k0 in range(0, 17, 8):
            nb = min(8, 17 - k0)
            for j in range(nb):
                src = Mb[:, (k0 + j) * 128:(k0 + j + 1) * 128] if k0 + j < 16 else MBn[:]
                self.pe(lambda e, j=j, src=src: e.transpose(out=pt[:, j * 128:(j + 1) * 128], in_=src, identity=self.ident_b[:]),
                        r=[Mb_r, MBn_r, self.ident_b_r], w=[pt_r])
            self.dve(lambda e, k0=k0, nb=nb: e.tensor_scalar(out=NMT[:, k0:k0 + nb, :], in0=pt[:, 0:nb * 128].rearrange("p (a t) -> p a t", t=128),
                                                             scalar1=-1.0, scalar2=30000.0, op0=ALU.add, op1=ALU.mult), r=[pt_r], w=[NMT_r])
        P.barrier()
        s1.close()
        Kgs = [T("Kg", [128, 16, 256]) for _ in range(2)]
        Vgs = [T("Vg", [128, 16, 256]) for _ in range(2)]
        kTg, kTg_r = T("kTg", [64, 16, 4, 128], BF16)
        Vxg, Vxg_r = T("Vxg", [128, 16, 4, 65], BF16)
        self.pool(lambda e: e.memset(Vxg[:, :, :, 64:65], 1.0), w=[Vxg_r])
        pxs = [T("spx", [128, 256], BF16) for _ in range(3)]
        rc, rc_r = T("src", [64, 4])
        osb = [T("osb", [64, 4, 64]) for _ in range(2)]
        pxc = {"i": 0}
        for n in range(N_S):
            Kg, Kg_r = Kgs[n % 2]
            Vg, Vg_r = Vgs[n % 2]
            for j in range(16):
                c = 16 * n + j
                self.P.dma("pool", lambda e, Kg=Kg, j=j, c=c: e.indirect_dma_start(
                    out=Kg[:, j, :], out_offset=None, in_=self.cache_k,
                    in_offset=bass.IndirectOffsetOnAxis(ap=idx[:, c:c + 1], axis=0)), [idx_r], [Kg_r])
                self.P.dma("pool", lambda e, Vg=Vg, j=j, c=c: e.indirect_dma_start(
                    out=Vg[:, j, :], out_offset=None, in_=self.cache_v,
                    in_offset=bass.IndirectOffsetOnAxis(ap=idx[:, c:c + 1], axis=0)), [idx_r], [Vg_r])
            self.act(lambda e, Vg=Vg: e.copy(out=Vxg[:, :, :, 0:64], in_=Vg[:].rearrange("p j (g d) -> p j g d", g=4)), r=[Vg_r], w=[Vxg_r])
            for j in range(16):
                ps, ps_r = self.next_ps()
                for g in range(4):
                    self.pe(lambda e, ps=ps, j=j, g=g, Kg=Kg: e.transpose(out=ps[0:64, g * 128:(g + 1) * 128], in_=Kg[:, j, 64 * g:64 * g + 64],
                                                                        identity=self.ident_f[:]), r=[Kg_r, self.ident_f_r], w=[ps_r])
                self.act(lambda e, ps=ps, j=j: e.copy(out=kTg[:, j, :, :], in_=ps[0:64, :].rearrange("p (g t) -> p g t", t=128)), r=[ps_r], w=[kTg_r])
            def s_sm(j, n=n):
                ps, ps_r = self.next_ps()
                for g in range(4):
                    lhsT = kTg[:, j, g, :] if j < 16 else kTn[:, g, :]
                    rr = kTg_r if j < 16 else kTn_r
                    self.pe(lambda e, ps=ps, g=g, lhsT=lhsT, n=n: e.matmul(
                        ps[:, 64 * g:64 * g + 64].rearrange("p (h t) -> p h t", t=8), lhsT=lhsT, rhs=qTs[:, n, 8 * g:8 * g + 8, :],
                        start=(g == 0), stop=False, skip_group_check=True), r=[rr, qTs_r], w=[ps_r])
                self.pe(lambda e, ps=ps, j=j, n=n: e.matmul(ps[:, 0:256].rearrange("p (h t) -> p h t", t=8), lhsT=self.ident_b[:],
                                                            rhs=bc(NMT[:, j, 8 * n:8 * n + 8], 1, 32), start=False, stop=True, skip_group_check=True),
                        r=[self.ident_b_r, NMT_r], w=[ps_r])
                px, px_r = pxs[pxc["i"] % 3]
                pxc["i"] += 1
                self.act(lambda e, px=px, ps=ps: e.activation(out=px[:], in_=ps[:, 0:256], func=AF.Exp, scale=0.125), r=[ps_r], w=[px_r])
                return px, px_r

            def s_pv(j, px, px_r):
                for g in range(4):
                    rhs = Vxg[:, j, g, :] if j < 16 else Vxn[:, 65 * g:65 * g + 65]
                    rr = Vxg_r if j < 16 else Vxn_r
                    self.pe(lambda e, g=g, px=px, rhs=rhs, j=j: e.matmul(acc[0:64, 65 * g:65 * g + 65], lhsT=px[:, 64 * g:64 * g + 64], rhs=rhs,
                                                                        start=(j == 0 and g == 0), stop=(j == 16), skip_group_check=True),
                            r=[px_r, rr], w=[acc_r])

            pend = s_sm(0)
            for j in range(17):
                nxt = s_sm(j + 1) if j + 1 < 17 else None
                s_pv(j, *pend)
                pend = nxt
            accv = acc[0:64, 0:260].rearrange("p (g c) -> p g c", c=65)
            ob, ob_r = osb[n % 2]
            self.dve(lambda e, accv=accv: e.reciprocal(out=rc[:], in_=accv[:, :, 64]), r=[acc_r], w=[rc_r])
            self.dve(lambda e, accv=accv, ob=ob: e.tensor_tensor(out=ob[:], in0=accv[:, :, 0:64], in1=bc(rc[:], 2, 64), op=ALU.mult),
                     r=[acc_r, rc_r], w=[ob_r])
            for h in range(8):
                self.ld(self.os_d[8 * n:8 * n + 8, :].rearrange("t (g h d) -> h t g d", g=4, h=8)[h], ob[8 * h:8 * h + 8, :, :],
                        r=[ob_r], w=[self.dsa_r])
        osl, osl_r = T("osl", [128, E])
        self.ld(osl[:], self.os_d, r=[self.dsa_r], w=[osl_r])
        g2, g2_r = T("sg2", [128, E], BF16)
        self.dve(lambda e: e.tensor_tensor(out=g2[:], in0=osl[:], in1=szs_[:], op=ALU.mult), r=[osl_r, szs_r], w=[g2_r])
        g2T, g2T_r = T("sg2T", [128, 16, 128], BF16)
        hb, hb_r = T("shb", [128, D_MODEL])
        self.attn_out_tile(NTILE - 1, g2, g2_r, g2T, g2T_r, wo, wo_r, hb, hb_r)
        self.ps_lim = 7

    def ml_layer(self):
        P = self.P
        with ExitStack() as st:
            self.ml_proj(st)
        P.barrier()
        with ExitStack() as st:
            self.ml_feat(st)
        P.barrier()
        if self.ml_stage >= 2:
            with ExitStack() as st:
                self.ml_chunks(st)
            P.barrier()
        if self.ml_stage >= 3:
            with ExitStack() as st:
                self.ml_sample(st)
            P.barrier()

    def ml_proj(self, st):
        P = self.P
        win, win_r = P.sb(st, "mwin", [128, 8, 2 * E], BF16)
        wv = self.ml_w_in.rearrange("(k p) n -> p k n", p=128)
        for k in range(8):
            for hf in range(2):
                self.ldc(win[:, k, hf * E:(hf + 1) * E], wv[:, k, hf * E:(hf + 1) * E], w=[win_r])
        gt, gt_r = P.sb(st, "mgt", [128, D_MODEL])
        self.ld(gt[:], self.ml_norm.partition_broadcast(128), w=[gt_r])
        hts = [P.sb(st, "mht", [128, D_MODEL]) for _ in range(2)]
        junk, junk_r = P.sb(st, "mjunk", [128, D_MODEL], BF16)
        sss = [P.sb(st, "mss", [128, 1]) for _ in range(2)]
        xns = [P.sb(st, "mxn", [128, D_MODEL], BF16) for _ in range(2)]
        xTs = [P.sb(st, "mxT", [128, 8, 512], BF16) for _ in range(2)]
        obufs = [P.sb(st, "mobuf", [128, 512], BF16) for _ in range(4)]
        utok, utok_r = P.sb(st, "utok", [128, E])
        ob_i = 0
        ti = 0
        for tg in range(NGRP):
            tok0, ntok = grp_tok(tg)
            xT, xT_r = xTs[tg % 2]
            for il in range(ntok // 128):
                i = tok0 // 128 + il
                ht, ht_r = hts[ti % 2]
                ss, ss_r = sss[ti % 2]
                xn, xn_r = xns[ti % 2]
                ti += 1
                self.norm_tile(self.H[i * 128:(i + 1) * 128, :], self.H_r[i], gt, gt_r, ht, ht_r, junk, junk_r, ss, ss_r,
                               xn, xn_r, xT[:, :, il * 128:(il + 1) * 128], xT_r)
            for fo in range(32):
                ps, ps_r = self.next_ps()
                for k in range(8):
                    self.pe(lambda e, k=k, fo=fo, ps=ps, xT=xT, ntok=ntok: e.matmul(
                        ps[:, 0:ntok], lhsT=win[:, k, fo * 128:(fo + 1) * 128], rhs=xT[:, k, 0:ntok],
                        start=(k == 0), stop=(k == 7)), r=[win_r, xT_r], w=[ps_r])
                ob, ob_r = obufs[ob_i % 4]
                ob_i += 1
                if fo < 16:
                    self.act(lambda e, ob=ob, ps=ps, ntok=ntok: e.copy(out=ob[:, 0:ntok], in_=ps[:, 0:ntok]), r=[ps_r], w=[ob_r])
                    self.ld(self.muT_d[fo, :, tok0:tok0 + ntok], ob[:, 0:ntok], r=[ob_r], w=[self.ml_r])
                else:
                    self.act(lambda e, ob=ob, ps=ps, ntok=ntok: e.activation(
                        out=ob[:, 0:ntok], in_=ps[:, 0:ntok], func=AF.Silu), r=[ps_r], w=[ob_r])
                    self.ld(self.szT_d[fo - 16, :, tok0:tok0 + ntok], ob[:, 0:ntok], r=[ob_r], w=[self.szT_r[fo - 16]])
            if tg >= 7:
                c0 = ntok - 128
                for cc in range(4):
                    ps, ps_r = self.next_ps()
                    for k in range(8):
                        self.pe(lambda e, k=k, cc=cc, ps=ps, xT=xT, c0=c0: e.matmul(
                            ps[:], lhsT=xT[:, k, c0:c0 + 128], rhs=win[:, k, cc * 512:(cc + 1) * 512],
                            start=(k == 0), stop=(k == 7)), r=[win_r, xT_r], w=[ps_r])
                    self.act(lambda e, ps=ps, cc=cc: e.copy(out=utok[:, cc * 512:(cc + 1) * 512], in_=ps[:]), r=[ps_r], w=[utok_r])
                if tg == 7:
                    self.store(self.o_mconv_p, utok[125:128, :], r=[utok_r])
                else:
                    for n in range(N_S):
                        self.store(self.o_mconv_s[n], utok[8 * n + 5:8 * n + 8, :], r=[utok_r])

    def ml_feat(self, st):
        P = self.P

        def T(name, shape, dt=F32):
            return P.sb(st, name, shape, dt)
        ws = {}
        for nm, src in (("q", self.ml_w_q), ("k", self.ml_w_k), ("v", self.ml_w_v), ("o", self.ml_w_o)):
            w_, w_r = T("mw" + nm, [128, 8, 2, 256], BF16)
            for h in range(8):
                self.ldc(w_[:, h, :, :], src[h].rearrange("(dk p) e -> p dk e", p=128), w=[w_r])
            ws[nm] = (w_, w_r)
        wg, wg_r = T("mwg", [128, 48, 16], BF16)
        self.ldc(wg[:], self.ml_w_gates.rearrange("(c p) g -> p c g", p=128), w=[wg_r])
        bo, bo_r = T("mbo", [128, E])
        self.ld(bo[:], self.ml_b_o.partition_broadcast(128), w=[bo_r])
        cw, cw_r = T("mcw", [128, 16, 5])
        for j in range(4):
            self.P.dma("sp", lambda e, j=j: e.dma_start(out=cw[:, :, j], in_=self.ml_conv_w[j].rearrange("(f p) -> p f", p=128),
                                                       allow_slow_non_contiguous=True), (), [cw_r])
        self.P.dma("sp", lambda e: e.dma_start(out=cw[:, :, 4], in_=self.ml_conv_b.rearrange("(f p) -> p f", p=128),
                                              allow_slow_non_contiguous=True), (), [cw_r])
        bgi, bg_r = T("mbgi", [8, 1])
        bgf, _ = T("mbgf", [8, 1])
        nbgf, _ = T("mnbgf", [8, 1])
        self.ld(bgi[:], self.ml_b_gates[0:8, :], w=[bg_r])
        self.ld(bgf[:], self.ml_b_gates[8:16, :], w=[bg_r])
        self.dve(lambda e: e.tensor_scalar(out=nbgf[:], in0=bgf[:], scalar1=-1.0, scalar2=None, op0=ALU.mult), r=[bg_r], w=[bg_r])
        uX, uX_r = T("uX", [128, 16, 515], BF16)
        uSc, uSc_r = T("uSc", [128, 16, T_S], BF16)
        accs = [T("cacc", [128, 512]) for _ in range(2)]
        caT, caT_r = T("caT", [128, 16, 512], BF16)
        qTg, qTg_r = T("mqT", [128, 16, 512], BF16)
        kTg, kTg_r = T("mkT", [128, 16, 512], BF16)
        vTg, vTg_r = T("mvT", [128, 16, 512], BF16)
        obufs = [T("mfob", [128, 512], BF16) for _ in range(3)]
        vxs = [T("mvx", [128, 8, 257], BF16) for _ in range(2)]
        for (t_, r_) in vxs:
            self.pool(lambda e, t_=t_: e.memset(t_[:, :, 256:257], 1.0), w=[r_])
        kts = [T("mkt", [128, E], BF16) for _ in range(2)]
        ots = [T("mot", [128, E], BF16) for _ in range(2)]
        otmp = [T("motmp", [128, 512]) for _ in range(2)]
        ig, g_r = T("g_ig", [8, 512])
        lf, _ = T("g_lf", [8, 512])
        Bc, _ = T("g_B", [8, 512])
        ones, _ = T("g_ones", [8, 512])
        aa, _ = T("g_a", [8, 512])
        GG, _ = T("g_G", [8, 512])
        nG, _ = T("g_nG", [8, 512])
        wi_, _ = T("g_wi", [8, 512])
        em, _ = T("g_em", [8, 512])
        Gp, _ = T("g_Gp", [8, 8])
        Bl, _ = T("g_Bl", [8, 1])
        Gl, _ = T("g_Gl", [8, 1])
        m0s, _ = T("g_m0", [8, N_S])
        mms, _ = T("g_mm", [8, N_S])
        self.pool(lambda e: e.memset(ones[:], 1.0), w=[g_r])
        self.pool(lambda e: e.memset(Bl[:], 0.0), w=[g_r])
        self.pool(lambda e: e.memset(Gl[:], 0.0), w=[g_r])
        self.P.dma("sp", lambda e: e.dma_start(out=m0s[:], in_=self.ml_m0.rearrange("n h -> h n"), allow_slow_non_contiguous=True), (), [g_r])
        cv, cv_r = T("mcv", [48, E])
        self.ld(cv[:], self.ml_conv0, w=[cv_r])
        ob_i = 0
        vi = 0
        for tg in range(NGRP):
            tok0, ntok = grp_tok(tg)
            smp = (tg == 8)
            if not smp:
                self.ld(uX[:, :, 3:515], self.muT_d[:, :, tok0:tok0 + 512].rearrange("f p t -> p f t"), r=[self.ml_r], w=[uX_r])
                if tg == 0:
                    self.pool(lambda e: e.memset(uX[:, :, 0:3], 0.0), w=[uX_r])
                else:
                    self.ld(uX[:, :, 0:3], self.muT_d[:, :, tok0 - 3:tok0].rearrange("f p t -> p f t"), r=[self.ml_r], w=[uX_r])
            else:
                uS = uX[:, :, 0:N_S * 11].rearrange("p f (n t) -> p f n t", t=11)
                self.ld(uSc[:], self.muT_d[:, :, T_P:TOK].rearrange("f p t -> p f t"), r=[self.ml_r], w=[uSc_r])
                for f in range(16):
                    self.ld(uS[:, f, :, 3:11], self.muT_d[f, :, T_P:TOK].rearrange("p (n t) -> p n t", t=8), r=[self.ml_r], w=[uX_r])
                for f4 in range(0, 16, 4):
                    ps, ps_r = self.next_ps()
                    for j in range(4):
                        self.pe(lambda e, ps=ps, j=j, f4=f4: e.transpose(out=ps[:, j * 48:(j + 1) * 48], in_=cv[:, (f4 + j) * 128:(f4 + j + 1) * 128],
                                                                       identity=self.ident_f[0:48, 0:48]), r=[cv_r, self.ident_f_r], w=[ps_r])
                    self.act(lambda e, ps=ps, f4=f4, uS=uS: e.copy(out=uS[:, f4:f4 + 4, :, 0:3],
                                                                  in_=ps[:, 0:192].rearrange("p (f n j) -> p f n j", f=4, j=3)), r=[ps_r], w=[uX_r])
            for f in range(16):
                acc, acc_r = accs[f % 2]
                if not smp:
                    srcs = [uX[:, f, j:j + 512] for j in range(4)]
                    accv = acc[:, 0:512]
                    cav = caT[:, f, 0:512]
                else:
                    srcs = [uS[:, f, :, j:j + 8] for j in range(4)]
                    accv = acc[:, 0:128].rearrange("p (n t) -> p n t", t=8)
                    cav = caT[:, f, 0:128].rearrange("p (n t) -> p n t", t=8)
                self.dve(lambda e, accv=accv, srcs=srcs, f=f: e.tensor_scalar(out=accv, in0=srcs[0], scalar1=cw[:, f, 0:1], scalar2=cw[:, f, 4:5],
                                                                             op0=ALU.mult, op1=ALU.add), r=[uX_r, cw_r], w=[acc_r])
                for j in range(1, 4):
                    self.dve(lambda e, accv=accv, srcs=srcs, f=f, j=j: e.scalar_tensor_tensor(out=accv, in0=srcs[j], scalar=cw[:, f, j:j + 1], in1=accv,
                                                                                          op0=ALU.mult, op1=ALU.add), r=[uX_r, cw_r, acc_r], w=[acc_r])
                self.act(lambda e, accv=accv, cav=cav: e.activation(out=cav, in_=accv, func=AF.Silu), r=[acc_r], w=[caT_r])
            self.ld(self.mca_d[:, :, tok0:tok0 + ntok].rearrange("f p t -> p f t"), caT[:, :, 0:ntok], r=[caT_r], w=[self.ml_r])

            def uview(f, lo_, n_):
                if not smp:
                    return uX[:, f, 3 + lo_:3 + lo_ + n_]
                return uSc[:, f, lo_:lo_ + n_]

            for nm, src_is_ca, dst, dst_r, dscr in (("q", True, qTg, qTg_r, self.mq_d), ("k", True, kTg, kTg_r, self.mk_d),
                                                    ("v", False, vTg, vTg_r, None)):
                w_, w_r = ws[nm]
                for h in range(8):
                    for ec in range(2):
                        ps, ps_r = self.next_ps()
                        for dk in range(2):
                            if src_is_ca:
                                rhs = caT[:, 2 * h + dk, 0:ntok]
                                outp = ps[:, 0:ntok]
                            else:
                                rhs = uview(2 * h + dk, 0, ntok)
                                outp = ps[:, 0:ntok]
                            self.pe(lambda e, outp=outp, rhs=rhs, w_=w_, h=h, dk=dk, ec=ec: e.matmul(
                                outp, lhsT=w_[:, h, dk, ec * 128:(ec + 1) * 128], rhs=rhs, start=(dk == 0), stop=(dk == 1)),
                                r=[w_r, caT_r, uX_r, uSc_r], w=[ps_r])
                        self.act(lambda e, ps=ps, dst=dst, h=h, ec=ec, ntok=ntok: e.copy(out=dst[:, 2 * h + ec, 0:ntok], in_=ps[:, 0:ntok]),
                                 r=[ps_r], w=[dst_r])
                if dscr is not None:
                    self.ld(dscr[:, :, tok0:tok0 + ntok].rearrange("f p t -> p f t"), dst[:, :, 0:ntok], r=[dst_r], w=[self.ml_r])
            for gi in range(2):
                ps, ps_r = self.next_ps()
                for c in range(48):
                    src_t, src_r = ((qTg, qTg_r), (kTg, kTg_r), (vTg, vTg_r))[c // 16]
                    self.pe(lambda e, ps=ps, c=c, gi=gi, src_t=src_t, ntok=ntok: e.matmul(
                        ps[0:8, 0:ntok], lhsT=wg[:, c, 8 * gi:8 * gi + 8], rhs=src_t[:, c % 16, 0:ntok], start=(c == 0), stop=(c == 47)),
                        r=[wg_r, src_r], w=[ps_r])
                if gi == 0:
                    self.act(lambda e, ps=ps, ntok=ntok: e.activation(out=ig[:, 0:ntok], in_=ps[0:8, 0:ntok], func=AF.Copy, bias=0.0), r=[ps_r], w=[g_r])
                    self.dve(lambda e, ntok=ntok: e.tensor_scalar(out=ig[:, 0:ntok], in0=ig[:, 0:ntok], scalar1=bgi[:, 0:1], scalar2=None, op0=ALU.add),
                             r=[g_r, bg_r], w=[g_r])
                else:
                    self.act(lambda e, ps=ps, ntok=ntok: e.activation(out=lf[:, 0:ntok], in_=ps[0:8, 0:ntok], func=AF.Exp, scale=-1.0, bias=nbgf[:, 0:1]),
                             r=[ps_r, bg_r], w=[g_r])
                    self.act(lambda e, ntok=ntok: e.activation(out=lf[:, 0:ntok], in_=lf[:, 0:ntok], func=AF.Ln, bias=1.0), r=[g_r], w=[g_r])
                    self.dve(lambda e, ntok=ntok: e.tensor_scalar(out=lf[:, 0:ntok], in0=lf[:, 0:ntok], scalar1=-1.0, scalar2=None, op0=ALU.mult),
                             r=[g_r], w=[g_r])
            G = [g_r]
            if not smp:
                self.dve(lambda e: e.tensor_tensor_scan(out=Bc[:], data0=ones[:], data1=lf[:], initial=Bl[:, 0:1], op0=ALU.mult, op1=ALU.add), r=G, w=G)
                self.dve(lambda e: e.tensor_tensor(out=aa[:], in0=ig[:], in1=Bc[:], op=ALU.subtract), r=G, w=G)
                self.dve(lambda e: e.tensor_tensor_scan(out=GG[:], data0=aa[:], data1=aa[:], initial=Gl[:, 0:1], op0=ALU.max, op1=ALU.max), r=G, w=G)
                self.dve(lambda e: e.tensor_copy(out=Gp[:, 0:1], in_=Gl[:, 0:1]), r=G, w=G)
                self.dve(lambda e: e.tensor_copy(out=Gp[:, 1:8], in_=GG[:].rearrange("p (c t) -> p c t", t=64)[:, 0:7, 63]), r=G, w=G)
                self.dve(lambda e: e.tensor_tensor(out=wi_[:].rearrange("p (c t) -> p c t", t=64), in0=bc(Gp[:], 2, 64),
                                                   in1=GG[:].rearrange("p (c t) -> p c t", t=64), op=ALU.subtract), r=G, w=G)
                self.dve(lambda e: e.tensor_copy(out=Bl[:], in_=Bc[:, 511:512]), r=G, w=G)
                self.dve(lambda e: e.tensor_copy(out=Gl[:], in_=GG[:, 511:512]), r=G, w=G)
            else:
                v3 = lambda t_: t_[:, 0:128].rearrange("p (n t) -> p n t", t=8)
                B3, l3, a3, G3, i3 = v3(Bc), v3(lf), v3(aa), v3(GG), v3(ig)
                self.dve(lambda e: e.tensor_copy(out=B3[:, :, 0], in_=l3[:, :, 0]), r=G, w=G)
                for t in range(1, 8):
                    self.dve(lambda e, t=t: e.tensor_tensor(out=B3[:, :, t], in0=B3[:, :, t - 1], in1=l3[:, :, t], op=ALU.add), r=G, w=G)
                self.dve(lambda e: e.tensor_tensor(out=aa[:, 0:128], in0=ig[:, 0:128], in1=Bc[:, 0:128], op=ALU.subtract), r=G, w=G)
                self.dve(lambda e: e.tensor_tensor(out=G3[:, :, 0], in0=a3[:, :, 0], in1=m0s[:], op=ALU.max), r=G, w=G)
                for t in range(1, 8):
                    self.dve(lambda e, t=t: e.tensor_tensor(out=G3[:, :, t], in0=G3[:, :, t - 1], in1=a3[:, :, t], op=ALU.max), r=G, w=G)
                self.dve(lambda e: e.tensor_tensor(out=v3(wi_), in0=bc(m0s[:], 2, 8), in1=G3, op=ALU.subtract), r=G, w=G)
            nt = ntok
            self.act(lambda e, nt=nt: e.activation(out=wi_[:, 0:nt], in_=wi_[:, 0:nt], func=AF.Exp), r=G, w=G)
            self.dve(lambda e, nt=nt: e.tensor_scalar(out=nG[:, 0:nt], in0=GG[:, 0:nt], scalar1=-1.0, scalar2=None, op0=ALU.mult), r=G, w=G)
            self.dve(lambda e, nt=nt: e.tensor_tensor(out=Bc[:, 0:nt], in0=Bc[:, 0:nt], in1=GG[:, 0:nt], op=ALU.add), r=G, w=G)
            self.act(lambda e, nt=nt: e.activation(out=em[:, 0:nt], in_=Bc[:, 0:nt], func=AF.Exp, scale=-1.0), r=G, w=G)
            if tg == 7:
                self.store(self.o_mm_p, Bc[:, 511:512], r=G)
            if smp:
                self.dve(lambda e: e.tensor_copy(out=mms[:], in_=Bc[:, 0:128].rearrange("p (n t) -> p n t", t=8)[:, :, 7]), r=G, w=G)
                self.store(self.o_mm_s, mms[:], r=G)
            for qi_, t_ in enumerate((aa, nG, wi_, em)):
                self.ld(self.gq_d[qi_, :, tok0:tok0 + nt], t_[:, 0:nt], r=G, w=[self.ml_r])
            for il in range(ntok // 128):
                i = tok0 // 128 + il
                vx, vx_r = vxs[vi % 2]
                kt, kt_r = kts[vi % 2]
                ot, ot_r = ots[vi % 2]
                vi += 1
                for nm in ("v", "k", "o"):
                    w_, w_r = ws[nm]
                    for h2 in range(4):
                        ps, ps_r = self.next_ps()
                        for hh in range(2):
                            h = 2 * h2 + hh
                            for dk in range(2):
                                if nm == "k":
                                    lhsT = caT[:, 2 * h + dk, il * 128:(il + 1) * 128]
                                else:
                                    lhsT = uview(2 * h + dk, il * 128, 128)
                                    if smp:
                                        lhsT = lhsT
                                self.pe(lambda e, ps=ps, lhsT=lhsT, w_=w_, h=h, hh=hh, dk=dk: e.matmul(
                                    ps[:, hh * 256:(hh + 1) * 256], lhsT=lhsT, rhs=w_[:, h, dk, :], start=(dk == 0 and hh == 0), stop=(dk == 1),
                                    skip_group_check=True), r=[w_r, caT_r, uX_r, uSc_r], w=[ps_r])
                        if nm == "v":
                            self.act(lambda e, ps=ps, vx=vx, h2=h2: e.copy(out=vx[:, 2 * h2:2 * h2 + 2, 0:256],
                                                                          in_=ps[:].rearrange("p (a e) -> p a e", a=2)), r=[ps_r], w=[vx_r])
                        elif nm == "k":
                            self.act(lambda e, ps=ps, kt=kt, h2=h2: e.activation(out=kt[:, h2 * 512:(h2 + 1) * 512], in_=ps[:], func=AF.Copy, scale=1.0 / 16.0),
                                     r=[ps_r], w=[kt_r])
                        else:
                            tmp, tmp_r = otmp[h2 % 2]
                            self.dve(lambda e, ps=ps, tmp=tmp, h2=h2: e.tensor_tensor(out=tmp[:], in0=ps[:], in1=bo[:, h2 * 512:(h2 + 1) * 512], op=ALU.add),
                                     r=[ps_r, bo_r], w=[tmp_r])
                            self.act(lambda e, tmp=tmp, ot=ot, h2=h2: e.activation(out=ot[:, h2 * 512:(h2 + 1) * 512], in_=tmp[:], func=AF.Sigmoid),
                                     r=[tmp_r], w=[ot_r])
                self.ld(self.mvt_d[i * 128:(i + 1) * 128, :], vx[:].rearrange("p h e -> p (h e)"), r=[vx_r], w=[self.ml_r])
                self.ld(self.mkt_d[i * 128:(i + 1) * 128, :], kt[:], r=[kt_r], w=[self.ml_r])
                self.ld(self.mo_d[i * 128:(i + 1) * 128, :], ot[:], r=[ot_r], w=[self.ml_r])

    def ml_post_tile(self, i, hout, hout_r, K_):
        (st6, mv, rstd, sm_r, lnw, lnw_r, skip, skip_r, wout, wout_r, ot, ot_r, caTt, caTt_r, szTt, szTt_r,
         hn3, hn3_r, g2T, g2T_r, t1, t1_r, hb, hb_r) = K_
        tok0 = i * 128
        self.ld(ot[:], self.mo_d[tok0:tok0 + 128, :], r=[self.ml_r], w=[ot_r])
        self.ld(caTt[:], self.mca_d[:, :, tok0:tok0 + 128].rearrange("f p t -> p f t"), r=[self.ml_r], w=[caTt_r])
        self.ld(szTt[:], self.szT_d[:, :, tok0:tok0 + 128].rearrange("f p t -> p f t"), r=self.szT_r, w=[szTt_r])
        for h in range(8):
            self.dve(lambda e, h=h: e.bn_stats(out=st6[:, h, :], in_=hout[:, h, :]), r=[hout_r], w=[sm_r])
            self.dve(lambda e, h=h: e.bn_aggr(out=mv[:, h, :], in_=st6[:, h, :]), r=[sm_r], w=[sm_r])
        self.dve(lambda e: e.tensor_scalar(out=rstd[:], in0=mv[:, :, 1], scalar1=1e-5, scalar2=None, op0=ALU.add), r=[sm_r], w=[sm_r])
        self.act(lambda e: e.activation(out=rstd[:], in_=rstd[:], func=AF.Sqrt), r=[sm_r], w=[sm_r])
        self.dve(lambda e: e.reciprocal(out=rstd[:], in_=rstd[:]), r=[sm_r], w=[sm_r])
        for h in range(8):
            eng = self.dve if h % 2 == 0 else self.pool
            eng(lambda e, h=h: e.tensor_scalar(out=hout[:, h, :], in0=hout[:, h, :], scalar1=mv[:, h, 0:1], scalar2=rstd[:, h:h + 1],
                                               op0=ALU.subtract, op1=ALU.mult), r=[hout_r, sm_r], w=[hout_r])
        hf = hout[:].rearrange("p h e -> p (h e)")
        self.pool(lambda e: e.tensor_tensor(out=hf, in0=hf, in1=lnw[:], op=ALU.mult), r=[hout_r, lnw_r], w=[hout_r])
        self.dve(lambda e: e.tensor_tensor(out=hn3[:], in0=hf, in1=ot[:], op=ALU.mult), r=[hout_r, ot_r], w=[hn3_r])
        pt, pt_r = self.pst
        for b0 in range(0, 16, 8):
            for j in range(8):
                self.pe(lambda e, j=j, b0=b0: e.transpose(out=pt[:, j * 128:(j + 1) * 128], in_=hn3[:, (b0 + j) * 128:(b0 + j + 1) * 128],
                                                          identity=self.ident_b[:]), r=[hn3_r, self.ident_b_r], w=[pt_r])
            self.pool(lambda e, b0=b0: e.tensor_tensor(out=t1[:], in0=caTt[:, b0:b0 + 8, :], in1=bc(skip[:, b0:b0 + 8], 2, 128), op=ALU.mult),
                      r=[caTt_r, skip_r], w=[t1_r])
            self.dve(lambda e: e.tensor_tensor(out=t1[:], in0=t1[:], in1=pt[:].rearrange("p (a t) -> p a t", t=128), op=ALU.add),
                     r=[t1_r, pt_r], w=[t1_r])
            self.pool(lambda e, b0=b0: e.tensor_tensor(out=g2T[:, b0:b0 + 8, :], in0=t1[:], in1=szTt[:, b0:b0 + 8, :], op=ALU.mult),
                      r=[t1_r, szTt_r], w=[g2T_r])
        self.ld(hb[:], self.H[tok0:tok0 + 128, :], r=[self.H_r[i]], w=[hb_r])
        for hh in range(2):
            ps, ps_r = self.next_ps()
            for f in range(16):
                self.pe(lambda e, ps=ps, f=f, hh=hh: e.matmul(ps[:], lhsT=g2T[:, f, :], rhs=wout[:, f, hh * 512:(hh + 1) * 512],
                                                              start=(f == 0), stop=(f == 15)), r=[wout_r, g2T_r], w=[ps_r])
            self.dve(lambda e, ps=ps, hh=hh: e.tensor_tensor(out=hb[:, hh * 512:(hh + 1) * 512], in0=ps[:], in1=hb[:, hh * 512:(hh + 1) * 512],
                                                            op=ALU.add), r=[ps_r, hb_r], w=[hb_r])
        self.ld(self.H[tok0:tok0 + 128, :], hb[:], r=[hb_r], w=[self.H_r[i]])

    def ml_post_alloc(self, T):
        st6, sm_r = T("st6", [128, 8, 6])
        mv, _ = T("mv", [128, 8, 2])
        rstd, _ = T("rstd", [128, 8])
        lnw, lnw_r = T("lnw", [128, E])
        self.ld(lnw[:], self.ml_ln_w.partition_broadcast(128), w=[lnw_r])
        skip, skip_r = T("mskip", [128, 16])
        self.P.dma("sp", lambda e: e.dma_start(out=skip[:], in_=self.ml_skip.rearrange("(f p) -> p f", p=128), allow_slow_non_contiguous=True),
                   (), [skip_r])
        wout, wout_r = T("mwout", [128, 16, D_MODEL], BF16)
        wov = self.ml_w_out.rearrange("(k p) n -> p k n", p=128)
        for k in range(0, 16, 2):
            self.ldc(wout[:, k:k + 2, :], wov[:, k:k + 2, :], w=[wout_r])
        ot, ot_r = T("mot2", [128, E], BF16)
        caTt, caTt_r = T("caTt", [128, 16, 128], BF16)
        szTt, szTt_r = T("szTt", [128, 16, 128], BF16)
        hn3, hn3_r = T("hn3", [128, E], BF16)
        g2T, g2T_r = T("mg2T", [128, 16, 128], BF16)
        t1, t1_r = T("mt1", [128, 8, 128])
        hb, hb_r = T("mhb", [128, D_MODEL])
        return (st6, mv, rstd, sm_r, lnw, lnw_r, skip, skip_r, wout, wout_r, ot, ot_r, caTt, caTt_r, szTt, szTt_r,
                hn3, hn3_r, g2T, g2T_r, t1, t1_r, hb, hb_r)

    def ml_chunks(self, st):
        P = self.P

        def T(name, shape, dt=F32):
            return P.sb(st, name, shape, dt)
        K_ = self.ml_post_alloc(T)
        hmask, hmask_r = T("hmask", [8, 8, 128])
        self.ld(hmask[:].rearrange("p a t -> p (a t)"), self.hmask_d, w=[hmask_r])
        ones8, ones8_r = T("ones8", [8, 128])
        self.ld(ones8[:], self.ones8_d, w=[ones8_r])
        cm64, cm64_r = T("cm64", [64, 64])
        self.ld(cm64[:], self.cm64_d, w=[cm64_r])
        C = [T("Cst", [128, 2, 257]) for _ in range(8)]
        Cb = [T("Cbf", [128, 2, 257], BF16) for _ in range(8)]
        for h in range(8):
            self.pool(lambda e, h=h: e.memset(C[h][0][:], 0.0), w=[C[h][1]])
            self.pool(lambda e, h=h: e.memset(Cb[h][0][:], 0.0), w=[Cb[h][1]])
        qTg, qTg_r = T("cqT", [128, 16, 512], BF16)
        kTg, kTg_r = T("ckT", [128, 16, 512], BF16)
        gq, gq_r = T("cgq", [8, 4, 512])
        vts = [T("vtc", [64, 8 * 257], BF16) for _ in range(2)]
        kts = [T("ktc", [64, E], BF16) for _ in range(2)]
        nGe, nGe_r = T("nGe", [8, 8, 64])
        wie, wie_r = T("wie", [8, 8, 64])
        wm, wm_r = T("wm", [64, 8, 64])
        PT, PT_r = T("PT", [64, 8, 64], BF16)
        qw, qw_r = T("qw", [128, 16, 64], BF16)
        wpv, wpv_r = T("wpv", [128, 8])
        emT, emT_r = T("emT", [128, 8])
        dn, dn_r = T("dn", [128, 1])
        kws = [T("kw", [64, 256], BF16) for _ in range(8)]
        hout, hout_r = T("hout", [128, 8, 256])
        for c in range(T_P // 64):
            cl = c % 8
            cs0 = cl * 64
            t0 = c * 64
            po = 64 * (c % 2)
            if cl == 0:
                g0 = c * 64
                self.ld(qTg[:], self.mq_d[:, :, g0:g0 + 512].rearrange("f p t -> p f t"), r=[self.ml_r], w=[qTg_r])
                self.ld(kTg[:], self.mk_d[:, :, g0:g0 + 512].rearrange("f p t -> p f t"), r=[self.ml_r], w=[kTg_r])
                self.ld(gq[:], self.gq_d[:, :, g0:g0 + 512].rearrange("q h t -> h q t"), r=[self.ml_r], w=[gq_r])
            vt, vt_r = vts[c % 2]
            kt, kt_r = kts[c % 2]
            self.ld(vt[:], self.mvt_d[t0:t0 + 64, :], r=[self.ml_r], w=[vt_r])
            self.ld(kt[:], self.mkt_d[t0:t0 + 64, :], r=[self.ml_r], w=[kt_r])
            self.dve(lambda e, cs0=cs0: e.tensor_tensor(out=nGe[:], in0=bc(gq[:, 1, cs0:cs0 + 64], 1, 8), in1=hmask[:, :, 0:64], op=ALU.mult),
                     r=[gq_r, hmask_r], w=[nGe_r])
            self.pool(lambda e, cs0=cs0: e.tensor_tensor(out=wie[:], in0=bc(gq[:, 2, cs0:cs0 + 64], 1, 8), in1=hmask[:, :, 0:64], op=ALU.mult),
                      r=[gq_r, hmask_r], w=[wie_r])
            Dps, Dps_r = self.next_ps()
            Dv = Dps[0:64, :].rearrange("p (a t) -> p a t", t=64)
            self.pe(lambda e, Dv=Dv: e.matmul(Dv, lhsT=ones8[:, 0:64], rhs=nGe[:], start=True, stop=False), r=[ones8_r, nGe_r], w=[Dps_r])
            self.pe(lambda e, Dv=Dv, cs0=cs0: e.matmul(Dv, lhsT=gq[:, 0, cs0:cs0 + 64], rhs=hmask[:, :, 0:64], start=False, stop=True),
                    r=[gq_r, hmask_r], w=[Dps_r])
            Wps, Wps_r = self.next_ps()
            Wv = Wps[:].rearrange("p (a t) -> p a t", t=64)
            self.pe(lambda e, Wv=Wv: e.matmul(Wv, lhsT=ones8[:], rhs=wie[:], start=True, stop=True), r=[ones8_r, wie_r], w=[Wps_r])
            if c % 2 == 0:
                Eps, Eps_r = self.next_ps()
                self.pe(lambda e, Eps=Eps, cs0=cs0: e.matmul(Eps[:, 0:8], lhsT=gq[:, 3, cs0:cs0 + 128], rhs=self.ident_f[0:8, 0:8], start=True, stop=True),
                        r=[gq_r, self.ident_f_r], w=[Eps_r])
                self.act(lambda e, Eps=Eps: e.copy(out=emT[:], in_=Eps[:, 0:8]), r=[Eps_r], w=[emT_r])
            self.act(lambda e, Dv=Dv: e.activation(out=wm[:], in_=Dv, func=AF.Exp), r=[Dps_r], w=[wm_r])
            self.dve(lambda e: e.tensor_tensor(out=wm[:], in0=wm[:], in1=bc(cm64[:], 1, 8), op=ALU.mult), r=[wm_r, cm64_r], w=[wm_r])
            Sps, Sps_r = self.next_ps()
            for h in range(8):
                for dk in range(2):
                    self.pe(lambda e, Sps=Sps, h=h, dk=dk, cs0=cs0: e.matmul(
                        Sps[0:64, 64 * h:64 * h + 64], lhsT=kTg[:, 2 * h + dk, cs0:cs0 + 64], rhs=qTg[:, 2 * h + dk, cs0:cs0 + 64],
                        start=(h == 0 and dk == 0), stop=(dk == 1), skip_group_check=True), r=[kTg_r, qTg_r], w=[Sps_r])
            self.dve(lambda e, Sps=Sps: e.scalar_tensor_tensor(out=PT[:], in0=Sps[0:64, :].rearrange("p (a t) -> p a t", t=64), scalar=1.0 / 16.0,
                                                               in1=wm[:], op0=ALU.mult, op1=ALU.mult), r=[Sps_r, wm_r], w=[PT_r])
            self.dve(lambda e, Wv=Wv, cs0=cs0: e.tensor_tensor(out=qw[:].rearrange("p (h k) t -> p h k t", k=2),
                                                               in0=qTg[:, :, cs0:cs0 + 64].rearrange("p (h k) t -> p h k t", k=2),
                                                               in1=bc(Wv, 2, 2), op=ALU.mult), r=[qTg_r, Wps_r], w=[qw_r])
            self.act(lambda e, Wv=Wv: e.copy(out=wpv[:], in_=Wv[:, :, 63]), r=[Wps_r], w=[wpv_r])
            for h in range(8):
                kw, kw_r = kws[h]
                self.act(lambda e, kw=kw, kt=kt, h=h: e.activation(out=kw[:], in_=kt[:, 256 * h:256 * h + 256], func=AF.Copy,
                                                                  scale=wm[:, h, 63:64]), r=[kt_r, wm_r], w=[kw_r])
            for h in range(8):
                nps, nps_r = self.next_ps()
                outp = nps[po:po + 64, 0:257]
                self.pe(lambda e, outp=outp, h=h, vt=vt, po=po: e.matmul(outp, lhsT=PT[:, h, :], rhs=vt[:, 257 * h:257 * h + 257], start=True, stop=False,
                                                                        tile_position=(0, po)), r=[PT_r, vt_r], w=[nps_r])
                for dk in range(2):
                    self.pe(lambda e, outp=outp, h=h, dk=dk, po=po: e.matmul(outp, lhsT=qw[:, 2 * h + dk, :], rhs=Cb[h][0][:, dk, :], start=False, stop=(dk == 1),
                                                                            tile_position=(0, po)), r=[qw_r, Cb[h][1]], w=[nps_r])
                self.act(lambda e, nps=nps, po=po: e.activation(out=dn[po:po + 64, :], in_=nps[po:po + 64, 256:257], func=AF.Abs),
                         r=[nps_r], w=[dn_r])
                self.dve(lambda e, h=h, po=po: e.tensor_tensor(out=dn[po:po + 64, :], in0=dn[po:po + 64, :], in1=emT[po:po + 64, h:h + 1], op=ALU.max),
                         r=[dn_r, emT_r], w=[dn_r])
                self.dve(lambda e, po=po: e.reciprocal(out=dn[po:po + 64, :], in_=dn[po:po + 64, :]), r=[dn_r], w=[dn_r])
                self.act(lambda e, nps=nps, h=h, po=po: e.activation(out=hout[po:po + 64, h, :], in_=nps[po:po + 64, 0:256], func=AF.Copy,
                                                                    scale=dn[po:po + 64, 0:1]), r=[nps_r, dn_r], w=[hout_r])
                kw, kw_r = kws[h]
                for dk in range(2):
                    ups, ups_r = self.next_ps()
                    self.pe(lambda e, ups=ups, kw=kw, dk=dk, h=h, vt=vt: e.matmul(ups[:, 0:257], lhsT=kw[:, 128 * dk:128 * dk + 128],
                                                                                rhs=vt[:, 257 * h:257 * h + 257], start=True, stop=True),
                            r=[kw_r, vt_r], w=[ups_r])
                    self.dve(lambda e, ups=ups, h=h, dk=dk: e.scalar_tensor_tensor(out=C[h][0][:, dk, :], in0=C[h][0][:, dk, :], scalar=wpv[:, h:h + 1],
                                                                                  in1=ups[:, 0:257], op0=ALU.mult, op1=ALU.add),
                             r=[C[h][1], wpv_r, ups_r], w=[C[h][1]])
            for h in range(8):
                self.pool(lambda e, h=h: e.tensor_copy(out=Cb[h][0][:], in_=C[h][0][:]), r=[C[h][1]], w=[Cb[h][1]])
            if c % 2 == 1:
                self.ml_post_tile(c // 2, hout, hout_r, K_)
        nn, nn_r = T("nfin", [128, 16])
        for h in range(8):
            self.store(self.o_mc_p[h].rearrange("(dk p) e -> p dk e", p=128), C[h][0][:, :, 0:256], r=[C[h][1]])
            self.pool(lambda e, h=h: e.tensor_copy(out=nn[:, 2 * h:2 * h + 2], in_=C[h][0][:, :, 256]), r=[C[h][1]], w=[nn_r])
        ps, ps_r = self.next_ps()
        self.pe(lambda e, ps=ps: e.transpose(out=ps[0:16, 0:128], in_=nn[:], identity=self.ident_f[:]), r=[nn_r, self.ident_f_r], w=[ps_r])
        nt_, nt_r = T("nfinT", [16, 128])
        self.act(lambda e, ps=ps: e.copy(out=nt_[:], in_=ps[0:16, 0:128]), r=[ps_r], w=[nt_r])
        self.store(self.o_mn_p, nt_[:], r=[nt_r])

    def ml_sample(self, st):
        P = self.P

        def T(name, shape, dt=F32):
            return P.sb(st, name, shape, dt)
        K_ = self.ml_post_alloc(T)
        hmask, hmask_r = T("shmask", [8, 8, 128])
        self.ld(hmask[:].rearrange("p a t -> p (a t)"), self.hmask_d, w=[hmask_r])
        ones8, ones8_r = T("sones8", [8, 128])
        self.ld(ones8[:], self.ones8_d, w=[ones8_r])
        smask, smask_r = T("smask", [128, 128])
        self.ld(smask[:], self.smask_d, w=[smask_r])
        seqsel, seqsel_r = T("seqsel", [128, N_S])
        self.ld(seqsel[:], self.seqsel_d, w=[seqsel_r])
        qT, qT_r = T("sqT", [128, 16, T_S], BF16)
        kT, kT_r = T("skT", [128, 16, T_S], BF16)
        self.ld(qT[:], self.mq_d[:, :, T_P:TOK].rearrange("f p t -> p f t"), r=[self.ml_r], w=[qT_r])
        self.ld(kT[:], self.mk_d[:, :, T_P:TOK].rearrange("f p t -> p f t"), r=[self.ml_r], w=[kT_r])
        vt, vt_r = T("svt", [128, 8 * 257], BF16)
        kt, kt_r = T("skt", [128, E], BF16)
        self.ld(vt[:], self.mvt_d[T_P:TOK, :], r=[self.ml_r], w=[vt_r])
        self.ld(kt[:], self.mkt_d[T_P:TOK, :], r=[self.ml_r], w=[kt_r])
        gq, gq_r = T("sgq", [8, 4, T_S])
        self.ld(gq[:], self.gq_d[:, :, T_P:TOK].rearrange("q h t -> h q t"), r=[self.ml_r], w=[gq_r])
        nGe, nGe_r = T("snGe", [8, 8, 128])
        wie, wie_r = T("swie", [8, 8, 128])
        wsg, wsg_r = T("swsg", [8, 128])
        self.dve(lambda e: e.tensor_tensor(out=nGe[:], in0=bc(gq[:, 1, :], 1, 8), in1=hmask[:], op=ALU.mult), r=[gq_r, hmask_r], w=[nGe_r])
        self.pool(lambda e: e.tensor_tensor(out=wie[:], in0=bc(gq[:, 2, :], 1, 8), in1=hmask[:], op=ALU.mult), r=[gq_r, hmask_r], w=[wie_r])
        self.dve(lambda e: e.tensor_tensor(out=wsg[:].rearrange("p (n t) -> p n t", t=8), in0=gq[:, 0, :].rearrange("p (n t) -> p n t", t=8),
                                           in1=bc(gq[:, 1, :].rearrange("p (n t) -> p n t", t=8)[:, :, 7], 2, 8), op=ALU.add), r=[gq_r], w=[wsg_r])
        self.act(lambda e: e.activation(out=wsg[:], in_=wsg[:], func=AF.Exp), r=[wsg_r], w=[wsg_r])
        wm, wm_r = T("swm", [128, 8, 128])
        PT, PT_r = T("sPT", [128, 8, 128], BF16)
        qw, qw_r = T("sqw", [128, 16, 128], BF16)
        wpv, wpv_r = T("swpv", [128, 8, N_S])
        emT, emT_r = T("semT", [128, 8])
        wstT, wstT_r = T("swstT", [128, 8])
        Wsbs = [T("sWsb", [128, 512]) for _ in range(2)]
        ps, ps_r = self.next_ps()
        self.pe(lambda e, ps=ps: e.matmul(ps[:, 0:8], lhsT=gq[:, 3, :], rhs=self.ident_f[0:8, 0:8], start=True, stop=True), r=[gq_r, self.ident_f_r], w=[ps_r])
        self.act(lambda e, ps=ps: e.copy(out=emT[:], in_=ps[:, 0:8]), r=[ps_r], w=[emT_r])
        ps, ps_r = self.next_ps()
        self.pe(lambda e, ps=ps: e.matmul(ps[:, 0:8], lhsT=wsg[:], rhs=self.ident_f[0:8, 0:8], start=True, stop=True), r=[wsg_r, self.ident_f_r], w=[ps_r])
        self.act(lambda e, ps=ps: e.copy(out=wstT[:], in_=ps[:, 0:8]), r=[ps_r], w=[wstT_r])
        dstop = getattr(self, "dbg_stop", 99)
        if dstop <= 1:
            return
        for hb in range(2):
            hs = slice(4 * hb, 4 * hb + 4)
            Dps, Dps_r = self.next_ps()
            Dv = Dps[:].rearrange("p (a t) -> p a t", t=128)
            self.pe(lambda e, Dv=Dv, hs=hs: e.matmul(Dv, lhsT=ones8[:], rhs=nGe[:, hs, :], start=True, stop=False), r=[ones8_r, nGe_r], w=[Dps_r])
            self.pe(lambda e, Dv=Dv, hs=hs: e.matmul(Dv, lhsT=gq[:, 0, :], rhs=hmask[:, hs, :], start=False, stop=True), r=[gq_r, hmask_r], w=[Dps_r])
            self.act(lambda e, Dv=Dv, hs=hs: e.activation(out=wm[:, hs, :], in_=Dv, func=AF.Exp), r=[Dps_r], w=[wm_r])
            self.dve(lambda e, hs=hs: e.tensor_tensor(out=wm[:, hs, :], in0=wm[:, hs, :], in1=bc(smask[:], 1, 4), op=ALU.mult), r=[wm_r, smask_r], w=[wm_r])
            if dstop <= 1.2:
                continue
            Wps, Wps_r = self.next_ps()
            Wv = Wps[:].rearrange("p (a t) -> p a t", t=128)
            self.pe(lambda e, Wv=Wv, hs=hs: e.matmul(Wv, lhsT=ones8[:], rhs=wie[:, hs, :], start=True, stop=True), r=[ones8_r, wie_r], w=[Wps_r])
            if dstop <= 1.3:
                continue
            Wsb, Wsb_r = Wsbs[hb]
            self.act(lambda e, Wps=Wps, Wsb=Wsb: e.copy(out=Wsb[:], in_=Wps[:]), r=[Wps_r], w=[Wsb_r])
            for k2 in range(2):
                self.dve(lambda e, Wsb=Wsb, hb=hb, k2=k2: e.tensor_tensor(
                    out=qw[:, 8 * hb:8 * hb + 8, :].rearrange("p (h k) t -> p h k t", k=2)[:, :, k2, :],
                    in0=qT[:, 8 * hb:8 * hb + 8, :].rearrange("p (h k) t -> p h k t", k=2)[:, :, k2, :],
                    in1=Wsb[:].rearrange("p (a t) -> p a t", t=128), op=ALU.mult), r=[qT_r, Wsb_r], w=[qw_r])
            self.dve(lambda e, Wsb=Wsb, hs=hs: e.tensor_copy(out=wpv[:, hs, :], in_=Wsb[:].rearrange("p (a n t) -> p a n t", n=N_S, t=8)[:, :, :, 7]),
                     r=[Wsb_r], w=[wpv_r])
            if dstop <= 1.5:
                continue
            Sps, Sps_r = self.next_ps()
            for hh in range(4):
                h = 4 * hb + hh
                for dk in range(2):
                    self.pe(lambda e, Sps=Sps, hh=hh, h=h, dk=dk: e.matmul(Sps[:, 128 * hh:128 * hh + 128], lhsT=kT[:, 2 * h + dk, :], rhs=qT[:, 2 * h + dk, :],
                                                                          start=(hh == 0 and dk == 0), stop=(dk == 1), skip_group_check=True),
                            r=[kT_r, qT_r], w=[Sps_r])
            self.dve(lambda e, Sps=Sps, hs=hs: e.scalar_tensor_tensor(out=PT[:, hs, :], in0=Sps[:].rearrange("p (a t) -> p a t", t=128), scalar=1.0 / 16.0,
                                                                      in1=wm[:, hs, :], op0=ALU.mult, op1=ALU.mult), r=[Sps_r, wm_r], w=[PT_r])
        if dstop <= 2:
            return
        hout, hout_r = T("shout", [128, 8, 256])
        numacc, numacc_r = T("numacc", [128, 257])
        tot, tot_r = T("stot", [128, 257])
        dn, dn_r = T("sdn", [128, 1])
        Vexps = [T("Vexp", [128, N_S, 257], BF16) for _ in range(2)]
        kws = [T("skw", [128, 256], BF16) for _ in range(2)]
        C0x = [T("C0x", [128, 2, 257]) for _ in range(2)]
        C0b = [T("C0b", [128, 2, 257], BF16) for _ in range(2)]
        Cn = [T("Cn", [128, 2, 257]) for _ in range(2)]
        NN, NN_r = T("sNN", [128, N_S, 8, 2])
        ci = 0
        for h in range(8):
            Vexp, Vexp_r = Vexps[h % 2]
            kw, kw_r = kws[h % 2]
            self.pool(lambda e, Vexp=Vexp, h=h: e.tensor_tensor(out=Vexp[:], in0=bc(vt[:, 257 * h:257 * h + 257], 1, N_S), in1=bc(seqsel[:], 2, 257),
                                                                op=ALU.mult), r=[vt_r, seqsel_r], w=[Vexp_r])
            self.act(lambda e, kw=kw, h=h: e.activation(out=kw[:], in_=kt[:, 256 * h:256 * h + 256], func=AF.Copy, scale=wstT[:, h:h + 1]),
                     r=[kt_r, wstT_r], w=[kw_r])
            for n in range(N_S):
                cx, cx_r = C0x[ci % 2]
                cb, cb_r = C0b[ci % 2]
                cn, cn_r = Cn[ci % 2]
                ci += 1
                self.ld(cx[:, :, 0:256], self.ml_c0[n, h].rearrange("(dk p) e -> p dk e", p=128), w=[cx_r])
                self.P.dma("sp", lambda e, cx=cx, n=n, h=h: e.dma_start(out=cx[:, :, 256], in_=self.ml_n0[n, h].rearrange("(dk p) -> p dk", p=128),
                                                                     allow_slow_non_contiguous=True), (), [cx_r])
                self.pool(lambda e, cx=cx, cb=cb: e.tensor_copy(out=cb[:], in_=cx[:]), r=[cx_r], w=[cb_r])
                ips, ips_r = self.next_ps()
                for dk in range(2):
                    self.pe(lambda e, ips=ips, h=h, dk=dk, cb=cb: e.matmul(ips[:, 0:257], lhsT=qw[:, 2 * h + dk, :], rhs=cb[:, dk, :], start=(dk == 0), stop=(dk == 1)),
                            r=[qw_r, cb_r], w=[ips_r])
                if n == 0:
                    self.dve(lambda e, ips=ips, n=n: e.tensor_scalar(out=numacc[:], in0=ips[:, 0:257], scalar1=seqsel[:, n:n + 1], scalar2=None, op0=ALU.mult),
                             r=[ips_r, seqsel_r], w=[numacc_r])
                else:
                    self.dve(lambda e, ips=ips, n=n: e.scalar_tensor_tensor(out=numacc[:], in0=ips[:, 0:257], scalar=seqsel[:, n:n + 1], in1=numacc[:],
                                                                           op0=ALU.mult, op1=ALU.add), r=[ips_r, seqsel_r, numacc_r], w=[numacc_r])
                for dk in range(2):
                    ups, ups_r = self.next_ps()
                    self.pe(lambda e, ups=ups, kw=kw, dk=dk, n=n, Vexp=Vexp: e.matmul(ups[:, 0:257], lhsT=kw[:, 128 * dk:128 * dk + 128], rhs=Vexp[:, n, :],
                                                                                    start=True, stop=True), r=[kw_r, Vexp_r], w=[ups_r])
                    self.dve(lambda e, ups=ups, cn=cn, cx=cx, dk=dk, h=h, n=n: e.scalar_tensor_tensor(out=cn[:, dk, :], in0=cx[:, dk, :], scalar=wpv[:, h, n:n + 1],
                                                                                                 in1=ups[:, 0:257], op0=ALU.mult, op1=ALU.add),
                             r=[cx_r, wpv_r, ups_r], w=[cn_r])
                self.store(self.o_mc_s[n, h].rearrange("(dk p) e -> p dk e", p=128), cn[:, :, 0:256], r=[cn_r])
                self.pool(lambda e, cn=cn, n=n, h=h: e.tensor_copy(out=NN[:, n, h, :], in_=cn[:, :, 256]), r=[cn_r], w=[NN_r])
            nps, nps_r = self.next_ps()
            self.pe(lambda e, nps=nps, h=h: e.matmul(nps[:, 0:257], lhsT=PT[:, h, :], rhs=vt[:, 257 * h:257 * h + 257], start=True, stop=True),
                    r=[PT_r, vt_r], w=[nps_r])
            self.dve(lambda e, nps=nps: e.tensor_tensor(out=tot[:], in0=nps[:, 0:257], in1=numacc[:], op=ALU.add), r=[nps_r, numacc_r], w=[tot_r])
            self.act(lambda e: e.activation(out=dn[:], in_=tot[:, 256:257], func=AF.Abs), r=[tot_r], w=[dn_r])
            self.dve(lambda e, h=h: e.tensor_tensor(out=dn[:], in0=dn[:], in1=emT[:, h:h + 1], op=ALU.max), r=[dn_r, emT_r], w=[dn_r])
            self.dve(lambda e: e.reciprocal(out=dn[:], in_=dn[:]), r=[dn_r], w=[dn_r])
            self.act(lambda e, h=h: e.activation(out=hout[:, h, :], in_=tot[:, 0:256], func=AF.Copy, scale=dn[:, 0:1]), r=[tot_r, dn_r], w=[hout_r])
        if dstop <= 3:
            return
        self.ml_post_tile(NTILE - 1, hout, hout_r, K_)
        if dstop <= 4:
            return
        NNv = NN[:].rearrange("p n h k -> p (n h k)")
        ntT, ntT_r = T("sntT", [128, 2, 128])
        for b in range(2):
            ps, ps_r = self.next_ps()
            self.pe(lambda e, ps=ps, b=b: e.transpose(out=ps[:, 0:128], in_=NNv[:, 128 * b:128 * b + 128], identity=self.ident_f[:]),
                    r=[NN_r, self.ident_f_r], w=[ps_r])
            self.act(lambda e, ps=ps, b=b: e.copy(out=ntT[:, b, :], in_=ps[:, 0:128]), r=[ps_r], w=[ntT_r])
        self.store(self.o_mn_s.rearrange("(b r) p -> r b p", b=2), ntT[:], r=[ntT_r])

    def final_phase(self, from_x=False):
        P = self.P
        with ExitStack() as st:
            gt, gt_r = P.sb(st, "gtf", [128, D_MODEL])
            self.ld(gt[:], self.final_norm.partition_broadcast(128), w=[gt_r])
            hts = [P.sb(st, "htf", [128, D_MODEL]) for _ in range(3)]
            junk, junk_r = P.sb(st, "junkf", [128, D_MODEL], BF16)
            sss = [P.sb(st, "ssf", [128, 1]) for _ in range(3)]
            for i in range(NTILE):
                ht, ht_r = hts[i % 3]
                ss, ss_r = sss[i % 3]
                self.ld(ht[:], self.h_src(from_x, i), r=[self.H_r[i]], w=[ht_r])
                self.act(lambda e, ht=ht, ss=ss: e.activation(out=junk[:], in_=ht[:], func=AF.Square, accum_out=ss[:]),
                         r=[ht_r], w=[junk_r, ss_r])
                self.dve(lambda e, ss=ss: e.tensor_scalar(out=ss[:], in0=ss[:], scalar1=1.0 / D_MODEL, scalar2=1e-6,
                                                          op0=ALU.mult, op1=ALU.add), r=[ss_r], w=[ss_r])
                self.act(lambda e, ss=ss: e.activation(out=ss[:], in_=ss[:], func=AF.Sqrt), r=[ss_r], w=[ss_r])
                self.dve(lambda e, ss=ss: e.reciprocal(out=ss[:], in_=ss[:]), r=[ss_r], w=[ss_r])
                self.dve(lambda e, ht=ht, ss=ss: e.scalar_tensor_tensor(out=ht[:], in0=ht[:], scalar=ss[:, 0:1], in1=gt[:],
                                                                        op0=ALU.mult, op1=ALU.mult), r=[ht_r, ss_r, gt_r], w=[ht_r])
                self.store(self.y[i * 128:(i + 1) * 128, :], ht[:], r=[ht_r])


_CACHE = {}


def _get_prog(nlayers=4):
    if nlayers not in _CACHE:
        kb = KB(nlayers=nlayers)
        kb.build()
        _CACHE[nlayers] = kb
    return _CACHE[nlayers]


def make_in_maps(inp, kb):
    f32 = np.float32
    maps = []
    ident = np.eye(128, dtype=f32)
    for c in range(8):
        b = c % 4
        m = {}
        xs = np.asarray(inp["x_sample"][16 * c:16 * c + 16], f32).reshape(T_S, D_MODEL)
        m["xin"] = np.concatenate([np.asarray(inp["x_prompt"][b], f32), xs], axis=0)
        m["ident_f"] = ident
        m["final_norm"] = np.asarray(inp["final_norm"], f32).reshape(1, D_MODEL)
        for k in ["ssm_norm", "ssm_w_in", "ssm_a_re", "ssm_a_im", "ssm_b_re", "ssm_b_im", "ssm_c_re", "ssm_c_im",
                  "ssm_d", "ssm_w_glu", "ssm_b_glu", "ssm_w_out"]:
            m[k] = np.asarray(inp[k], f32)
        m["ssm_log_dt"] = np.asarray(inp["ssm_log_dt"], f32).reshape(2, 128, 1)
        m["st_re"] = np.asarray(inp["state_ssm_re"][:, 16 * c:16 * c + 16], f32).reshape(2, N_S, 8192)
        m["st_im"] = np.asarray(inp["state_ssm_im"][:, 16 * c:16 * c + 16], f32).reshape(2, N_S, 8192)
        m["attn_norm"] = np.asarray(inp["attn_norm"], f32).reshape(1, D_MODEL)
        m["attn_w_in"] = np.asarray(inp["attn_w_in"], f32).reshape(D_MODEL, ATTN_IN)
        m["attn_w_out"] = np.asarray(inp["attn_w_out"], f32).reshape(E, D_MODEL)
        m["rope_cs"] = rope_table()
        for k in ["mlstm_conv_w", "mlstm_conv_b", "mlstm_w_q", "mlstm_w_k", "mlstm_w_v", "mlstm_w_o", "mlstm_skip"]:
            m[k] = np.asarray(inp[k][0], f32)
        m["mlstm_norm"] = np.asarray(inp["mlstm_norm"], f32).reshape(1, D_MODEL)
        m["mlstm_w_in"] = np.asarray(inp["mlstm_w_in"][0], f32)
        m["mlstm_b_o"] = np.asarray(inp["mlstm_b_o"], f32).reshape(1, E)
        m["mlstm_w_gates"] = np.asarray(inp["mlstm_w_gates"][0], f32)
        m["mlstm_b_gates"] = np.asarray(inp["mlstm_b_gates"], f32).reshape(16, 1)
        m["mlstm_ln_w"] = np.asarray(inp["mlstm_ln_w"], f32).reshape(1, E)
        m["mlstm_w_out"] = np.asarray(inp["mlstm_w_out"][0], f32)
        m["ml_c0"] = np.asarray(inp["state_mlstm_c"][0, 16 * c:16 * c + 16], f32)
        m["ml_n0"] = np.asarray(inp["state_mlstm_n"][0, 16 * c:16 * c + 16], f32)
        m["ml_m0"] = np.asarray(inp["state_mlstm_m"][0, 16 * c:16 * c + 16], f32)
        m["ml_conv0"] = np.asarray(inp["state_mlstm_conv"][0, 16 * c:16 * c + 16], f32).reshape(N_S * 3, E)
        m["hmask"] = np.repeat(np.eye(8, dtype=f32), 128, axis=1)
        m["ones8"] = np.ones((8, 128), f32)
        m["cm64"] = (np.arange(64)[:, None] <= np.arange(64)[None, :]).astype(f32)
        m["seqsel"] = (np.arange(128)[:, None] // 8 == np.arange(N_S)[None, :]).astype(f32)
        ii = np.arange(128)
        m["smask"] = ((ii[:, None] // 8 == ii[None, :] // 8) & (ii[:, None] % 8 <= ii[None, :] % 8)).astype(f32)
        m["cmask"] = np.where(np.arange(128)[None, :] <= np.arange(128)[:, None], 0.0, -1e30).astype(f32)
        m["cmask_s"] = np.where(np.arange(8)[None, :] <= (np.arange(128) % 8)[:, None], 0.0, -1e30).astype(f32)
        m["selm"] = (np.arange(64)[:, None] % 8 == np.arange(8)[None, :]).astype(f32)
        m["blockm"] = (np.arange(128)[:, None] // 8 == np.arange(128)[None, :] // 8).astype(f32)
        m["iota_p"] = np.arange(128, dtype=f32).reshape(128, 1)
        m["page_table"] = np.asarray(inp["page_table"][16 * c:16 * c + 16], np.int32).reshape(1, N_S * 16)
        m["cache_k"] = np.asarray(inp["cache_k"], f32).reshape(2560 * 128, 256)
        m["cache_v"] = np.asarray(inp["cache_v"], f32).reshape(2560 * 128, 256)
        m["cache_kidx"] = np.asarray(inp["cache_kidx"], f32).reshape(2560 * 128, 64)
        maps.append({k: np.ascontiguousarray(v) for k, v in m.items() if k in kb.inputs})
    return maps


def rope_table():
    half = 32
    inv = (np.float32(10000.0) ** (-np.arange(half, dtype=np.float32) / np.float32(half))).astype(np.float32)
    pos = np.concatenate([np.arange(T_P), np.tile(2048 + np.arange(8), N_S)]).astype(np.float32)
    ang = (pos[:, None] * inv[None, :]).astype(np.float32)
    return np.concatenate([np.cos(ang), np.sin(ang)], axis=1).astype(np.float32)


def kernel(**inp):
    kb = _get_prog(4)
    maps = make_in_maps(inp, kb)
    res = run_bass_kernel_spmd(kb.nc, maps, core_ids=list(range(8))).results
    return assemble(res)


def assemble(res):
    f32 = np.float32

    def A(x):
        return np.ascontiguousarray(np.asarray(x, f32))
    P4 = range(4)
    C8 = range(8)
    y_p = A(np.stack([res[b]["y"][:T_P] for b in P4]))
    y_s = A(np.concatenate([res[c]["y"][T_P:].reshape(N_S, 8, D_MODEL) for c in C8]))
    sre_p = A(np.stack([res[b]["o_sre_p"].reshape(2, 128, 64) for b in P4], axis=1))
    sim_p = A(np.stack([res[b]["o_sim_p"].reshape(2, 128, 64) for b in P4], axis=1))
    sre_s = A(np.concatenate([res[c]["o_sre_s"].reshape(2, N_S, 128, 64) for c in C8], axis=1))
    sim_s = A(np.concatenate([res[c]["o_sim_s"].reshape(2, N_S, 128, 64) for c in C8], axis=1))
    k_p = A(np.stack([res[b]["o_k_p"].reshape(T_P, 4, 64) for b in P4]))[None]
    v_p = A(np.stack([res[b]["o_v_p"].reshape(T_P, 4, 64) for b in P4]))[None]
    ki_p = A(np.stack([res[b]["o_kidx_p"].reshape(T_P, 64) for b in P4]))[None]
    k_s = A(np.concatenate([res[c]["o_k_s"].reshape(N_S, 8, 4, 64) for c in C8]))[None]
    v_s = A(np.concatenate([res[c]["o_v_s"].reshape(N_S, 8, 4, 64) for c in C8]))[None]
    ki_s = A(np.concatenate([res[c]["o_kidx_s"].reshape(N_S, 8, 64) for c in C8]))[None]
    mc_p = A(np.stack([res[b]["o_mc_p"].reshape(8, 256, 256) for b in P4]))[None]
    mn_p = A(np.stack([res[b]["o_mn_p"].reshape(8, 256) for b in P4]))[None]
    mm_p = A(np.stack([res[b]["o_mm_p"].reshape(8) for b in P4]))[None]
    mv_p = A(np.stack([res[b]["o_mconv_p"].reshape(3, E) for b in P4]))[None]
    mc_s = A(np.concatenate([res[c]["o_mc_s"].reshape(N_S, 8, 256, 256) for c in C8]))[None]
    mn_s = A(np.concatenate([res[c]["o_mn_s"].reshape(N_S, 8, 256) for c in C8]))[None]
    mm_s = A(np.concatenate([res[c]["o_mm_s"].reshape(8, N_S).T for c in C8]))[None]
    mv_s = A(np.concatenate([res[c]["o_mconv_s"].reshape(N_S, 3, E) for c in C8]))[None]
    return (y_p, y_s, sre_p, sim_p, sre_s, sim_s, k_p, v_p, ki_p, k_s, v_s, ki_s,
            mc_p, mn_p, mm_p, mv_p, mc_s, mn_s, mm_s, mv_s)
```
